# Optimizing a Trainium2 kernel written in Bass

```python
import jax, jax.numpy as jnp
from jax import lax
import numpy as np

D_MODEL = 1024
BATCH = 8
SEQ = 2048
DEPTH = 1
DEC_BATCH = 128
DEC_SEQ = 8
PAST_LEN = 16384
PAGE_SIZE = 128

MIX_WIDTH = D_MODEL
CONV_WIDTH = MIX_WIDTH // 2
RWKV_WIDTH = MIX_WIDTH - CONV_WIDTH
HEAD_SIZE = 64
N_HEADS = RWKV_WIDTH // HEAD_SIZE
CONV_K = 3
D_DECAY_LORA = 64
D_AAA_LORA = 64
D_GATE_LORA = 128
D_FF = 4 * D_MODEL
RWKV_COLS = 3 * RWKV_WIDTH + D_DECAY_LORA + D_AAA_LORA + D_GATE_LORA
IN_COLS = 3 * CONV_WIDTH + RWKV_COLS
RMS_EPS = 1e-6
GN_EPS = 64e-5
NORM_EPS = 1e-12

kernel_name = "hymba_shortconv_rwkv7_step"


def _rmsnorm(x, g):
    xf = x.astype(jnp.float32)
    y = xf * lax.rsqrt(jnp.mean(xf * xf, axis=-1, keepdims=True) + RMS_EPS) * g.astype(jnp.float32)
    return y.astype(x.dtype)


def _short_conv(u, buf, conv_w):
    t_len = u.shape[1]
    full = jnp.concatenate([buf.astype(u.dtype), u], axis=1)
    out = sum(full[:, j:j + t_len] * conv_w[j] for j in range(CONV_K))
    return out, full[:, -(CONV_K - 1):]


def _wkv_scan(s0, r, decay, k, v, kk, a):
    def step(s, inp):
        r_t, w_t, k_t, v_t, kk_t, a_t = inp
        s_kk = jnp.einsum('bhvk,bhk->bhv', s, kk_t)
        s = (s * w_t[:, :, None, :]
             - s_kk[..., None] * (kk_t * a_t)[:, :, None, :]
             + v_t[..., None] * k_t[:, :, None, :])
        y_t = jnp.einsum('bhvk,bhk->bhv', s, r_t)
        return s, y_t
    xs = tuple(jnp.swapaxes(t, 0, 1) for t in (r, decay, k, v, kk, a))
    s_final, ys = lax.scan(step, s0, xs)
    return jnp.swapaxes(ys, 0, 1), s_final


def _layer(x, conv_buf, shift_buf, wkv_state, norm1_g, w_in, conv_w, mu, w0, w2, a0, a2, g2,
           k_k, k_a, r_k, lnx_g, lnx_b, w_out, norm2_g, w_ff1, w_ff2):
    bsz, t_len, _ = x.shape
    f32 = jnp.float32
    h = _rmsnorm(x, norm1_g)
    proj = h @ w_in
    c_b, c_c, c_x, p_rw = jnp.split(proj, [CONV_WIDTH, 2 * CONV_WIDTH, 3 * CONV_WIDTH], axis=-1)

    conv_out, new_conv = _short_conv(c_c * c_x, conv_buf, conv_w)
    y_conv = c_b * conv_out

    prev = jnp.concatenate([shift_buf[:, None, :].astype(p_rw.dtype), p_rw[:, :-1]], axis=1)
    xm = (p_rw + (prev - p_rw) * mu).astype(f32)
    new_shift = p_rw[:, -1]
    r, k, v, pw, pa, pg = jnp.split(
        xm, [RWKV_WIDTH, 2 * RWKV_WIDTH, 3 * RWKV_WIDTH, 3 * RWKV_WIDTH + D_DECAY_LORA,
             3 * RWKV_WIDTH + D_DECAY_LORA + D_AAA_LORA], axis=-1)
    w = -jax.nn.softplus(-(w0.astype(f32) + jnp.tanh(pw) @ w2.astype(f32))) - 0.5
    decay = jnp.exp(-jnp.exp(w))
    a = jax.nn.sigmoid(a0.astype(f32) + pa @ a2.astype(f32))
    g = jax.nn.sigmoid(pg) @ g2.astype(f32)

    def heads(t):
        return t.reshape(bsz, t_len, N_HEADS, HEAD_SIZE)
    r, k, v, decay, a = heads(r), heads(k), heads(v), heads(decay), heads(a)
    kk = k * k_k.astype(f32).reshape(N_HEADS, HEAD_SIZE)
    kk = kk / jnp.maximum(jnp.sqrt(jnp.sum(kk * kk, axis=-1, keepdims=True)), NORM_EPS)
    k = k * (1.0 + (a - 1.0) * k_a.astype(f32).reshape(N_HEADS, HEAD_SIZE))
    y, new_wkv = _wkv_scan(wkv_state.astype(f32), r, decay, k, v, kk, a)
    mean = jnp.mean(y, axis=-1, keepdims=True)
    var = jnp.mean(jnp.square(y - mean), axis=-1, keepdims=True)
    y = ((y - mean) * lax.rsqrt(var + GN_EPS)).reshape(bsz, t_len, RWKV_WIDTH)
    y = y * lnx_g.astype(f32) + lnx_b.astype(f32)
    bonus = jnp.sum(r * k * r_k.astype(f32), axis=-1, keepdims=True) * v
    y_rwkv = ((y + bonus.reshape(bsz, t_len, RWKV_WIDTH)) * g).astype(x.dtype)

    x = x + jnp.concatenate([y_conv, y_rwkv], axis=-1) @ w_out
    h2 = _rmsnorm(x, norm2_g)
    x = x + jnp.square(jax.nn.relu(h2 @ w_ff1)) @ w_ff2
    return x, new_conv, new_shift, new_wkv.astype(x.dtype)


def setup_inputs(seed: int = 0) -> dict:
    key = jax.random.key(seed)
    ks = jax.random.split(key, 32)
    f32 = jnp.float32
    nrm = lambda k, shape, s: jax.random.normal(k, shape, f32) * s
    return {
        "x_prompt": nrm(ks[0], (BATCH, SEQ, D_MODEL), 1.0),
        "x_sample": nrm(ks[1], (DEC_BATCH, DEC_SEQ, D_MODEL), 1.0),
        "state_conv": nrm(ks[2], (DEPTH, DEC_BATCH, CONV_K - 1, CONV_WIDTH), 1.0),
        "state_shift": nrm(ks[3], (DEPTH, DEC_BATCH, RWKV_COLS), 1.0),
        "state_wkv": nrm(ks[4], (DEPTH, DEC_BATCH, N_HEADS, HEAD_SIZE, HEAD_SIZE), 0.1),
        "norm1_g": 1.0 + nrm(ks[5], (DEPTH, D_MODEL), 0.02),
        "w_in": nrm(ks[6], (DEPTH, D_MODEL, IN_COLS), D_MODEL ** -0.5),
        "conv_w": nrm(ks[7], (DEPTH, CONV_K, CONV_WIDTH), CONV_K ** -0.5),
        "mu": jax.random.uniform(ks[8], (DEPTH, RWKV_COLS), f32, 0.0, 1.0),
        "w0": jax.random.uniform(ks[9], (DEPTH, RWKV_WIDTH), f32, -4.0, 0.0),
        "w2": nrm(ks[10], (DEPTH, D_DECAY_LORA, RWKV_WIDTH), 0.1 * D_DECAY_LORA ** -0.5),
        "a0": nrm(ks[11], (DEPTH, RWKV_WIDTH), 0.1),
        "a2": nrm(ks[12], (DEPTH, D_AAA_LORA, RWKV_WIDTH), 0.1 * D_AAA_LORA ** -0.5),
        "g2": nrm(ks[13], (DEPTH, D_GATE_LORA, RWKV_WIDTH), D_GATE_LORA ** -0.5),
        "k_k": 0.85 + nrm(ks[14], (DEPTH, RWKV_WIDTH), 0.02),
        "k_a": 1.0 + nrm(ks[15], (DEPTH, RWKV_WIDTH), 0.02),
        "r_k": nrm(ks[16], (DEPTH, N_HEADS, HEAD_SIZE), 0.1),
        "lnx_g": 1.0 + nrm(ks[17], (DEPTH, RWKV_WIDTH), 0.02),
        "lnx_b": nrm(ks[18], (DEPTH, RWKV_WIDTH), 0.02),
        "w_out": nrm(ks[19], (DEPTH, MIX_WIDTH, D_MODEL), MIX_WIDTH ** -0.5),
        "norm2_g": 1.0 + nrm(ks[20], (DEPTH, D_MODEL), 0.02),
        "w_ff1": nrm(ks[21], (DEPTH, D_MODEL, D_FF), D_MODEL ** -0.5),
        "w_ff2": nrm(ks[22], (DEPTH, D_FF, D_MODEL), D_FF ** -0.5),
        "normf_g": 1.0 + nrm(ks[23], (D_MODEL,), 0.02),
    }


def reference(x_prompt, x_sample, state_conv, state_shift, state_wkv, norm1_g, w_in, conv_w, mu,
              w0, w2, a0, a2, g2, k_k, k_a, r_k, lnx_g, lnx_b, w_out, norm2_g, w_ff1, w_ff2,
              normf_g):
    dt = x_prompt.dtype
    bp = x_prompt.shape[0]
    hp, hs = x_prompt, x_sample
    conv_p, shift_p, wkv_p, conv_s, shift_s, wkv_s = [], [], [], [], [], []
    for layer in range(DEPTH):
        weights = (norm1_g[layer], w_in[layer], conv_w[layer], mu[layer], w0[layer], w2[layer],
                   a0[layer], a2[layer], g2[layer], k_k[layer], k_a[layer], r_k[layer],
                   lnx_g[layer], lnx_b[layer], w_out[layer], norm2_g[layer], w_ff1[layer],
                   w_ff2[layer])
        hp, c_p, s_p, m_p = _layer(
            hp, jnp.zeros((bp, CONV_K - 1, CONV_WIDTH), dt), jnp.zeros((bp, RWKV_COLS), dt),
            jnp.zeros((bp, N_HEADS, HEAD_SIZE, HEAD_SIZE), dt), *weights)
        hs, c_s, s_s, m_s = _layer(hs, state_conv[layer], state_shift[layer], state_wkv[layer],
                                   *weights)
        conv_p.append(c_p); shift_p.append(s_p); wkv_p.append(m_p)
        conv_s.append(c_s); shift_s.append(s_s); wkv_s.append(m_s)
    y_prompt = _rmsnorm(hp, normf_g)
    y_sample = _rmsnorm(hs, normf_g)
    return (y_prompt, y_sample, jnp.stack(conv_p), jnp.stack(shift_p), jnp.stack(wkv_p),
            jnp.stack(conv_s), jnp.stack(shift_s), jnp.stack(wkv_s))
```

```python
import os
import numpy as np
import concourse.bass as bass
import concourse.mybir as mybir
from concourse.bass_utils import run_bass_kernel_spmd

F32 = mybir.dt.float32
BF16 = mybir.dt.bfloat16
ALU = mybir.AluOpType
AF = mybir.ActivationFunctionType
AX = mybir.AxisListType

D = 1024
CW_ = 512
RW = 512
NH = 8
HS = 64
RC = 1792
IC = 3328
DFF = 4096
NSEQ = 16
DT_ = 8
CDEC = float(np.exp(-0.5))


class Op:
    __slots__ = ("eng", "emit", "deps", "dma", "inc", "done", "dom", "alldeps", "cost", "idx", "fin", "succ", "npend", "prio")

    def __init__(self, eng, emit, dma):
        self.eng = eng
        self.emit = emit
        self.dma = dma
        self.deps = []
        self.alldeps = []
        self.cost = 0.0
        self.inc = False
        self.done = None
        self.dom = (eng + "_dma") if dma else eng


class Prog:
    NSLOT = {"sp": 24, "pool": 6, "act": 4}

    def __init__(self, nc, self_sync=False):
        self.nc = nc
        self.ops = []
        self.lastw = {}
        self.readers = {}
        self.self_sync = self_sync
        self.ndma = {}
        self.lastslot = {}

    COST = {"pe": 0.12, "act": 0.6, "dve": 0.7, "pool": 1.3}

    def op(self, eng, emit, reads=(), writes=(), dma=False, cost=None):
        o = Op(eng, emit, dma)
        o.cost = cost if cost is not None else (3.0 if dma else self.COST[eng])
        deps = {}
        if dma:
            n = self.ndma.get(eng, 0)
            self.ndma[eng] = n + 1
            o.dom = f"{eng}_dma{n % self.NSLOT[eng]}"
            prev = self.lastslot.get(o.dom)
            if prev is not None:
                deps[id(prev)] = (prev, True)
            self.lastslot[o.dom] = o
        for k in reads:
            w = self.lastw.get(k)
            if w is not None:
                deps[id(w)] = (w, True)
        for k in writes:
            w = self.lastw.get(k)
            if w is not None:
                deps[id(w)] = (w, True)
            for r in self.readers.get(k, ()):
                if id(r) not in deps:
                    deps[id(r)] = (r, False)
        for k in reads:
            self.readers.setdefault(k, []).append(o)
        for k in writes:
            self.lastw[k] = o
            self.readers[k] = []
        for d, hard in deps.values():
            if d is o:
                continue
            o.alldeps.append(d)
            if d.dom == o.dom and not o.dma:
                if o.eng != "pe":
                    o.deps.append(d)
                continue
            o.deps.append(d)
        self.ops.append(o)
        return o

    def pe(self, emit, r=(), w=()):
        return self.op("pe", emit, r, w)

    def act(self, emit, r=(), w=()):
        return self.op("act", emit, r, w)

    def dve(self, emit, r=(), w=()):
        return self.op("dve", emit, r, w)

    def pool(self, emit, r=(), w=()):
        return self.op("pool", emit, r, w)

    def dma(self, eng, emit, r=(), w=()):
        return self.op(eng, emit, r, w, dma=True)

    def barrier(self):
        lastdom = {}
        for o in self.ops:
            if o.emit is not None:
                lastdom[o.dom] = o
        for eng in ("pe", "act", "dve", "pool", "sp"):
            b = Op(eng, None, False)
            b.deps = [o for dom, o in lastdom.items() if dom != eng]
            self.ops.append(b)
        self.lastw = {}
        self.readers = {}

    def reschedule(self, hop=float(os.environ.get("KHOP", "1.0"))):
        segs, cur = [], []
        for o in self.ops:
            if o.emit is None:
                segs.append(cur)
                segs.append([o])
                cur = []
            else:
                cur.append(o)
        segs.append(cur)
        new_ops = []
        for seg in segs:
            if len(seg) <= 1 or seg[0].emit is None:
                new_ops.extend(seg)
                continue
            inseg = {id(o) for o in seg}
            for o in seg:
                o.succ = []
                o.fin = 0.0
            fixed = set(os.environ.get("KFIXED", "pe").split(","))
            lastfixed = {}
            for o in seg:
                o.npend = 0
                for d in o.alldeps:
                    if id(d) in inseg:
                        d.succ.append(o)
                        o.npend += 1
                if o.eng in fixed and not o.dma:
                    p_ = lastfixed.get(o.eng)
                    if p_ is not None and all(p_ is not d for d in o.alldeps):
                        p_.succ.append(o)
                        o.npend += 1
                    lastfixed[o.eng] = o
            for o in reversed(seg):
                o.prio = o.cost + max([x.prio + (hop if x.eng != o.eng else 0.0) for x in o.succ], default=0.0)
            ready = {}
            for o in seg:
                if o.npend == 0:
                    ready.setdefault(o.eng, []).append(o)
            cursor = {}
            order = []
            nleft = len(seg)
            while nleft:
                best = None
                for eng, lst in ready.items():
                    if not lst:
                        continue
                    cur_t = cursor.get(eng, 0.0)
                    for o in lst:
                        est = cur_t
                        for d in o.alldeps:
                            if id(d) in inseg:
                                t = d.fin + (hop if d.eng != o.eng or d.dma != o.dma else 0.0)
                                if t > est:
                                    est = t
                        key = (est, -o.prio)
                        if best is None or key < best[0]:
                            best = (key, o, est)
                _, o, est = best
                ready[o.eng].remove(o)
                o.fin = est + o.cost
                cursor[o.eng] = (est + 0.1) if o.dma else o.fin
                order.append(o)
                nleft -= 1
                for x in o.succ:
                    x.npend -= 1
                    if x.npend == 0:
                        ready.setdefault(x.eng, []).append(x)
            new_ops.extend(order)
        self.ops = new_ops

    def finalize(self, final_eng="sp"):
        nc = self.nc
        ops = self.ops
        fin = Op(final_eng, None, False)
        lastdom = {}
        for o in ops:
            if o.emit is not None:
                lastdom[o.dom] = o
        for dom, o in lastdom.items():
            if "_dma" in dom:
                fin.deps.append(o)
        ops = ops + [fin]
        for o in ops:
            if o.dma:
                o.inc = True
            for d in o.deps:
                d.inc = True
        cnt = {}
        for o in ops:
            if o.inc:
                cnt[o.dom] = cnt.get(o.dom, 0) + (16 if o.dma else 1)
                o.done = cnt[o.dom]
        doms = sorted({o.dom for o in ops if o.inc})
        sems = {d: nc.alloc_semaphore(name="s_" + d) for d in doms}
        per_eng = {}
        for o in ops:
            per_eng.setdefault(o.eng, []).append(o)
        engs = {"pe": "tensor", "act": "scalar", "dve": "vector", "pool": "gpsimd", "sp": "sync"}

        def run(engname, e):
            waited = {}
            for o in per_eng.get(engname, []):
                need = {}
                for d in o.deps:
                    if d.done > need.get(d.dom, 0):
                        need[d.dom] = d.done
                for dom, v in need.items():
                    if v > waited.get(dom, 0):
                        e.wait_ge(sems[dom], v)
                        waited[dom] = v
                if o.emit is None:
                    continue
                ins = o.emit(e)
                if o.inc:
                    ins.then_inc(sems[o.dom], 16 if o.dma else 1)

        with nc.Block() as block:
            for engname, attr in engs.items():
                if engname not in per_eng:
                    continue

                def mk(engname=engname):
                    def f(e):
                        run(engname, e)
                    return f
                getattr(block, attr)(mk())
        return dict(nops=len(ops), cnt=cnt)


def _consts():
    c = {}
    idx = np.arange(128)
    c["ident"] = np.eye(128, dtype=np.float32)
    c["id2"] = (idx[:, None] % 64 == np.arange(64)[None, :]).astype(np.float32)
    for ty in ("P", "S"):
        if ty == "P":
            blk = np.zeros(128, np.int64)
        else:
            blk = idx // DT_
        same = blk[:, None] == blk[None, :]
        s = idx[:, None]
        t = idx[None, :]
        lt = (same & (s < t)).astype(np.float32)
        le = (same & (s <= t)).astype(np.float32)
        gt = (same & (s > t)).astype(np.float32)
        c["tri" + ty] = (-CDEC) * le
        c["tgt" + ty] = (-CDEC) * gt
        c["ma" + ty] = np.concatenate([-lt, le, -lt, le], 1)
        c["mb" + ty] = np.concatenate([lt, le, lt, le], 1)
        c["mt" + ty] = np.concatenate([-gt] * 4, 1)
    c["segP"] = np.full((128, 1), -CDEC, np.float32)
    bm = (idx[:, None] // DT_ == np.arange(NSEQ)[None, :]).astype(np.float32)
    c["segS"] = (-CDEC) * bm
    c["bm"] = bm
    return c


CONST_SHAPES = {k: v.shape for k, v in _consts().items()}


def build(NPT=16, dbg=None, _dry=False, _order=None):
    if not _dry and _order is None:
        _order = build(NPT, None, _dry=True)
    STOP = os.environ.get('KSTOP', '')
    GI_MODE = os.environ.get('KGI', 'td')
    SKIP2 = os.environ.get('KSKIP2', '') == '1'
    NT = NPT + 1
    nc = bass.Bass("TRN2", target_bir_lowering=False)
    P = Prog(nc)
    din = {}

    def DI(name, shape):
        din[name] = nc.dram_tensor(name, list(shape), F32, kind="ExternalInput").ap()
        return din[name]

    def DO(name, shape):
        return nc.dram_tensor(name, list(shape), F32, kind="ExternalOutput").ap()

    xp = DI("xp", [NPT * 128, D])
    xs = DI("xs", [128, D])
    stc = DI("stc", [NSEQ, 2, CW_])
    sts = DI("sts", [NSEQ, RC])
    stw = DI("stw", [NSEQ, NH, HS, HS])
    w_in = DI("w_in", [D, IC])
    w_out = DI("w_out", [D, D])
    w_ff1 = DI("w_ff1", [D, DFF])
    w_ff2 = DI("w_ff2", [DFF, D])
    g1T_d = DI("g1T", [128, 8])
    g2T_d = DI("g2T", [128, 8])
    mu_d = DI("mu", [1, RC])
    cw_d = DI("convw", [1, 3 * CW_])
    kk_d = DI("k_k", [1, RW])
    ka_d = DI("k_a", [1, RW])
    rk_d = DI("r_k", [1, RW])
    lg_d = DI("lnx_g", [1, RW])
    lb_d = DI("lnx_b", [1, RW])
    nf_d = DI("normf", [1, D])
    w0_d = DI("w0", [1, RW])
    a0_d = DI("a0", [1, RW])
    w2a_d = DI("w2a", [128, RW])
    g2_d = DI("g2", [128, RW])
    cd = {k: DI("c_" + k, shp) for k, shp in CONST_SHAPES.items()}

    y_p = DO("y_p", [NPT * 128, D])
    y_s = DO("y_s", [128, D])
    conv_p = DO("conv_p", [2, CW_])
    shift_p = DO("shift_p", [1, RC])
    wkv_p = DO("wkv_p", [NH, HS, HS])
    conv_s = DO("conv_s", [NSEQ, 2, CW_])
    shift_s = DO("shift_s", [NSEQ, RC])
    wkv_s = DO("wkv_s", [NSEQ, NH, HS, HS])
    x1s = nc.dram_tensor("x1s", [NT * 128, D], F32).ap()
    wbi = nc.dram_tensor("wbi", [7, 128, 8, 512], BF16).ap()
    wbo = nc.dram_tensor("wbo", [2, 128, 8, 512], BF16).ap()
    bnd_p = nc.dram_tensor("bnd_p", [NT, RC], F32).ap()
    bnd_c = nc.dram_tensor("bnd_c", [NT, 2, CW_], F32).ap()
    dbg_out = {}
    if dbg:
        for name, shape in dbg.items():
            if not name.startswith("_"):
                dbg_out[name] = DO("dbg_" + name, shape)

    from contextlib import ExitStack

    with ExitStack() as st0:
        TW = [st0.enter_context(nc.sbuf_tensor(f"TW{i}", [128, IC], BF16)) for i in range(2)]
        n0 = 0
        for (src_w, dst_w, wd) in ((w_in, wbi, IC), (w_out, wbo, D)):
            for k in range(8):
                tw, twk = TW[n0 % 2], f"TW{n0 % 2}"
                n0 += 1
                P.dma("pool", lambda e, tw=tw, src_w=src_w, k=k, wd=wd: e.dma_start(out=tw[:, 0:wd], in_=src_w[k * 128:(k + 1) * 128, :]), w=[twk])
                nfull = wd // 512
                P.dma("sp", lambda e, tw=tw, dst_w=dst_w, k=k, nfull=nfull: e.dma_start(out=dst_w[0:nfull, :, k, :].rearrange("j p n -> p j n"),
                                                                                  in_=tw[:, 0:nfull * 512].rearrange("p (j n) -> p j n", n=512)), r=[twk], w=["wb"])
                if wd % 512:
                    P.dma("sp", lambda e, tw=tw, dst_w=dst_w, k=k, nfull=nfull, wd=wd: e.dma_start(out=dst_w[nfull, :, k, 0:wd - nfull * 512], in_=tw[:, nfull * 512:wd]), r=[twk], w=["wb"])
    P.barrier()

    with ExitStack() as st1:
        def SB(name, shape, dt=F32):
            return st1.enter_context(nc.sbuf_tensor(name, list(shape), dt))

        def PSF(name, shape, dt=F32):
            return st1.enter_context(nc.psum_tensor(name, list(shape), dt))

        NB = 5
        PS = [PSF(f"ps{i}", [128, 512]) for i in range(NB)]
        PSY = PSF("psy", [128, 512])
        PSS = PSF("pss", [128, 512])
        PSB = PSF("psb", [128, 1024], BF16)
        bank_i = [0]

        def nb():
            i = bank_i[0] % NB
            bank_i[0] += 1
            return PS[i], f"ps{i}"

        NRING = 3
        WR = [SB(f"WR{i}", [128, 8, 512], BF16) for i in range(NRING)]
        wq = list(_order) if _order is not None else []
        wsrc = {"wbi": wbi, "wbo": wbo}
        wstate = {"issued": 0, "used": 0}

        def wchunk(srcw, c0, wd):
            name = "wbi" if srcw is wbi else "wbo"
            n = wstate["used"]
            wstate["used"] += 1
            if _dry:
                wq.append((name, c0, wd))
                return WR[n % NRING], f"WR{n % NRING}"
            assert wq[n] == (name, c0, wd), (n, wq[n], name, c0, wd)
            while wstate["issued"] < min(len(wq), n + NRING):
                m = wstate["issued"]
                nm2, c2, wd2 = wq[m]
                P.dma("sp", lambda e, m=m, nm2=nm2, c2=c2, wd2=wd2: e.dma_start(out=WR[m % NRING][:, :, 0:wd2], in_=wsrc[nm2][c2 // 512, :, :, 0:wd2]),
                      w=[f"WR{m % NRING}"])
                wstate["issued"] += 1
            return WR[n % NRING], f"WR{n % NRING}"

        def bc_tile(name, dram, n):
            t = SB(name, [128, n])
            P.dma("sp", lambda e: e.dma_start(out=t[:], in_=dram.partition_broadcast(128)), w=[name])
            return t

        def ld_tile(name, dram, shape, dt=F32, eng="sp"):
            t = SB(name, shape, dt)
            P.dma(eng, lambda e: e.dma_start(out=t[:], in_=dram), w=[name])
            return t

        IDN = ld_tile("IDN", cd["ident"], [128, 128])
        IDB = ld_tile("IDB", cd["ident"], [128, 128], BF16, eng="pool")
        ID2 = ld_tile("ID2", cd["id2"], [128, 64])
        G1T = ld_tile("G1T", g1T_d, [128, 8])
        MU = bc_tile("MU", mu_d, RC)
        CWt = bc_tile("CWt", cw_d, 3 * CW_)
        KKb = bc_tile("KKb", kk_d, RW)
        KAb = bc_tile("KAb", ka_d, RW)
        RKb = bc_tile("RKb", rk_d, RW)
        LGb = bc_tile("LGb", lg_d, RW)
        LBb = bc_tile("LBb", lb_d, RW)
        W0r = ld_tile("W0r", w0_d, [1, RW])
        A0r = ld_tile("A0r", a0_d, [1, RW])
        W2A = ld_tile("W2A", w2a_d, [128, RW])
        G2 = ld_tile("G2", g2_d, [128, RW])
        ONE1 = SB("ONE1", [1, 128])
        P.pool(lambda e: e.memset(ONE1[:], 1.0), w=["ONE1"])
        MKT = {}
        for nm in ("tri", "tgt"):
            MKT[nm] = ld_tile("M_" + nm, cd[nm + "P"], [128, 128])
        for nm in ("ma", "mb", "mt"):
            MKT[nm] = ld_tile("M_" + nm, cd[nm + "P"], [128, 512])
        MK = {}
        for ty in ("P", "S"):
            for nm in ("tri", "tgt", "ma", "mb", "mt"):
                MK[nm + ty] = MKT[nm]
        SEGP = ld_tile("SEGP", cd["segP"], [128, 1])
        SEGS = ld_tile("SEGS", cd["segS"], [128, NSEQ])
        BM = ld_tile("BM", cd["bm"], [128, NSEQ])

        X = [SB(f"X{i}", [128, D]) for i in range(2)]
        SS = SB("SS", [128, 1])
        RS = SB("RS", [128, 1])
        HB = SB("HB", [128, D], BF16)
        HT = SB("HT", [128, 8, 128], BF16)
        PR0 = SB("PR0", [128, IC])
        PR = [PR0, PR0]
        CX0 = SB("CX0", [128, CW_])
        CX = [CX0, CX0]
        CONV4 = SB("CONV4", [128, 4 * CW_])
        SH1, SH2, CT, CT2 = (CONV4[:, i * 512:(i + 1) * 512] for i in range(4))
        YCs = [SB(f"YC{i}", [128, D], BF16) for i in range(2)]
        X1T = SB("X1T", [128, D])
        JUNK = CONV4[:, 0:1024]
        YCTb = SB("YCTb", [128, 8, 128], BF16)
        PV = SB("PV", [128, RC])
        XM = PV
        LOR = SB("LOR", [128, 256])
        LT = SB("LT", [128, 2, 128])
        SG = SB("SG", [128, RW])
        YYb = SB("YYb", [128, RW])
        GNS = SB("GNS", [128, RW])
        S8c = SB("S8c", [128, 8])
        S8d = SB("S8d", [128, 8])
        S8e = SB("S8e", [128, 8])
        AA = SB("AA", [128, RW])
        GGs = [SB(f"GG{i}", [128, RW]) for i in range(2)]
        EXP4 = SB("EXP4", [128, 4 * RW])
        EIN, EINV, EEX, EEND = (EXP4[:, i * 512:(i + 1) * 512] for i in range(4))
        BTm = SB("BTm", [128, RW], BF16)
        KTl = SB("KTl", [128, RW], BF16)
        KS2 = SB("KS2", [128, 2 * RW])
        KK0, SQ = KS2[:, 0:512], KS2[:, 512:1024]
        S8 = SB("S8", [128, 8])
        S8b = SB("S8b", [128, 8])
        BB = SB("BB", [128, RW])
        KP = SB("KP", [128, RW])
        BEs = [SB(f"BEb{i}", [128, RW], BF16) for i in range(2)]
        KEs = [SB(f"KEb{i}", [128, RW], BF16) for i in range(2)]
        VBs = [SB(f"VB{i}", [128, RW], BF16) for i in range(2)]
        BONs = [SB(f"BON{i}", [128, RW]) for i in range(2)]
        RTms = [SB(f"RTm{i}", [128, RW], BF16) for i in range(2)]
        KTMs = [SB(f"KTM{i}", [128, RW], BF16) for i in range(2)]
        KRs = [SB(f"KR{i}", [128, 4, 2, 128], BF16) for i in range(2)]
        BFs = [SB(f"BF{i}", [128, 4, 128], BF16) for i in range(2)]
        KFs = [SB(f"KF{i}", [128, 4, 128], BF16) for i in range(2)]
        PCs = [SB(f"PC{i}", [128, 4]) for i in range(2)]
        PCS = SB("PCS", [64, NH, NSEQ])
        GA = [SB(f"GA{g}", [128, 4, 2, 128], BF16) for g in range(2)]
        GB = [SB(f"GB{g}", [128, 4, 2, 128], BF16) for g in range(2)]
        GT = [SB(f"GT{g}", [128, 4, 128], BF16) for g in range(2)]
        RN = [[SB(f"RN{g}{i}", [128, 4, 128], BF16) for i in range(2)] for g in range(2)]
        RTN = [[SB(f"RTN{g}{i}", [128, 4, 128], BF16) for i in range(2)] for g in range(2)]
        TT = [[SB(f"TT{g}{i}", [128, 4, 128], BF16) for i in range(2)] for g in range(2)]
        X1N = [SB(f"X1N{g}", [128, 4, 64], BF16) for g in range(2)]
        UK = [SB(f"UK{g}", [128, 4, 2, 64], BF16) for g in range(2)]
        RHb = SB("RHb", [128, 2, 2, 128])
        RH = [RHb[:, 0, :, :], RHb[:, 1, :, :]]
        RHS = RHb[0:64, :, :, :].rearrange("p a b t -> p (a b) t")
        MM = [SB(f"MM{g}", [128, 2, 64]) for g in range(2)]
        STt = [SB(f"ST{i}", [128, 4, 64]) for i in range(2)]
        S0 = PR0[:, 0:1024].rearrange("p (b k) -> p b k", k=64)
        S0T = EXP4[0:64, :].rearrange("p (b t) -> p b t", t=128)
        RHm = CONV4[0:64, :].rearrange("p (b t) -> p b t", t=128)
        KHm = PR0[:, 1024:1536].bitcast(BF16).rearrange("p (b k) -> p b k", k=64)
        UTm = PR0[:, 1536:2048].bitcast(BF16).rearrange("p (b k) -> p b k", k=64)
        Vm = PR0[:, 2048:2560].bitcast(BF16).rearrange("p (b k) -> p b k", k=64)
        DPC = PV[0:64, 0:1024].rearrange("p (b k) -> p b k", k=64)
        MS = KS2[0:64, :].rearrange("p (b k) -> p b k", k=64)
        WPO = RHb[0:64, :, :, :].rearrange("p a b t -> p (a b) t")
        SO = [DPC, DPC]

        def v3(ap, k=64):
            return ap.rearrange("p (h k) -> p h k", k=k)

        def dbgdump(name, ap, key):
            if name in dbg_out:
                P.dma("sp", lambda e: e.dma_start(out=dbg_out[name], in_=ap), r=[key])

        def front(ti):
            ty = "S" if ti == NPT else "P"
            first = (ti == 0)
            lastp = (ti == NPT - 1)
            par = ti % 2
            K = lambda n: f"{n}#{par}"
            x = X[par]
            xk = f"X{par}"
            YC, GG, BON, KR, BF, KF, VB = YCs[par], GGs[par], BONs[par], KRs[par], BFs[par], KFs[par], VBs[par]
            KTM, RTm, BE, KE, PC = KTMs[par], RTms[par], BEs[par], KEs[par], PCs[par]
            src = xs if ty == "S" else xp[ti * 128:(ti + 1) * 128, :]
            P.dma("sp", lambda e: e.dma_start(out=x[:], in_=src), w=[xk])
            P.act(lambda e: e.activation(out=JUNK[:], in_=x[:], func=AF.Square, accum_out=SS[:]), r=[xk], w=["SH1", "SH2", "SS"])
            P.dve(lambda e: e.tensor_scalar(out=RS[:], in0=SS[:], scalar1=1.0 / D, scalar2=1e-6, op0=ALU.mult, op1=ALU.add), r=["SS"], w=["RS"])
            P.act(lambda e: e.activation(out=RS[:], in_=RS[:], func=AF.Sqrt), r=["RS"], w=["RS"])
            P.dve(lambda e: e.reciprocal(out=RS[:], in_=RS[:]), r=["RS"], w=["RS"])
            P.act(lambda e: e.activation(out=HB[:], in_=x[:], func=AF.Copy, scale=RS[:, 0:1]), r=[xk, "RS"], w=["HB"])
            for k in range(8):
                P.pe(lambda e, k=k: e.transpose(PSB[:, k * 128:(k + 1) * 128], HB[:, k * 128:(k + 1) * 128], IDB[:]), r=["HB", "IDB"], w=["psb"])
            P.dve(lambda e: e.tensor_tensor(out=HT[:], in0=PSB[:].rearrange("p (k t) -> p k t", t=128),
                                            in1=G1T[:].unsqueeze(2).broadcast_to([128, 8, 128]), op=ALU.mult), r=["psb", "G1T"], w=["HT"])
            pr = PR[par]

            def proj_chunks(js):
                for j in js:
                    wd = 512 if j < 6 else 256
                    bk, bkk = nb()
                    wr, wrk = wchunk(wbi, j * 512, wd)
                    for k in range(8):
                        P.pe(lambda e, k=k, wd=wd, bk=bk, wr=wr: e.matmul(bk[:, 0:wd], lhsT=HT[:, k, :], rhs=wr[:, k, 0:wd], start=(k == 0), stop=(k == 7)),
                             r=["HT", wrk], w=[bkk])
                    P.act(lambda e, j=j, wd=wd, bk=bk: e.copy(out=pr[:, j * 512:j * 512 + wd], in_=bk[:, 0:wd]), r=[bkk], w=["PRb" if j >= 3 else "PRa"])
                    yield

            yield from proj_chunks((3, 4, 5, 6))
            yield
            prw = pr[:, 1536:IC]
            P.dma("sp", lambda e: e.dma_start(out=PV[1:113, :], in_=pr[0:112, 1536:IC]), r=["PRb"], w=["XM"])
            P.dma("sp", lambda e: e.dma_start(out=PV[113:128, :], in_=pr[112:127, 1536:IC]), r=["PRb"], w=["XM"])
            if ty == "S":
                P.dma("sp", lambda e: e.dma_start(out=PV[0:128:8, :], in_=sts), w=["XM"])
            elif first:
                P.pool(lambda e: e.memset(PV[0:1, :], 0.0), w=["XM"])
            else:
                P.dma("sp", lambda e: e.dma_start(out=PV[0:1, :], in_=bnd_p[ti - 1:ti, :]), r=["bnd"], w=["XM"])
            P.dve(lambda e: e.tensor_tensor(out=PV[:], in0=PV[:], in1=prw, op=ALU.subtract), r=["XM", "PRb"], w=["XM"])
            P.dve(lambda e: e.tensor_tensor(out=PV[:], in0=PV[:], in1=MU[:], op=ALU.mult), r=["XM", "MU"], w=["XM"])
            P.dve(lambda e: e.tensor_tensor(out=XM[:], in0=PV[:], in1=prw, op=ALU.add), r=["XM", "PRb"], w=["XM"])
            r_ = XM[:, 0:512]
            k_ = XM[:, 512:1024]
            v_ = XM[:, 1024:1536]
            if STOP == 'B':
                return
            if ty == "S":
                P.dma("sp", lambda e: e.dma_start(out=shift_s, in_=pr[7:128:8, 1536:IC]), r=["PRb"])
            else:
                P.dma("sp", lambda e: e.dma_start(out=bnd_p[ti:ti + 1, :], in_=pr[127:128, 1536:IC]), r=["PRb"], w=["bnd"])
            if lastp:
                P.dma("sp", lambda e: e.dma_start(out=shift_p, in_=pr[127:128, 1536:IC]), r=["PRb"])
            yield from proj_chunks((0, 1, 2))
            yield
            cx = CX[par]
            cxk = "CX"
            P.pool(lambda e: e.tensor_tensor(out=cx[:], in0=pr[:, 512:1024], in1=pr[:, 1024:1536], op=ALU.mult), r=["PRa"], w=[cxk])
            P.dma("sp", lambda e: e.dma_start(out=SH1[1:113, :], in_=cx[0:112, :]), r=[cxk], w=["SH1"])
            P.dma("sp", lambda e: e.dma_start(out=SH1[113:128, :], in_=cx[112:127, :]), r=[cxk], w=["SH1"])
            P.dma("sp", lambda e: e.dma_start(out=SH2[2:114, :], in_=cx[0:112, :]), r=[cxk], w=["SH2"])
            P.dma("sp", lambda e: e.dma_start(out=SH2[114:128, :], in_=cx[112:126, :]), r=[cxk], w=["SH2"])
            if ty == "S":
                P.dma("sp", lambda e: e.dma_start(out=SH1[0:128:8, :], in_=stc[:, 1, :]), w=["SH1"])
                P.dma("sp", lambda e: e.dma_start(out=SH2[0:128:8, :], in_=stc[:, 0, :]), w=["SH2"])
                P.dma("sp", lambda e: e.dma_start(out=SH2[1:128:8, :], in_=stc[:, 1, :]), w=["SH2"])
            elif first:
                P.pool(lambda e: e.memset(SH1[0:1, :], 0.0), w=["SH1"])
                P.pool(lambda e: e.memset(SH2[0:2, :], 0.0), w=["SH2"])
            else:
                P.dma("sp", lambda e: e.dma_start(out=SH1[0:1, :], in_=bnd_c[ti - 1, 1:2, :]), r=["bnd"], w=["SH1"])
                P.dma("sp", lambda e: e.dma_start(out=SH2[0:2, :], in_=bnd_c[ti - 1, :, :]), r=["bnd"], w=["SH2"])
            P.pool(lambda e: e.tensor_tensor(out=CT[:], in0=SH2[:], in1=CWt[:, 0:512], op=ALU.mult), r=["SH2", "CWt"], w=["CT"])
            P.pool(lambda e: e.tensor_tensor(out=CT2[:], in0=SH1[:], in1=CWt[:, 512:1024], op=ALU.mult), r=["SH1", "CWt"], w=["CT2"])
            P.pool(lambda e: e.tensor_tensor(out=CT[:], in0=CT[:], in1=CT2[:], op=ALU.add), r=["CT", "CT2"], w=["CT"])
            P.pool(lambda e: e.tensor_tensor(out=CT2[:], in0=cx[:], in1=CWt[:, 1024:1536], op=ALU.mult), r=[cxk, "CWt"], w=["CT2"])
            P.pool(lambda e: e.tensor_tensor(out=CT[:], in0=CT[:], in1=CT2[:], op=ALU.add), r=["CT", "CT2"], w=["CT"])
            P.pool(lambda e: e.tensor_tensor(out=YC[:, 0:512], in0=CT[:], in1=pr[:, 0:512], op=ALU.mult), r=["CT", "PRa"], w=[K("YCa")])
            if ty == "S":
                P.dma("sp", lambda e: e.dma_start(out=conv_s[:, 0, :], in_=cx[6:128:8, :]), r=[cxk])
                P.dma("sp", lambda e: e.dma_start(out=conv_s[:, 1, :], in_=cx[7:128:8, :]), r=[cxk])
            else:
                P.dma("sp", lambda e: e.dma_start(out=bnd_c[ti, :, :], in_=cx[126:128, :]), r=[cxk], w=["bnd"])
            if lastp:
                P.dma("sp", lambda e: e.dma_start(out=conv_p, in_=cx[126:128, :]), r=[cxk])
            yield
            P.act(lambda e: e.activation(out=LOR[:, 0:64], in_=XM[:, 1536:1600], func=AF.Tanh), r=["XM"], w=["LOR"])
            P.act(lambda e: e.copy(out=LOR[:, 64:128], in_=XM[:, 1600:1664]), r=["XM"], w=["LOR"])
            P.act(lambda e: e.activation(out=LOR[:, 128:256], in_=XM[:, 1664:1792], func=AF.Sigmoid), r=["XM"], w=["LOR"])
            bk, bkk = nb()
            for i in range(2):
                P.pe(lambda e, i=i, bk=bk: e.transpose(bk[:, i * 128:(i + 1) * 128], LOR[:, i * 128:(i + 1) * 128], IDN[:]), r=["LOR", "IDN"], w=[bkk])
            P.act(lambda e, bk=bk: e.copy(out=LT[:].rearrange("p a t -> p (a t)"), in_=bk[:, 0:256]), r=[bkk], w=["LT"])
            bk, bkk = nb()
            P.pe(lambda e, bk=bk: e.matmul(bk[:], lhsT=LT[0:64, 0, :], rhs=W2A[0:64, :], start=True, stop=False), r=["LT", "W2A"], w=[bkk])
            P.pe(lambda e, bk=bk: e.matmul(bk[:], lhsT=ONE1[:], rhs=W0r[:], start=False, stop=True), r=["ONE1", "W0r"], w=[bkk])
            P.act(lambda e, bk=bk: e.activation(out=SG[:], in_=bk[:], func=AF.Sigmoid), r=[bkk], w=["SG"])
            bk, bkk = nb()
            P.pe(lambda e, bk=bk: e.matmul(bk[:], lhsT=LT[64:128, 0, :], rhs=W2A[64:128, :], start=True, stop=False), r=["LT", "W2A"], w=[bkk])
            P.pe(lambda e, bk=bk: e.matmul(bk[:], lhsT=ONE1[:], rhs=A0r[:], start=False, stop=True), r=["ONE1", "A0r"], w=[bkk])
            P.act(lambda e, bk=bk: e.activation(out=AA[:], in_=bk[:], func=AF.Sigmoid), r=[bkk], w=["AA"])
            bk, bkk = nb()
            P.pe(lambda e, bk=bk: e.matmul(bk[:], lhsT=LT[:, 1, :], rhs=G2[:], start=True, stop=True), r=["LT", "G2"], w=[bkk])
            P.act(lambda e, bk=bk: e.copy(out=GG[:], in_=bk[:]), r=[bkk], w=[K("GG")])
            yield
            bk, bkk = nb()
            P.pe(lambda e, bk=bk: e.matmul(bk[:], lhsT=MK["tri" + ty][:], rhs=SG[:], start=True, stop=True), r=["SG", "M_tri"], w=[bkk])
            P.act(lambda e, bk=bk: e.activation(out=EIN[:], in_=bk[:], func=AF.Exp), r=[bkk], w=["EIN"])
            P.act(lambda e, bk=bk: e.activation(out=EINV[:], in_=bk[:], func=AF.Exp, scale=-1.0), r=[bkk], w=["EINV"])
            P.dve(lambda e, bk=bk: e.scalar_tensor_tensor(out=EEX[:], in0=SG[:], scalar=CDEC, in1=bk[:], op0=ALU.mult, op1=ALU.add), r=[bkk, "SG"], w=["EEX"])
            P.act(lambda e: e.activation(out=EEX[:], in_=EEX[:], func=AF.Exp), r=["EEX"], w=["EEX"])
            bk, bkk = nb()
            P.pe(lambda e, bk=bk: e.matmul(bk[:], lhsT=MK["tgt" + ty][:], rhs=SG[:], start=True, stop=True), r=["SG", "M_tgt"], w=[bkk])
            P.act(lambda e, bk=bk: e.activation(out=EEND[:], in_=bk[:], func=AF.Exp), r=[bkk], w=["EEND"])
            bk, bkk = nb()
            if ty == "P":
                for p in range(4):
                    P.pe(lambda e, p=p, bk=bk: e.matmul(bk[:, p:p + 1], lhsT=SG[:, p * 128:(p + 1) * 128], rhs=SEGP[:], start=True, stop=True), r=["SG", "SEGP"], w=[bkk])
                P.act(lambda e, bk=bk: e.activation(out=PC[:], in_=bk[:, 0:4], func=AF.Exp), r=[bkk], w=[K("PC")])
            else:
                for h in range(NH):
                    P.pe(lambda e, h=h, bk=bk: e.matmul(bk[0:64, h * NSEQ:(h + 1) * NSEQ], lhsT=SG[:, h * 64:(h + 1) * 64], rhs=SEGS[:], start=True, stop=True), r=["SG", "SEGS"], w=[bkk])
                P.act(lambda e, bk=bk: e.activation(out=PCS[:].rearrange("p h b -> p (h b)"), in_=bk[0:64, 0:NH * NSEQ], func=AF.Exp), r=[bkk], w=["PCS"])
            yield
            P.dve(lambda e: e.tensor_tensor(out=KK0[:], in0=k_, in1=KKb[:], op=ALU.mult), r=["XM", "KKb"], w=["KK0"])
            P.dve(lambda e: e.tensor_tensor(out=SQ[:], in0=KK0[:], in1=KK0[:], op=ALU.mult), r=["KK0"], w=["SQ"])
            P.dve(lambda e: e.tensor_reduce(out=S8[:], in_=v3(SQ[:]), axis=AX.X, op=ALU.add), r=["SQ"], w=["S8"])
            P.dve(lambda e: e.tensor_scalar(out=S8[:], in0=S8[:], scalar1=1e-24, scalar2=None, op0=ALU.max), r=["S8"], w=["S8"])
            P.act(lambda e: e.activation(out=S8[:], in_=S8[:], func=AF.Sqrt), r=["S8"], w=["S8"])
            P.dve(lambda e: e.reciprocal(out=S8[:], in_=S8[:]), r=["S8"], w=["S8"])
            P.dve(lambda e: e.tensor_tensor(out=v3(KK0[:]), in0=v3(KK0[:]), in1=S8[:].unsqueeze(2).broadcast_to([128, 8, 64]), op=ALU.mult), r=["KK0", "S8"], w=["KK0"])
            P.dve(lambda e: e.tensor_tensor(out=BB[:], in0=KK0[:], in1=AA[:], op=ALU.mult), r=["KK0", "AA"], w=["BB"])
            P.dve(lambda e: e.scalar_tensor_tensor(out=KP[:], in0=AA[:], scalar=-1.0, in1=KAb[:], op0=ALU.add, op1=ALU.mult), r=["AA", "KAb"], w=["KP"])
            P.dve(lambda e: e.scalar_tensor_tensor(out=KP[:], in0=KP[:], scalar=1.0, in1=k_, op0=ALU.add, op1=ALU.mult), r=["KP", "XM"], w=["KP"])
            P.pool(lambda e: e.tensor_tensor(out=SQ[:], in0=r_, in1=KP[:], op=ALU.mult), r=["XM", "KP", "S8"], w=["SQ"])
            P.pool(lambda e: e.tensor_tensor(out=SQ[:], in0=SQ[:], in1=RKb[:], op=ALU.mult), r=["SQ", "RKb"], w=["SQ"])
            P.dve(lambda e: e.tensor_reduce(out=S8b[:], in_=v3(SQ[:]), axis=AX.X, op=ALU.add), r=["SQ"], w=["S8b"])
            P.dve(lambda e: e.tensor_tensor(out=v3(BON[:]), in0=v3(v_), in1=S8b[:].unsqueeze(2).broadcast_to([128, 8, 64]), op=ALU.mult), r=["XM", "S8b"], w=[K("BON")])
            P.pool(lambda e: e.tensor_tensor(out=BON[:], in0=BON[:], in1=LBb[:], op=ALU.add), r=[K("BON"), "LBb"], w=[K("BON")])
            P.pool(lambda e: e.tensor_tensor(out=BON[:], in0=BON[:], in1=GG[:], op=ALU.mult), r=[K("BON"), K("GG")], w=[K("BON")])
            P.pool(lambda e: e.tensor_tensor(out=GG[:], in0=GG[:], in1=LGb[:], op=ALU.mult), r=[K("GG"), "LGb"], w=[K("GG")])
            P.pool(lambda e: e.tensor_tensor(out=RTm[:], in0=r_, in1=EIN[:], op=ALU.mult), r=["XM", "EIN"], w=[K("RTm")])
            P.dve(lambda e: e.tensor_tensor(out=KTM[:], in0=KK0[:], in1=EEX[:], op=ALU.mult), r=["KK0", "EEX"], w=[K("KTM")])
            P.dve(lambda e: e.tensor_tensor(out=BTm[:], in0=BB[:], in1=EINV[:], op=ALU.mult), r=["BB", "EINV"], w=["BTm"])
            P.pool(lambda e: e.tensor_tensor(out=KTl[:], in0=KP[:], in1=EINV[:], op=ALU.mult), r=["KP", "EINV"], w=["KTl"])
            P.dve(lambda e: e.tensor_tensor(out=BE[:], in0=BB[:], in1=EEND[:], op=ALU.mult), r=["BB", "EEND"], w=[K("BEb")])
            P.pool(lambda e: e.tensor_tensor(out=KE[:], in0=KP[:], in1=EEND[:], op=ALU.mult), r=["KP", "EEND"], w=[K("KEb")])
            P.act(lambda e: e.copy(out=VB[:], in_=v_), r=["XM"], w=[K("VB")])
            if STOP == 'C':
                return
            yield
            for src_t, srck, dst, dstk in ((KTM, K("KTM"), KR[:, :, 0, :], K("KR")), (RTm, K("RTm"), KR[:, :, 1, :], K("KR")),
                                           (BTm, "BTm", BF[:], K("BF")), (KTl, "KTl", KF[:], K("KF"))):
                bk, bkk = nb()
                bkb = bk[:].bitcast(BF16)
                for p in range(4):
                    P.pe(lambda e, p=p, bkb=bkb, src_t=src_t: e.transpose(bkb[:, p * 128:(p + 1) * 128], src_t[:, p * 128:(p + 1) * 128], IDB[:]), r=[srck, "IDB"], w=[bkk])
                P.act(lambda e, bkb=bkb, dst=dst: e.copy(out=dst, in_=bkb[:, 0:512].rearrange("p (a t) -> p a t", t=128)), r=[bkk], w=[dstk])

            if STOP == 'D':
                return
            yield

        def back(ti):
            ty = "S" if ti == NPT else "P"
            first = (ti == 0)
            lastp = (ti == NPT - 1)
            par = ti % 2
            K = lambda n: f"{n}#{par}"
            x = X[par]
            xk = f"X{par}"
            YC, GG, BON, KR, BF, KF, VB = YCs[par], GGs[par], BONs[par], KRs[par], BFs[par], KFs[par], VBs[par]
            KTM, RTm, BE, KE, PC = KTMs[par], RTms[par], BEs[par], KEs[par], PCs[par]
            pr, cx, cxk = PR[par], CX[par], "CX"
            YY, SQ, S8, S8b, YCT = YYb, GNS, S8c, S8d, YCTb
            stprev = STt[1 - par]
            stpk = f"ST{1 - par}"
            def mach(g):
                ga, gb, gt = GA[g], GB[g], GT[g]
                for (lf, lfk, dst, dstk, mk) in ((BF, K("BF"), ga, f"GA{g}", "ma"), (KF, K("KF"), gb, f"GB{g}", "mb")):
                    bks = [nb(), nb()]
                    for i in range(4):
                        h = 4 * g + i
                        p, b0 = h // 2, 64 * (h % 2)
                        bk, bkk = bks[i % 2]
                        c0 = (i // 2) * 256
                        P.pe(lambda e, c0=c0, p=p, b0=b0, bk=bk, lf=lf: e.matmul(bk[:, c0:c0 + 256], lhsT=lf[b0:b0 + 64, p, :],
                                                                                rhs=KR[b0:b0 + 64, p, :, :].rearrange("k a t -> k (a t)"), start=True, stop=True),
                             r=[lfk, K("KR")], w=[bkk])
                    for par2 in range(2):
                        bk, bkk = bks[par2]
                        P.dve(lambda e, bk=bk, dst=dst, par2=par2, mk=mk: e.tensor_tensor(
                            out=dst[:, par2:4:2, :, :].rearrange("p h a t -> p h (a t)"), in0=bk[:].rearrange("p (h x) -> p h x", x=256),
                            in1=MK[mk + ty][:].rearrange("p (h x) -> p h x", x=256), op=ALU.mult),
                            r=[bkk, "M_" + mk], w=[dstk])
                bks = [nb(), nb()]
                for i in range(4):
                    h = 4 * g + i
                    p, b0 = h // 2, 64 * (h % 2)
                    bk, bkk = bks[i % 2]
                    c0 = (i // 2) * 128
                    P.pe(lambda e, c0=c0, p=p, b0=b0, bk=bk: e.matmul(bk[:, c0:c0 + 128], lhsT=KR[b0:b0 + 64, p, 0, :], rhs=BF[b0:b0 + 64, p, :], start=True, stop=True),
                         r=[K("KR"), K("BF")], w=[bkk])
                for par2 in range(2):
                    bk, bkk = bks[par2]
                    P.dve(lambda e, bk=bk, gt=gt, par2=par2: e.tensor_tensor(out=gt[:, par2:4:2, :], in0=bk[:, 0:256].rearrange("p (h t) -> p h t", t=128),
                                                                            in1=MK["mt" + ty][:, 0:256].rearrange("p (h t) -> p h t", t=128), op=ALU.mult),
                          r=[bkk, "M_mt"], w=[f"GT{g}"])
                if STOP == 'E0':
                    return
                yield
                P.pool(lambda e, ga=ga, g=g: e.tensor_tensor(out=TT[g][0][:], in0=ga[:, :, 0, :], in1=IDB[:].unsqueeze(1).broadcast_to([128, 4, 128]), op=ALU.add),
                       r=[f"GA{g}", "IDB"], w=[f"TT{g}0"])
                if STOP == 'E1':
                    return
                Rc, Rck = ga[:, :, 0, :], f"GA{g}"
                RTc, RTck = gt[:], f"GT{g}"
                Tc, Tck = TT[g][0], f"TT{g}0"
                NLV = 6 if ty == "P" else 2
                for lvl in range(1, NLV + 1):
                    sl = lvl % 2
                    if lvl < NLV:
                        bk, bkk = nb()
                        for i in range(4):
                            P.pe(lambda e, i=i, bk=bk, Rc=Rc, RTc=RTc: e.matmul(bk[:, i * 128:(i + 1) * 128], lhsT=RTc[:, i, :], rhs=Rc[:, i, :], start=True, stop=True),
                                 r=[Rck, RTck], w=[bkk])
                        rn, rnk = RN[g][sl], f"RN{g}{sl}"
                        P.act(lambda e, bk=bk, rn=rn: e.copy(out=rn[:].rearrange("p h t -> p (h t)"), in_=bk[:]), r=[bkk], w=[rnk])
                    bk, bkk = nb()
                    for i in range(4):
                        P.pe(lambda e, i=i, bk=bk, Rc=Rc, RTc=RTc: e.matmul(bk[:, i * 128:(i + 1) * 128], lhsT=Rc[:, i, :], rhs=RTc[:, i, :], start=True, stop=True),
                             r=[Rck, RTck], w=[bkk])
                    rtn, rtnk = RTN[g][sl], f"RTN{g}{sl}"
                    P.act(lambda e, bk=bk, rtn=rtn: e.copy(out=rtn[:].rearrange("p h t -> p (h t)"), in_=bk[:]), r=[bkk], w=[rtnk])
                    yield
                    bk, bkk = nb()
                    for i in range(4):
                        P.pe(lambda e, i=i, bk=bk, rtn=rtn, Tc=Tc: e.matmul(bk[:, i * 128:(i + 1) * 128], lhsT=rtn[:, i, :], rhs=Tc[:, i, :], start=True, stop=True),
                             r=[rtnk, Tck], w=[bkk])
                    tn, tnk = TT[g][sl], f"TT{g}{sl}"
                    P.dve(lambda e, bk=bk, tn=tn, Tc=Tc: e.tensor_tensor(out=tn[:].rearrange("p h t -> p (h t)"), in0=bk[:], in1=Tc[:].rearrange("p h t -> p (h t)"), op=ALU.add),
                          r=[bkk, Tck], w=[tnk])
                    yield
                    if lvl < NLV:
                        Rc, Rck = rn[:], rnk
                    RTc, RTck = rtn[:], rtnk
                    Tc, Tck = tn, tnk
                if STOP == 'E':
                    return
                yield
                yield "TD_DONE"
                bk, bkk = nb()
                for i in range(4):
                    h = 4 * g + i
                    P.pe(lambda e, i=i, h=h, bk=bk, gb=gb: e.matmul(bk[:, i * 64:(i + 1) * 64], lhsT=gb[:, i, 0, :], rhs=VB[:, 64 * h:64 * (h + 1)], start=True, stop=True),
                         r=[f"GB{g}", K("VB")], w=[bkk])
                P.act(lambda e, bk=bk, g=g: e.activation(out=X1N[g][:].rearrange("p h v -> p (h v)"), in_=bk[:, 0:256], func=AF.Copy, scale=-1.0), r=[bkk], w=[f"X1N{g}"])
                yield
                bk, bkk = nb()
                for i in range(4):
                    h = 4 * g + i
                    P.pe(lambda e, i=i, bk=bk, Tc=Tc, g=g: e.matmul(bk[:, i * 128:i * 128 + 64], lhsT=Tc[:, i, :], rhs=X1N[g][:, i, :], start=True, stop=True), r=[Tck, f"X1N{g}"], w=[bkk])
                    P.pe(lambda e, i=i, h=h, bk=bk, Tc=Tc: e.matmul(bk[:, i * 128 + 64:(i + 1) * 128], lhsT=Tc[:, i, :], rhs=KTM[:, 64 * h:64 * (h + 1)], start=True, stop=True), r=[Tck, K("KTM")], w=[bkk])
                uk, ukk = UK[g], f"UK{g}"
                bk4 = bk[:].rearrange("p (h a v) -> p h a v", a=2, v=64)
                P.act(lambda e, bk4=bk4, uk=uk: e.copy(out=uk[:, :, 0, :], in_=bk4[:, :, 0, :]), r=[bkk], w=[ukk])
                P.act(lambda e, bk4=bk4, uk=uk: e.activation(out=uk[:, :, 1, :], in_=bk4[:, :, 1, :], func=AF.Copy, scale=-1.0), r=[bkk], w=[ukk])
                if STOP == 'F':
                    return
                yield
                bk, bkk = nb()
                for i in range(4):
                    h = 4 * g + i
                    if ty == "P":
                        ob, col = 64 * (h % 2), (i // 2) * 128
                    else:
                        ob, col = 0, i * 128
                    P.pe(lambda e, h=h, ob=ob, col=col, bk=bk: e.matmul(bk[ob:ob + 64, col:col + 128], lhsT=RTm[:, 64 * h:64 * (h + 1)], rhs=IDB[:], start=True, stop=False), r=[K("RTm"), "IDB"], w=[bkk])
                    P.pe(lambda e, i=i, ob=ob, col=col, bk=bk, uk=uk, ga=ga: e.matmul(bk[ob:ob + 64, col:col + 128], lhsT=uk[:, i, 1, :], rhs=ga[:, i, 1, :], start=False, stop=True), r=[ukk, f"GA{g}"], w=[bkk])
                if ty == "P":
                    rh, rhk = RH[g], [f"RH{g}"]
                else:
                    rh, rhk = RHS, ["RH0", "RH1"]
                if ty == "P":
                    P.act(lambda e, bk=bk, rh=rh: e.copy(out=rh, in_=bk[:, 0:256].rearrange("p (a t) -> p a t", t=128)), r=[bkk], w=rhk)
                else:
                    P.act(lambda e, bk=bk, rh=rh: e.copy(out=rh, in_=bk[0:64, :].rearrange("p (a t) -> p a t", t=128)), r=[bkk], w=rhk)
                if ty == "P":
                    bk, bkk = nb()
                    for i in range(4):
                        h = 4 * g + i
                        ob, col = 64 * (h % 2), (i // 2) * 64
                        P.pe(lambda e, i=i, h=h, ob=ob, col=col, bk=bk, uk=uk: e.matmul(bk[ob:ob + 64, col:col + 64], lhsT=uk[:, i, 1, :], rhs=BE[:, 64 * h:64 * (h + 1)], start=True, stop=True), r=[ukk, K("BEb")], w=[bkk])
                    for j in range(2):
                        P.dve(lambda e, j=j, bk=bk, g=g: e.scalar_tensor_tensor(out=MM[g][:, j, :], in0=ID2[:], scalar=PC[:, 2 * g + j:2 * g + j + 1], in1=bk[:, j * 64:(j + 1) * 64], op0=ALU.mult, op1=ALU.add),
                              r=[bkk, "ID2", K("PC")], w=[f"MM{g}"])
                    for i in range(4):
                        h = 4 * g + i
                        p, b0, j = h // 2, 64 * (h % 2), i // 2
                        yield
                        vh = VB[:, 64 * h:64 * (h + 1)]
                        P.pe(lambda e, i=i, h=h, ga=ga, uk=uk: e.matmul(PSY[:, 64 * h:64 * (h + 1)], lhsT=ga[:, i, 1, :], rhs=uk[:, i, 0, :], start=True, stop=False), r=[f"GA{g}", ukk], w=["psy"])
                        P.pe(lambda e, i=i, h=h, gb=gb, vh=vh: e.matmul(PSY[:, 64 * h:64 * (h + 1)], lhsT=gb[:, i, 1, :], rhs=vh, start=False, stop=first), r=[f"GB{g}", K("VB")], w=["psy"])
                        if not first:
                            P.pe(lambda e, h=h, b0=b0, j=j, p=p, rh=rh: e.matmul(PSY[:, 64 * h:64 * (h + 1)], lhsT=rh[b0:b0 + 64, j, :], rhs=stprev[b0:b0 + 64, p, :], start=False, stop=True), r=[*rhk, stpk], w=["psy"])
                        P.pe(lambda e, i=i, h=h, b0=b0, p=p, uk=uk: e.matmul(PSS[b0:b0 + 64, p * 64:(p + 1) * 64], lhsT=BE[:, 64 * h:64 * (h + 1)], rhs=uk[:, i, 0, :], start=True, stop=False), r=[K("BEb"), ukk], w=["pss"])
                        P.pe(lambda e, h=h, b0=b0, p=p, vh=vh: e.matmul(PSS[b0:b0 + 64, p * 64:(p + 1) * 64], lhsT=KE[:, 64 * h:64 * (h + 1)], rhs=vh, start=False, stop=first), r=[K("KEb"), K("VB")], w=["pss"])
                        if not first:
                            P.pe(lambda e, b0=b0, j=j, p=p, g=g: e.matmul(PSS[b0:b0 + 64, p * 64:(p + 1) * 64], lhsT=MM[g][b0:b0 + 64, j, :], rhs=stprev[b0:b0 + 64, p, :], start=False, stop=True), r=[f"MM{g}", stpk], w=["pss"])
                else:
                    for i in range(4):
                        h = 4 * g + i
                        p, h2 = h // 2, h % 2
                        yield
                        vh = VB[:, 64 * h:64 * (h + 1)]
                        if h2 == 0:
                            P.dma("sp", lambda e, p=p: e.dma_start(out=S0[:], in_=stw[:, 2 * p:2 * p + 2, :, :].rearrange("b h v k -> (h v) b k")), w=["PRa", "PRb"])
                            for q in range(4):
                                bk, bkk = nb()
                                for bb in range(4):
                                    b = q * 4 + bb
                                    P.pe(lambda e, b=b, bb=bb, bk=bk: e.transpose(bk[0:64, bb * 128:(bb + 1) * 128], S0[:, b, :], IDN[:]), r=["PRa", "PRb", "IDN"], w=[bkk])
                                P.act(lambda e, q=q, bk=bk: e.copy(out=S0T[:, 4 * q:4 * q + 4, :].rearrange("p b t -> p (b t)"), in_=bk[0:64, :]), r=[bkk], w=["EIN", "EINV", "EEX", "EEND"])
                        if h == 0:
                            P.pool(lambda e: e.memset(RHm[:], 0.0), w=["SH1", "SH2", "CT", "CT2"])
                        rflat = CONV4[0:64, :]
                        P.pool(lambda e, i=i, rh=rh: e.tensor_copy(out=rflat[:, 0:2040].rearrange("p (b x) -> p b x", x=136)[:, :, 0:8],
                                                                    in_=rh[:, i, 0:120].rearrange("p (b t) -> p b t", t=8)), r=rhk, w=["SH1", "SH2", "CT", "CT2"])
                        P.pool(lambda e, i=i, rh=rh: e.tensor_copy(out=rflat[:, 2040:2048], in_=rh[:, i, 120:128]), r=rhk, w=["SH1", "SH2", "CT", "CT2"])
                        bmb = BM[:].unsqueeze(2).broadcast_to([128, NSEQ, 64])
                        P.pool(lambda e, i=i, uk=uk: e.tensor_tensor(out=KHm[:], in0=uk[:, i, 1, :].unsqueeze(1).broadcast_to([128, NSEQ, 64]), in1=bmb, op=ALU.mult), r=[ukk, "BM"], w=["PRa", "PRb"])
                        P.dve(lambda e, i=i, uk=uk: e.tensor_tensor(out=UTm[:], in0=uk[:, i, 0, :].unsqueeze(1).broadcast_to([128, NSEQ, 64]), in1=bmb, op=ALU.mult), r=[ukk, "BM"], w=["PRa", "PRb"])
                        P.dve(lambda e, vh=vh: e.tensor_tensor(out=Vm[:], in0=vh.unsqueeze(1).broadcast_to([128, NSEQ, 64]), in1=bmb, op=ALU.mult), r=[K("VB"), "BM"], w=["PRa", "PRb"])
                        P.pool(lambda e, h=h: e.tensor_tensor(out=DPC[:], in0=IDN[0:64, 0:64].unsqueeze(1).broadcast_to([64, NSEQ, 64]),
                                                              in1=PCS[:, h, :].unsqueeze(2).broadcast_to([64, NSEQ, 64]), op=ALU.mult), r=["IDN", "PCS"], w=["XM"])
                        P.pe(lambda e, i=i, h=h, ga=ga, uk=uk: e.matmul(PSY[:, 64 * h:64 * (h + 1)], lhsT=ga[:, i, 1, :], rhs=uk[:, i, 0, :], start=True, stop=False), r=[f"GA{g}", ukk], w=["psy"])
                        P.pe(lambda e, i=i, h=h, gb=gb, vh=vh: e.matmul(PSY[:, 64 * h:64 * (h + 1)], lhsT=gb[:, i, 1, :], rhs=vh, start=False, stop=False), r=[f"GB{g}", K("VB")], w=["psy"])
                        for b in range(NSEQ):
                            P.pe(lambda e, b=b, h=h, h2=h2: e.matmul(PSY[:, 64 * h:64 * (h + 1)], lhsT=RHm[:, b, :], rhs=S0T[:, b, 64 * h2:64 * (h2 + 1)], start=False, stop=(b == NSEQ - 1)), r=["SH1", "SH2", "CT", "CT2", "EIN", "EINV", "EEX", "EEND"], w=["psy"])
                        for q in range(2):
                            bk, bkk = nb()
                            for bb in range(8):
                                b = q * 8 + bb
                                P.pe(lambda e, b=b, bb=bb, h=h, bk=bk: e.matmul(bk[0:64, bb * 64:(bb + 1) * 64], lhsT=KHm[:, b, :], rhs=BE[:, 64 * h:64 * (h + 1)], start=True, stop=True), r=["PRa", "PRb", K("BEb")], w=[bkk])
                            P.dve(lambda e, q=q, bk=bk: e.tensor_tensor(out=MS[:, 8 * q:8 * q + 8, :].rearrange("p b k -> p (b k)"), in0=bk[0:64, :],
                                                                        in1=DPC[:, 8 * q:8 * q + 8, :].rearrange("p b k -> p (b k)"), op=ALU.add), r=[bkk, "XM"], w=["KK0", "SQ"])
                        so, sok = DPC, "XM"
                        for q in range(2):
                            bk, bkk = nb()
                            for bb in range(8):
                                b = q * 8 + bb
                                o_ = bk[0:64, bb * 64:(bb + 1) * 64]
                                P.pe(lambda e, b=b, o_=o_, h2=h2: e.matmul(o_, lhsT=S0T[:, b, 64 * h2:64 * (h2 + 1)], rhs=MS[:, b, :], start=True, stop=False), r=["EIN", "EINV", "EEX", "EEND", "KK0", "SQ"], w=[bkk])
                                P.pe(lambda e, b=b, o_=o_, h=h: e.matmul(o_, lhsT=UTm[:, b, :], rhs=BE[:, 64 * h:64 * (h + 1)], start=False, stop=False), r=["PRa", "PRb", K("BEb")], w=[bkk])
                                P.pe(lambda e, b=b, o_=o_, h=h: e.matmul(o_, lhsT=Vm[:, b, :], rhs=KE[:, 64 * h:64 * (h + 1)], start=False, stop=True), r=["PRa", "PRb", K("KEb")], w=[bkk])
                            P.act(lambda e, q=q, bk=bk, so=so: e.copy(out=so[:, 8 * q:8 * q + 8, :].rearrange("p b k -> p (b k)"), in_=bk[0:64, :]), r=[bkk], w=[sok])
                        P.dma("sp", lambda e, h=h, so=so: e.dma_start(out=wkv_s[:, h, :, :].rearrange("b v k -> v b k"), in_=so[:]), r=[sok])
            if ty == "P" and GI_MODE != "none":
                gens = [mach(0), mach(1)]
                alive = [True, True]
                passed = [False, False]
                while any(alive) and not (GI_MODE == "td" and all(passed)):
                    for gi in range(2):
                        if alive[gi] and not (GI_MODE == "td" and passed[gi]):
                            try:
                                if next(gens[gi]) == "TD_DONE":
                                    passed[gi] = True
                            except StopIteration:
                                alive[gi] = False
                    yield
                for gi in range(2):
                    if alive[gi]:
                        for _ in gens[gi]:
                            yield
            else:
                for g in range(2):
                    for _ in mach(g):
                        yield
            yield
            P.act(lambda e: e.copy(out=YY[:], in_=PSY[:]), r=["psy"], w=["YY"])
            if ty == "P":
                stn, stnk = STt[par], f"ST{par}"
                P.act(lambda e, stn=stn: e.copy(out=stn[:].rearrange("p a v -> p (a v)"), in_=PSS[:, 0:256]), r=["pss"], w=[stnk])
                if lastp:
                    bk, bkk = nb()
                    for p in range(4):
                        P.pe(lambda e, p=p, bk=bk, stn=stn: e.transpose(bk[0:64, p * 128:(p + 1) * 128], stn[:, p, :], IDN[:]), r=[stnk, "IDN"], w=[bkk])
                    P.act(lambda e, bk=bk: e.copy(out=WPO[:].rearrange("p a t -> p (a t)"), in_=bk[0:64, :]), r=[bkk], w=["RH0", "RH1"])
                    P.dma("sp", lambda e: e.dma_start(out=wkv_p.rearrange("(p h2) v k -> v p h2 k", h2=2), in_=WPO[:].rearrange("v p (h2 k) -> v p h2 k", h2=2)), r=["RH0", "RH1"])
            yield
            b8 = lambda t: t[:].unsqueeze(2).broadcast_to([128, 8, 64])
            P.dve(lambda e: e.tensor_reduce(out=S8[:], in_=v3(YY[:]), axis=AX.X, op=ALU.add), r=["YY"], w=["S8c"])
            P.dve(lambda e: e.tensor_tensor(out=SQ[:], in0=YY[:], in1=YY[:], op=ALU.mult), r=["YY"], w=["GNS"])
            P.dve(lambda e: e.tensor_reduce(out=S8b[:], in_=v3(SQ[:]), axis=AX.X, op=ALU.add), r=["GNS"], w=["S8d"])
            P.dve(lambda e: e.tensor_scalar(out=S8[:], in0=S8[:], scalar1=1.0 / HS, scalar2=None, op0=ALU.mult), r=["S8c"], w=["S8c"])
            P.dve(lambda e: e.tensor_tensor(out=S8e[:], in0=S8[:], in1=S8[:], op=ALU.mult), r=["S8c"], w=["S8e"])
            P.dve(lambda e: e.scalar_tensor_tensor(out=S8b[:], in0=S8b[:], scalar=1.0 / HS, in1=S8e[:], op0=ALU.mult, op1=ALU.subtract), r=["S8d", "S8e"], w=["S8d"])
            P.dve(lambda e: e.tensor_scalar(out=S8b[:], in0=S8b[:], scalar1=64e-5, scalar2=None, op0=ALU.add), r=["S8d"], w=["S8d"])
            P.act(lambda e: e.activation(out=S8b[:], in_=S8b[:], func=AF.Sqrt), r=["S8d"], w=["S8d"])
            P.dve(lambda e: e.reciprocal(out=S8b[:], in_=S8b[:]), r=["S8d"], w=["S8d"])
            P.dve(lambda e: e.tensor_tensor(out=v3(YY[:]), in0=v3(YY[:]), in1=b8(S8), op=ALU.subtract), r=["YY", "S8c"], w=["YY"])
            P.dve(lambda e: e.tensor_tensor(out=v3(YY[:]), in0=v3(YY[:]), in1=b8(S8b), op=ALU.mult), r=["YY", "S8d"], w=["YY"])
            P.dve(lambda e: e.tensor_tensor(out=YY[:], in0=YY[:], in1=GG[:], op=ALU.mult), r=["YY", K("GG")], w=["YY"])
            P.dve(lambda e: e.tensor_tensor(out=YC[:, 512:1024], in0=YY[:], in1=BON[:], op=ALU.add), r=["YY", K("BON")], w=[K("YCb")])
            if "ycat" in dbg_out:
                dbgdump("ycat", YC[:], K("YCa")) if ti == dbg_ti[0] else None
            if STOP == 'H':
                return
            yield
            for k in range(8):
                P.pe(lambda e, k=k: e.transpose(PSB[:, k * 128:(k + 1) * 128], YC[:, k * 128:(k + 1) * 128], IDB[:]), r=[K("YCa"), K("YCb"), "IDB"], w=["psb"])
            P.act(lambda e: e.copy(out=YCT[:].rearrange("p k t -> p (k t)"), in_=PSB[:]), r=["psb"], w=["YCT"])
            for j in range(2):
                bk, bkk = nb()
                wr, wrk = wchunk(wbo, j * 512, 512)
                for k in range(8):
                    P.pe(lambda e, k=k, bk=bk, wr=wr: e.matmul(bk[:], lhsT=YCT[:, k, :], rhs=wr[:, k, :], start=(k == 0), stop=(k == 7)), r=["YCT", wrk], w=[bkk])
                P.dve(lambda e, j=j, bk=bk: e.tensor_tensor(out=X1T[:, j * 512:(j + 1) * 512], in0=bk[:], in1=x[:, j * 512:(j + 1) * 512], op=ALU.add), r=[bkk, xk], w=["X1T"])
            P.dma("sp", lambda e: e.dma_start(out=x1s[ti * 128:(ti + 1) * 128, :], in_=X1T[:]), r=["X1T"], w=["x1s"])
            yield

        def drain(gen):
            for _ in gen:
                pass

        def load_masks(names):
            for nm in names:
                P.dma("sp", lambda e, nm=nm: e.dma_start(out=MKT[nm][:], in_=cd[nm + "S"]), w=["M_" + nm])

        dbg_ti = [dbg.get("_ti", 0) if dbg else 0]
        PIPE = os.environ.get("KNOPIPE", "") != "1"
        if not PIPE:
            for ti in range(NT):
                if ti == NPT:
                    load_masks(("tri", "tgt", "ma", "mb", "mt"))
                drain(front(ti))
                drain(back(ti))
        else:
            drain(front(0))
            for ti in range(NT):
                if ti == NPT:
                    load_masks(("ma", "mb", "mt"))
                bgen = back(ti)
                fgen = None
                if ti + 1 < NT:
                    if ti + 1 == NPT:
                        load_masks(("tri", "tgt"))
                    fgen = front(ti + 1)
                bdone = fdone = False
                while not (bdone and (fgen is None or fdone)):
                    for _ in range(3):
                        if not bdone:
                            try:
                                next(bgen)
                            except StopIteration:
                                bdone = True
                    if fgen is not None and not fdone:
                        try:
                            next(fgen)
                        except StopIteration:
                            fdone = True

    if _dry:
        return wq
    P.barrier()

    if SKIP2:
        return nc, P.finalize()
    with ExitStack() as st2:
        def SB2(name, shape, dt=F32):
            return st2.enter_context(nc.sbuf_tensor(name, list(shape), dt))

        PS2 = [st2.enter_context(nc.psum_tensor(f"q{i}", [128, 512], F32)) for i in range(6)]
        PSB2 = st2.enter_context(nc.psum_tensor("qb", [128, 1024], BF16))
        bank2 = [0]

        def nb2():
            i = bank2[0] % 6
            bank2[0] += 1
            return PS2[i], f"q{i}"

        NSTG = 8
        RNG = 3
        W1R = [SB2(f"W1R{i}", [128, 8, 512], BF16) for i in range(RNG)]
        W2R = [SB2(f"W2R{i}", [128, 4, D], BF16) for i in range(RNG)]
        X1 = SB2("X1A", [128, NT, D])
        H2T = SB2("H2T", [128, 8, NT * 128], BF16)
        G2T = SB2("G2T", [128, 8])
        NFb = SB2("NFb", [128, D])
        IDB2 = SB2("IDB2", [128, 128], BF16)
        JK2 = SB2("JK2", [128, D])
        HB2 = SB2("HB2", [128, D], BF16)
        SS2 = SB2("SS2", [128, 1])
        RS2 = SB2("RS2", [128, 1])
        HR = [SB2(f"HR{i}", [128, 512]) for i in range(2)]
        HID = [SB2(f"HID{i}", [128, 4, 512], BF16) for i in range(2)]
        YO = [SB2(f"YO{i}", [128, D]) for i in range(2)]
        P.dma("sp", lambda e: e.dma_start(out=G2T[:], in_=g2T_d), w=["G2T"])
        P.dma("sp", lambda e: e.dma_start(out=NFb[:], in_=nf_d.partition_broadcast(128)), w=["NFb"])
        P.dma("pool", lambda e: e.dma_start(out=IDB2[:], in_=cd["ident"]), w=["IDB2"])

        def load_stage(s):
            rb = s % RNG
            P.dma("pool", lambda e: e.dma_start(out=W1R[rb][:], in_=w_ff1[:, s * 512:(s + 1) * 512].rearrange("(k p) f -> p k f", p=128)), w=[f"W1R{rb}"])
            P.dma("pool", lambda e: e.dma_start(out=W2R[rb][:], in_=w_ff2[s * 512:(s + 1) * 512, :].rearrange("(c p) d -> p c d", p=128)), w=[f"W2R{rb}"])

        for s in range(min(RNG, NSTG)):
            load_stage(s)
        for ti in range(NT):
            P.dma("sp", lambda e, ti=ti: e.dma_start(out=X1[:, ti, :], in_=x1s[ti * 128:(ti + 1) * 128, :]), w=[f"X1_{ti}"])
        def preamble(ti):
            xk = f"X1_{ti}"
            P.act(lambda e, ti=ti: e.activation(out=JK2[:], in_=X1[:, ti, :], func=AF.Square, accum_out=SS2[:]), r=[xk], w=["JK2", "SS2"])
            P.dve(lambda e: e.tensor_scalar(out=RS2[:], in0=SS2[:], scalar1=1.0 / D, scalar2=1e-6, op0=ALU.mult, op1=ALU.add), r=["SS2"], w=["RS2"])
            P.act(lambda e: e.activation(out=RS2[:], in_=RS2[:], func=AF.Sqrt), r=["RS2"], w=["RS2"])
            P.dve(lambda e: e.reciprocal(out=RS2[:], in_=RS2[:]), r=["RS2"], w=["RS2"])
            P.act(lambda e, ti=ti: e.activation(out=HB2[:], in_=X1[:, ti, :], func=AF.Copy, scale=RS2[:, 0:1]), r=[xk, "RS2"], w=["HB2"])
            for k in range(8):
                P.pe(lambda e, k=k: e.transpose(PSB2[:, k * 128:(k + 1) * 128], HB2[:, k * 128:(k + 1) * 128], IDB2[:]), r=["HB2", "IDB2"], w=["qb"])
            P.dve(lambda e, ti=ti: e.tensor_tensor(out=H2T[:, :, ti * 128:(ti + 1) * 128], in0=PSB2[:].rearrange("p (k t) -> p k t", t=128),
                                                  in1=G2T[:].unsqueeze(2).broadcast_to([128, 8, 128]), op=ALU.mult), r=["qb", "G2T"], w=[f"H2T_{ti}"])
        groups = []
        t0 = 0
        while t0 < NT:
            n = min(4, NT - t0)
            groups.append((t0, n))
            t0 += n
        def final_tile(ti):
            xk = f"X1_{ti}"
            yo, yok = YO[ti % 2], f"YO{ti % 2}"
            P.act(lambda e, ti=ti: e.activation(out=JK2[:], in_=X1[:, ti, :], func=AF.Square, accum_out=SS2[:]), r=[xk], w=["JK2", "SS2"])
            P.dve(lambda e: e.tensor_scalar(out=RS2[:], in0=SS2[:], scalar1=1.0 / D, scalar2=1e-6, op0=ALU.mult, op1=ALU.add), r=["SS2"], w=["RS2"])
            P.act(lambda e: e.activation(out=RS2[:], in_=RS2[:], func=AF.Sqrt), r=["RS2"], w=["RS2"])
            P.dve(lambda e: e.reciprocal(out=RS2[:], in_=RS2[:]), r=["RS2"], w=["RS2"])
            P.dve(lambda e, ti=ti, yo=yo: e.scalar_tensor_tensor(out=yo[:], in0=X1[:, ti, :], scalar=RS2[:, 0:1], in1=NFb[:], op0=ALU.mult, op1=ALU.mult), r=[xk, "RS2", "NFb"], w=[yok])
            dst = y_s if ti == NPT else y_p[ti * 128:(ti + 1) * 128, :]
            P.dma("sp", lambda e, yo=yo, dst=dst: e.dma_start(out=dst, in_=yo[:]), r=[yok])


        items = [(s_, t0, n) for s_ in range(NSTG) for (t0, n) in groups]

        def ffn1(k):
            s_, t0, n = items[k]
            rb = s_ % RNG
            ntok = n * 128
            hid, hidk = HID[k % 2], f"HID{k % 2}"
            for fc in range(4):
                bk, bkk = nb2()
                for kk in range(8):
                    P.pe(lambda e, kk=kk, fc=fc, bk=bk, t0=t0, ntok=ntok, rb=rb: e.matmul(bk[:, 0:ntok], lhsT=W1R[rb][:, kk, fc * 128:(fc + 1) * 128], rhs=H2T[:, kk, t0 * 128:t0 * 128 + ntok], start=(kk == 0), stop=(kk == 7)),
                         r=[f"W1R{rb}"] + [f"H2T_{t}" for t in range(t0, t0 + n)], w=[bkk])
                hr, hrk = HR[fc % 2], f"HR{fc % 2}"
                P.act(lambda e, bk=bk, hr=hr, ntok=ntok: e.activation(out=hr[:, 0:ntok], in_=bk[:, 0:ntok], func=AF.Relu), r=[bkk], w=[hrk])
                P.pool(lambda e, hr=hr, hid=hid, fc=fc, ntok=ntok: e.tensor_tensor(out=hid[:, fc, 0:ntok], in0=hr[:, 0:ntok], in1=hr[:, 0:ntok], op=ALU.mult), r=[hrk], w=[hidk])

        def ffn2(k):
            s_, t0, n = items[k]
            rb = s_ % RNG
            hid, hidk = HID[k % 2], f"HID{k % 2}"
            for tl in range(n):
                ti = t0 + tl
                for half in range(2):
                    bk, bkk = nb2()
                    for fc in range(4):
                        P.pe(lambda e, fc=fc, bk=bk, tl=tl, half=half, hid=hid, rb=rb: e.matmul(bk[:], lhsT=hid[:, fc, tl * 128:(tl + 1) * 128], rhs=W2R[rb][:, fc, half * 512:(half + 1) * 512], start=(fc == 0), stop=(fc == 3)),
                             r=[hidk, f"W2R{rb}"], w=[bkk])
                    P.dve(lambda e, bk=bk, ti=ti, half=half: e.tensor_tensor(out=X1[:, ti, half * 512:(half + 1) * 512], in0=bk[:], in1=X1[:, ti, half * 512:(half + 1) * 512], op=ALU.add),
                          r=[bkk, f"X1_{ti}"], w=[f"X1_{ti}"])
                if s_ == NSTG - 1:
                    final_tile(ti)

        FFNP = os.environ.get("KNOFFNP", "") != "1"

        def pre_group(k):
            s_, t0, n = items[k]
            if s_ == 0:
                for t in range(t0, t0 + n):
                    preamble(t)

        if FFNP:
            pre_group(0)
            ffn1(0)
        for k in range(len(items)):
            if FFNP:
                if k + 1 < len(items):
                    pre_group(k + 1)
                    ffn1(k + 1)
            else:
                pre_group(k)
                ffn1(k)
            ffn2(k)
            s_, t0, n = items[k]
            if (t0, n) == groups[-1] and s_ + RNG < NSTG:
                load_stage(s_ + RNG)
        if os.environ.get('KSCHED', '1') == '1':
            P.reschedule()
        stats = P.finalize()
    return nc, stats


_CACHE = {}


def make_in_maps(inputs, NPT=16, ncores=8):
    f = lambda a: np.ascontiguousarray(np.asarray(a, dtype=np.float32))
    c = _consts()
    shared = {
        "w_in": f(inputs["w_in"][0]), "w_out": f(inputs["w_out"][0]),
        "w_ff1": f(inputs["w_ff1"][0]), "w_ff2": f(inputs["w_ff2"][0]),
        "g1T": f(np.asarray(inputs["norm1_g"][0]).reshape(8, 128).T),
        "g2T": f(np.asarray(inputs["norm2_g"][0]).reshape(8, 128).T),
        "mu": f(inputs["mu"][0]).reshape(1, RC),
        "convw": f(inputs["conv_w"][0]).reshape(1, 3 * CW_),
        "k_k": f(inputs["k_k"][0]).reshape(1, RW), "k_a": f(inputs["k_a"][0]).reshape(1, RW),
        "r_k": f(inputs["r_k"][0]).reshape(1, RW),
        "lnx_g": f(inputs["lnx_g"][0]).reshape(1, RW), "lnx_b": f(inputs["lnx_b"][0]).reshape(1, RW),
        "normf": f(inputs["normf_g"]).reshape(1, D),
        "w0": f(inputs["w0"][0]).reshape(1, RW), "a0": f(inputs["a0"][0]).reshape(1, RW),
        "w2a": f(np.concatenate([np.asarray(inputs["w2"][0]), np.asarray(inputs["a2"][0])], 0)),
        "g2": f(inputs["g2"][0]),
    }
    for k, v in c.items():
        shared["c_" + k] = f(v)
    xpr = np.asarray(inputs["x_prompt"], dtype=np.float32)
    xsm = np.asarray(inputs["x_sample"], dtype=np.float32)
    maps = []
    for ci in range(ncores):
        m = dict(shared)
        m["xp"] = f(xpr[ci, :NPT * 128])
        sl = slice(ci * NSEQ, (ci + 1) * NSEQ)
        m["xs"] = f(xsm[sl].reshape(NSEQ * DT_, D))
        m["stc"] = f(inputs["state_conv"][0][sl])
        m["sts"] = f(inputs["state_shift"][0][sl])
        m["stw"] = f(inputs["state_wkv"][0][sl])
        maps.append(m)
    return maps


def gather(results, NPT=16, ncores=8):
    g = lambda name: [np.asarray(r[name], dtype=np.float32) for r in results]
    y_p = np.stack(g("y_p"), 0)
    y_s = np.stack(g("y_s"), 0).reshape(ncores * NSEQ, DT_, D)
    conv_p = np.stack(g("conv_p"), 0)[None]
    shift_p = np.concatenate(g("shift_p"), 0)[None]
    wkv_p = np.stack(g("wkv_p"), 0)[None]
    conv_s = np.concatenate(g("conv_s"), 0)[None]
    shift_s = np.concatenate(g("shift_s"), 0)[None]
    wkv_s = np.concatenate(g("wkv_s"), 0)[None]
    return (y_p, y_s, conv_p, shift_p, wkv_p, conv_s, shift_s, wkv_s)


def kernel(**inputs):
    NPT = 16
    if "nc" not in _CACHE:
        _CACHE["nc"] = build(NPT)[0]
    nc = _CACHE["nc"]
    maps = make_in_maps(inputs, NPT, 8)
    res = run_bass_kernel_spmd(nc, maps, core_ids=list(range(8)))
    return gather(res.results, NPT, 8)
```

```python
import os
import numpy as np
import concourse.bass as bass
import concourse.mybir as mybir
from concourse.bass_utils import run_bass_kernel_spmd

F32 = mybir.dt.float32
BF16 = mybir.dt.bfloat16
ALU = mybir.AluOpType
AF = mybir.ActivationFunctionType
AX = mybir.AxisListType

D = 1024
CW_ = 512
RW = 512
NH = 8
HS = 64
RC = 1792
IC = 3328
DFF = 4096
NSEQ = 16
DT_ = 8
CDEC = float(np.exp(-0.5))


class Op:
    __slots__ = ("eng", "emit", "deps", "dma", "inc", "done", "dom", "alldeps", "cost", "idx", "fin", "succ", "npend", "prio")

    def __init__(self, eng, emit, dma):
        self.eng = eng
        self.emit = emit
        self.dma = dma
        self.deps = []
        self.alldeps = []
        self.cost = 0.0
        self.inc = False
        self.done = None
        self.dom = (eng + "_dma") if dma else eng


class Prog:
    NSLOT = {"sp": 24, "pool": 6, "act": 4}

    def __init__(self, nc, self_sync=False):
        self.nc = nc
        self.ops = []
        self.lastw = {}
        self.readers = {}
        self.self_sync = self_sync
        self.ndma = {}
        self.lastslot = {}

    COST = {"pe": 0.12, "act": 0.6, "dve": 0.7, "pool": 1.3}

    def op(self, eng, emit, reads=(), writes=(), dma=False, cost=None):
        o = Op(eng, emit, dma)
        o.cost = cost if cost is not None else (3.0 if dma else self.COST[eng])
        deps = {}
        if dma:
            n = self.ndma.get(eng, 0)
            self.ndma[eng] = n + 1
            o.dom = f"{eng}_dma{n % self.NSLOT[eng]}"
            prev = self.lastslot.get(o.dom)
            if prev is not None:
                deps[id(prev)] = (prev, True)
            self.lastslot[o.dom] = o
        for k in reads:
            w = self.lastw.get(k)
            if w is not None:
                deps[id(w)] = (w, True)
        for k in writes:
            w = self.lastw.get(k)
            if w is not None:
                deps[id(w)] = (w, True)
            for r in self.readers.get(k, ()):
                if id(r) not in deps:
                    deps[id(r)] = (r, False)
        for k in reads:
            self.readers.setdefault(k, []).append(o)
        for k in writes:
            self.lastw[k] = o
            self.readers[k] = []
        for d, hard in deps.values():
            if d is o:
                continue
            o.alldeps.append(d)
            if d.dom == o.dom and not o.dma:
                if o.eng != "pe":
                    o.deps.append(d)
                continue
            o.deps.append(d)
        self.ops.append(o)
        return o

    def pe(self, emit, r=(), w=()):
        return self.op("pe", emit, r, w)

    def act(self, emit, r=(), w=()):
        return self.op("act", emit, r, w)

    def dve(self, emit, r=(), w=()):
        return self.op("dve", emit, r, w)

    def pool(self, emit, r=(), w=()):
        return self.op("pool", emit, r, w)

    def dma(self, eng, emit, r=(), w=()):
        return self.op(eng, emit, r, w, dma=True)

    def barrier(self):
        lastdom = {}
        for o in self.ops:
            if o.emit is not None:
                lastdom[o.dom] = o
        for eng in ("pe", "act", "dve", "pool", "sp"):
            b = Op(eng, None, False)
            b.deps = [o for dom, o in lastdom.items() if dom != eng]
            self.ops.append(b)
        self.lastw = {}
        self.readers = {}

    def reschedule(self, hop=float(os.environ.get("KHOP", "0.15"))):
        segs, cur = [], []
        for o in self.ops:
            if o.emit is None:
                segs.append(cur)
                segs.append([o])
                cur = []
            else:
                cur.append(o)
        segs.append(cur)
        new_ops = []
        for seg in segs:
            if len(seg) <= 1 or seg[0].emit is None:
                new_ops.extend(seg)
                continue
            inseg = {id(o) for o in seg}
            for o in seg:
                o.succ = []
                o.fin = 0.0
            fixed = set(os.environ.get("KFIXED", "pe").split(","))
            lastfixed = {}
            for o in seg:
                o.npend = 0
                for d in o.alldeps:
                    if id(d) in inseg:
                        d.succ.append(o)
                        o.npend += 1
                if o.eng in fixed and not o.dma:
                    p_ = lastfixed.get(o.eng)
                    if p_ is not None and all(p_ is not d for d in o.alldeps):
                        p_.succ.append(o)
                        o.npend += 1
                    lastfixed[o.eng] = o
            for o in reversed(seg):
                o.prio = o.cost + max([x.prio + (hop if x.eng != o.eng else 0.0) for x in o.succ], default=0.0)
            ready = {}
            for o in seg:
                if o.npend == 0:
                    ready.setdefault(o.eng, []).append(o)
            cursor = {}
            order = []
            nleft = len(seg)
            while nleft:
                best = None
                for eng, lst in ready.items():
                    if not lst:
                        continue
                    cur_t = cursor.get(eng, 0.0)
                    for o in lst:
                        est = cur_t
                        for d in o.alldeps:
                            if id(d) in inseg:
                                t = d.fin + (hop if d.eng != o.eng or d.dma != o.dma else 0.0)
                                if t > est:
                                    est = t
                        key = (est, -o.prio)
                        if best is None or key < best[0]:
                            best = (key, o, est)
                _, o, est = best
                ready[o.eng].remove(o)
                o.fin = est + o.cost
                cursor[o.eng] = (est + 0.1) if o.dma else o.fin
                order.append(o)
                nleft -= 1
                for x in o.succ:
                    x.npend -= 1
                    if x.npend == 0:
                        ready.setdefault(x.eng, []).append(x)
            new_ops.extend(order)
        self.ops = new_ops

    def finalize(self, final_eng="sp"):
        nc = self.nc
        ops = self.ops
        fin = Op(final_eng, None, False)
        lastdom = {}
        for o in ops:
            if o.emit is not None:
                lastdom[o.dom] = o
        for dom, o in lastdom.items():
            if "_dma" in dom:
                fin.deps.append(o)
        ops = ops + [fin]
        for o in ops:
            if o.dma:
                o.inc = True
            for d in o.deps:
                d.inc = True
        cnt = {}
        for o in ops:
            if o.inc:
                cnt[o.dom] = cnt.get(o.dom, 0) + (16 if o.dma else 1)
                o.done = cnt[o.dom]
        doms = sorted({o.dom for o in ops if o.inc})
        sems = {d: nc.alloc_semaphore(name="s_" + d) for d in doms}
        per_eng = {}
        for o in ops:
            per_eng.setdefault(o.eng, []).append(o)
        engs = {"pe": "tensor", "act": "scalar", "dve": "vector", "pool": "gpsimd", "sp": "sync"}

        def run(engname, e):
            waited = {}
            for o in per_eng.get(engname, []):
                need = {}
                for d in o.deps:
                    if d.done > need.get(d.dom, 0):
                        need[d.dom] = d.done
                for dom, v in need.items():
                    if v > waited.get(dom, 0):
                        e.wait_ge(sems[dom], v)
                        waited[dom] = v
                if o.emit is None:
                    continue
                ins = o.emit(e)
                if o.inc:
                    ins.then_inc(sems[o.dom], 16 if o.dma else 1)

        with nc.Block() as block:
            for engname, attr in engs.items():
                if engname not in per_eng:
                    continue

                def mk(engname=engname):
                    def f(e):
                        run(engname, e)
                    return f
                getattr(block, attr)(mk())
        return dict(nops=len(ops), cnt=cnt)


def _consts():
    c = {}
    idx = np.arange(128)
    c["ident"] = np.eye(128, dtype=np.float32)
    c["id2"] = (idx[:, None] % 64 == np.arange(64)[None, :]).astype(np.float32)
    for ty in ("P", "S"):
        if ty == "P":
            blk = np.zeros(128, np.int64)
        else:
            blk = idx // DT_
        same = blk[:, None] == blk[None, :]
        s = idx[:, None]
        t = idx[None, :]
        lt = (same & (s < t)).astype(np.float32)
        le = (same & (s <= t)).astype(np.float32)
        gt = (same & (s > t)).astype(np.float32)
        c["tri" + ty] = (-CDEC) * le
        c["tgt" + ty] = (-CDEC) * gt
        c["ma" + ty] = np.concatenate([-lt, le, -lt, le], 1)
        c["mb" + ty] = np.concatenate([lt, le, lt, le], 1)
        c["mt" + ty] = np.concatenate([-gt] * 4, 1)
    c["segP"] = np.full((128, 1), -CDEC, np.float32)
    bm = (idx[:, None] // DT_ == np.arange(NSEQ)[None, :]).astype(np.float32)
    c["segS"] = (-CDEC) * bm
    c["bm"] = bm
    return c


CONST_SHAPES = {k: v.shape for k, v in _consts().items()}


def build(NPT=16, dbg=None, _dry=False, _order=None):
    if not _dry and _order is None:
        _order = build(NPT, None, _dry=True)
    STOP = os.environ.get('KSTOP', '')
    GI_MODE = os.environ.get('KGI', 'td')
    SKIP2 = os.environ.get('KSKIP2', '') == '1'
    NT = NPT + 1
    nc = bass.Bass("TRN2", target_bir_lowering=False)
    P = Prog(nc)
    din = {}

    def DI(name, shape):
        din[name] = nc.dram_tensor(name, list(shape), F32, kind="ExternalInput").ap()
        return din[name]

    def DO(name, shape):
        return nc.dram_tensor(name, list(shape), F32, kind="ExternalOutput").ap()

    xp = DI("xp", [NPT * 128, D])
    xs = DI("xs", [128, D])
    stc = DI("stc", [NSEQ, 2, CW_])
    sts = DI("sts", [NSEQ, RC])
    stw = DI("stw", [NSEQ, NH, HS, HS])
    w_in = DI("w_in", [D, IC])
    w_out = DI("w_out", [D, D])
    w_ff1 = DI("w_ff1", [D, DFF])
    w_ff2 = DI("w_ff2", [DFF, D])
    g1T_d = DI("g1T", [128, 8])
    g2T_d = DI("g2T", [128, 8])
    mu_d = DI("mu", [1, RC])
    cw_d = DI("convw", [1, 3 * CW_])
    kk_d = DI("k_k", [1, RW])
    ka_d = DI("k_a", [1, RW])
    rk_d = DI("r_k", [1, RW])
    lg_d = DI("lnx_g", [1, RW])
    lb_d = DI("lnx_b", [1, RW])
    nf_d = DI("normf", [1, D])
    w0_d = DI("w0", [1, RW])
    a0_d = DI("a0", [1, RW])
    w2a_d = DI("w2a", [128, RW])
    g2_d = DI("g2", [128, RW])
    cd = {k: DI("c_" + k, shp) for k, shp in CONST_SHAPES.items()}

    y_p = DO("y_p", [NPT * 128, D])
    y_s = DO("y_s", [128, D])
    conv_p = DO("conv_p", [2, CW_])
    shift_p = DO("shift_p", [1, RC])
    wkv_p = DO("wkv_p", [NH, HS, HS])
    conv_s = DO("conv_s", [NSEQ, 2, CW_])
    shift_s = DO("shift_s", [NSEQ, RC])
    wkv_s = DO("wkv_s", [NSEQ, NH, HS, HS])
    x1s = nc.dram_tensor("x1s", [NT * 128, D], F32).ap()
    wbi = nc.dram_tensor("wbi", [7, 128, 8, 512], BF16).ap()
    wbo = nc.dram_tensor("wbo", [2, 128, 8, 512], BF16).ap()
    bnd_p = nc.dram_tensor("bnd_p", [NT, RC], F32).ap()
    bnd_c = nc.dram_tensor("bnd_c", [NT, 2, CW_], F32).ap()
    dbg_out = {}
    if dbg:
        for name, shape in dbg.items():
            if not name.startswith("_"):
                dbg_out[name] = DO("dbg_" + name, shape)

    from contextlib import ExitStack

    with ExitStack() as st0:
        TW = [st0.enter_context(nc.sbuf_tensor(f"TW{i}", [128, IC], BF16)) for i in range(2)]
        n0 = 0
        for (src_w, dst_w, wd) in ((w_in, wbi, IC), (w_out, wbo, D)):
            for k in range(8):
                tw, twk = TW[n0 % 2], f"TW{n0 % 2}"
                n0 += 1
                P.dma("pool", lambda e, tw=tw, src_w=src_w, k=k, wd=wd: e.dma_start(out=tw[:, 0:wd], in_=src_w[k * 128:(k + 1) * 128, :]), w=[twk])
                nfull = wd // 512
                P.dma("sp", lambda e, tw=tw, dst_w=dst_w, k=k, nfull=nfull: e.dma_start(out=dst_w[0:nfull, :, k, :].rearrange("j p n -> p j n"),
                                                                                  in_=tw[:, 0:nfull * 512].rearrange("p (j n) -> p j n", n=512)), r=[twk], w=["wb"])
                if wd % 512:
                    P.dma("sp", lambda e, tw=tw, dst_w=dst_w, k=k, nfull=nfull, wd=wd: e.dma_start(out=dst_w[nfull, :, k, 0:wd - nfull * 512], in_=tw[:, nfull * 512:wd]), r=[twk], w=["wb"])
    P.barrier()

    with ExitStack() as st1:
        def SB(name, shape, dt=F32):
            return st1.enter_context(nc.sbuf_tensor(name, list(shape), dt))

        def PSF(name, shape, dt=F32):
            return st1.enter_context(nc.psum_tensor(name, list(shape), dt))

        NB = 5
        PS = [PSF(f"ps{i}", [128, 512]) for i in range(NB)]
        PSY = PSF("psy", [128, 512])
        PSS = PSF("pss", [128, 512])
        PSB = PSF("psb", [128, 1024], BF16)
        bank_i = [0]

        def nb():
            i = bank_i[0] % NB
            bank_i[0] += 1
            return PS[i], f"ps{i}"

        NRING = 3
        WR = [SB(f"WR{i}", [128, 8, 512], BF16) for i in range(NRING)]
        wq = list(_order) if _order is not None else []
        wsrc = {"wbi": wbi, "wbo": wbo}
        wstate = {"issued": 0, "used": 0}

        def wchunk(srcw, c0, wd):
            name = "wbi" if srcw is wbi else "wbo"
            n = wstate["used"]
            wstate["used"] += 1
            if _dry:
                wq.append((name, c0, wd))
                return WR[n % NRING], f"WR{n % NRING}"
            assert wq[n] == (name, c0, wd), (n, wq[n], name, c0, wd)
            while wstate["issued"] < min(len(wq), n + NRING):
                m = wstate["issued"]
                nm2, c2, wd2 = wq[m]
                P.dma("sp", lambda e, m=m, nm2=nm2, c2=c2, wd2=wd2: e.dma_start(out=WR[m % NRING][:, :, 0:wd2], in_=wsrc[nm2][c2 // 512, :, :, 0:wd2]),
                      w=[f"WR{m % NRING}"])
                wstate["issued"] += 1
            return WR[n % NRING], f"WR{n % NRING}"

        def bc_tile(name, dram, n):
            t = SB(name, [128, n])
            P.dma("sp", lambda e: e.dma_start(out=t[:], in_=dram.partition_broadcast(128)), w=[name])
            return t

        def ld_tile(name, dram, shape, dt=F32, eng="sp"):
            t = SB(name, shape, dt)
            P.dma(eng, lambda e: e.dma_start(out=t[:], in_=dram), w=[name])
            return t

        IDN = ld_tile("IDN", cd["ident"], [128, 128])
        IDB = ld_tile("IDB", cd["ident"], [128, 128], BF16, eng="pool")
        ID2 = ld_tile("ID2", cd["id2"], [128, 64])
        G1T = ld_tile("G1T", g1T_d, [128, 8])
        MU = bc_tile("MU", mu_d, RC)
        CWt = bc_tile("CWt", cw_d, 3 * CW_)
        KKb = bc_tile("KKb", kk_d, RW)
        KAb = bc_tile("KAb", ka_d, RW)
        RKb = bc_tile("RKb", rk_d, RW)
        LGb = bc_tile("LGb", lg_d, RW)
        LBb = bc_tile("LBb", lb_d, RW)
        W0r = ld_tile("W0r", w0_d, [1, RW])
        A0r = ld_tile("A0r", a0_d, [1, RW])
        W2A = ld_tile("W2A", w2a_d, [128, RW])
        G2 = ld_tile("G2", g2_d, [128, RW])
        ONE1 = SB("ONE1", [1, 128])
        P.pool(lambda e: e.memset(ONE1[:], 1.0), w=["ONE1"])
        MKT = {}
        for nm in ("tri", "tgt"):
            MKT[nm] = ld_tile("M_" + nm, cd[nm + "P"], [128, 128])
        for nm in ("ma", "mb", "mt"):
            MKT[nm] = ld_tile("M_" + nm, cd[nm + "P"], [128, 512])
        MK = {}
        for ty in ("P", "S"):
            for nm in ("tri", "tgt", "ma", "mb", "mt"):
                MK[nm + ty] = MKT[nm]
        SEGP = ld_tile("SEGP", cd["segP"], [128, 1])
        SEGS = ld_tile("SEGS", cd["segS"], [128, NSEQ])
        BM = ld_tile("BM", cd["bm"], [128, NSEQ])

        X = [SB(f"X{i}", [128, D]) for i in range(2)]
        SS = SB("SS", [128, 1])
        RS = SB("RS", [128, 1])
        HB = SB("HB", [128, D], BF16)
        HT = SB("HT", [128, 8, 128], BF16)
        PR0 = SB("PR0", [128, IC])
        PR = [PR0, PR0]
        CX0 = SB("CX0", [128, CW_])
        CX = [CX0, CX0]
        CONV4 = SB("CONV4", [128, 4 * CW_])
        SH1, SH2, CT, CT2 = (CONV4[:, i * 512:(i + 1) * 512] for i in range(4))
        YCs = [SB(f"YC{i}", [128, D], BF16) for i in range(2)]
        X1T = SB("X1T", [128, D])
        JUNK = CONV4[:, 0:1024]
        YCTb = SB("YCTb", [128, 8, 128], BF16)
        PV = SB("PV", [128, RC])
        XM = PV
        LOR = SB("LOR", [128, 256])
        LT = SB("LT", [128, 2, 128])
        SG = SB("SG", [128, RW])
        YYb = SB("YYb", [128, RW])
        GNS = SB("GNS", [128, RW])
        S8c = SB("S8c", [128, 8])
        S8d = SB("S8d", [128, 8])
        S8e = SB("S8e", [128, 8])
        AA = SB("AA", [128, RW])
        GGs = [SB(f"GG{i}", [128, RW]) for i in range(2)]
        EXP4 = SB("EXP4", [128, 4 * RW])
        EIN, EINV, EEX, EEND = (EXP4[:, i * 512:(i + 1) * 512] for i in range(4))
        BTm = SB("BTm", [128, RW], BF16)
        KTl = SB("KTl", [128, RW], BF16)
        KS2 = SB("KS2", [128, 2 * RW])
        KK0, SQ = KS2[:, 0:512], KS2[:, 512:1024]
        S8 = SB("S8", [128, 8])
        S8b = SB("S8b", [128, 8])
        BB = SB("BB", [128, RW])
        KP = SB("KP", [128, RW])
        BEs = [SB(f"BEb{i}", [128, RW], BF16) for i in range(2)]
        KEs = [SB(f"KEb{i}", [128, RW], BF16) for i in range(2)]
        VBs = [SB(f"VB{i}", [128, RW], BF16) for i in range(2)]
        BONs = [SB(f"BON{i}", [128, RW]) for i in range(2)]
        RTms = [SB(f"RTm{i}", [128, RW], BF16) for i in range(2)]
        KTMs = [SB(f"KTM{i}", [128, RW], BF16) for i in range(2)]
        KRs = [SB(f"KR{i}", [128, 4, 2, 128], BF16) for i in range(2)]
        BFs = [SB(f"BF{i}", [128, 4, 128], BF16) for i in range(2)]
        KFs = [SB(f"KF{i}", [128, 4, 128], BF16) for i in range(2)]
        PCs = [SB(f"PC{i}", [128, 4]) for i in range(2)]
        PCS = SB("PCS", [64, NH, NSEQ])
        GA = [SB(f"GA{g}", [128, 4, 2, 128], BF16) for g in range(2)]
        GB = [SB(f"GB{g}", [128, 4, 2, 128], BF16) for g in range(2)]
        GT = [SB(f"GT{g}", [128, 4, 128], BF16) for g in range(2)]
        RN = [[SB(f"RN{g}{i}", [128, 4, 128], BF16) for i in range(2)] for g in range(2)]
        RTN = [[SB(f"RTN{g}{i}", [128, 4, 128], BF16) for i in range(2)] for g in range(2)]
        TT = [[SB(f"TT{g}{i}", [128, 4, 128], BF16) for i in range(2)] for g in range(2)]
        X1N = [SB(f"X1N{g}", [128, 4, 64], BF16) for g in range(2)]
        UK = [SB(f"UK{g}", [128, 4, 2, 64], BF16) for g in range(2)]
        RHb = SB("RHb", [128, 2, 2, 128])
        RH = [RHb[:, 0, :, :], RHb[:, 1, :, :]]
        RHS = RHb[0:64, :, :, :].rearrange("p a b t -> p (a b) t")
        MM = [SB(f"MM{g}", [128, 2, 64]) for g in range(2)]
        STt = [SB(f"ST{i}", [128, 4, 64]) for i in range(2)]
        S0 = PR0[:, 0:1024].rearrange("p (b k) -> p b k", k=64)
        S0T = EXP4[0:64, :].rearrange("p (b t) -> p b t", t=128)
        RHm = CONV4[0:64, :].rearrange("p (b t) -> p b t", t=128)
        KHm = PR0[:, 1024:1536].bitcast(BF16).rearrange("p (b k) -> p b k", k=64)
        UTm = PR0[:, 1536:2048].bitcast(BF16).rearrange("p (b k) -> p b k", k=64)
        Vm = PR0[:, 2048:2560].bitcast(BF16).rearrange("p (b k) -> p b k", k=64)
        DPC = PV[0:64, 0:1024].rearrange("p (b k) -> p b k", k=64)
        MS = KS2[0:64, :].rearrange("p (b k) -> p b k", k=64)
        WPO = RHb[0:64, :, :, :].rearrange("p a b t -> p (a b) t")
        SO = [DPC, DPC]

        def v3(ap, k=64):
            return ap.rearrange("p (h k) -> p h k", k=k)

        def dbgdump(name, ap, key):
            if name in dbg_out:
                P.dma("sp", lambda e: e.dma_start(out=dbg_out[name], in_=ap), r=[key])

        def front(ti):
            ty = "S" if ti == NPT else "P"
            first = (ti == 0)
            lastp = (ti == NPT - 1)
            par = ti % 2
            K = lambda n: f"{n}#{par}"
            x = X[par]
            xk = f"X{par}"
            YC, GG, BON, KR, BF, KF, VB = YCs[par], GGs[par], BONs[par], KRs[par], BFs[par], KFs[par], VBs[par]
            KTM, RTm, BE, KE, PC = KTMs[par], RTms[par], BEs[par], KEs[par], PCs[par]
            src = xs if ty == "S" else xp[ti * 128:(ti + 1) * 128, :]
            P.dma("sp", lambda e: e.dma_start(out=x[:], in_=src), w=[xk])
            P.act(lambda e: e.activation(out=JUNK[:], in_=x[:], func=AF.Square, accum_out=SS[:]), r=[xk], w=["SH1", "SH2", "SS"])
            P.dve(lambda e: e.tensor_scalar(out=RS[:], in0=SS[:], scalar1=1.0 / D, scalar2=1e-6, op0=ALU.mult, op1=ALU.add), r=["SS"], w=["RS"])
            P.act(lambda e: e.activation(out=RS[:], in_=RS[:], func=AF.Sqrt), r=["RS"], w=["RS"])
            P.dve(lambda e: e.reciprocal(out=RS[:], in_=RS[:]), r=["RS"], w=["RS"])
            P.act(lambda e: e.activation(out=HB[:], in_=x[:], func=AF.Copy, scale=RS[:, 0:1]), r=[xk, "RS"], w=["HB"])
            for k in range(8):
                P.pe(lambda e, k=k: e.transpose(PSB[:, k * 128:(k + 1) * 128], HB[:, k * 128:(k + 1) * 128], IDB[:]), r=["HB", "IDB"], w=["psb"])
            P.dve(lambda e: e.tensor_tensor(out=HT[:], in0=PSB[:].rearrange("p (k t) -> p k t", t=128),
                                            in1=G1T[:].unsqueeze(2).broadcast_to([128, 8, 128]), op=ALU.mult), r=["psb", "G1T"], w=["HT"])
            pr = PR[par]

            def proj_chunks(js):
                for j in js:
                    wd = 512 if j < 6 else 256
                    bk, bkk = nb()
                    wr, wrk = wchunk(wbi, j * 512, wd)
                    for k in range(8):
                        P.pe(lambda e, k=k, wd=wd, bk=bk, wr=wr: e.matmul(bk[:, 0:wd], lhsT=HT[:, k, :], rhs=wr[:, k, 0:wd], start=(k == 0), stop=(k == 7)),
                             r=["HT", wrk], w=[bkk])
                    P.act(lambda e, j=j, wd=wd, bk=bk: e.copy(out=pr[:, j * 512:j * 512 + wd], in_=bk[:, 0:wd]), r=[bkk], w=["PRb" if j >= 3 else "PRa"])
                    yield

            yield from proj_chunks((3, 4, 5, 6))
            yield
            prw = pr[:, 1536:IC]
            P.dma("sp", lambda e: e.dma_start(out=PV[1:113, :], in_=pr[0:112, 1536:IC]), r=["PRb"], w=["XM"])
            P.dma("sp", lambda e: e.dma_start(out=PV[113:128, :], in_=pr[112:127, 1536:IC]), r=["PRb"], w=["XM"])
            if ty == "S":
                P.dma("sp", lambda e: e.dma_start(out=PV[0:128:8, :], in_=sts), w=["XM"])
            elif first:
                P.pool(lambda e: e.memset(PV[0:1, :], 0.0), w=["XM"])
            else:
                P.dma("sp", lambda e: e.dma_start(out=PV[0:1, :], in_=bnd_p[ti - 1:ti, :]), r=["bnd"], w=["XM"])
            P.dve(lambda e: e.tensor_tensor(out=PV[:], in0=PV[:], in1=prw, op=ALU.subtract), r=["XM", "PRb"], w=["XM"])
            P.dve(lambda e: e.tensor_tensor(out=PV[:], in0=PV[:], in1=MU[:], op=ALU.mult), r=["XM", "MU"], w=["XM"])
            P.dve(lambda e: e.tensor_tensor(out=XM[:], in0=PV[:], in1=prw, op=ALU.add), r=["XM", "PRb"], w=["XM"])
            r_ = XM[:, 0:512]
            k_ = XM[:, 512:1024]
            v_ = XM[:, 1024:1536]
            if STOP == 'B':
                return
            if ty == "S":
                P.dma("sp", lambda e: e.dma_start(out=shift_s, in_=pr[7:128:8, 1536:IC]), r=["PRb"])
            else:
                P.dma("sp", lambda e: e.dma_start(out=bnd_p[ti:ti + 1, :], in_=pr[127:128, 1536:IC]), r=["PRb"], w=["bnd"])
            if lastp:
                P.dma("sp", lambda e: e.dma_start(out=shift_p, in_=pr[127:128, 1536:IC]), r=["PRb"])
            yield from proj_chunks((0, 1, 2))
            yield
            cx = CX[par]
            cxk = "CX"
            P.pool(lambda e: e.tensor_tensor(out=cx[:], in0=pr[:, 512:1024], in1=pr[:, 1024:1536], op=ALU.mult), r=["PRa"], w=[cxk])
            P.dma("sp", lambda e: e.dma_start(out=SH1[1:113, :], in_=cx[0:112, :]), r=[cxk], w=["SH1"])
            P.dma("sp", lambda e: e.dma_start(out=SH1[113:128, :], in_=cx[112:127, :]), r=[cxk], w=["SH1"])
            P.dma("sp", lambda e: e.dma_start(out=SH2[2:114, :], in_=cx[0:112, :]), r=[cxk], w=["SH2"])
            P.dma("sp", lambda e: e.dma_start(out=SH2[114:128, :], in_=cx[112:126, :]), r=[cxk], w=["SH2"])
            if ty == "S":
                P.dma("sp", lambda e: e.dma_start(out=SH1[0:128:8, :], in_=stc[:, 1, :]), w=["SH1"])
                P.dma("sp", lambda e: e.dma_start(out=SH2[0:128:8, :], in_=stc[:, 0, :]), w=["SH2"])
                P.dma("sp", lambda e: e.dma_start(out=SH2[1:128:8, :], in_=stc[:, 1, :]), w=["SH2"])
            elif first:
                P.pool(lambda e: e.memset(SH1[0:1, :], 0.0), w=["SH1"])
                P.pool(lambda e: e.memset(SH2[0:2, :], 0.0), w=["SH2"])
            else:
                P.dma("sp", lambda e: e.dma_start(out=SH1[0:1, :], in_=bnd_c[ti - 1, 1:2, :]), r=["bnd"], w=["SH1"])
                P.dma("sp", lambda e: e.dma_start(out=SH2[0:2, :], in_=bnd_c[ti - 1, :, :]), r=["bnd"], w=["SH2"])
            P.pool(lambda e: e.tensor_tensor(out=CT[:], in0=SH2[:], in1=CWt[:, 0:512], op=ALU.mult), r=["SH2", "CWt"], w=["CT"])
            P.pool(lambda e: e.tensor_tensor(out=CT2[:], in0=SH1[:], in1=CWt[:, 512:1024], op=ALU.mult), r=["SH1", "CWt"], w=["CT2"])
            P.pool(lambda e: e.tensor_tensor(out=CT[:], in0=CT[:], in1=CT2[:], op=ALU.add), r=["CT", "CT2"], w=["CT"])
            P.pool(lambda e: e.tensor_tensor(out=CT2[:], in0=cx[:], in1=CWt[:, 1024:1536], op=ALU.mult), r=[cxk, "CWt"], w=["CT2"])
            P.pool(lambda e: e.tensor_tensor(out=CT[:], in0=CT[:], in1=CT2[:], op=ALU.add), r=["CT", "CT2"], w=["CT"])
            P.pool(lambda e: e.tensor_tensor(out=YC[:, 0:512], in0=CT[:], in1=pr[:, 0:512], op=ALU.mult), r=["CT", "PRa"], w=[K("YCa")])
            if ty == "S":
                P.dma("sp", lambda e: e.dma_start(out=conv_s[:, 0, :], in_=cx[6:128:8, :]), r=[cxk])
                P.dma("sp", lambda e: e.dma_start(out=conv_s[:, 1, :], in_=cx[7:128:8, :]), r=[cxk])
            else:
                P.dma("sp", lambda e: e.dma_start(out=bnd_c[ti, :, :], in_=cx[126:128, :]), r=[cxk], w=["bnd"])
            if lastp:
                P.dma("sp", lambda e: e.dma_start(out=conv_p, in_=cx[126:128, :]), r=[cxk])
            yield
            P.act(lambda e: e.activation(out=LOR[:, 0:64], in_=XM[:, 1536:1600], func=AF.Tanh), r=["XM"], w=["LOR"])
            P.act(lambda e: e.copy(out=LOR[:, 64:128], in_=XM[:, 1600:1664]), r=["XM"], w=["LOR"])
            P.act(lambda e: e.activation(out=LOR[:, 128:256], in_=XM[:, 1664:1792], func=AF.Sigmoid), r=["XM"], w=["LOR"])
            bk, bkk = nb()
            for i in range(2):
                P.pe(lambda e, i=i, bk=bk: e.transpose(bk[:, i * 128:(i + 1) * 128], LOR[:, i * 128:(i + 1) * 128], IDN[:]), r=["LOR", "IDN"], w=[bkk])
            P.act(lambda e, bk=bk: e.copy(out=LT[:].rearrange("p a t -> p (a t)"), in_=bk[:, 0:256]), r=[bkk], w=["LT"])
            bk, bkk = nb()
            P.pe(lambda e, bk=bk: e.matmul(bk[:], lhsT=LT[0:64, 0, :], rhs=W2A[0:64, :], start=True, stop=False), r=["LT", "W2A"], w=[bkk])
            P.pe(lambda e, bk=bk: e.matmul(bk[:], lhsT=ONE1[:], rhs=W0r[:], start=False, stop=True), r=["ONE1", "W0r"], w=[bkk])
            P.act(lambda e, bk=bk: e.activation(out=SG[:], in_=bk[:], func=AF.Sigmoid), r=[bkk], w=["SG"])
            bk, bkk = nb()
            P.pe(lambda e, bk=bk: e.matmul(bk[:], lhsT=LT[64:128, 0, :], rhs=W2A[64:128, :], start=True, stop=False), r=["LT", "W2A"], w=[bkk])
            P.pe(lambda e, bk=bk: e.matmul(bk[:], lhsT=ONE1[:], rhs=A0r[:], start=False, stop=True), r=["ONE1", "A0r"], w=[bkk])
            P.act(lambda e, bk=bk: e.activation(out=AA[:], in_=bk[:], func=AF.Sigmoid), r=[bkk], w=["AA"])
            bk, bkk = nb()
            P.pe(lambda e, bk=bk: e.matmul(bk[:], lhsT=LT[:, 1, :], rhs=G2[:], start=True, stop=True), r=["LT", "G2"], w=[bkk])
            P.act(lambda e, bk=bk: e.copy(out=GG[:], in_=bk[:]), r=[bkk], w=[K("GG")])
            yield
            bk, bkk = nb()
            P.pe(lambda e, bk=bk: e.matmul(bk[:], lhsT=MK["tri" + ty][:], rhs=SG[:], start=True, stop=True), r=["SG", "M_tri"], w=[bkk])
            P.act(lambda e, bk=bk: e.activation(out=EIN[:], in_=bk[:], func=AF.Exp), r=[bkk], w=["EIN"])
            P.act(lambda e, bk=bk: e.activation(out=EINV[:], in_=bk[:], func=AF.Exp, scale=-1.0), r=[bkk], w=["EINV"])
            P.dve(lambda e, bk=bk: e.scalar_tensor_tensor(out=EEX[:], in0=SG[:], scalar=CDEC, in1=bk[:], op0=ALU.mult, op1=ALU.add), r=[bkk, "SG"], w=["EEX"])
            P.act(lambda e: e.activation(out=EEX[:], in_=EEX[:], func=AF.Exp), r=["EEX"], w=["EEX"])
            bk, bkk = nb()
            P.pe(lambda e, bk=bk: e.matmul(bk[:], lhsT=MK["tgt" + ty][:], rhs=SG[:], start=True, stop=True), r=["SG", "M_tgt"], w=[bkk])
            P.act(lambda e, bk=bk: e.activation(out=EEND[:], in_=bk[:], func=AF.Exp), r=[bkk], w=["EEND"])
            bk, bkk = nb()
            if ty == "P":
                for p in range(4):
                    P.pe(lambda e, p=p, bk=bk: e.matmul(bk[:, p:p + 1], lhsT=SG[:, p * 128:(p + 1) * 128], rhs=SEGP[:], start=True, stop=True), r=["SG", "SEGP"], w=[bkk])
                P.act(lambda e, bk=bk: e.activation(out=PC[:], in_=bk[:, 0:4], func=AF.Exp), r=[bkk], w=[K("PC")])
            else:
                for h in range(NH):
                    P.pe(lambda e, h=h, bk=bk: e.matmul(bk[0:64, h * NSEQ:(h + 1) * NSEQ], lhsT=SG[:, h * 64:(h + 1) * 64], rhs=SEGS[:], start=True, stop=True), r=["SG", "SEGS"], w=[bkk])
                P.act(lambda e, bk=bk: e.activation(out=PCS[:].rearrange("p h b -> p (h b)"), in_=bk[0:64, 0:NH * NSEQ], func=AF.Exp), r=[bkk], w=["PCS"])
            yield
            P.dve(lambda e: e.tensor_tensor(out=KK0[:], in0=k_, in1=KKb[:], op=ALU.mult), r=["XM", "KKb"], w=["KK0"])
            P.dve(lambda e: e.tensor_tensor(out=SQ[:], in0=KK0[:], in1=KK0[:], op=ALU.mult), r=["KK0"], w=["SQ"])
            P.dve(lambda e: e.tensor_reduce(out=S8[:], in_=v3(SQ[:]), axis=AX.X, op=ALU.add), r=["SQ"], w=["S8"])
            P.dve(lambda e: e.tensor_scalar(out=S8[:], in0=S8[:], scalar1=1e-24, scalar2=None, op0=ALU.max), r=["S8"], w=["S8"])
            P.act(lambda e: e.activation(out=S8[:], in_=S8[:], func=AF.Sqrt), r=["S8"], w=["S8"])
            P.dve(lambda e: e.reciprocal(out=S8[:], in_=S8[:]), r=["S8"], w=["S8"])
            P.dve(lambda e: e.tensor_tensor(out=v3(KK0[:]), in0=v3(KK0[:]), in1=S8[:].unsqueeze(2).broadcast_to([128, 8, 64]), op=ALU.mult), r=["KK0", "S8"], w=["KK0"])
            P.dve(lambda e: e.tensor_tensor(out=BB[:], in0=KK0[:], in1=AA[:], op=ALU.mult), r=["KK0", "AA"], w=["BB"])
            P.dve(lambda e: e.scalar_tensor_tensor(out=KP[:], in0=AA[:], scalar=-1.0, in1=KAb[:], op0=ALU.add, op1=ALU.mult), r=["AA", "KAb"], w=["KP"])
            P.dve(lambda e: e.scalar_tensor_tensor(out=KP[:], in0=KP[:], scalar=1.0, in1=k_, op0=ALU.add, op1=ALU.mult), r=["KP", "XM"], w=["KP"])
            P.pool(lambda e: e.tensor_tensor(out=SQ[:], in0=r_, in1=KP[:], op=ALU.mult), r=["XM", "KP", "S8"], w=["SQ"])
            P.pool(lambda e: e.tensor_tensor(out=SQ[:], in0=SQ[:], in1=RKb[:], op=ALU.mult), r=["SQ", "RKb"], w=["SQ"])
            P.dve(lambda e: e.tensor_reduce(out=S8b[:], in_=v3(SQ[:]), axis=AX.X, op=ALU.add), r=["SQ"], w=["S8b"])
            P.dve(lambda e: e.tensor_tensor(out=v3(BON[:]), in0=v3(v_), in1=S8b[:].unsqueeze(2).broadcast_to([128, 8, 64]), op=ALU.mult), r=["XM", "S8b"], w=[K("BON")])
            P.pool(lambda e: e.tensor_tensor(out=BON[:], in0=BON[:], in1=LBb[:], op=ALU.add), r=[K("BON"), "LBb"], w=[K("BON")])
            P.pool(lambda e: e.tensor_tensor(out=BON[:], in0=BON[:], in1=GG[:], op=ALU.mult), r=[K("BON"), K("GG")], w=[K("BON")])
            P.pool(lambda e: e.tensor_tensor(out=GG[:], in0=GG[:], in1=LGb[:], op=ALU.mult), r=[K("GG"), "LGb"], w=[K("GG")])
            P.pool(lambda e: e.tensor_tensor(out=RTm[:], in0=r_, in1=EIN[:], op=ALU.mult), r=["XM", "EIN"], w=[K("RTm")])
            P.dve(lambda e: e.tensor_tensor(out=KTM[:], in0=KK0[:], in1=EEX[:], op=ALU.mult), r=["KK0", "EEX"], w=[K("KTM")])
            P.dve(lambda e: e.tensor_tensor(out=BTm[:], in0=BB[:], in1=EINV[:], op=ALU.mult), r=["BB", "EINV"], w=["BTm"])
            P.pool(lambda e: e.tensor_tensor(out=KTl[:], in0=KP[:], in1=EINV[:], op=ALU.mult), r=["KP", "EINV"], w=["KTl"])
            P.dve(lambda e: e.tensor_tensor(out=BE[:], in0=BB[:], in1=EEND[:], op=ALU.mult), r=["BB", "EEND"], w=[K("BEb")])
            P.pool(lambda e: e.tensor_tensor(out=KE[:], in0=KP[:], in1=EEND[:], op=ALU.mult), r=["KP", "EEND"], w=[K("KEb")])
            P.act(lambda e: e.copy(out=VB[:], in_=v_), r=["XM"], w=[K("VB")])
            if STOP == 'C':
                return
            yield
            for src_t, srck, dst, dstk in ((KTM, K("KTM"), KR[:, :, 0, :], K("KR")), (RTm, K("RTm"), KR[:, :, 1, :], K("KR")),
                                           (BTm, "BTm", BF[:], K("BF")), (KTl, "KTl", KF[:], K("KF"))):
                bk, bkk = nb()
                bkb = bk[:].bitcast(BF16)
                for p in range(4):
                    P.pe(lambda e, p=p, bkb=bkb, src_t=src_t: e.transpose(bkb[:, p * 128:(p + 1) * 128], src_t[:, p * 128:(p + 1) * 128], IDB[:]), r=[srck, "IDB"], w=[bkk])
                P.act(lambda e, bkb=bkb, dst=dst: e.copy(out=dst, in_=bkb[:, 0:512].rearrange("p (a t) -> p a t", t=128)), r=[bkk], w=[dstk])

            if STOP == 'D':
                return
            yield

        def back(ti):
            ty = "S" if ti == NPT else "P"
            first = (ti == 0)
            lastp = (ti == NPT - 1)
            par = ti % 2
            K = lambda n: f"{n}#{par}"
            x = X[par]
            xk = f"X{par}"
            YC, GG, BON, KR, BF, KF, VB = YCs[par], GGs[par], BONs[par], KRs[par], BFs[par], KFs[par], VBs[par]
            KTM, RTm, BE, KE, PC = KTMs[par], RTms[par], BEs[par], KEs[par], PCs[par]
            pr, cx, cxk = PR[par], CX[par], "CX"
            YY, SQ, S8, S8b, YCT = YYb, GNS, S8c, S8d, YCTb
            stprev = STt[1 - par]
            stpk = f"ST{1 - par}"
            def mach(g):
                ga, gb, gt = GA[g], GB[g], GT[g]
                for (lf, lfk, dst, dstk, mk) in ((BF, K("BF"), ga, f"GA{g}", "ma"), (KF, K("KF"), gb, f"GB{g}", "mb")):
                    bks = [nb(), nb()]
                    for i in range(4):
                        h = 4 * g + i
                        p, b0 = h // 2, 64 * (h % 2)
                        bk, bkk = bks[i % 2]
                        c0 = (i // 2) * 256
                        P.pe(lambda e, c0=c0, p=p, b0=b0, bk=bk, lf=lf: e.matmul(bk[:, c0:c0 + 256], lhsT=lf[b0:b0 + 64, p, :],
                                                                                rhs=KR[b0:b0 + 64, p, :, :].rearrange("k a t -> k (a t)"), start=True, stop=True),
                             r=[lfk, K("KR")], w=[bkk])
                    for par2 in range(2):
                        bk, bkk = bks[par2]
                        P.dve(lambda e, bk=bk, dst=dst, par2=par2, mk=mk: e.tensor_tensor(
                            out=dst[:, par2:4:2, :, :].rearrange("p h a t -> p h (a t)"), in0=bk[:].rearrange("p (h x) -> p h x", x=256),
                            in1=MK[mk + ty][:].rearrange("p (h x) -> p h x", x=256), op=ALU.mult),
                            r=[bkk, "M_" + mk], w=[dstk])
                bks = [nb(), nb()]
                for i in range(4):
                    h = 4 * g + i
                    p, b0 = h // 2, 64 * (h % 2)
                    bk, bkk = bks[i % 2]
                    c0 = (i // 2) * 128
                    P.pe(lambda e, c0=c0, p=p, b0=b0, bk=bk: e.matmul(bk[:, c0:c0 + 128], lhsT=KR[b0:b0 + 64, p, 0, :], rhs=BF[b0:b0 + 64, p, :], start=True, stop=True),
                         r=[K("KR"), K("BF")], w=[bkk])
                for par2 in range(2):
                    bk, bkk = bks[par2]
                    P.dve(lambda e, bk=bk, gt=gt, par2=par2: e.tensor_tensor(out=gt[:, par2:4:2, :], in0=bk[:, 0:256].rearrange("p (h t) -> p h t", t=128),
                                                                            in1=MK["mt" + ty][:, 0:256].rearrange("p (h t) -> p h t", t=128), op=ALU.mult),
                          r=[bkk, "M_mt"], w=[f"GT{g}"])
                if STOP == 'E0':
                    return
                yield
                P.pool(lambda e, ga=ga, g=g: e.tensor_tensor(out=TT[g][0][:], in0=ga[:, :, 0, :], in1=IDB[:].unsqueeze(1).broadcast_to([128, 4, 128]), op=ALU.add),
                       r=[f"GA{g}", "IDB"], w=[f"TT{g}0"])
                if STOP == 'E1':
                    return
                Rc, Rck = ga[:, :, 0, :], f"GA{g}"
                RTc, RTck = gt[:], f"GT{g}"
                Tc, Tck = TT[g][0], f"TT{g}0"
                NLV = 6 if ty == "P" else 2
                for lvl in range(1, NLV + 1):
                    sl = lvl % 2
                    if lvl < NLV:
                        bk, bkk = nb()
                        for i in range(4):
                            P.pe(lambda e, i=i, bk=bk, Rc=Rc, RTc=RTc: e.matmul(bk[:, i * 128:(i + 1) * 128], lhsT=RTc[:, i, :], rhs=Rc[:, i, :], start=True, stop=True),
                                 r=[Rck, RTck], w=[bkk])
                        rn, rnk = RN[g][sl], f"RN{g}{sl}"
                        P.act(lambda e, bk=bk, rn=rn: e.copy(out=rn[:].rearrange("p h t -> p (h t)"), in_=bk[:]), r=[bkk], w=[rnk])
                    bk, bkk = nb()
                    for i in range(4):
                        P.pe(lambda e, i=i, bk=bk, Rc=Rc, RTc=RTc: e.matmul(bk[:, i * 128:(i + 1) * 128], lhsT=Rc[:, i, :], rhs=RTc[:, i, :], start=True, stop=True),
                             r=[Rck, RTck], w=[bkk])
                    rtn, rtnk = RTN[g][sl], f"RTN{g}{sl}"
                    P.act(lambda e, bk=bk, rtn=rtn: e.copy(out=rtn[:].rearrange("p h t -> p (h t)"), in_=bk[:]), r=[bkk], w=[rtnk])
                    yield
                    bk, bkk = nb()
                    for i in range(4):
                        P.pe(lambda e, i=i, bk=bk, rtn=rtn, Tc=Tc: e.matmul(bk[:, i * 128:(i + 1) * 128], lhsT=rtn[:, i, :], rhs=Tc[:, i, :], start=True, stop=True),
                             r=[rtnk, Tck], w=[bkk])
                    tn, tnk = TT[g][sl], f"TT{g}{sl}"
                    P.dve(lambda e, bk=bk, tn=tn, Tc=Tc: e.tensor_tensor(out=tn[:].rearrange("p h t -> p (h t)"), in0=bk[:], in1=Tc[:].rearrange("p h t -> p (h t)"), op=ALU.add),
                          r=[bkk, Tck], w=[tnk])
                    yield
                    if lvl < NLV:
                        Rc, Rck = rn[:], rnk
                    RTc, RTck = rtn[:], rtnk
                    Tc, Tck = tn, tnk
                if STOP == 'E':
                    return
                yield
                yield "TD_DONE"
                bk, bkk = nb()
                for i in range(4):
                    h = 4 * g + i
                    P.pe(lambda e, i=i, h=h, bk=bk, gb=gb: e.matmul(bk[:, i * 64:(i + 1) * 64], lhsT=gb[:, i, 0, :], rhs=VB[:, 64 * h:64 * (h + 1)], start=True, stop=True),
                         r=[f"GB{g}", K("VB")], w=[bkk])
                P.act(lambda e, bk=bk, g=g: e.activation(out=X1N[g][:].rearrange("p h v -> p (h v)"), in_=bk[:, 0:256], func=AF.Copy, scale=-1.0), r=[bkk], w=[f"X1N{g}"])
                yield
                bk, bkk = nb()
                for i in range(4):
                    h = 4 * g + i
                    P.pe(lambda e, i=i, bk=bk, Tc=Tc, g=g: e.matmul(bk[:, i * 128:i * 128 + 64], lhsT=Tc[:, i, :], rhs=X1N[g][:, i, :], start=True, stop=True), r=[Tck, f"X1N{g}"], w=[bkk])
                    P.pe(lambda e, i=i, h=h, bk=bk, Tc=Tc: e.matmul(bk[:, i * 128 + 64:(i + 1) * 128], lhsT=Tc[:, i, :], rhs=KTM[:, 64 * h:64 * (h + 1)], start=True, stop=True), r=[Tck, K("KTM")], w=[bkk])
                uk, ukk = UK[g], f"UK{g}"
                bk4 = bk[:].rearrange("p (h a v) -> p h a v", a=2, v=64)
                P.act(lambda e, bk4=bk4, uk=uk: e.copy(out=uk[:, :, 0, :], in_=bk4[:, :, 0, :]), r=[bkk], w=[ukk])
                P.act(lambda e, bk4=bk4, uk=uk: e.activation(out=uk[:, :, 1, :], in_=bk4[:, :, 1, :], func=AF.Copy, scale=-1.0), r=[bkk], w=[ukk])
                if STOP == 'F':
                    return
                yield
                bk, bkk = nb()
                for i in range(4):
                    h = 4 * g + i
                    if ty == "P":
                        ob, col = 64 * (h % 2), (i // 2) * 128
                    else:
                        ob, col = 0, i * 128
                    P.pe(lambda e, h=h, ob=ob, col=col, bk=bk: e.matmul(bk[ob:ob + 64, col:col + 128], lhsT=RTm[:, 64 * h:64 * (h + 1)], rhs=IDB[:], start=True, stop=False), r=[K("RTm"), "IDB"], w=[bkk])
                    P.pe(lambda e, i=i, ob=ob, col=col, bk=bk, uk=uk, ga=ga: e.matmul(bk[ob:ob + 64, col:col + 128], lhsT=uk[:, i, 1, :], rhs=ga[:, i, 1, :], start=False, stop=True), r=[ukk, f"GA{g}"], w=[bkk])
                if ty == "P":
                    rh, rhk = RH[g], [f"RH{g}"]
                else:
                    rh, rhk = RHS, ["RH0", "RH1"]
                if ty == "P":
                    P.act(lambda e, bk=bk, rh=rh: e.copy(out=rh, in_=bk[:, 0:256].rearrange("p (a t) -> p a t", t=128)), r=[bkk], w=rhk)
                else:
                    P.act(lambda e, bk=bk, rh=rh: e.copy(out=rh, in_=bk[0:64, :].rearrange("p (a t) -> p a t", t=128)), r=[bkk], w=rhk)
                if ty == "P":
                    bk, bkk = nb()
                    for i in range(4):
                        h = 4 * g + i
                        ob, col = 64 * (h % 2), (i // 2) * 64
                        P.pe(lambda e, i=i, h=h, ob=ob, col=col, bk=bk, uk=uk: e.matmul(bk[ob:ob + 64, col:col + 64], lhsT=uk[:, i, 1, :], rhs=BE[:, 64 * h:64 * (h + 1)], start=True, stop=True), r=[ukk, K("BEb")], w=[bkk])
                    for j in range(2):
                        P.dve(lambda e, j=j, bk=bk, g=g: e.scalar_tensor_tensor(out=MM[g][:, j, :], in0=ID2[:], scalar=PC[:, 2 * g + j:2 * g + j + 1], in1=bk[:, j * 64:(j + 1) * 64], op0=ALU.mult, op1=ALU.add),
                              r=[bkk, "ID2", K("PC")], w=[f"MM{g}"])
                    for i in range(4):
                        h = 4 * g + i
                        p, b0, j = h // 2, 64 * (h % 2), i // 2
                        yield
                        vh = VB[:, 64 * h:64 * (h + 1)]
                        P.pe(lambda e, i=i, h=h, ga=ga, uk=uk: e.matmul(PSY[:, 64 * h:64 * (h + 1)], lhsT=ga[:, i, 1, :], rhs=uk[:, i, 0, :], start=True, stop=False), r=[f"GA{g}", ukk], w=["psy"])
                        P.pe(lambda e, i=i, h=h, gb=gb, vh=vh: e.matmul(PSY[:, 64 * h:64 * (h + 1)], lhsT=gb[:, i, 1, :], rhs=vh, start=False, stop=first), r=[f"GB{g}", K("VB")], w=["psy"])
                        if not first:
                            P.pe(lambda e, h=h, b0=b0, j=j, p=p, rh=rh: e.matmul(PSY[:, 64 * h:64 * (h + 1)], lhsT=rh[b0:b0 + 64, j, :], rhs=stprev[b0:b0 + 64, p, :], start=False, stop=True), r=[*rhk, stpk], w=["psy"])
                        P.pe(lambda e, i=i, h=h, b0=b0, p=p, uk=uk: e.matmul(PSS[b0:b0 + 64, p * 64:(p + 1) * 64], lhsT=BE[:, 64 * h:64 * (h + 1)], rhs=uk[:, i, 0, :], start=True, stop=False), r=[K("BEb"), ukk], w=["pss"])
                        P.pe(lambda e, h=h, b0=b0, p=p, vh=vh: e.matmul(PSS[b0:b0 + 64, p * 64:(p + 1) * 64], lhsT=KE[:, 64 * h:64 * (h + 1)], rhs=vh, start=False, stop=first), r=[K("KEb"), K("VB")], w=["pss"])
                        if not first:
                            P.pe(lambda e, b0=b0, j=j, p=p, g=g: e.matmul(PSS[b0:b0 + 64, p * 64:(p + 1) * 64], lhsT=MM[g][b0:b0 + 64, j, :], rhs=stprev[b0:b0 + 64, p, :], start=False, stop=True), r=[f"MM{g}", stpk], w=["pss"])
                else:
                    for i in range(4):
                        h = 4 * g + i
                        p, h2 = h // 2, h % 2
                        yield
                        vh = VB[:, 64 * h:64 * (h + 1)]
                        if h2 == 0:
                            P.dma("sp", lambda e, p=p: e.dma_start(out=S0[:], in_=stw[:, 2 * p:2 * p + 2, :, :].rearrange("b h v k -> (h v) b k")), w=["PRa", "PRb"])
                            for q in range(4):
                                bk, bkk = nb()
                                for bb in range(4):
                                    b = q * 4 + bb
                                    P.pe(lambda e, b=b, bb=bb, bk=bk: e.transpose(bk[0:64, bb * 128:(bb + 1) * 128], S0[:, b, :], IDN[:]), r=["PRa", "PRb", "IDN"], w=[bkk])
                                P.act(lambda e, q=q, bk=bk: e.copy(out=S0T[:, 4 * q:4 * q + 4, :].rearrange("p b t -> p (b t)"), in_=bk[0:64, :]), r=[bkk], w=["EIN", "EINV", "EEX", "EEND"])
                        if h == 0:
                            P.pool(lambda e: e.memset(RHm[:], 0.0), w=["SH1", "SH2", "CT", "CT2"])
                        rflat = CONV4[0:64, :]
                        P.pool(lambda e, i=i, rh=rh: e.tensor_copy(out=rflat[:, 0:2040].rearrange("p (b x) -> p b x", x=136)[:, :, 0:8],
                                                                    in_=rh[:, i, 0:120].rearrange("p (b t) -> p b t", t=8)), r=rhk, w=["SH1", "SH2", "CT", "CT2"])
                        P.pool(lambda e, i=i, rh=rh: e.tensor_copy(out=rflat[:, 2040:2048], in_=rh[:, i, 120:128]), r=rhk, w=["SH1", "SH2", "CT", "CT2"])
                        bmb = BM[:].unsqueeze(2).broadcast_to([128, NSEQ, 64])
                        P.pool(lambda e, i=i, uk=uk: e.tensor_tensor(out=KHm[:], in0=uk[:, i, 1, :].unsqueeze(1).broadcast_to([128, NSEQ, 64]), in1=bmb, op=ALU.mult), r=[ukk, "BM"], w=["PRa", "PRb"])
                        P.dve(lambda e, i=i, uk=uk: e.tensor_tensor(out=UTm[:], in0=uk[:, i, 0, :].unsqueeze(1).broadcast_to([128, NSEQ, 64]), in1=bmb, op=ALU.mult), r=[ukk, "BM"], w=["PRa", "PRb"])
                        P.dve(lambda e, vh=vh: e.tensor_tensor(out=Vm[:], in0=vh.unsqueeze(1).broadcast_to([128, NSEQ, 64]), in1=bmb, op=ALU.mult), r=[K("VB"), "BM"], w=["PRa", "PRb"])
                        P.pool(lambda e, h=h: e.tensor_tensor(out=DPC[:], in0=IDN[0:64, 0:64].unsqueeze(1).broadcast_to([64, NSEQ, 64]),
                                                              in1=PCS[:, h, :].unsqueeze(2).broadcast_to([64, NSEQ, 64]), op=ALU.mult), r=["IDN", "PCS"], w=["XM"])
                        P.pe(lambda e, i=i, h=h, ga=ga, uk=uk: e.matmul(PSY[:, 64 * h:64 * (h + 1)], lhsT=ga[:, i, 1, :], rhs=uk[:, i, 0, :], start=True, stop=False), r=[f"GA{g}", ukk], w=["psy"])
                        P.pe(lambda e, i=i, h=h, gb=gb, vh=vh: e.matmul(PSY[:, 64 * h:64 * (h + 1)], lhsT=gb[:, i, 1, :], rhs=vh, start=False, stop=False), r=[f"GB{g}", K("VB")], w=["psy"])
                        for b in range(NSEQ):
                            P.pe(lambda e, b=b, h=h, h2=h2: e.matmul(PSY[:, 64 * h:64 * (h + 1)], lhsT=RHm[:, b, :], rhs=S0T[:, b, 64 * h2:64 * (h2 + 1)], start=False, stop=(b == NSEQ - 1)), r=["SH1", "SH2", "CT", "CT2", "EIN", "EINV", "EEX", "EEND"], w=["psy"])
                        for q in range(2):
                            bk, bkk = nb()
                            for bb in range(8):
                                b = q * 8 + bb
                                P.pe(lambda e, b=b, bb=bb, h=h, bk=bk: e.matmul(bk[0:64, bb * 64:(bb + 1) * 64], lhsT=KHm[:, b, :], rhs=BE[:, 64 * h:64 * (h + 1)], start=True, stop=True), r=["PRa", "PRb", K("BEb")], w=[bkk])
                            P.dve(lambda e, q=q, bk=bk: e.tensor_tensor(out=MS[:, 8 * q:8 * q + 8, :].rearrange("p b k -> p (b k)"), in0=bk[0:64, :],
                                                                        in1=DPC[:, 8 * q:8 * q + 8, :].rearrange("p b k -> p (b k)"), op=ALU.add), r=[bkk, "XM"], w=["KK0", "SQ"])
                        so, sok = DPC, "XM"
                        for q in range(2):
                            bk, bkk = nb()
                            for bb in range(8):
                                b = q * 8 + bb
                                o_ = bk[0:64, bb * 64:(bb + 1) * 64]
                                P.pe(lambda e, b=b, o_=o_, h2=h2: e.matmul(o_, lhsT=S0T[:, b, 64 * h2:64 * (h2 + 1)], rhs=MS[:, b, :], start=True, stop=False), r=["EIN", "EINV", "EEX", "EEND", "KK0", "SQ"], w=[bkk])
                                P.pe(lambda e, b=b, o_=o_, h=h: e.matmul(o_, lhsT=UTm[:, b, :], rhs=BE[:, 64 * h:64 * (h + 1)], start=False, stop=False), r=["PRa", "PRb", K("BEb")], w=[bkk])
                                P.pe(lambda e, b=b, o_=o_, h=h: e.matmul(o_, lhsT=Vm[:, b, :], rhs=KE[:, 64 * h:64 * (h + 1)], start=False, stop=True), r=["PRa", "PRb", K("KEb")], w=[bkk])
                            P.act(lambda e, q=q, bk=bk, so=so: e.copy(out=so[:, 8 * q:8 * q + 8, :].rearrange("p b k -> p (b k)"), in_=bk[0:64, :]), r=[bkk], w=[sok])
                        P.dma("sp", lambda e, h=h, so=so: e.dma_start(out=wkv_s[:, h, :, :].rearrange("b v k -> v b k"), in_=so[:]), r=[sok])
            if ty == "P" and GI_MODE != "none":
                gens = [mach(0), mach(1)]
                alive = [True, True]
                passed = [False, False]
                while any(alive) and not (GI_MODE == "td" and all(passed)):
                    for gi in range(2):
                        if alive[gi] and not (GI_MODE == "td" and passed[gi]):
                            try:
                                if next(gens[gi]) == "TD_DONE":
                                    passed[gi] = True
                            except StopIteration:
                                alive[gi] = False
                    yield
                for gi in range(2):
                    if alive[gi]:
                        for _ in gens[gi]:
                            yield
            else:
                for g in range(2):
                    for _ in mach(g):
                        yield
            yield
            P.act(lambda e: e.copy(out=YY[:], in_=PSY[:]), r=["psy"], w=["YY"])
            if ty == "P":
                stn, stnk = STt[par], f"ST{par}"
                P.act(lambda e, stn=stn: e.copy(out=stn[:].rearrange("p a v -> p (a v)"), in_=PSS[:, 0:256]), r=["pss"], w=[stnk])
                if lastp:
                    bk, bkk = nb()
                    for p in range(4):
                        P.pe(lambda e, p=p, bk=bk, stn=stn: e.transpose(bk[0:64, p * 128:(p + 1) * 128], stn[:, p, :], IDN[:]), r=[stnk, "IDN"], w=[bkk])
                    P.act(lambda e, bk=bk: e.copy(out=WPO[:].rearrange("p a t -> p (a t)"), in_=bk[0:64, :]), r=[bkk], w=["RH0", "RH1"])
                    P.dma("sp", lambda e: e.dma_start(out=wkv_p.rearrange("(p h2) v k -> v p h2 k", h2=2), in_=WPO[:].rearrange("v p (h2 k) -> v p h2 k", h2=2)), r=["RH0", "RH1"])
            yield
            b8 = lambda t: t[:].unsqueeze(2).broadcast_to([128, 8, 64])
            P.dve(lambda e: e.tensor_reduce(out=S8[:], in_=v3(YY[:]), axis=AX.X, op=ALU.add), r=["YY"], w=["S8c"])
            P.dve(lambda e: e.tensor_tensor(out=SQ[:], in0=YY[:], in1=YY[:], op=ALU.mult), r=["YY"], w=["GNS"])
            P.dve(lambda e: e.tensor_reduce(out=S8b[:], in_=v3(SQ[:]), axis=AX.X, op=ALU.add), r=["GNS"], w=["S8d"])
            P.dve(lambda e: e.tensor_scalar(out=S8[:], in0=S8[:], scalar1=1.0 / HS, scalar2=None, op0=ALU.mult), r=["S8c"], w=["S8c"])
            P.dve(lambda e: e.tensor_tensor(out=S8e[:], in0=S8[:], in1=S8[:], op=ALU.mult), r=["S8c"], w=["S8e"])
            P.dve(lambda e: e.scalar_tensor_tensor(out=S8b[:], in0=S8b[:], scalar=1.0 / HS, in1=S8e[:], op0=ALU.mult, op1=ALU.subtract), r=["S8d", "S8e"], w=["S8d"])
            P.dve(lambda e: e.tensor_scalar(out=S8b[:], in0=S8b[:], scalar1=64e-5, scalar2=None, op0=ALU.add), r=["S8d"], w=["S8d"])
            P.act(lambda e: e.activation(out=S8b[:], in_=S8b[:], func=AF.Sqrt), r=["S8d"], w=["S8d"])
            P.dve(lambda e: e.reciprocal(out=S8b[:], in_=S8b[:]), r=["S8d"], w=["S8d"])
            P.dve(lambda e: e.tensor_tensor(out=v3(YY[:]), in0=v3(YY[:]), in1=b8(S8), op=ALU.subtract), r=["YY", "S8c"], w=["YY"])
            P.dve(lambda e: e.tensor_tensor(out=v3(YY[:]), in0=v3(YY[:]), in1=b8(S8b), op=ALU.mult), r=["YY", "S8d"], w=["YY"])
            P.dve(lambda e: e.tensor_tensor(out=YY[:], in0=YY[:], in1=GG[:], op=ALU.mult), r=["YY", K("GG")], w=["YY"])
            P.dve(lambda e: e.tensor_tensor(out=YC[:, 512:1024], in0=YY[:], in1=BON[:], op=ALU.add), r=["YY", K("BON")], w=[K("YCb")])
            if "ycat" in dbg_out:
                dbgdump("ycat", YC[:], K("YCa")) if ti == dbg_ti[0] else None
            if STOP == 'H':
                return
            yield
            for k in range(8):
                P.pe(lambda e, k=k: e.transpose(PSB[:, k * 128:(k + 1) * 128], YC[:, k * 128:(k + 1) * 128], IDB[:]), r=[K("YCa"), K("YCb"), "IDB"], w=["psb"])
            P.act(lambda e: e.copy(out=YCT[:].rearrange("p k t -> p (k t)"), in_=PSB[:]), r=["psb"], w=["YCT"])
            for j in range(2):
                bk, bkk = nb()
                wr, wrk = wchunk(wbo, j * 512, 512)
                for k in range(8):
                    P.pe(lambda e, k=k, bk=bk, wr=wr: e.matmul(bk[:], lhsT=YCT[:, k, :], rhs=wr[:, k, :], start=(k == 0), stop=(k == 7)), r=["YCT", wrk], w=[bkk])
                P.dve(lambda e, j=j, bk=bk: e.tensor_tensor(out=X1T[:, j * 512:(j + 1) * 512], in0=bk[:], in1=x[:, j * 512:(j + 1) * 512], op=ALU.add), r=[bkk, xk], w=["X1T"])
            P.dma("sp", lambda e: e.dma_start(out=x1s[ti * 128:(ti + 1) * 128, :], in_=X1T[:]), r=["X1T"], w=["x1s"])
            yield

        def drain(gen):
            for _ in gen:
                pass

        def load_masks(names):
            for nm in names:
                P.dma("sp", lambda e, nm=nm: e.dma_start(out=MKT[nm][:], in_=cd[nm + "S"]), w=["M_" + nm])

        dbg_ti = [dbg.get("_ti", 0) if dbg else 0]
        PIPE = os.environ.get("KNOPIPE", "") != "1"
        if not PIPE:
            for ti in range(NT):
                if ti == NPT:
                    load_masks(("tri", "tgt", "ma", "mb", "mt"))
                drain(front(ti))
                drain(back(ti))
        else:
            drain(front(0))
            for ti in range(NT):
                if ti == NPT:
                    load_masks(("ma", "mb", "mt"))
                bgen = back(ti)
                fgen = None
                if ti + 1 < NT:
                    if ti + 1 == NPT:
                        load_masks(("tri", "tgt"))
                    fgen = front(ti + 1)
                bdone = fdone = False
                while not (bdone and (fgen is None or fdone)):
                    for _ in range(3):
                        if not bdone:
                            try:
                                next(bgen)
                            except StopIteration:
                                bdone = True
                    if fgen is not None and not fdone:
                        try:
                            next(fgen)
                        except StopIteration:
                            fdone = True

    if _dry:
        return wq
    P.barrier()

    if SKIP2:
        return nc, P.finalize()
    with ExitStack() as st2:
        def SB2(name, shape, dt=F32):
            return st2.enter_context(nc.sbuf_tensor(name, list(shape), dt))

        PS2 = [st2.enter_context(nc.psum_tensor(f"q{i}", [128, 512], F32)) for i in range(6)]
        PSB2 = st2.enter_context(nc.psum_tensor("qb", [128, 1024], BF16))
        bank2 = [0]

        def nb2():
            i = bank2[0] % 6
            bank2[0] += 1
            return PS2[i], f"q{i}"

        NSTG = 8
        RNG = 3
        W1R = [SB2(f"W1R{i}", [128, 8, 512], BF16) for i in range(RNG)]
        W2R = [SB2(f"W2R{i}", [128, 4, D], BF16) for i in range(RNG)]
        X1 = SB2("X1A", [128, NT, D])
        H2T = SB2("H2T", [128, 8, NT * 128], BF16)
        G2T = SB2("G2T", [128, 8])
        NFb = SB2("NFb", [128, D])
        IDB2 = SB2("IDB2", [128, 128], BF16)
        JK2 = SB2("JK2", [128, D])
        HB2 = SB2("HB2", [128, D], BF16)
        SS2 = SB2("SS2", [128, 1])
        RS2 = SB2("RS2", [128, 1])
        HR = [SB2(f"HR{i}", [128, 512]) for i in range(2)]
        HID = [SB2(f"HID{i}", [128, 4, 512], BF16) for i in range(2)]
        YO = [SB2(f"YO{i}", [128, D]) for i in range(2)]
        P.dma("sp", lambda e: e.dma_start(out=G2T[:], in_=g2T_d), w=["G2T"])
        P.dma("sp", lambda e: e.dma_start(out=NFb[:], in_=nf_d.partition_broadcast(128)), w=["NFb"])
        P.dma("pool", lambda e: e.dma_start(out=IDB2[:], in_=cd["ident"]), w=["IDB2"])

        def load_stage(s):
            rb = s % RNG
            P.dma("pool", lambda e: e.dma_start(out=W1R[rb][:], in_=w_ff1[:, s * 512:(s + 1) * 512].rearrange("(k p) f -> p k f", p=128)), w=[f"W1R{rb}"])
            P.dma("pool", lambda e: e.dma_start(out=W2R[rb][:], in_=w_ff2[s * 512:(s + 1) * 512, :].rearrange("(c p) d -> p c d", p=128)), w=[f"W2R{rb}"])

        for s in range(min(RNG, NSTG)):
            load_stage(s)
        for ti in range(NT):
            P.dma("sp", lambda e, ti=ti: e.dma_start(out=X1[:, ti, :], in_=x1s[ti * 128:(ti + 1) * 128, :]), w=[f"X1_{ti}"])
        def preamble(ti):
            xk = f"X1_{ti}"
            P.act(lambda e, ti=ti: e.activation(out=JK2[:], in_=X1[:, ti, :], func=AF.Square, accum_out=SS2[:]), r=[xk], w=["JK2", "SS2"])
            P.dve(lambda e: e.tensor_scalar(out=RS2[:], in0=SS2[:], scalar1=1.0 / D, scalar2=1e-6, op0=ALU.mult, op1=ALU.add), r=["SS2"], w=["RS2"])
            P.act(lambda e: e.activation(out=RS2[:], in_=RS2[:], func=AF.Sqrt), r=["RS2"], w=["RS2"])
            P.dve(lambda e: e.reciprocal(out=RS2[:], in_=RS2[:]), r=["RS2"], w=["RS2"])
            P.act(lambda e, ti=ti: e.activation(out=HB2[:], in_=X1[:, ti, :], func=AF.Copy, scale=RS2[:, 0:1]), r=[xk, "RS2"], w=["HB2"])
            for k in range(8):
                P.pe(lambda e, k=k: e.transpose(PSB2[:, k * 128:(k + 1) * 128], HB2[:, k * 128:(k + 1) * 128], IDB2[:]), r=["HB2", "IDB2"], w=["qb"])
            P.dve(lambda e, ti=ti: e.tensor_tensor(out=H2T[:, :, ti * 128:(ti + 1) * 128], in0=PSB2[:].rearrange("p (k t) -> p k t", t=128),
                                                  in1=G2T[:].unsqueeze(2).broadcast_to([128, 8, 128]), op=ALU.mult), r=["qb", "G2T"], w=[f"H2T_{ti}"])
        groups = []
        t0 = 0
        while t0 < NT:
            n = min(4, NT - t0)
            groups.append((t0, n))
            t0 += n
        def final_tile(ti):
            xk = f"X1_{ti}"
            yo, yok = YO[ti % 2], f"YO{ti % 2}"
            P.act(lambda e, ti=ti: e.activation(out=JK2[:], in_=X1[:, ti, :], func=AF.Square, accum_out=SS2[:]), r=[xk], w=["JK2", "SS2"])
            P.dve(lambda e: e.tensor_scalar(out=RS2[:], in0=SS2[:], scalar1=1.0 / D, scalar2=1e-6, op0=ALU.mult, op1=ALU.add), r=["SS2"], w=["RS2"])
            P.act(lambda e: e.activation(out=RS2[:], in_=RS2[:], func=AF.Sqrt), r=["RS2"], w=["RS2"])
            P.dve(lambda e: e.reciprocal(out=RS2[:], in_=RS2[:]), r=["RS2"], w=["RS2"])
            P.dve(lambda e, ti=ti, yo=yo: e.scalar_tensor_tensor(out=yo[:], in0=X1[:, ti, :], scalar=RS2[:, 0:1], in1=NFb[:], op0=ALU.mult, op1=ALU.mult), r=[xk, "RS2", "NFb"], w=[yok])
            dst = y_s if ti == NPT else y_p[ti * 128:(ti + 1) * 128, :]
            P.dma("sp", lambda e, yo=yo, dst=dst: e.dma_start(out=dst, in_=yo[:]), r=[yok])


        items = [(s_, t0, n) for s_ in range(NSTG) for (t0, n) in groups]

        def ffn1(k):
            s_, t0, n = items[k]
            rb = s_ % RNG
            ntok = n * 128
            hid, hidk = HID[k % 2], f"HID{k % 2}"
            for fc in range(4):
                bk, bkk = nb2()
                for kk in range(8):
                    P.pe(lambda e, kk=kk, fc=fc, bk=bk, t0=t0, ntok=ntok, rb=rb: e.matmul(bk[:, 0:ntok], lhsT=W1R[rb][:, kk, fc * 128:(fc + 1) * 128], rhs=H2T[:, kk, t0 * 128:t0 * 128 + ntok], start=(kk == 0), stop=(kk == 7)),
                         r=[f"W1R{rb}"] + [f"H2T_{t}" for t in range(t0, t0 + n)], w=[bkk])
                hr, hrk = HR[fc % 2], f"HR{fc % 2}"
                P.act(lambda e, bk=bk, hr=hr, ntok=ntok: e.activation(out=hr[:, 0:ntok], in_=bk[:, 0:ntok], func=AF.Relu), r=[bkk], w=[hrk])
                P.pool(lambda e, hr=hr, hid=hid, fc=fc, ntok=ntok: e.tensor_tensor(out=hid[:, fc, 0:ntok], in0=hr[:, 0:ntok], in1=hr[:, 0:ntok], op=ALU.mult), r=[hrk], w=[hidk])

        def ffn2(k):
            s_, t0, n = items[k]
            rb = s_ % RNG
            hid, hidk = HID[k % 2], f"HID{k % 2}"
            for tl in range(n):
                ti = t0 + tl
                for half in range(2):
                    bk, bkk = nb2()
                    for fc in range(4):
                        P.pe(lambda e, fc=fc, bk=bk, tl=tl, half=half, hid=hid, rb=rb: e.matmul(bk[:], lhsT=hid[:, fc, tl * 128:(tl + 1) * 128], rhs=W2R[rb][:, fc, half * 512:(half + 1) * 512], start=(fc == 0), stop=(fc == 3)),
                             r=[hidk, f"W2R{rb}"], w=[bkk])
                    P.dve(lambda e, bk=bk, ti=ti, half=half: e.tensor_tensor(out=X1[:, ti, half * 512:(half + 1) * 512], in0=bk[:], in1=X1[:, ti, half * 512:(half + 1) * 512], op=ALU.add),
                          r=[bkk, f"X1_{ti}"], w=[f"X1_{ti}"])
                if s_ == NSTG - 1:
                    final_tile(ti)

        FFNP = os.environ.get("KNOFFNP", "") != "1"

        def pre_group(k):
            s_, t0, n = items[k]
            if s_ == 0:
                for t in range(t0, t0 + n):
                    preamble(t)

        if FFNP:
            pre_group(0)
            ffn1(0)
        for k in range(len(items)):
            if FFNP:
                if k + 1 < len(items):
                    pre_group(k + 1)
                    ffn1(k + 1)
            else:
                pre_group(k)
                ffn1(k)
            ffn2(k)
            s_, t0, n = items[k]
            if (t0, n) == groups[-1] and s_ + RNG < NSTG:
                load_stage(s_ + RNG)
        if os.environ.get('KSCHED', '1') == '1':
            P.reschedule()
        stats = P.finalize()
    return nc, stats


_CACHE = {}


def make_in_maps(inputs, NPT=16, ncores=8):
    f = lambda a: np.ascontiguousarray(np.asarray(a, dtype=np.float32))
    c = _consts()
    shared = {
        "w_in": f(inputs["w_in"][0]), "w_out": f(inputs["w_out"][0]),
        "w_ff1": f(inputs["w_ff1"][0]), "w_ff2": f(inputs["w_ff2"][0]),
        "g1T": f(np.asarray(inputs["norm1_g"][0]).reshape(8, 128).T),
        "g2T": f(np.asarray(inputs["norm2_g"][0]).reshape(8, 128).T),
        "mu": f(inputs["mu"][0]).reshape(1, RC),
        "convw": f(inputs["conv_w"][0]).reshape(1, 3 * CW_),
        "k_k": f(inputs["k_k"][0]).reshape(1, RW), "k_a": f(inputs["k_a"][0]).reshape(1, RW),
        "r_k": f(inputs["r_k"][0]).reshape(1, RW),
        "lnx_g": f(inputs["lnx_g"][0]).reshape(1, RW), "lnx_b": f(inputs["lnx_b"][0]).reshape(1, RW),
        "normf": f(inputs["normf_g"]).reshape(1, D),
        "w0": f(inputs["w0"][0]).reshape(1, RW), "a0": f(inputs["a0"][0]).reshape(1, RW),
        "w2a": f(np.concatenate([np.asarray(inputs["w2"][0]), np.asarray(inputs["a2"][0])], 0)),
        "g2": f(inputs["g2"][0]),
    }
    for k, v in c.items():
        shared["c_" + k] = f(v)
    xpr = np.asarray(inputs["x_prompt"], dtype=np.float32)
    xsm = np.asarray(inputs["x_sample"], dtype=np.float32)
    maps = []
    for ci in range(ncores):
        m = dict(shared)
        m["xp"] = f(xpr[ci, :NPT * 128])
        sl = slice(ci * NSEQ, (ci + 1) * NSEQ)
        m["xs"] = f(xsm[sl].reshape(NSEQ * DT_, D))
        m["stc"] = f(inputs["state_conv"][0][sl])
        m["sts"] = f(inputs["state_shift"][0][sl])
        m["stw"] = f(inputs["state_wkv"][0][sl])
        maps.append(m)
    return maps


def gather(results, NPT=16, ncores=8):
    g = lambda name: [np.asarray(r[name], dtype=np.float32) for r in results]
    y_p = np.stack(g("y_p"), 0)
    y_s = np.stack(g("y_s"), 0).reshape(ncores * NSEQ, DT_, D)
    conv_p = np.stack(g("conv_p"), 0)[None]
    shift_p = np.concatenate(g("shift_p"), 0)[None]
    wkv_p = np.stack(g("wkv_p"), 0)[None]
    conv_s = np.concatenate(g("conv_s"), 0)[None]
    shift_s = np.concatenate(g("shift_s"), 0)[None]
    wkv_s = np.concatenate(g("wkv_s"), 0)[None]
    return (y_p, y_s, conv_p, shift_p, wkv_p, conv_s, shift_s, wkv_s)


def kernel(**inputs):
    NPT = 16
    if "nc" not in _CACHE:
        _CACHE["nc"] = build(NPT)[0]
    nc = _CACHE["nc"]
    maps = make_in_maps(inputs, NPT, 8)
    res = run_bass_kernel_spmd(nc, maps, core_ids=list(range(8)))
    return gather(res.results, NPT, 8)
```

```python
import os
import numpy as np
import concourse.bass as bass
import concourse.mybir as mybir
from concourse.bass_utils import run_bass_kernel_spmd

F32 = mybir.dt.float32
BF16 = mybir.dt.bfloat16
ALU = mybir.AluOpType
AF = mybir.ActivationFunctionType
AX = mybir.AxisListType

D = 1024
CW_ = 512
RW = 512
NH = 8
HS = 64
RC = 1792
IC = 3328
DFF = 4096
NSEQ = 16
DT_ = 8
CDEC = float(np.exp(-0.5))


class Op:
    __slots__ = ("eng", "emit", "deps", "dma", "inc", "done", "dom", "alldeps", "cost", "idx", "fin", "succ", "npend", "prio")

    def __init__(self, eng, emit, dma):
        self.eng = eng
        self.emit = emit
        self.dma = dma
        self.deps = []
        self.alldeps = []
        self.cost = 0.0
        self.inc = False
        self.done = None
        self.dom = (eng + "_dma") if dma else eng


class Prog:
    NSLOT = {"sp": 24, "pool": 6, "act": 4}

    def __init__(self, nc, self_sync=False):
        self.nc = nc
        self.ops = []
        self.lastw = {}
        self.readers = {}
        self.self_sync = self_sync
        self.ndma = {}
        self.lastslot = {}

    COST = {"pe": 0.12, "act": 0.6, "dve": 0.7, "pool": 1.3}

    def op(self, eng, emit, reads=(), writes=(), dma=False, cost=None):
        o = Op(eng, emit, dma)
        o.cost = cost if cost is not None else (3.0 if dma else self.COST[eng])
        deps = {}
        if dma:
            n = self.ndma.get(eng, 0)
            self.ndma[eng] = n + 1
            o.dom = f"{eng}_dma{n % self.NSLOT[eng]}"
            prev = self.lastslot.get(o.dom)
            if prev is not None:
                deps[id(prev)] = (prev, True)
            self.lastslot[o.dom] = o
        for k in reads:
            w = self.lastw.get(k)
            if w is not None:
                deps[id(w)] = (w, True)
        for k in writes:
            w = self.lastw.get(k)
            if w is not None:
                deps[id(w)] = (w, True)
            for r in self.readers.get(k, ()):
                if id(r) not in deps:
                    deps[id(r)] = (r, False)
        for k in reads:
            self.readers.setdefault(k, []).append(o)
        for k in writes:
            self.lastw[k] = o
            self.readers[k] = []
        for d, hard in deps.values():
            if d is o:
                continue
            o.alldeps.append(d)
            if d.dom == o.dom and not o.dma:
                if o.eng != "pe":
                    o.deps.append(d)
                continue
            o.deps.append(d)
        self.ops.append(o)
        return o

    def pe(self, emit, r=(), w=()):
        return self.op("pe", emit, r, w)

    def act(self, emit, r=(), w=()):
        return self.op("act", emit, r, w)

    def dve(self, emit, r=(), w=()):
        return self.op("dve", emit, r, w)

    def pool(self, emit, r=(), w=()):
        return self.op("pool", emit, r, w)

    def dma(self, eng, emit, r=(), w=()):
        return self.op(eng, emit, r, w, dma=True)

    def barrier(self):
        lastdom = {}
        for o in self.ops:
            if o.emit is not None:
                lastdom[o.dom] = o
        for eng in ("pe", "act", "dve", "pool", "sp"):
            b = Op(eng, None, False)
            b.deps = [o for dom, o in lastdom.items() if dom != eng]
            self.ops.append(b)
        self.lastw = {}
        self.readers = {}

    def reschedule(self, hop=float(os.environ.get("KHOP", "0.05"))):
        segs, cur = [], []
        for o in self.ops:
            if o.emit is None:
                segs.append(cur)
                segs.append([o])
                cur = []
            else:
                cur.append(o)
        segs.append(cur)
        new_ops = []
        for seg in segs:
            if len(seg) <= 1 or seg[0].emit is None:
                new_ops.extend(seg)
                continue
            inseg = {id(o) for o in seg}
            for o in seg:
                o.succ = []
                o.fin = 0.0
            fixed = set(os.environ.get("KFIXED", "pe").split(","))
            lastfixed = {}
            for o in seg:
                o.npend = 0
                for d in o.alldeps:
                    if id(d) in inseg:
                        d.succ.append(o)
                        o.npend += 1
                if o.eng in fixed and not o.dma:
                    p_ = lastfixed.get(o.eng)
                    if p_ is not None and all(p_ is not d for d in o.alldeps):
                        p_.succ.append(o)
                        o.npend += 1
                    lastfixed[o.eng] = o
            for o in reversed(seg):
                o.prio = o.cost + max([x.prio + (hop if x.eng != o.eng else 0.0) for x in o.succ], default=0.0)
            ready = {}
            for o in seg:
                if o.npend == 0:
                    ready.setdefault(o.eng, []).append(o)
            cursor = {}
            order = []
            nleft = len(seg)
            while nleft:
                best = None
                for eng, lst in ready.items():
                    if not lst:
                        continue
                    cur_t = cursor.get(eng, 0.0)
                    for o in lst:
                        est = cur_t
                        for d in o.alldeps:
                            if id(d) in inseg:
                                t = d.fin + (hop if d.eng != o.eng or d.dma != o.dma else 0.0)
                                if t > est:
                                    est = t
                        key = (est, -o.prio)
                        if best is None or key < best[0]:
                            best = (key, o, est)
                _, o, est = best
                ready[o.eng].remove(o)
                o.fin = est + o.cost
                cursor[o.eng] = (est + 0.1) if o.dma else o.fin
                order.append(o)
                nleft -= 1
                for x in o.succ:
                    x.npend -= 1
                    if x.npend == 0:
                        ready.setdefault(x.eng, []).append(x)
            new_ops.extend(order)
        self.ops = new_ops

    def finalize(self, final_eng="sp"):
        nc = self.nc
        ops = self.ops
        fin = Op(final_eng, None, False)
        lastdom = {}
        for o in ops:
            if o.emit is not None:
                lastdom[o.dom] = o
        for dom, o in lastdom.items():
            if "_dma" in dom:
                fin.deps.append(o)
        ops = ops + [fin]
        for o in ops:
            if o.dma:
                o.inc = True
            for d in o.deps:
                d.inc = True
        cnt = {}
        for o in ops:
            if o.inc:
                cnt[o.dom] = cnt.get(o.dom, 0) + (16 if o.dma else 1)
                o.done = cnt[o.dom]
        doms = sorted({o.dom for o in ops if o.inc})
        sems = {d: nc.alloc_semaphore(name="s_" + d) for d in doms}
        per_eng = {}
        for o in ops:
            per_eng.setdefault(o.eng, []).append(o)
        engs = {"pe": "tensor", "act": "scalar", "dve": "vector", "pool": "gpsimd", "sp": "sync"}

        def run(engname, e):
            waited = {}
            for o in per_eng.get(engname, []):
                need = {}
                for d in o.deps:
                    if d.done > need.get(d.dom, 0):
                        need[d.dom] = d.done
                for dom, v in need.items():
                    if v > waited.get(dom, 0):
                        e.wait_ge(sems[dom], v)
                        waited[dom] = v
                if o.emit is None:
                    continue
                ins = o.emit(e)
                if o.inc:
                    ins.then_inc(sems[o.dom], 16 if o.dma else 1)

        with nc.Block() as block:
            for engname, attr in engs.items():
                if engname not in per_eng:
                    continue

                def mk(engname=engname):
                    def f(e):
                        run(engname, e)
                    return f
                getattr(block, attr)(mk())
        return dict(nops=len(ops), cnt=cnt)


def _consts():
    c = {}
    idx = np.arange(128)
    c["ident"] = np.eye(128, dtype=np.float32)
    c["id2"] = (idx[:, None] % 64 == np.arange(64)[None, :]).astype(np.float32)
    for ty in ("P", "S"):
        if ty == "P":
            blk = np.zeros(128, np.int64)
        else:
            blk = idx // DT_
        same = blk[:, None] == blk[None, :]
        s = idx[:, None]
        t = idx[None, :]
        lt = (same & (s < t)).astype(np.float32)
        le = (same & (s <= t)).astype(np.float32)
        gt = (same & (s > t)).astype(np.float32)
        c["tri" + ty] = (-CDEC) * le
        c["tgt" + ty] = (-CDEC) * gt
        c["ma" + ty] = np.concatenate([-lt, le, -lt, le], 1)
        c["mb" + ty] = np.concatenate([lt, le, lt, le], 1)
        c["mt" + ty] = np.concatenate([-gt] * 4, 1)
    c["segP"] = np.full((128, 1), -CDEC, np.float32)
    bm = (idx[:, None] // DT_ == np.arange(NSEQ)[None, :]).astype(np.float32)
    c["segS"] = (-CDEC) * bm
    c["bm"] = bm
    return c


CONST_SHAPES = {k: v.shape for k, v in _consts().items()}


def build(NPT=16, dbg=None, _dry=False, _order=None):
    if not _dry and _order is None:
        _order = build(NPT, None, _dry=True)
    STOP = os.environ.get('KSTOP', '')
    GI_MODE = os.environ.get('KGI', 'td')
    SKIP2 = os.environ.get('KSKIP2', '') == '1'
    NT = NPT + 1
    nc = bass.Bass("TRN2", target_bir_lowering=False)
    P = Prog(nc)
    din = {}

    def DI(name, shape):
        din[name] = nc.dram_tensor(name, list(shape), F32, kind="ExternalInput").ap()
        return din[name]

    def DO(name, shape):
        return nc.dram_tensor(name, list(shape), F32, kind="ExternalOutput").ap()

    xp = DI("xp", [NPT * 128, D])
    xs = DI("xs", [128, D])
    stc = DI("stc", [NSEQ, 2, CW_])
    sts = DI("sts", [NSEQ, RC])
    stw = DI("stw", [NSEQ, NH, HS, HS])
    w_in = DI("w_in", [D, IC])
    w_out = DI("w_out", [D, D])
    w_ff1 = DI("w_ff1", [D, DFF])
    w_ff2 = DI("w_ff2", [DFF, D])
    g1T_d = DI("g1T", [128, 8])
    g2T_d = DI("g2T", [128, 8])
    mu_d = DI("mu", [1, RC])
    cw_d = DI("convw", [1, 3 * CW_])
    kk_d = DI("k_k", [1, RW])
    ka_d = DI("k_a", [1, RW])
    rk_d = DI("r_k", [1, RW])
    lg_d = DI("lnx_g", [1, RW])
    lb_d = DI("lnx_b", [1, RW])
    nf_d = DI("normf", [1, D])
    w0_d = DI("w0", [1, RW])
    a0_d = DI("a0", [1, RW])
    w2a_d = DI("w2a", [128, RW])
    g2_d = DI("g2", [128, RW])
    cd = {k: DI("c_" + k, shp) for k, shp in CONST_SHAPES.items()}

    y_p = DO("y_p", [NPT * 128, D])
    y_s = DO("y_s", [128, D])
    conv_p = DO("conv_p", [2, CW_])
    shift_p = DO("shift_p", [1, RC])
    wkv_p = DO("wkv_p", [NH, HS, HS])
    conv_s = DO("conv_s", [NSEQ, 2, CW_])
    shift_s = DO("shift_s", [NSEQ, RC])
    wkv_s = DO("wkv_s", [NSEQ, NH, HS, HS])
    x1s = nc.dram_tensor("x1s", [NT * 128, D], F32).ap()
    wbi = nc.dram_tensor("wbi", [7, 128, 8, 512], BF16).ap()
    wbo = nc.dram_tensor("wbo", [2, 128, 8, 512], BF16).ap()
    bnd_p = nc.dram_tensor("bnd_p", [NT, RC], F32).ap()
    bnd_c = nc.dram_tensor("bnd_c", [NT, 2, CW_], F32).ap()
    dbg_out = {}
    if dbg:
        for name, shape in dbg.items():
            if not name.startswith("_"):
                dbg_out[name] = DO("dbg_" + name, shape)

    from contextlib import ExitStack

    with ExitStack() as st0:
        TW = [st0.enter_context(nc.sbuf_tensor(f"TW{i}", [128, IC], BF16)) for i in range(2)]
        n0 = 0
        for (src_w, dst_w, wd) in ((w_in, wbi, IC), (w_out, wbo, D)):
            for k in range(8):
                tw, twk = TW[n0 % 2], f"TW{n0 % 2}"
                n0 += 1
                P.dma("pool", lambda e, tw=tw, src_w=src_w, k=k, wd=wd: e.dma_start(out=tw[:, 0:wd], in_=src_w[k * 128:(k + 1) * 128, :]), w=[twk])
                nfull = wd // 512
                P.dma("sp", lambda e, tw=tw, dst_w=dst_w, k=k, nfull=nfull: e.dma_start(out=dst_w[0:nfull, :, k, :].rearrange("j p n -> p j n"),
                                                                                  in_=tw[:, 0:nfull * 512].rearrange("p (j n) -> p j n", n=512)), r=[twk], w=["wb"])
                if wd % 512:
                    P.dma("sp", lambda e, tw=tw, dst_w=dst_w, k=k, nfull=nfull, wd=wd: e.dma_start(out=dst_w[nfull, :, k, 0:wd - nfull * 512], in_=tw[:, nfull * 512:wd]), r=[twk], w=["wb"])
    P.barrier()

    with ExitStack() as st1:
        def SB(name, shape, dt=F32):
            return st1.enter_context(nc.sbuf_tensor(name, list(shape), dt))

        def PSF(name, shape, dt=F32):
            return st1.enter_context(nc.psum_tensor(name, list(shape), dt))

        NB = 5
        PS = [PSF(f"ps{i}", [128, 512]) for i in range(NB)]
        PSY = PSF("psy", [128, 512])
        PSS = PSF("pss", [128, 512])
        PSB = PSF("psb", [128, 1024], BF16)
        bank_i = [0]

        def nb():
            i = bank_i[0] % NB
            bank_i[0] += 1
            return PS[i], f"ps{i}"

        NRING = 3
        WR = [SB(f"WR{i}", [128, 8, 512], BF16) for i in range(NRING)]
        wq = list(_order) if _order is not None else []
        wsrc = {"wbi": wbi, "wbo": wbo}
        wstate = {"issued": 0, "used": 0}

        def wchunk(srcw, c0, wd):
            name = "wbi" if srcw is wbi else "wbo"
            n = wstate["used"]
            wstate["used"] += 1
            if _dry:
                wq.append((name, c0, wd))
                return WR[n % NRING], f"WR{n % NRING}"
            assert wq[n] == (name, c0, wd), (n, wq[n], name, c0, wd)
            while wstate["issued"] < min(len(wq), n + NRING):
                m = wstate["issued"]
                nm2, c2, wd2 = wq[m]
                P.dma("sp", lambda e, m=m, nm2=nm2, c2=c2, wd2=wd2: e.dma_start(out=WR[m % NRING][:, :, 0:wd2], in_=wsrc[nm2][c2 // 512, :, :, 0:wd2]),
                      w=[f"WR{m % NRING}"])
                wstate["issued"] += 1
            return WR[n % NRING], f"WR{n % NRING}"

        def bc_tile(name, dram, n):
            t = SB(name, [128, n])
            P.dma("sp", lambda e: e.dma_start(out=t[:], in_=dram.partition_broadcast(128)), w=[name])
            return t

        def ld_tile(name, dram, shape, dt=F32, eng="sp"):
            t = SB(name, shape, dt)
            P.dma(eng, lambda e: e.dma_start(out=t[:], in_=dram), w=[name])
            return t

        IDN = ld_tile("IDN", cd["ident"], [128, 128])
        IDB = ld_tile("IDB", cd["ident"], [128, 128], BF16, eng="pool")
        ID2 = ld_tile("ID2", cd["id2"], [128, 64])
        G1T = ld_tile("G1T", g1T_d, [128, 8])
        MU = bc_tile("MU", mu_d, RC)
        CWt = bc_tile("CWt", cw_d, 3 * CW_)
        KKb = bc_tile("KKb", kk_d, RW)
        KAb = bc_tile("KAb", ka_d, RW)
        RKb = bc_tile("RKb", rk_d, RW)
        LGb = bc_tile("LGb", lg_d, RW)
        LBb = bc_tile("LBb", lb_d, RW)
        W0r = ld_tile("W0r", w0_d, [1, RW])
        A0r = ld_tile("A0r", a0_d, [1, RW])
        W2A = ld_tile("W2A", w2a_d, [128, RW])
        G2 = ld_tile("G2", g2_d, [128, RW])
        ONE1 = SB("ONE1", [1, 128])
        P.pool(lambda e: e.memset(ONE1[:], 1.0), w=["ONE1"])
        MKT = {}
        for nm in ("tri", "tgt"):
            MKT[nm] = ld_tile("M_" + nm, cd[nm + "P"], [128, 128])
        for nm in ("ma", "mb", "mt"):
            MKT[nm] = ld_tile("M_" + nm, cd[nm + "P"], [128, 512])
        MK = {}
        for ty in ("P", "S"):
            for nm in ("tri", "tgt", "ma", "mb", "mt"):
                MK[nm + ty] = MKT[nm]
        SEGP = ld_tile("SEGP", cd["segP"], [128, 1])
        SEGS = ld_tile("SEGS", cd["segS"], [128, NSEQ])
        BM = ld_tile("BM", cd["bm"], [128, NSEQ])

        X = [SB(f"X{i}", [128, D]) for i in range(2)]
        SS = SB("SS", [128, 1])
        RS = SB("RS", [128, 1])
        HB = SB("HB", [128, D], BF16)
        HT = SB("HT", [128, 8, 128], BF16)
        PR0 = SB("PR0", [128, IC])
        PR = [PR0, PR0]
        CX0 = SB("CX0", [128, CW_])
        CX = [CX0, CX0]
        CONV4 = SB("CONV4", [128, 4 * CW_])
        SH1, SH2, CT, CT2 = (CONV4[:, i * 512:(i + 1) * 512] for i in range(4))
        YCs = [SB(f"YC{i}", [128, D], BF16) for i in range(2)]
        X1T = SB("X1T", [128, D])
        JUNK = CONV4[:, 0:1024]
        YCTb = SB("YCTb", [128, 8, 128], BF16)
        PV = SB("PV", [128, RC])
        XM = PV
        LOR = SB("LOR", [128, 256])
        LT = SB("LT", [128, 2, 128])
        SG = SB("SG", [128, RW])
        YYb = SB("YYb", [128, RW])
        GNS = SB("GNS", [128, RW])
        S8c = SB("S8c", [128, 8])
        S8d = SB("S8d", [128, 8])
        S8e = SB("S8e", [128, 8])
        AA = SB("AA", [128, RW])
        GGs = [SB(f"GG{i}", [128, RW]) for i in range(2)]
        EXP4 = SB("EXP4", [128, 4 * RW])
        EIN, EINV, EEX, EEND = (EXP4[:, i * 512:(i + 1) * 512] for i in range(4))
        BTm = SB("BTm", [128, RW], BF16)
        KTl = SB("KTl", [128, RW], BF16)
        KS2 = SB("KS2", [128, 2 * RW])
        KK0, SQ = KS2[:, 0:512], KS2[:, 512:1024]
        S8 = SB("S8", [128, 8])
        S8b = SB("S8b", [128, 8])
        BB = SB("BB", [128, RW])
        KP = SB("KP", [128, RW])
        BEs = [SB(f"BEb{i}", [128, RW], BF16) for i in range(2)]
        KEs = [SB(f"KEb{i}", [128, RW], BF16) for i in range(2)]
        VBs = [SB(f"VB{i}", [128, RW], BF16) for i in range(2)]
        BONs = [SB(f"BON{i}", [128, RW]) for i in range(2)]
        RTms = [SB(f"RTm{i}", [128, RW], BF16) for i in range(2)]
        KTMs = [SB(f"KTM{i}", [128, RW], BF16) for i in range(2)]
        KRs = [SB(f"KR{i}", [128, 4, 2, 128], BF16) for i in range(2)]
        BFs = [SB(f"BF{i}", [128, 4, 128], BF16) for i in range(2)]
        KFs = [SB(f"KF{i}", [128, 4, 128], BF16) for i in range(2)]
        PCs = [SB(f"PC{i}", [128, 4]) for i in range(2)]
        PCS = SB("PCS", [64, NH, NSEQ])
        GA = [SB(f"GA{g}", [128, 4, 2, 128], BF16) for g in range(2)]
        GB = [SB(f"GB{g}", [128, 4, 2, 128], BF16) for g in range(2)]
        GT = [SB(f"GT{g}", [128, 4, 128], BF16) for g in range(2)]
        RN = [[SB(f"RN{g}{i}", [128, 4, 128], BF16) for i in range(2)] for g in range(2)]
        RTN = [[SB(f"RTN{g}{i}", [128, 4, 128], BF16) for i in range(2)] for g in range(2)]
        TT = [[SB(f"TT{g}{i}", [128, 4, 128], BF16) for i in range(2)] for g in range(2)]
        X1N = [SB(f"X1N{g}", [128, 4, 64], BF16) for g in range(2)]
        UK = [SB(f"UK{g}", [128, 4, 2, 64], BF16) for g in range(2)]
        RHb = SB("RHb", [128, 2, 2, 128])
        RH = [RHb[:, 0, :, :], RHb[:, 1, :, :]]
        RHS = RHb[0:64, :, :, :].rearrange("p a b t -> p (a b) t")
        MM = [SB(f"MM{g}", [128, 2, 64]) for g in range(2)]
        STt = [SB(f"ST{i}", [128, 4, 64]) for i in range(2)]
        S0 = PR0[:, 0:1024].rearrange("p (b k) -> p b k", k=64)
        S0T = EXP4[0:64, :].rearrange("p (b t) -> p b t", t=128)
        RHm = CONV4[0:64, :].rearrange("p (b t) -> p b t", t=128)
        KHm = PR0[:, 1024:1536].bitcast(BF16).rearrange("p (b k) -> p b k", k=64)
        UTm = PR0[:, 1536:2048].bitcast(BF16).rearrange("p (b k) -> p b k", k=64)
        Vm = PR0[:, 2048:2560].bitcast(BF16).rearrange("p (b k) -> p b k", k=64)
        DPC = PV[0:64, 0:1024].rearrange("p (b k) -> p b k", k=64)
        MS = KS2[0:64, :].rearrange("p (b k) -> p b k", k=64)
        WPO = RHb[0:64, :, :, :].rearrange("p a b t -> p (a b) t")
        SO = [DPC, DPC]

        def v3(ap, k=64):
            return ap.rearrange("p (h k) -> p h k", k=k)

        def dbgdump(name, ap, key):
            if name in dbg_out:
                P.dma("sp", lambda e: e.dma_start(out=dbg_out[name], in_=ap), r=[key])

        def front(ti):
            ty = "S" if ti == NPT else "P"
            first = (ti == 0)
            lastp = (ti == NPT - 1)
            par = ti % 2
            K = lambda n: f"{n}#{par}"
            x = X[par]
            xk = f"X{par}"
            YC, GG, BON, KR, BF, KF, VB = YCs[par], GGs[par], BONs[par], KRs[par], BFs[par], KFs[par], VBs[par]
            KTM, RTm, BE, KE, PC = KTMs[par], RTms[par], BEs[par], KEs[par], PCs[par]
            src = xs if ty == "S" else xp[ti * 128:(ti + 1) * 128, :]
            P.dma("sp", lambda e: e.dma_start(out=x[:], in_=src), w=[xk])
            P.act(lambda e: e.activation(out=JUNK[:], in_=x[:], func=AF.Square, accum_out=SS[:]), r=[xk], w=["SH1", "SH2", "SS"])
            P.dve(lambda e: e.tensor_scalar(out=RS[:], in0=SS[:], scalar1=1.0 / D, scalar2=1e-6, op0=ALU.mult, op1=ALU.add), r=["SS"], w=["RS"])
            P.act(lambda e: e.activation(out=RS[:], in_=RS[:], func=AF.Sqrt), r=["RS"], w=["RS"])
            P.dve(lambda e: e.reciprocal(out=RS[:], in_=RS[:]), r=["RS"], w=["RS"])
            P.act(lambda e: e.activation(out=HB[:], in_=x[:], func=AF.Copy, scale=RS[:, 0:1]), r=[xk, "RS"], w=["HB"])
            for k in range(8):
                P.pe(lambda e, k=k: e.transpose(PSB[:, k * 128:(k + 1) * 128], HB[:, k * 128:(k + 1) * 128], IDB[:]), r=["HB", "IDB"], w=["psb"])
            P.dve(lambda e: e.tensor_tensor(out=HT[:], in0=PSB[:].rearrange("p (k t) -> p k t", t=128),
                                            in1=G1T[:].unsqueeze(2).broadcast_to([128, 8, 128]), op=ALU.mult), r=["psb", "G1T"], w=["HT"])
            pr = PR[par]

            def proj_chunks(js):
                for j in js:
                    wd = 512 if j < 6 else 256
                    bk, bkk = nb()
                    wr, wrk = wchunk(wbi, j * 512, wd)
                    for k in range(8):
                        P.pe(lambda e, k=k, wd=wd, bk=bk, wr=wr: e.matmul(bk[:, 0:wd], lhsT=HT[:, k, :], rhs=wr[:, k, 0:wd], start=(k == 0), stop=(k == 7)),
                             r=["HT", wrk], w=[bkk])
                    P.act(lambda e, j=j, wd=wd, bk=bk: e.copy(out=pr[:, j * 512:j * 512 + wd], in_=bk[:, 0:wd]), r=[bkk], w=["PRb" if j >= 3 else "PRa"])
                    yield

            yield from proj_chunks((3, 4, 5, 6))
            yield
            prw = pr[:, 1536:IC]
            P.dma("sp", lambda e: e.dma_start(out=PV[1:113, :], in_=pr[0:112, 1536:IC]), r=["PRb"], w=["XM"])
            P.dma("sp", lambda e: e.dma_start(out=PV[113:128, :], in_=pr[112:127, 1536:IC]), r=["PRb"], w=["XM"])
            if ty == "S":
                P.dma("sp", lambda e: e.dma_start(out=PV[0:128:8, :], in_=sts), w=["XM"])
            elif first:
                P.pool(lambda e: e.memset(PV[0:1, :], 0.0), w=["XM"])
            else:
                P.dma("sp", lambda e: e.dma_start(out=PV[0:1, :], in_=bnd_p[ti - 1:ti, :]), r=["bnd"], w=["XM"])
            P.dve(lambda e: e.tensor_tensor(out=PV[:], in0=PV[:], in1=prw, op=ALU.subtract), r=["XM", "PRb"], w=["XM"])
            P.dve(lambda e: e.tensor_tensor(out=PV[:], in0=PV[:], in1=MU[:], op=ALU.mult), r=["XM", "MU"], w=["XM"])
            P.dve(lambda e: e.tensor_tensor(out=XM[:], in0=PV[:], in1=prw, op=ALU.add), r=["XM", "PRb"], w=["XM"])
            r_ = XM[:, 0:512]
            k_ = XM[:, 512:1024]
            v_ = XM[:, 1024:1536]
            if STOP == 'B':
                return
            if ty == "S":
                P.dma("sp", lambda e: e.dma_start(out=shift_s, in_=pr[7:128:8, 1536:IC]), r=["PRb"])
            else:
                P.dma("sp", lambda e: e.dma_start(out=bnd_p[ti:ti + 1, :], in_=pr[127:128, 1536:IC]), r=["PRb"], w=["bnd"])
            if lastp:
                P.dma("sp", lambda e: e.dma_start(out=shift_p, in_=pr[127:128, 1536:IC]), r=["PRb"])
            yield from proj_chunks((0, 1, 2))
            yield
            cx = CX[par]
            cxk = "CX"
            P.pool(lambda e: e.tensor_tensor(out=cx[:], in0=pr[:, 512:1024], in1=pr[:, 1024:1536], op=ALU.mult), r=["PRa"], w=[cxk])
            P.dma("sp", lambda e: e.dma_start(out=SH1[1:113, :], in_=cx[0:112, :]), r=[cxk], w=["SH1"])
            P.dma("sp", lambda e: e.dma_start(out=SH1[113:128, :], in_=cx[112:127, :]), r=[cxk], w=["SH1"])
            P.dma("sp", lambda e: e.dma_start(out=SH2[2:114, :], in_=cx[0:112, :]), r=[cxk], w=["SH2"])
            P.dma("sp", lambda e: e.dma_start(out=SH2[114:128, :], in_=cx[112:126, :]), r=[cxk], w=["SH2"])
            if ty == "S":
                P.dma("sp", lambda e: e.dma_start(out=SH1[0:128:8, :], in_=stc[:, 1, :]), w=["SH1"])
                P.dma("sp", lambda e: e.dma_start(out=SH2[0:128:8, :], in_=stc[:, 0, :]), w=["SH2"])
                P.dma("sp", lambda e: e.dma_start(out=SH2[1:128:8, :], in_=stc[:, 1, :]), w=["SH2"])
            elif first:
                P.pool(lambda e: e.memset(SH1[0:1, :], 0.0), w=["SH1"])
                P.pool(lambda e: e.memset(SH2[0:2, :], 0.0), w=["SH2"])
            else:
                P.dma("sp", lambda e: e.dma_start(out=SH1[0:1, :], in_=bnd_c[ti - 1, 1:2, :]), r=["bnd"], w=["SH1"])
                P.dma("sp", lambda e: e.dma_start(out=SH2[0:2, :], in_=bnd_c[ti - 1, :, :]), r=["bnd"], w=["SH2"])
            P.pool(lambda e: e.tensor_tensor(out=CT[:], in0=SH2[:], in1=CWt[:, 0:512], op=ALU.mult), r=["SH2", "CWt"], w=["CT"])
            P.pool(lambda e: e.tensor_tensor(out=CT2[:], in0=SH1[:], in1=CWt[:, 512:1024], op=ALU.mult), r=["SH1", "CWt"], w=["CT2"])
            P.pool(lambda e: e.tensor_tensor(out=CT[:], in0=CT[:], in1=CT2[:], op=ALU.add), r=["CT", "CT2"], w=["CT"])
            P.pool(lambda e: e.tensor_tensor(out=CT2[:], in0=cx[:], in1=CWt[:, 1024:1536], op=ALU.mult), r=[cxk, "CWt"], w=["CT2"])
            P.pool(lambda e: e.tensor_tensor(out=CT[:], in0=CT[:], in1=CT2[:], op=ALU.add), r=["CT", "CT2"], w=["CT"])
            P.pool(lambda e: e.tensor_tensor(out=YC[:, 0:512], in0=CT[:], in1=pr[:, 0:512], op=ALU.mult), r=["CT", "PRa"], w=[K("YCa")])
            if ty == "S":
                P.dma("sp", lambda e: e.dma_start(out=conv_s[:, 0, :], in_=cx[6:128:8, :]), r=[cxk])
                P.dma("sp", lambda e: e.dma_start(out=conv_s[:, 1, :], in_=cx[7:128:8, :]), r=[cxk])
            else:
                P.dma("sp", lambda e: e.dma_start(out=bnd_c[ti, :, :], in_=cx[126:128, :]), r=[cxk], w=["bnd"])
            if lastp:
                P.dma("sp", lambda e: e.dma_start(out=conv_p, in_=cx[126:128, :]), r=[cxk])
            yield
            P.act(lambda e: e.activation(out=LOR[:, 0:64], in_=XM[:, 1536:1600], func=AF.Tanh), r=["XM"], w=["LOR"])
            P.act(lambda e: e.copy(out=LOR[:, 64:128], in_=XM[:, 1600:1664]), r=["XM"], w=["LOR"])
            P.act(lambda e: e.activation(out=LOR[:, 128:256], in_=XM[:, 1664:1792], func=AF.Sigmoid), r=["XM"], w=["LOR"])
            bk, bkk = nb()
            for i in range(2):
                P.pe(lambda e, i=i, bk=bk: e.transpose(bk[:, i * 128:(i + 1) * 128], LOR[:, i * 128:(i + 1) * 128], IDN[:]), r=["LOR", "IDN"], w=[bkk])
            P.act(lambda e, bk=bk: e.copy(out=LT[:].rearrange("p a t -> p (a t)"), in_=bk[:, 0:256]), r=[bkk], w=["LT"])
            bk, bkk = nb()
            P.pe(lambda e, bk=bk: e.matmul(bk[:], lhsT=LT[0:64, 0, :], rhs=W2A[0:64, :], start=True, stop=False), r=["LT", "W2A"], w=[bkk])
            P.pe(lambda e, bk=bk: e.matmul(bk[:], lhsT=ONE1[:], rhs=W0r[:], start=False, stop=True), r=["ONE1", "W0r"], w=[bkk])
            P.act(lambda e, bk=bk: e.activation(out=SG[:], in_=bk[:], func=AF.Sigmoid), r=[bkk], w=["SG"])
            bk, bkk = nb()
            P.pe(lambda e, bk=bk: e.matmul(bk[:], lhsT=LT[64:128, 0, :], rhs=W2A[64:128, :], start=True, stop=False), r=["LT", "W2A"], w=[bkk])
            P.pe(lambda e, bk=bk: e.matmul(bk[:], lhsT=ONE1[:], rhs=A0r[:], start=False, stop=True), r=["ONE1", "A0r"], w=[bkk])
            P.act(lambda e, bk=bk: e.activation(out=AA[:], in_=bk[:], func=AF.Sigmoid), r=[bkk], w=["AA"])
            bk, bkk = nb()
            P.pe(lambda e, bk=bk: e.matmul(bk[:], lhsT=LT[:, 1, :], rhs=G2[:], start=True, stop=True), r=["LT", "G2"], w=[bkk])
            P.act(lambda e, bk=bk: e.copy(out=GG[:], in_=bk[:]), r=[bkk], w=[K("GG")])
            yield
            bk, bkk = nb()
            P.pe(lambda e, bk=bk: e.matmul(bk[:], lhsT=MK["tri" + ty][:], rhs=SG[:], start=True, stop=True), r=["SG", "M_tri"], w=[bkk])
            P.act(lambda e, bk=bk: e.activation(out=EIN[:], in_=bk[:], func=AF.Exp), r=[bkk], w=["EIN"])
            P.act(lambda e, bk=bk: e.activation(out=EINV[:], in_=bk[:], func=AF.Exp, scale=-1.0), r=[bkk], w=["EINV"])
            P.dve(lambda e, bk=bk: e.scalar_tensor_tensor(out=EEX[:], in0=SG[:], scalar=CDEC, in1=bk[:], op0=ALU.mult, op1=ALU.add), r=[bkk, "SG"], w=["EEX"])
            P.act(lambda e: e.activation(out=EEX[:], in_=EEX[:], func=AF.Exp), r=["EEX"], w=["EEX"])
            bk, bkk = nb()
            P.pe(lambda e, bk=bk: e.matmul(bk[:], lhsT=MK["tgt" + ty][:], rhs=SG[:], start=True, stop=True), r=["SG", "M_tgt"], w=[bkk])
            P.act(lambda e, bk=bk: e.activation(out=EEND[:], in_=bk[:], func=AF.Exp), r=[bkk], w=["EEND"])
            bk, bkk = nb()
            if ty == "P":
                for p in range(4):
                    P.pe(lambda e, p=p, bk=bk: e.matmul(bk[:, p:p + 1], lhsT=SG[:, p * 128:(p + 1) * 128], rhs=SEGP[:], start=True, stop=True), r=["SG", "SEGP"], w=[bkk])
                P.act(lambda e, bk=bk: e.activation(out=PC[:], in_=bk[:, 0:4], func=AF.Exp), r=[bkk], w=[K("PC")])
            else:
                for h in range(NH):
                    P.pe(lambda e, h=h, bk=bk: e.matmul(bk[0:64, h * NSEQ:(h + 1) * NSEQ], lhsT=SG[:, h * 64:(h + 1) * 64], rhs=SEGS[:], start=True, stop=True), r=["SG", "SEGS"], w=[bkk])
                P.act(lambda e, bk=bk: e.activation(out=PCS[:].rearrange("p h b -> p (h b)"), in_=bk[0:64, 0:NH * NSEQ], func=AF.Exp), r=[bkk], w=["PCS"])
            yield
            P.dve(lambda e: e.tensor_tensor(out=KK0[:], in0=k_, in1=KKb[:], op=ALU.mult), r=["XM", "KKb"], w=["KK0"])
            P.dve(lambda e: e.tensor_tensor(out=SQ[:], in0=KK0[:], in1=KK0[:], op=ALU.mult), r=["KK0"], w=["SQ"])
            P.dve(lambda e: e.tensor_reduce(out=S8[:], in_=v3(SQ[:]), axis=AX.X, op=ALU.add), r=["SQ"], w=["S8"])
            P.dve(lambda e: e.tensor_scalar(out=S8[:], in0=S8[:], scalar1=1e-24, scalar2=None, op0=ALU.max), r=["S8"], w=["S8"])
            P.act(lambda e: e.activation(out=S8[:], in_=S8[:], func=AF.Sqrt), r=["S8"], w=["S8"])
            P.dve(lambda e: e.reciprocal(out=S8[:], in_=S8[:]), r=["S8"], w=["S8"])
            P.dve(lambda e: e.tensor_tensor(out=v3(KK0[:]), in0=v3(KK0[:]), in1=S8[:].unsqueeze(2).broadcast_to([128, 8, 64]), op=ALU.mult), r=["KK0", "S8"], w=["KK0"])
            P.dve(lambda e: e.tensor_tensor(out=BB[:], in0=KK0[:], in1=AA[:], op=ALU.mult), r=["KK0", "AA"], w=["BB"])
            P.dve(lambda e: e.scalar_tensor_tensor(out=KP[:], in0=AA[:], scalar=-1.0, in1=KAb[:], op0=ALU.add, op1=ALU.mult), r=["AA", "KAb"], w=["KP"])
            P.dve(lambda e: e.scalar_tensor_tensor(out=KP[:], in0=KP[:], scalar=1.0, in1=k_, op0=ALU.add, op1=ALU.mult), r=["KP", "XM"], w=["KP"])
            P.pool(lambda e: e.tensor_tensor(out=SQ[:], in0=r_, in1=KP[:], op=ALU.mult), r=["XM", "KP", "S8"], w=["SQ"])
            P.pool(lambda e: e.tensor_tensor(out=SQ[:], in0=SQ[:], in1=RKb[:], op=ALU.mult), r=["SQ", "RKb"], w=["SQ"])
            P.dve(lambda e: e.tensor_reduce(out=S8b[:], in_=v3(SQ[:]), axis=AX.X, op=ALU.add), r=["SQ"], w=["S8b"])
            P.dve(lambda e: e.tensor_tensor(out=v3(BON[:]), in0=v3(v_), in1=S8b[:].unsqueeze(2).broadcast_to([128, 8, 64]), op=ALU.mult), r=["XM", "S8b"], w=[K("BON")])
            P.pool(lambda e: e.tensor_tensor(out=BON[:], in0=BON[:], in1=LBb[:], op=ALU.add), r=[K("BON"), "LBb"], w=[K("BON")])
            P.pool(lambda e: e.tensor_tensor(out=BON[:], in0=BON[:], in1=GG[:], op=ALU.mult), r=[K("BON"), K("GG")], w=[K("BON")])
            P.pool(lambda e: e.tensor_tensor(out=GG[:], in0=GG[:], in1=LGb[:], op=ALU.mult), r=[K("GG"), "LGb"], w=[K("GG")])
            P.pool(lambda e: e.tensor_tensor(out=RTm[:], in0=r_, in1=EIN[:], op=ALU.mult), r=["XM", "EIN"], w=[K("RTm")])
            P.dve(lambda e: e.tensor_tensor(out=KTM[:], in0=KK0[:], in1=EEX[:], op=ALU.mult), r=["KK0", "EEX"], w=[K("KTM")])
            P.dve(lambda e: e.tensor_tensor(out=BTm[:], in0=BB[:], in1=EINV[:], op=ALU.mult), r=["BB", "EINV"], w=["BTm"])
            P.pool(lambda e: e.tensor_tensor(out=KTl[:], in0=KP[:], in1=EINV[:], op=ALU.mult), r=["KP", "EINV"], w=["KTl"])
            P.dve(lambda e: e.tensor_tensor(out=BE[:], in0=BB[:], in1=EEND[:], op=ALU.mult), r=["BB", "EEND"], w=[K("BEb")])
            P.pool(lambda e: e.tensor_tensor(out=KE[:], in0=KP[:], in1=EEND[:], op=ALU.mult), r=["KP", "EEND"], w=[K("KEb")])
            P.act(lambda e: e.copy(out=VB[:], in_=v_), r=["XM"], w=[K("VB")])
            if STOP == 'C':
                return
            yield
            for src_t, srck, dst, dstk in ((KTM, K("KTM"), KR[:, :, 0, :], K("KR")), (RTm, K("RTm"), KR[:, :, 1, :], K("KR")),
                                           (BTm, "BTm", BF[:], K("BF")), (KTl, "KTl", KF[:], K("KF"))):
                bk, bkk = nb()
                bkb = bk[:].bitcast(BF16)
                for p in range(4):
                    P.pe(lambda e, p=p, bkb=bkb, src_t=src_t: e.transpose(bkb[:, p * 128:(p + 1) * 128], src_t[:, p * 128:(p + 1) * 128], IDB[:]), r=[srck, "IDB"], w=[bkk])
                P.act(lambda e, bkb=bkb, dst=dst: e.copy(out=dst, in_=bkb[:, 0:512].rearrange("p (a t) -> p a t", t=128)), r=[bkk], w=[dstk])

            if STOP == 'D':
                return
            yield

        def back(ti):
            ty = "S" if ti == NPT else "P"
            first = (ti == 0)
            lastp = (ti == NPT - 1)
            par = ti % 2
            K = lambda n: f"{n}#{par}"
            x = X[par]
            xk = f"X{par}"
            YC, GG, BON, KR, BF, KF, VB = YCs[par], GGs[par], BONs[par], KRs[par], BFs[par], KFs[par], VBs[par]
            KTM, RTm, BE, KE, PC = KTMs[par], RTms[par], BEs[par], KEs[par], PCs[par]
            pr, cx, cxk = PR[par], CX[par], "CX"
            YY, SQ, S8, S8b, YCT = YYb, GNS, S8c, S8d, YCTb
            stprev = STt[1 - par]
            stpk = f"ST{1 - par}"
            def mach(g):
                ga, gb, gt = GA[g], GB[g], GT[g]
                for (lf, lfk, dst, dstk, mk) in ((BF, K("BF"), ga, f"GA{g}", "ma"), (KF, K("KF"), gb, f"GB{g}", "mb")):
                    bks = [nb(), nb()]
                    for i in range(4):
                        h = 4 * g + i
                        p, b0 = h // 2, 64 * (h % 2)
                        bk, bkk = bks[i % 2]
                        c0 = (i // 2) * 256
                        P.pe(lambda e, c0=c0, p=p, b0=b0, bk=bk, lf=lf: e.matmul(bk[:, c0:c0 + 256], lhsT=lf[b0:b0 + 64, p, :],
                                                                                rhs=KR[b0:b0 + 64, p, :, :].rearrange("k a t -> k (a t)"), start=True, stop=True),
                             r=[lfk, K("KR")], w=[bkk])
                    for par2 in range(2):
                        bk, bkk = bks[par2]
                        P.dve(lambda e, bk=bk, dst=dst, par2=par2, mk=mk: e.tensor_tensor(
                            out=dst[:, par2:4:2, :, :].rearrange("p h a t -> p h (a t)"), in0=bk[:].rearrange("p (h x) -> p h x", x=256),
                            in1=MK[mk + ty][:].rearrange("p (h x) -> p h x", x=256), op=ALU.mult),
                            r=[bkk, "M_" + mk], w=[dstk])
                bks = [nb(), nb()]
                for i in range(4):
                    h = 4 * g + i
                    p, b0 = h // 2, 64 * (h % 2)
                    bk, bkk = bks[i % 2]
                    c0 = (i // 2) * 128
                    P.pe(lambda e, c0=c0, p=p, b0=b0, bk=bk: e.matmul(bk[:, c0:c0 + 128], lhsT=KR[b0:b0 + 64, p, 0, :], rhs=BF[b0:b0 + 64, p, :], start=True, stop=True),
                         r=[K("KR"), K("BF")], w=[bkk])
                for par2 in range(2):
                    bk, bkk = bks[par2]
                    P.dve(lambda e, bk=bk, gt=gt, par2=par2: e.tensor_tensor(out=gt[:, par2:4:2, :], in0=bk[:, 0:256].rearrange("p (h t) -> p h t", t=128),
                                                                            in1=MK["mt" + ty][:, 0:256].rearrange("p (h t) -> p h t", t=128), op=ALU.mult),
                          r=[bkk, "M_mt"], w=[f"GT{g}"])
                if STOP == 'E0':
                    return
                yield
                P.pool(lambda e, ga=ga, g=g: e.tensor_tensor(out=TT[g][0][:], in0=ga[:, :, 0, :], in1=IDB[:].unsqueeze(1).broadcast_to([128, 4, 128]), op=ALU.add),
                       r=[f"GA{g}", "IDB"], w=[f"TT{g}0"])
                if STOP == 'E1':
                    return
                Rc, Rck = ga[:, :, 0, :], f"GA{g}"
                RTc, RTck = gt[:], f"GT{g}"
                Tc, Tck = TT[g][0], f"TT{g}0"
                NLV = 6 if ty == "P" else 2
                for lvl in range(1, NLV + 1):
                    sl = lvl % 2
                    if lvl < NLV:
                        bk, bkk = nb()
                        for i in range(4):
                            P.pe(lambda e, i=i, bk=bk, Rc=Rc, RTc=RTc: e.matmul(bk[:, i * 128:(i + 1) * 128], lhsT=RTc[:, i, :], rhs=Rc[:, i, :], start=True, stop=True),
                                 r=[Rck, RTck], w=[bkk])
                        rn, rnk = RN[g][sl], f"RN{g}{sl}"
                        P.act(lambda e, bk=bk, rn=rn: e.copy(out=rn[:].rearrange("p h t -> p (h t)"), in_=bk[:]), r=[bkk], w=[rnk])
                    bk, bkk = nb()
                    for i in range(4):
                        P.pe(lambda e, i=i, bk=bk, Rc=Rc, RTc=RTc: e.matmul(bk[:, i * 128:(i + 1) * 128], lhsT=Rc[:, i, :], rhs=RTc[:, i, :], start=True, stop=True),
                             r=[Rck, RTck], w=[bkk])
                    rtn, rtnk = RTN[g][sl], f"RTN{g}{sl}"
                    P.act(lambda e, bk=bk, rtn=rtn: e.copy(out=rtn[:].rearrange("p h t -> p (h t)"), in_=bk[:]), r=[bkk], w=[rtnk])
                    yield
                    bk, bkk = nb()
                    for i in range(4):
                        P.pe(lambda e, i=i, bk=bk, rtn=rtn, Tc=Tc: e.matmul(bk[:, i * 128:(i + 1) * 128], lhsT=rtn[:, i, :], rhs=Tc[:, i, :], start=True, stop=True),
                             r=[rtnk, Tck], w=[bkk])
                    tn, tnk = TT[g][sl], f"TT{g}{sl}"
                    P.dve(lambda e, bk=bk, tn=tn, Tc=Tc: e.tensor_tensor(out=tn[:].rearrange("p h t -> p (h t)"), in0=bk[:], in1=Tc[:].rearrange("p h t -> p (h t)"), op=ALU.add),
                          r=[bkk, Tck], w=[tnk])
                    yield
                    if lvl < NLV:
                        Rc, Rck = rn[:], rnk
                    RTc, RTck = rtn[:], rtnk
                    Tc, Tck = tn, tnk
                if STOP == 'E':
                    return
                yield
                yield "TD_DONE"
                bk, bkk = nb()
                for i in range(4):
                    h = 4 * g + i
                    P.pe(lambda e, i=i, h=h, bk=bk, gb=gb: e.matmul(bk[:, i * 64:(i + 1) * 64], lhsT=gb[:, i, 0, :], rhs=VB[:, 64 * h:64 * (h + 1)], start=True, stop=True),
                         r=[f"GB{g}", K("VB")], w=[bkk])
                P.act(lambda e, bk=bk, g=g: e.activation(out=X1N[g][:].rearrange("p h v -> p (h v)"), in_=bk[:, 0:256], func=AF.Copy, scale=-1.0), r=[bkk], w=[f"X1N{g}"])
                yield
                bk, bkk = nb()
                for i in range(4):
                    h = 4 * g + i
                    P.pe(lambda e, i=i, bk=bk, Tc=Tc, g=g: e.matmul(bk[:, i * 128:i * 128 + 64], lhsT=Tc[:, i, :], rhs=X1N[g][:, i, :], start=True, stop=True), r=[Tck, f"X1N{g}"], w=[bkk])
                    P.pe(lambda e, i=i, h=h, bk=bk, Tc=Tc: e.matmul(bk[:, i * 128 + 64:(i + 1) * 128], lhsT=Tc[:, i, :], rhs=KTM[:, 64 * h:64 * (h + 1)], start=True, stop=True), r=[Tck, K("KTM")], w=[bkk])
                uk, ukk = UK[g], f"UK{g}"
                bk4 = bk[:].rearrange("p (h a v) -> p h a v", a=2, v=64)
                P.act(lambda e, bk4=bk4, uk=uk: e.copy(out=uk[:, :, 0, :], in_=bk4[:, :, 0, :]), r=[bkk], w=[ukk])
                P.act(lambda e, bk4=bk4, uk=uk: e.activation(out=uk[:, :, 1, :], in_=bk4[:, :, 1, :], func=AF.Copy, scale=-1.0), r=[bkk], w=[ukk])
                if STOP == 'F':
                    return
                yield
                bk, bkk = nb()
                for i in range(4):
                    h = 4 * g + i
                    if ty == "P":
                        ob, col = 64 * (h % 2), (i // 2) * 128
                    else:
                        ob, col = 0, i * 128
                    P.pe(lambda e, h=h, ob=ob, col=col, bk=bk: e.matmul(bk[ob:ob + 64, col:col + 128], lhsT=RTm[:, 64 * h:64 * (h + 1)], rhs=IDB[:], start=True, stop=False), r=[K("RTm"), "IDB"], w=[bkk])
                    P.pe(lambda e, i=i, ob=ob, col=col, bk=bk, uk=uk, ga=ga: e.matmul(bk[ob:ob + 64, col:col + 128], lhsT=uk[:, i, 1, :], rhs=ga[:, i, 1, :], start=False, stop=True), r=[ukk, f"GA{g}"], w=[bkk])
                if ty == "P":
                    rh, rhk = RH[g], [f"RH{g}"]
                else:
                    rh, rhk = RHS, ["RH0", "RH1"]
                if ty == "P":
                    P.act(lambda e, bk=bk, rh=rh: e.copy(out=rh, in_=bk[:, 0:256].rearrange("p (a t) -> p a t", t=128)), r=[bkk], w=rhk)
                else:
                    P.act(lambda e, bk=bk, rh=rh: e.copy(out=rh, in_=bk[0:64, :].rearrange("p (a t) -> p a t", t=128)), r=[bkk], w=rhk)
                if ty == "P":
                    bk, bkk = nb()
                    for i in range(4):
                        h = 4 * g + i
                        ob, col = 64 * (h % 2), (i // 2) * 64
                        P.pe(lambda e, i=i, h=h, ob=ob, col=col, bk=bk, uk=uk: e.matmul(bk[ob:ob + 64, col:col + 64], lhsT=uk[:, i, 1, :], rhs=BE[:, 64 * h:64 * (h + 1)], start=True, stop=True), r=[ukk, K("BEb")], w=[bkk])
                    for j in range(2):
                        P.dve(lambda e, j=j, bk=bk, g=g: e.scalar_tensor_tensor(out=MM[g][:, j, :], in0=ID2[:], scalar=PC[:, 2 * g + j:2 * g + j + 1], in1=bk[:, j * 64:(j + 1) * 64], op0=ALU.mult, op1=ALU.add),
                              r=[bkk, "ID2", K("PC")], w=[f"MM{g}"])
                    for i in range(4):
                        h = 4 * g + i
                        p, b0, j = h // 2, 64 * (h % 2), i // 2
                        yield
                        vh = VB[:, 64 * h:64 * (h + 1)]
                        P.pe(lambda e, i=i, h=h, ga=ga, uk=uk: e.matmul(PSY[:, 64 * h:64 * (h + 1)], lhsT=ga[:, i, 1, :], rhs=uk[:, i, 0, :], start=True, stop=False), r=[f"GA{g}", ukk], w=["psy"])
                        P.pe(lambda e, i=i, h=h, gb=gb, vh=vh: e.matmul(PSY[:, 64 * h:64 * (h + 1)], lhsT=gb[:, i, 1, :], rhs=vh, start=False, stop=first), r=[f"GB{g}", K("VB")], w=["psy"])
                        if not first:
                            P.pe(lambda e, h=h, b0=b0, j=j, p=p, rh=rh: e.matmul(PSY[:, 64 * h:64 * (h + 1)], lhsT=rh[b0:b0 + 64, j, :], rhs=stprev[b0:b0 + 64, p, :], start=False, stop=True), r=[*rhk, stpk], w=["psy"])
                        P.pe(lambda e, i=i, h=h, b0=b0, p=p, uk=uk: e.matmul(PSS[b0:b0 + 64, p * 64:(p + 1) * 64], lhsT=BE[:, 64 * h:64 * (h + 1)], rhs=uk[:, i, 0, :], start=True, stop=False), r=[K("BEb"), ukk], w=["pss"])
                        P.pe(lambda e, h=h, b0=b0, p=p, vh=vh: e.matmul(PSS[b0:b0 + 64, p * 64:(p + 1) * 64], lhsT=KE[:, 64 * h:64 * (h + 1)], rhs=vh, start=False, stop=first), r=[K("KEb"), K("VB")], w=["pss"])
                        if not first:
                            P.pe(lambda e, b0=b0, j=j, p=p, g=g: e.matmul(PSS[b0:b0 + 64, p * 64:(p + 1) * 64], lhsT=MM[g][b0:b0 + 64, j, :], rhs=stprev[b0:b0 + 64, p, :], start=False, stop=True), r=[f"MM{g}", stpk], w=["pss"])
                else:
                    for i in range(4):
                        h = 4 * g + i
                        p, h2 = h // 2, h % 2
                        yield
                        vh = VB[:, 64 * h:64 * (h + 1)]
                        if h2 == 0:
                            P.dma("sp", lambda e, p=p: e.dma_start(out=S0[:], in_=stw[:, 2 * p:2 * p + 2, :, :].rearrange("b h v k -> (h v) b k")), w=["PRa", "PRb"])
                            for q in range(4):
                                bk, bkk = nb()
                                for bb in range(4):
                                    b = q * 4 + bb
                                    P.pe(lambda e, b=b, bb=bb, bk=bk: e.transpose(bk[0:64, bb * 128:(bb + 1) * 128], S0[:, b, :], IDN[:]), r=["PRa", "PRb", "IDN"], w=[bkk])
                                P.act(lambda e, q=q, bk=bk: e.copy(out=S0T[:, 4 * q:4 * q + 4, :].rearrange("p b t -> p (b t)"), in_=bk[0:64, :]), r=[bkk], w=["EIN", "EINV", "EEX", "EEND"])
                        if h == 0:
                            P.pool(lambda e: e.memset(RHm[:], 0.0), w=["SH1", "SH2", "CT", "CT2"])
                        rflat = CONV4[0:64, :]
                        P.pool(lambda e, i=i, rh=rh: e.tensor_copy(out=rflat[:, 0:2040].rearrange("p (b x) -> p b x", x=136)[:, :, 0:8],
                                                                    in_=rh[:, i, 0:120].rearrange("p (b t) -> p b t", t=8)), r=rhk, w=["SH1", "SH2", "CT", "CT2"])
                        P.pool(lambda e, i=i, rh=rh: e.tensor_copy(out=rflat[:, 2040:2048], in_=rh[:, i, 120:128]), r=rhk, w=["SH1", "SH2", "CT", "CT2"])
                        bmb = BM[:].unsqueeze(2).broadcast_to([128, NSEQ, 64])
                        P.pool(lambda e, i=i, uk=uk: e.tensor_tensor(out=KHm[:], in0=uk[:, i, 1, :].unsqueeze(1).broadcast_to([128, NSEQ, 64]), in1=bmb, op=ALU.mult), r=[ukk, "BM"], w=["PRa", "PRb"])
                        P.dve(lambda e, i=i, uk=uk: e.tensor_tensor(out=UTm[:], in0=uk[:, i, 0, :].unsqueeze(1).broadcast_to([128, NSEQ, 64]), in1=bmb, op=ALU.mult), r=[ukk, "BM"], w=["PRa", "PRb"])
                        P.dve(lambda e, vh=vh: e.tensor_tensor(out=Vm[:], in0=vh.unsqueeze(1).broadcast_to([128, NSEQ, 64]), in1=bmb, op=ALU.mult), r=[K("VB"), "BM"], w=["PRa", "PRb"])
                        P.pool(lambda e, h=h: e.tensor_tensor(out=DPC[:], in0=IDN[0:64, 0:64].unsqueeze(1).broadcast_to([64, NSEQ, 64]),
                                                              in1=PCS[:, h, :].unsqueeze(2).broadcast_to([64, NSEQ, 64]), op=ALU.mult), r=["IDN", "PCS"], w=["XM"])
                        P.pe(lambda e, i=i, h=h, ga=ga, uk=uk: e.matmul(PSY[:, 64 * h:64 * (h + 1)], lhsT=ga[:, i, 1, :], rhs=uk[:, i, 0, :], start=True, stop=False), r=[f"GA{g}", ukk], w=["psy"])
                        P.pe(lambda e, i=i, h=h, gb=gb, vh=vh: e.matmul(PSY[:, 64 * h:64 * (h + 1)], lhsT=gb[:, i, 1, :], rhs=vh, start=False, stop=False), r=[f"GB{g}", K("VB")], w=["psy"])
                        for b in range(NSEQ):
                            P.pe(lambda e, b=b, h=h, h2=h2: e.matmul(PSY[:, 64 * h:64 * (h + 1)], lhsT=RHm[:, b, :], rhs=S0T[:, b, 64 * h2:64 * (h2 + 1)], start=False, stop=(b == NSEQ - 1)), r=["SH1", "SH2", "CT", "CT2", "EIN", "EINV", "EEX", "EEND"], w=["psy"])
                        for q in range(2):
                            bk, bkk = nb()
                            for bb in range(8):
                                b = q * 8 + bb
                                P.pe(lambda e, b=b, bb=bb, h=h, bk=bk: e.matmul(bk[0:64, bb * 64:(bb + 1) * 64], lhsT=KHm[:, b, :], rhs=BE[:, 64 * h:64 * (h + 1)], start=True, stop=True), r=["PRa", "PRb", K("BEb")], w=[bkk])
                            P.dve(lambda e, q=q, bk=bk: e.tensor_tensor(out=MS[:, 8 * q:8 * q + 8, :].rearrange("p b k -> p (b k)"), in0=bk[0:64, :],
                                                                        in1=DPC[:, 8 * q:8 * q + 8, :].rearrange("p b k -> p (b k)"), op=ALU.add), r=[bkk, "XM"], w=["KK0", "SQ"])
                        so, sok = DPC, "XM"
                        for q in range(2):
                            bk, bkk = nb()
                            for bb in range(8):
                                b = q * 8 + bb
                                o_ = bk[0:64, bb * 64:(bb + 1) * 64]
                                P.pe(lambda e, b=b, o_=o_, h2=h2: e.matmul(o_, lhsT=S0T[:, b, 64 * h2:64 * (h2 + 1)], rhs=MS[:, b, :], start=True, stop=False), r=["EIN", "EINV", "EEX", "EEND", "KK0", "SQ"], w=[bkk])
                                P.pe(lambda e, b=b, o_=o_, h=h: e.matmul(o_, lhsT=UTm[:, b, :], rhs=BE[:, 64 * h:64 * (h + 1)], start=False, stop=False), r=["PRa", "PRb", K("BEb")], w=[bkk])
                                P.pe(lambda e, b=b, o_=o_, h=h: e.matmul(o_, lhsT=Vm[:, b, :], rhs=KE[:, 64 * h:64 * (h + 1)], start=False, stop=True), r=["PRa", "PRb", K("KEb")], w=[bkk])
                            P.act(lambda e, q=q, bk=bk, so=so: e.copy(out=so[:, 8 * q:8 * q + 8, :].rearrange("p b k -> p (b k)"), in_=bk[0:64, :]), r=[bkk], w=[sok])
                        P.dma("sp", lambda e, h=h, so=so: e.dma_start(out=wkv_s[:, h, :, :].rearrange("b v k -> v b k"), in_=so[:]), r=[sok])
            if ty == "P" and GI_MODE != "none":
                gens = [mach(0), mach(1)]
                alive = [True, True]
                passed = [False, False]
                while any(alive) and not (GI_MODE == "td" and all(passed)):
                    for gi in range(2):
                        if alive[gi] and not (GI_MODE == "td" and passed[gi]):
                            try:
                                if next(gens[gi]) == "TD_DONE":
                                    passed[gi] = True
                            except StopIteration:
                                alive[gi] = False
                    yield
                for gi in range(2):
                    if alive[gi]:
                        for _ in gens[gi]:
                            yield
            else:
                for g in range(2):
                    for _ in mach(g):
                        yield
            yield
            P.act(lambda e: e.copy(out=YY[:], in_=PSY[:]), r=["psy"], w=["YY"])
            if ty == "P":
                stn, stnk = STt[par], f"ST{par}"
                P.act(lambda e, stn=stn: e.copy(out=stn[:].rearrange("p a v -> p (a v)"), in_=PSS[:, 0:256]), r=["pss"], w=[stnk])
                if lastp:
                    bk, bkk = nb()
                    for p in range(4):
                        P.pe(lambda e, p=p, bk=bk, stn=stn: e.transpose(bk[0:64, p * 128:(p + 1) * 128], stn[:, p, :], IDN[:]), r=[stnk, "IDN"], w=[bkk])
                    P.act(lambda e, bk=bk: e.copy(out=WPO[:].rearrange("p a t -> p (a t)"), in_=bk[0:64, :]), r=[bkk], w=["RH0", "RH1"])
                    P.dma("sp", lambda e: e.dma_start(out=wkv_p.rearrange("(p h2) v k -> v p h2 k", h2=2), in_=WPO[:].rearrange("v p (h2 k) -> v p h2 k", h2=2)), r=["RH0", "RH1"])
            yield
            b8 = lambda t: t[:].unsqueeze(2).broadcast_to([128, 8, 64])
            P.dve(lambda e: e.tensor_reduce(out=S8[:], in_=v3(YY[:]), axis=AX.X, op=ALU.add), r=["YY"], w=["S8c"])
            P.dve(lambda e: e.tensor_tensor(out=SQ[:], in0=YY[:], in1=YY[:], op=ALU.mult), r=["YY"], w=["GNS"])
            P.dve(lambda e: e.tensor_reduce(out=S8b[:], in_=v3(SQ[:]), axis=AX.X, op=ALU.add), r=["GNS"], w=["S8d"])
            P.dve(lambda e: e.tensor_scalar(out=S8[:], in0=S8[:], scalar1=1.0 / HS, scalar2=None, op0=ALU.mult), r=["S8c"], w=["S8c"])
            P.dve(lambda e: e.tensor_tensor(out=S8e[:], in0=S8[:], in1=S8[:], op=ALU.mult), r=["S8c"], w=["S8e"])
            P.dve(lambda e: e.scalar_tensor_tensor(out=S8b[:], in0=S8b[:], scalar=1.0 / HS, in1=S8e[:], op0=ALU.mult, op1=ALU.subtract), r=["S8d", "S8e"], w=["S8d"])
            P.dve(lambda e: e.tensor_scalar(out=S8b[:], in0=S8b[:], scalar1=64e-5, scalar2=None, op0=ALU.add), r=["S8d"], w=["S8d"])
            P.act(lambda e: e.activation(out=S8b[:], in_=S8b[:], func=AF.Sqrt), r=["S8d"], w=["S8d"])
            P.dve(lambda e: e.reciprocal(out=S8b[:], in_=S8b[:]), r=["S8d"], w=["S8d"])
            P.dve(lambda e: e.tensor_tensor(out=v3(YY[:]), in0=v3(YY[:]), in1=b8(S8), op=ALU.subtract), r=["YY", "S8c"], w=["YY"])
            P.dve(lambda e: e.tensor_tensor(out=v3(YY[:]), in0=v3(YY[:]), in1=b8(S8b), op=ALU.mult), r=["YY", "S8d"], w=["YY"])
            P.dve(lambda e: e.tensor_tensor(out=YY[:], in0=YY[:], in1=GG[:], op=ALU.mult), r=["YY", K("GG")], w=["YY"])
            P.dve(lambda e: e.tensor_tensor(out=YC[:, 512:1024], in0=YY[:], in1=BON[:], op=ALU.add), r=["YY", K("BON")], w=[K("YCb")])
            if "ycat" in dbg_out:
                dbgdump("ycat", YC[:], K("YCa")) if ti == dbg_ti[0] else None
            if STOP == 'H':
                return
            yield
            for k in range(8):
                P.pe(lambda e, k=k: e.transpose(PSB[:, k * 128:(k + 1) * 128], YC[:, k * 128:(k + 1) * 128], IDB[:]), r=[K("YCa"), K("YCb"), "IDB"], w=["psb"])
            P.act(lambda e: e.copy(out=YCT[:].rearrange("p k t -> p (k t)"), in_=PSB[:]), r=["psb"], w=["YCT"])
            for j in range(2):
                bk, bkk = nb()
                wr, wrk = wchunk(wbo, j * 512, 512)
                for k in range(8):
                    P.pe(lambda e, k=k, bk=bk, wr=wr: e.matmul(bk[:], lhsT=YCT[:, k, :], rhs=wr[:, k, :], start=(k == 0), stop=(k == 7)), r=["YCT", wrk], w=[bkk])
                P.dve(lambda e, j=j, bk=bk: e.tensor_tensor(out=X1T[:, j * 512:(j + 1) * 512], in0=bk[:], in1=x[:, j * 512:(j + 1) * 512], op=ALU.add), r=[bkk, xk], w=["X1T"])
            P.dma("sp", lambda e: e.dma_start(out=x1s[ti * 128:(ti + 1) * 128, :], in_=X1T[:]), r=["X1T"], w=["x1s"])
            yield

        def drain(gen):
            for _ in gen:
                pass

        def load_masks(names):
            for nm in names:
                P.dma("sp", lambda e, nm=nm: e.dma_start(out=MKT[nm][:], in_=cd[nm + "S"]), w=["M_" + nm])

        dbg_ti = [dbg.get("_ti", 0) if dbg else 0]
        PIPE = os.environ.get("KNOPIPE", "") != "1"
        if not PIPE:
            for ti in range(NT):
                if ti == NPT:
                    load_masks(("tri", "tgt", "ma", "mb", "mt"))
                drain(front(ti))
                drain(back(ti))
        else:
            drain(front(0))
            for ti in range(NT):
                if ti == NPT:
                    load_masks(("ma", "mb", "mt"))
                bgen = back(ti)
                fgen = None
                if ti + 1 < NT:
                    if ti + 1 == NPT:
                        load_masks(("tri", "tgt"))
                    fgen = front(ti + 1)
                bdone = fdone = False
                while not (bdone and (fgen is None or fdone)):
                    for _ in range(3):
                        if not bdone:
                            try:
                                next(bgen)
                            except StopIteration:
                                bdone = True
                    if fgen is not None and not fdone:
                        try:
                            next(fgen)
                        except StopIteration:
                            fdone = True

    if _dry:
        return wq
    P.barrier()

    if SKIP2:
        return nc, P.finalize()
    with ExitStack() as st2:
        def SB2(name, shape, dt=F32):
            return st2.enter_context(nc.sbuf_tensor(name, list(shape), dt))

        PS2 = [st2.enter_context(nc.psum_tensor(f"q{i}", [128, 512], F32)) for i in range(6)]
        PSB2 = st2.enter_context(nc.psum_tensor("qb", [128, 1024], BF16))
        bank2 = [0]

        def nb2():
            i = bank2[0] % 6
            bank2[0] += 1
            return PS2[i], f"q{i}"

        NSTG = 8
        RNG = 3
        W1R = [SB2(f"W1R{i}", [128, 8, 512], BF16) for i in range(RNG)]
        W2R = [SB2(f"W2R{i}", [128, 4, D], BF16) for i in range(RNG)]
        X1 = SB2("X1A", [128, NT, D])
        H2T = SB2("H2T", [128, 8, NT * 128], BF16)
        G2T = SB2("G2T", [128, 8])
        NFb = SB2("NFb", [128, D])
        IDB2 = SB2("IDB2", [128, 128], BF16)
        JK2 = SB2("JK2", [128, D])
        HB2 = SB2("HB2", [128, D], BF16)
        SS2 = SB2("SS2", [128, 1])
        RS2 = SB2("RS2", [128, 1])
        HR = [SB2(f"HR{i}", [128, 512]) for i in range(2)]
        HID = [SB2(f"HID{i}", [128, 4, 512], BF16) for i in range(2)]
        YO = [SB2(f"YO{i}", [128, D]) for i in range(2)]
        P.dma("sp", lambda e: e.dma_start(out=G2T[:], in_=g2T_d), w=["G2T"])
        P.dma("sp", lambda e: e.dma_start(out=NFb[:], in_=nf_d.partition_broadcast(128)), w=["NFb"])
        P.dma("pool", lambda e: e.dma_start(out=IDB2[:], in_=cd["ident"]), w=["IDB2"])

        def load_stage(s):
            rb = s % RNG
            P.dma("pool", lambda e: e.dma_start(out=W1R[rb][:], in_=w_ff1[:, s * 512:(s + 1) * 512].rearrange("(k p) f -> p k f", p=128)), w=[f"W1R{rb}"])
            P.dma("pool", lambda e: e.dma_start(out=W2R[rb][:], in_=w_ff2[s * 512:(s + 1) * 512, :].rearrange("(c p) d -> p c d", p=128)), w=[f"W2R{rb}"])

        for s in range(min(RNG, NSTG)):
            load_stage(s)
        for ti in range(NT):
            P.dma("sp", lambda e, ti=ti: e.dma_start(out=X1[:, ti, :], in_=x1s[ti * 128:(ti + 1) * 128, :]), w=[f"X1_{ti}"])
        def preamble(ti):
            xk = f"X1_{ti}"
            P.act(lambda e, ti=ti: e.activation(out=JK2[:], in_=X1[:, ti, :], func=AF.Square, accum_out=SS2[:]), r=[xk], w=["JK2", "SS2"])
            P.dve(lambda e: e.tensor_scalar(out=RS2[:], in0=SS2[:], scalar1=1.0 / D, scalar2=1e-6, op0=ALU.mult, op1=ALU.add), r=["SS2"], w=["RS2"])
            P.act(lambda e: e.activation(out=RS2[:], in_=RS2[:], func=AF.Sqrt), r=["RS2"], w=["RS2"])
            P.dve(lambda e: e.reciprocal(out=RS2[:], in_=RS2[:]), r=["RS2"], w=["RS2"])
            P.act(lambda e, ti=ti: e.activation(out=HB2[:], in_=X1[:, ti, :], func=AF.Copy, scale=RS2[:, 0:1]), r=[xk, "RS2"], w=["HB2"])
            for k in range(8):
                P.pe(lambda e, k=k: e.transpose(PSB2[:, k * 128:(k + 1) * 128], HB2[:, k * 128:(k + 1) * 128], IDB2[:]), r=["HB2", "IDB2"], w=["qb"])
            P.dve(lambda e, ti=ti: e.tensor_tensor(out=H2T[:, :, ti * 128:(ti + 1) * 128], in0=PSB2[:].rearrange("p (k t) -> p k t", t=128),
                                                  in1=G2T[:].unsqueeze(2).broadcast_to([128, 8, 128]), op=ALU.mult), r=["qb", "G2T"], w=[f"H2T_{ti}"])
        groups = []
        t0 = 0
        while t0 < NT:
            n = min(4, NT - t0)
            groups.append((t0, n))
            t0 += n
        def final_tile(ti):
            xk = f"X1_{ti}"
            yo, yok = YO[ti % 2], f"YO{ti % 2}"
            P.act(lambda e, ti=ti: e.activation(out=JK2[:], in_=X1[:, ti, :], func=AF.Square, accum_out=SS2[:]), r=[xk], w=["JK2", "SS2"])
            P.dve(lambda e: e.tensor_scalar(out=RS2[:], in0=SS2[:], scalar1=1.0 / D, scalar2=1e-6, op0=ALU.mult, op1=ALU.add), r=["SS2"], w=["RS2"])
            P.act(lambda e: e.activation(out=RS2[:], in_=RS2[:], func=AF.Sqrt), r=["RS2"], w=["RS2"])
            P.dve(lambda e: e.reciprocal(out=RS2[:], in_=RS2[:]), r=["RS2"], w=["RS2"])
            P.dve(lambda e, ti=ti, yo=yo: e.scalar_tensor_tensor(out=yo[:], in0=X1[:, ti, :], scalar=RS2[:, 0:1], in1=NFb[:], op0=ALU.mult, op1=ALU.mult), r=[xk, "RS2", "NFb"], w=[yok])
            dst = y_s if ti == NPT else y_p[ti * 128:(ti + 1) * 128, :]
            P.dma("sp", lambda e, yo=yo, dst=dst: e.dma_start(out=dst, in_=yo[:]), r=[yok])


        items = [(s_, t0, n) for s_ in range(NSTG) for (t0, n) in groups]

        def ffn1(k):
            s_, t0, n = items[k]
            rb = s_ % RNG
            ntok = n * 128
            hid, hidk = HID[k % 2], f"HID{k % 2}"
            for fc in range(4):
                bk, bkk = nb2()
                for kk in range(8):
                    P.pe(lambda e, kk=kk, fc=fc, bk=bk, t0=t0, ntok=ntok, rb=rb: e.matmul(bk[:, 0:ntok], lhsT=W1R[rb][:, kk, fc * 128:(fc + 1) * 128], rhs=H2T[:, kk, t0 * 128:t0 * 128 + ntok], start=(kk == 0), stop=(kk == 7)),
                         r=[f"W1R{rb}"] + [f"H2T_{t}" for t in range(t0, t0 + n)], w=[bkk])
                hr, hrk = HR[fc % 2], f"HR{fc % 2}"
                P.act(lambda e, bk=bk, hr=hr, ntok=ntok: e.activation(out=hr[:, 0:ntok], in_=bk[:, 0:ntok], func=AF.Relu), r=[bkk], w=[hrk])
                P.pool(lambda e, hr=hr, hid=hid, fc=fc, ntok=ntok: e.tensor_tensor(out=hid[:, fc, 0:ntok], in0=hr[:, 0:ntok], in1=hr[:, 0:ntok], op=ALU.mult), r=[hrk], w=[hidk])

        def ffn2(k):
            s_, t0, n = items[k]
            rb = s_ % RNG
            hid, hidk = HID[k % 2], f"HID{k % 2}"
            for tl in range(n):
                ti = t0 + tl
                for half in range(2):
                    bk, bkk = nb2()
                    for fc in range(4):
                        P.pe(lambda e, fc=fc, bk=bk, tl=tl, half=half, hid=hid, rb=rb: e.matmul(bk[:], lhsT=hid[:, fc, tl * 128:(tl + 1) * 128], rhs=W2R[rb][:, fc, half * 512:(half + 1) * 512], start=(fc == 0), stop=(fc == 3)),
                             r=[hidk, f"W2R{rb}"], w=[bkk])
                    P.dve(lambda e, bk=bk, ti=ti, half=half: e.tensor_tensor(out=X1[:, ti, half * 512:(half + 1) * 512], in0=bk[:], in1=X1[:, ti, half * 512:(half + 1) * 512], op=ALU.add),
                          r=[bkk, f"X1_{ti}"], w=[f"X1_{ti}"])
                if s_ == NSTG - 1:
                    final_tile(ti)

        FFNP = os.environ.get("KNOFFNP", "") != "1"

        def pre_group(k):
            s_, t0, n = items[k]
            if s_ == 0:
                for t in range(t0, t0 + n):
                    preamble(t)

        if FFNP:
            pre_group(0)
            ffn1(0)
        for k in range(len(items)):
            if FFNP:
                if k + 1 < len(items):
                    pre_group(k + 1)
                    ffn1(k + 1)
            else:
                pre_group(k)
                ffn1(k)
            ffn2(k)
            s_, t0, n = items[k]
            if (t0, n) == groups[-1] and s_ + RNG < NSTG:
                load_stage(s_ + RNG)
        if os.environ.get('KSCHED', '1') == '1':
            P.reschedule()
        stats = P.finalize()
    return nc, stats


_CACHE = {}


def make_in_maps(inputs, NPT=16, ncores=8):
    f = lambda a: np.ascontiguousarray(np.asarray(a, dtype=np.float32))
    c = _consts()
    shared = {
        "w_in": f(inputs["w_in"][0]), "w_out": f(inputs["w_out"][0]),
        "w_ff1": f(inputs["w_ff1"][0]), "w_ff2": f(inputs["w_ff2"][0]),
        "g1T": f(np.asarray(inputs["norm1_g"][0]).reshape(8, 128).T),
        "g2T": f(np.asarray(inputs["norm2_g"][0]).reshape(8, 128).T),
        "mu": f(inputs["mu"][0]).reshape(1, RC),
        "convw": f(inputs["conv_w"][0]).reshape(1, 3 * CW_),
        "k_k": f(inputs["k_k"][0]).reshape(1, RW), "k_a": f(inputs["k_a"][0]).reshape(1, RW),
        "r_k": f(inputs["r_k"][0]).reshape(1, RW),
        "lnx_g": f(inputs["lnx_g"][0]).reshape(1, RW), "lnx_b": f(inputs["lnx_b"][0]).reshape(1, RW),
        "normf": f(inputs["normf_g"]).reshape(1, D),
        "w0": f(inputs["w0"][0]).reshape(1, RW), "a0": f(inputs["a0"][0]).reshape(1, RW),
        "w2a": f(np.concatenate([np.asarray(inputs["w2"][0]), np.asarray(inputs["a2"][0])], 0)),
        "g2": f(inputs["g2"][0]),
    }
    for k, v in c.items():
        shared["c_" + k] = f(v)
    xpr = np.asarray(inputs["x_prompt"], dtype=np.float32)
    xsm = np.asarray(inputs["x_sample"], dtype=np.float32)
    maps = []
    for ci in range(ncores):
        m = dict(shared)
        m["xp"] = f(xpr[ci, :NPT * 128])
        sl = slice(ci * NSEQ, (ci + 1) * NSEQ)
        m["xs"] = f(xsm[sl].reshape(NSEQ * DT_, D))
        m["stc"] = f(inputs["state_conv"][0][sl])
        m["sts"] = f(inputs["state_shift"][0][sl])
        m["stw"] = f(inputs["state_wkv"][0][sl])
        maps.append(m)
    return maps


def gather(results, NPT=16, ncores=8):
    g = lambda name: [np.asarray(r[name], dtype=np.float32) for r in results]
    y_p = np.stack(g("y_p"), 0)
    y_s = np.stack(g("y_s"), 0).reshape(ncores * NSEQ, DT_, D)
    conv_p = np.stack(g("conv_p"), 0)[None]
    shift_p = np.concatenate(g("shift_p"), 0)[None]
    wkv_p = np.stack(g("wkv_p"), 0)[None]
    conv_s = np.concatenate(g("conv_s"), 0)[None]
    shift_s = np.concatenate(g("shift_s"), 0)[None]
    wkv_s = np.concatenate(g("wkv_s"), 0)[None]
    return (y_p, y_s, conv_p, shift_p, wkv_p, conv_s, shift_s, wkv_s)


def kernel(**inputs):
    NPT = 16
    if "nc" not in _CACHE:
        _CACHE["nc"] = build(NPT)[0]
    nc = _CACHE["nc"]
    maps = make_in_maps(inputs, NPT, 8)
    res = run_bass_kernel_spmd(nc, maps, core_ids=list(range(8)))
    return gather(res.results, NPT, 8)
```

```python
import os
import numpy as np
import concourse.bass as bass
import concourse.mybir as mybir
from concourse.bass_utils import run_bass_kernel_spmd

F32 = mybir.dt.float32
BF16 = mybir.dt.bfloat16
ALU = mybir.AluOpType
AF = mybir.ActivationFunctionType
AX = mybir.AxisListType

D = 1024
CW_ = 512
RW = 512
NH = 8
HS = 64
RC = 1792
IC = 3328
DFF = 4096
NSEQ = 16
DT_ = 8
CDEC = float(np.exp(-0.5))


class Op:
    __slots__ = ("eng", "emit", "deps", "dma", "inc", "done", "dom", "alldeps", "cost", "idx", "fin", "succ", "npend", "prio")

    def __init__(self, eng, emit, dma):
        self.eng = eng
        self.emit = emit
        self.dma = dma
        self.deps = []
        self.alldeps = []
        self.cost = 0.0
        self.inc = False
        self.done = None
        self.dom = (eng + "_dma") if dma else eng


class Prog:
    NSLOT = {"sp": 24, "pool": 6, "act": 4}

    def __init__(self, nc, self_sync=False):
        self.nc = nc
        self.ops = []
        self.lastw = {}
        self.readers = {}
        self.self_sync = self_sync
        self.ndma = {}
        self.lastslot = {}

    COST = {"pe": 0.12, "act": 0.6, "dve": 0.7, "pool": 1.3}

    def op(self, eng, emit, reads=(), writes=(), dma=False, cost=None):
        o = Op(eng, emit, dma)
        o.cost = cost if cost is not None else (3.0 if dma else self.COST[eng])
        deps = {}
        if dma:
            n = self.ndma.get(eng, 0)
            self.ndma[eng] = n + 1
            o.dom = f"{eng}_dma{n % self.NSLOT[eng]}"
            prev = self.lastslot.get(o.dom)
            if prev is not None:
                deps[id(prev)] = (prev, True)
            self.lastslot[o.dom] = o
        for k in reads:
            w = self.lastw.get(k)
            if w is not None:
                deps[id(w)] = (w, True)
        for k in writes:
            w = self.lastw.get(k)
            if w is not None:
                deps[id(w)] = (w, True)
            for r in self.readers.get(k, ()):
                if id(r) not in deps:
                    deps[id(r)] = (r, False)
        for k in reads:
            self.readers.setdefault(k, []).append(o)
        for k in writes:
            self.lastw[k] = o
            self.readers[k] = []
        for d, hard in deps.values():
            if d is o:
                continue
            o.alldeps.append(d)
            if d.dom == o.dom and not o.dma:
                if o.eng != "pe":
                    o.deps.append(d)
                continue
            o.deps.append(d)
        self.ops.append(o)
        return o

    def pe(self, emit, r=(), w=(), cost=None):
        return self.op("pe", emit, r, w, cost=cost)

    def act(self, emit, r=(), w=(), cost=None):
        return self.op("act", emit, r, w, cost=cost)

    def dve(self, emit, r=(), w=(), cost=None):
        return self.op("dve", emit, r, w, cost=cost)

    def pool(self, emit, r=(), w=(), cost=None):
        return self.op("pool", emit, r, w, cost=cost)

    def dma(self, eng, emit, r=(), w=(), cost=None):
        return self.op(eng, emit, r, w, dma=True, cost=cost)

    def barrier(self):
        lastdom = {}
        for o in self.ops:
            if o.emit is not None:
                lastdom[o.dom] = o
        for eng in ("pe", "act", "dve", "pool", "sp"):
            b = Op(eng, None, False)
            b.deps = [o for dom, o in lastdom.items() if dom != eng]
            self.ops.append(b)
        self.lastw = {}
        self.readers = {}

    def reschedule(self, hop=float(os.environ.get("KHOP", "0.15"))):
        segs, cur = [], []
        for o in self.ops:
            if o.emit is None:
                segs.append(cur)
                segs.append([o])
                cur = []
            else:
                cur.append(o)
        segs.append(cur)
        new_ops = []
        for seg in segs:
            if len(seg) <= 1 or seg[0].emit is None:
                new_ops.extend(seg)
                continue
            inseg = {id(o) for o in seg}
            for o in seg:
                o.succ = []
                o.fin = 0.0
            fixed = set(os.environ.get("KFIXED", "pe").split(","))
            lastfixed = {}
            for o in seg:
                o.npend = 0
                for d in o.alldeps:
                    if id(d) in inseg:
                        d.succ.append(o)
                        o.npend += 1
                if o.eng in fixed and not o.dma:
                    p_ = lastfixed.get(o.eng)
                    if p_ is not None and all(p_ is not d for d in o.alldeps):
                        p_.succ.append(o)
                        o.npend += 1
                    lastfixed[o.eng] = o
            for o in reversed(seg):
                o.prio = o.cost + max([x.prio + (hop if x.eng != o.eng else 0.0) for x in o.succ], default=0.0)
            ready = {}
            for o in seg:
                if o.npend == 0:
                    ready.setdefault(o.eng, []).append(o)
            cursor = {}
            order = []
            nleft = len(seg)
            while nleft:
                best = None
                for eng, lst in ready.items():
                    if not lst:
                        continue
                    cur_t = cursor.get(eng, 0.0)
                    for o in lst:
                        est = cur_t
                        for d in o.alldeps:
                            if id(d) in inseg:
                                t = d.fin + (hop if d.eng != o.eng or d.dma != o.dma else 0.0)
                                if t > est:
                                    est = t
                        key = (est, -o.prio)
                        if best is None or key < best[0]:
                            best = (key, o, est)
                _, o, est = best
                ready[o.eng].remove(o)
                o.fin = est + o.cost
                cursor[o.eng] = (est + 0.1) if o.dma else o.fin
                order.append(o)
                nleft -= 1
                for x in o.succ:
                    x.npend -= 1
                    if x.npend == 0:
                        ready.setdefault(x.eng, []).append(x)
            new_ops.extend(order)
        self.ops = new_ops

    def finalize(self, final_eng="sp"):
        nc = self.nc
        ops = self.ops
        fin = Op(final_eng, None, False)
        lastdom = {}
        for o in ops:
            if o.emit is not None:
                lastdom[o.dom] = o
        for dom, o in lastdom.items():
            if "_dma" in dom:
                fin.deps.append(o)
        ops = ops + [fin]
        for o in ops:
            if o.dma:
                o.inc = True
            for d in o.deps:
                d.inc = True
        cnt = {}
        for o in ops:
            if o.inc:
                cnt[o.dom] = cnt.get(o.dom, 0) + (16 if o.dma else 1)
                o.done = cnt[o.dom]
        doms = sorted({o.dom for o in ops if o.inc})
        sems = {d: nc.alloc_semaphore(name="s_" + d) for d in doms}
        per_eng = {}
        for o in ops:
            per_eng.setdefault(o.eng, []).append(o)
        engs = {"pe": "tensor", "act": "scalar", "dve": "vector", "pool": "gpsimd", "sp": "sync"}

        def run(engname, e):
            waited = {}
            for o in per_eng.get(engname, []):
                need = {}
                for d in o.deps:
                    if d.done > need.get(d.dom, 0):
                        need[d.dom] = d.done
                for dom, v in need.items():
                    if v > waited.get(dom, 0):
                        e.wait_ge(sems[dom], v)
                        waited[dom] = v
                if o.emit is None:
                    continue
                ins = o.emit(e)
                if o.inc:
                    ins.then_inc(sems[o.dom], 16 if o.dma else 1)

        with nc.Block() as block:
            for engname, attr in engs.items():
                if engname not in per_eng:
                    continue

                def mk(engname=engname):
                    def f(e):
                        run(engname, e)
                    return f
                getattr(block, attr)(mk())
        return dict(nops=len(ops), cnt=cnt)


def _consts():
    c = {}
    idx = np.arange(128)
    c["ident"] = np.eye(128, dtype=np.float32)
    c["id2"] = (idx[:, None] % 64 == np.arange(64)[None, :]).astype(np.float32)
    for ty in ("P", "S"):
        if ty == "P":
            blk = np.zeros(128, np.int64)
        else:
            blk = idx // DT_
        same = blk[:, None] == blk[None, :]
        s = idx[:, None]
        t = idx[None, :]
        lt = (same & (s < t)).astype(np.float32)
        le = (same & (s <= t)).astype(np.float32)
        gt = (same & (s > t)).astype(np.float32)
        c["tri" + ty] = (-CDEC) * le
        c["tgt" + ty] = (-CDEC) * gt
        c["ma" + ty] = np.concatenate([-lt, le, -lt, le], 1)
        c["mb" + ty] = np.concatenate([lt, le, lt, le], 1)
        c["mt" + ty] = np.concatenate([-gt] * 4, 1)
    c["segP"] = np.full((128, 1), -CDEC, np.float32)
    bm = (idx[:, None] // DT_ == np.arange(NSEQ)[None, :]).astype(np.float32)
    c["segS"] = (-CDEC) * bm
    c["bm"] = bm
    return c


CONST_SHAPES = {k: v.shape for k, v in _consts().items()}


def build(NPT=16, dbg=None, _dry=False, _order=None):
    if not _dry and _order is None:
        _order = build(NPT, None, _dry=True)
    STOP = os.environ.get('KSTOP', '')
    GI_MODE = os.environ.get('KGI', 'td')
    SKIP2 = os.environ.get('KSKIP2', '') == '1'
    NT = NPT + 1
    nc = bass.Bass("TRN2", target_bir_lowering=False)
    P = Prog(nc)
    din = {}

    def DI(name, shape):
        din[name] = nc.dram_tensor(name, list(shape), F32, kind="ExternalInput").ap()
        return din[name]

    def DO(name, shape):
        return nc.dram_tensor(name, list(shape), F32, kind="ExternalOutput").ap()

    xp = DI("xp", [NPT * 128, D])
    xs = DI("xs", [128, D])
    stc = DI("stc", [NSEQ, 2, CW_])
    sts = DI("sts", [NSEQ, RC])
    stw = DI("stw", [NSEQ, NH, HS, HS])
    w_in = DI("w_in", [D, IC])
    w_out = DI("w_out", [D, D])
    w_ff1 = DI("w_ff1", [D, DFF])
    w_ff2 = DI("w_ff2", [DFF, D])
    g1T_d = DI("g1T", [128, 8])
    g2T_d = DI("g2T", [128, 8])
    mu_d = DI("mu", [1, RC])
    cw_d = DI("convw", [1, 3 * CW_])
    kk_d = DI("k_k", [1, RW])
    ka_d = DI("k_a", [1, RW])
    rk_d = DI("r_k", [1, RW])
    lg_d = DI("lnx_g", [1, RW])
    lb_d = DI("lnx_b", [1, RW])
    nf_d = DI("normf", [1, D])
    w0_d = DI("w0", [1, RW])
    a0_d = DI("a0", [1, RW])
    w2a_d = DI("w2a", [128, RW])
    g2_d = DI("g2", [128, RW])
    cd = {k: DI("c_" + k, shp) for k, shp in CONST_SHAPES.items()}

    y_p = DO("y_p", [NPT * 128, D])
    y_s = DO("y_s", [128, D])
    conv_p = DO("conv_p", [2, CW_])
    shift_p = DO("shift_p", [1, RC])
    wkv_p = DO("wkv_p", [NH, HS, HS])
    conv_s = DO("conv_s", [NSEQ, 2, CW_])
    shift_s = DO("shift_s", [NSEQ, RC])
    wkv_s = DO("wkv_s", [NSEQ, NH, HS, HS])
    x1s = nc.dram_tensor("x1s", [NT * 128, D], F32).ap()
    wbi = nc.dram_tensor("wbi", [7, 128, 8, 512], BF16).ap()
    wbo = nc.dram_tensor("wbo", [2, 128, 8, 512], BF16).ap()
    bnd_p = nc.dram_tensor("bnd_p", [NT, RC], F32).ap()
    bnd_c = nc.dram_tensor("bnd_c", [NT, 2, CW_], F32).ap()
    dbg_out = {}
    if dbg:
        for name, shape in dbg.items():
            if not name.startswith("_"):
                dbg_out[name] = DO("dbg_" + name, shape)

    from contextlib import ExitStack

    with ExitStack() as st0:
        TW = [st0.enter_context(nc.sbuf_tensor(f"TW{i}", [128, IC], BF16)) for i in range(2)]
        n0 = 0
        for (src_w, dst_w, wd) in ((w_in, wbi, IC), (w_out, wbo, D)):
            for k in range(8):
                tw, twk = TW[n0 % 2], f"TW{n0 % 2}"
                n0 += 1
                P.dma("pool", lambda e, tw=tw, src_w=src_w, k=k, wd=wd: e.dma_start(out=tw[:, 0:wd], in_=src_w[k * 128:(k + 1) * 128, :]), w=[twk])
                nfull = wd // 512
                P.dma("sp", lambda e, tw=tw, dst_w=dst_w, k=k, nfull=nfull: e.dma_start(out=dst_w[0:nfull, :, k, :].rearrange("j p n -> p j n"),
                                                                                  in_=tw[:, 0:nfull * 512].rearrange("p (j n) -> p j n", n=512)), r=[twk], w=["wb"])
                if wd % 512:
                    P.dma("sp", lambda e, tw=tw, dst_w=dst_w, k=k, nfull=nfull, wd=wd: e.dma_start(out=dst_w[nfull, :, k, 0:wd - nfull * 512], in_=tw[:, nfull * 512:wd]), r=[twk], w=["wb"])
    P.barrier()

    with ExitStack() as st1:
        def SB(name, shape, dt=F32):
            return st1.enter_context(nc.sbuf_tensor(name, list(shape), dt))

        def PSF(name, shape, dt=F32):
            return st1.enter_context(nc.psum_tensor(name, list(shape), dt))

        NB = 5
        PS = [PSF(f"ps{i}", [128, 512]) for i in range(NB)]
        PSY = PSF("psy", [128, 512])
        PSS = PSF("pss", [128, 512])
        PSB = PSF("psb", [128, 1024], BF16)
        bank_i = [0]

        def nb():
            i = bank_i[0] % NB
            bank_i[0] += 1
            return PS[i], f"ps{i}"

        NRING = 3
        WR = [SB(f"WR{i}", [128, 8, 512], BF16) for i in range(NRING)]
        wq = list(_order) if _order is not None else []
        wsrc = {"wbi": wbi, "wbo": wbo}
        wstate = {"issued": 0, "used": 0}

        def wchunk(srcw, c0, wd):
            name = "wbi" if srcw is wbi else "wbo"
            n = wstate["used"]
            wstate["used"] += 1
            if _dry:
                wq.append((name, c0, wd))
                return WR[n % NRING], f"WR{n % NRING}"
            assert wq[n] == (name, c0, wd), (n, wq[n], name, c0, wd)
            while wstate["issued"] < min(len(wq), n + NRING):
                m = wstate["issued"]
                nm2, c2, wd2 = wq[m]
                P.dma("sp", lambda e, m=m, nm2=nm2, c2=c2, wd2=wd2: e.dma_start(out=WR[m % NRING][:, :, 0:wd2], in_=wsrc[nm2][c2 // 512, :, :, 0:wd2]),
                      w=[f"WR{m % NRING}"], cost=6.0)
                wstate["issued"] += 1
            return WR[n % NRING], f"WR{n % NRING}"

        def bc_tile(name, dram, n):
            t = SB(name, [128, n])
            P.dma("sp", lambda e: e.dma_start(out=t[:], in_=dram.partition_broadcast(128)), w=[name])
            return t

        def ld_tile(name, dram, shape, dt=F32, eng="sp"):
            t = SB(name, shape, dt)
            P.dma(eng, lambda e: e.dma_start(out=t[:], in_=dram), w=[name])
            return t

        IDN = ld_tile("IDN", cd["ident"], [128, 128])
        IDB = ld_tile("IDB", cd["ident"], [128, 128], BF16, eng="pool")
        ID2 = ld_tile("ID2", cd["id2"], [128, 64])
        G1T = ld_tile("G1T", g1T_d, [128, 8])
        MU = bc_tile("MU", mu_d, RC)
        CWt = bc_tile("CWt", cw_d, 3 * CW_)
        KKb = bc_tile("KKb", kk_d, RW)
        KAb = bc_tile("KAb", ka_d, RW)
        RKb = bc_tile("RKb", rk_d, RW)
        LGb = bc_tile("LGb", lg_d, RW)
        LBb = bc_tile("LBb", lb_d, RW)
        W0r = ld_tile("W0r", w0_d, [1, RW])
        A0r = ld_tile("A0r", a0_d, [1, RW])
        W2A = ld_tile("W2A", w2a_d, [128, RW])
        G2 = ld_tile("G2", g2_d, [128, RW])
        ONE1 = SB("ONE1", [1, 128])
        P.pool(lambda e: e.memset(ONE1[:], 1.0), w=["ONE1"])
        MKT = {}
        for nm in ("tri", "tgt"):
            MKT[nm] = ld_tile("M_" + nm, cd[nm + "P"], [128, 128])
        for nm in ("ma", "mb", "mt"):
            MKT[nm] = ld_tile("M_" + nm, cd[nm + "P"], [128, 512])
        MK = {}
        for ty in ("P", "S"):
            for nm in ("tri", "tgt", "ma", "mb", "mt"):
                MK[nm + ty] = MKT[nm]
        SEGP = ld_tile("SEGP", cd["segP"], [128, 1])
        SEGS = ld_tile("SEGS", cd["segS"], [128, NSEQ])
        BM = ld_tile("BM", cd["bm"], [128, NSEQ])

        X = [SB(f"X{i}", [128, D]) for i in range(2)]
        SS = SB("SS", [128, 1])
        RS = SB("RS", [128, 1])
        HB = SB("HB", [128, D], BF16)
        HT = SB("HT", [128, 8, 128], BF16)
        PR0 = SB("PR0", [128, IC])
        PR = [PR0, PR0]
        CX0 = SB("CX0", [128, CW_])
        CX = [CX0, CX0]
        CONV4 = SB("CONV4", [128, 4 * CW_])
        SH1, SH2, CT, CT2 = (CONV4[:, i * 512:(i + 1) * 512] for i in range(4))
        YCs = [SB(f"YC{i}", [128, D], BF16) for i in range(2)]
        X1T = SB("X1T", [128, D])
        JUNK = CONV4[:, 0:1024]
        YCTb = SB("YCTb", [128, 8, 128], BF16)
        PV = SB("PV", [128, RC])
        XM = PV
        LOR = SB("LOR", [128, 256])
        LT = SB("LT", [128, 2, 128])
        SG = SB("SG", [128, RW])
        YYb = SB("YYb", [128, RW])
        GNS = SB("GNS", [128, RW])
        S8c = SB("S8c", [128, 8])
        S8d = SB("S8d", [128, 8])
        S8e = SB("S8e", [128, 8])
        AA = SB("AA", [128, RW])
        GGs = [SB(f"GG{i}", [128, RW]) for i in range(2)]
        EXP4 = SB("EXP4", [128, 4 * RW])
        EIN, EINV, EEX, EEND = (EXP4[:, i * 512:(i + 1) * 512] for i in range(4))
        BTm = SB("BTm", [128, RW], BF16)
        KTl = SB("KTl", [128, RW], BF16)
        KS2 = SB("KS2", [128, 2 * RW])
        KK0, SQ = KS2[:, 0:512], KS2[:, 512:1024]
        S8 = SB("S8", [128, 8])
        S8b = SB("S8b", [128, 8])
        BB = SB("BB", [128, RW])
        KP = SB("KP", [128, RW])
        BEs = [SB(f"BEb{i}", [128, RW], BF16) for i in range(2)]
        KEs = [SB(f"KEb{i}", [128, RW], BF16) for i in range(2)]
        VBs = [SB(f"VB{i}", [128, RW], BF16) for i in range(2)]
        BONs = [SB(f"BON{i}", [128, RW]) for i in range(2)]
        RTms = [SB(f"RTm{i}", [128, RW], BF16) for i in range(2)]
        KTMs = [SB(f"KTM{i}", [128, RW], BF16) for i in range(2)]
        KRs = [SB(f"KR{i}", [128, 4, 2, 128], BF16) for i in range(2)]
        BFs = [SB(f"BF{i}", [128, 4, 128], BF16) for i in range(2)]
        KFs = [SB(f"KF{i}", [128, 4, 128], BF16) for i in range(2)]
        PCs = [SB(f"PC{i}", [128, 4]) for i in range(2)]
        PCS = SB("PCS", [64, NH, NSEQ])
        GA = [SB(f"GA{g}", [128, 4, 2, 128], BF16) for g in range(2)]
        GB = [SB(f"GB{g}", [128, 4, 2, 128], BF16) for g in range(2)]
        GT = [SB(f"GT{g}", [128, 4, 128], BF16) for g in range(2)]
        RN = [[SB(f"RN{g}{i}", [128, 4, 128], BF16) for i in range(2)] for g in range(2)]
        RTN = [[SB(f"RTN{g}{i}", [128, 4, 128], BF16) for i in range(2)] for g in range(2)]
        TT = [[SB(f"TT{g}{i}", [128, 4, 128], BF16) for i in range(2)] for g in range(2)]
        X1N = [SB(f"X1N{g}", [128, 4, 64], BF16) for g in range(2)]
        UK = [SB(f"UK{g}", [128, 4, 2, 64], BF16) for g in range(2)]
        RHb = SB("RHb", [128, 2, 2, 128])
        RH = [RHb[:, 0, :, :], RHb[:, 1, :, :]]
        RHS = RHb[0:64, :, :, :].rearrange("p a b t -> p (a b) t")
        MM = [SB(f"MM{g}", [128, 2, 64]) for g in range(2)]
        STt = [SB(f"ST{i}", [128, 4, 64]) for i in range(2)]
        S0 = PR0[:, 0:1024].rearrange("p (b k) -> p b k", k=64)
        S0T = EXP4[0:64, :].rearrange("p (b t) -> p b t", t=128)
        RHm = CONV4[0:64, :].rearrange("p (b t) -> p b t", t=128)
        KHm = PR0[:, 1024:1536].bitcast(BF16).rearrange("p (b k) -> p b k", k=64)
        UTm = PR0[:, 1536:2048].bitcast(BF16).rearrange("p (b k) -> p b k", k=64)
        Vm = PR0[:, 2048:2560].bitcast(BF16).rearrange("p (b k) -> p b k", k=64)
        DPC = PV[0:64, 0:1024].rearrange("p (b k) -> p b k", k=64)
        MS = KS2[0:64, :].rearrange("p (b k) -> p b k", k=64)
        WPO = RHb[0:64, :, :, :].rearrange("p a b t -> p (a b) t")
        SO = [DPC, DPC]

        def v3(ap, k=64):
            return ap.rearrange("p (h k) -> p h k", k=k)

        def dbgdump(name, ap, key):
            if name in dbg_out:
                P.dma("sp", lambda e: e.dma_start(out=dbg_out[name], in_=ap), r=[key])

        def front(ti):
            ty = "S" if ti == NPT else "P"
            first = (ti == 0)
            lastp = (ti == NPT - 1)
            par = ti % 2
            K = lambda n: f"{n}#{par}"
            x = X[par]
            xk = f"X{par}"
            YC, GG, BON, KR, BF, KF, VB = YCs[par], GGs[par], BONs[par], KRs[par], BFs[par], KFs[par], VBs[par]
            KTM, RTm, BE, KE, PC = KTMs[par], RTms[par], BEs[par], KEs[par], PCs[par]
            src = xs if ty == "S" else xp[ti * 128:(ti + 1) * 128, :]
            P.dma("sp", lambda e: e.dma_start(out=x[:], in_=src), w=[xk])
            P.act(lambda e: e.activation(out=JUNK[:], in_=x[:], func=AF.Square, accum_out=SS[:]), r=[xk], w=["SH1", "SH2", "SS"])
            P.dve(lambda e: e.tensor_scalar(out=RS[:], in0=SS[:], scalar1=1.0 / D, scalar2=1e-6, op0=ALU.mult, op1=ALU.add), r=["SS"], w=["RS"])
            P.act(lambda e: e.activation(out=RS[:], in_=RS[:], func=AF.Sqrt), r=["RS"], w=["RS"])
            P.dve(lambda e: e.reciprocal(out=RS[:], in_=RS[:]), r=["RS"], w=["RS"])
            P.act(lambda e: e.activation(out=HB[:], in_=x[:], func=AF.Copy, scale=RS[:, 0:1]), r=[xk, "RS"], w=["HB"], cost=1.2)
            for k in range(8):
                P.pe(lambda e, k=k: e.transpose(PSB[:, k * 128:(k + 1) * 128], HB[:, k * 128:(k + 1) * 128], IDB[:]), r=["HB", "IDB"], w=["psb"])
            P.dve(lambda e: e.tensor_tensor(out=HT[:], in0=PSB[:].rearrange("p (k t) -> p k t", t=128),
                                            in1=G1T[:].unsqueeze(2).broadcast_to([128, 8, 128]), op=ALU.mult), r=["psb", "G1T"], w=["HT"])
            pr = PR[par]

            def proj_chunks(js):
                for j in js:
                    wd = 512 if j < 6 else 256
                    bk, bkk = nb()
                    wr, wrk = wchunk(wbi, j * 512, wd)
                    for k in range(8):
                        P.pe(lambda e, k=k, wd=wd, bk=bk, wr=wr: e.matmul(bk[:, 0:wd], lhsT=HT[:, k, :], rhs=wr[:, k, 0:wd], start=(k == 0), stop=(k == 7)),
                             r=["HT", wrk], w=[bkk], cost=0.22)
                    P.act(lambda e, j=j, wd=wd, bk=bk: e.copy(out=pr[:, j * 512:j * 512 + wd], in_=bk[:, 0:wd]), r=[bkk], w=["PRb" if j >= 3 else "PRa"])
                    yield

            yield from proj_chunks((3, 4, 5, 6))
            yield
            prw = pr[:, 1536:IC]
            P.dma("sp", lambda e: e.dma_start(out=PV[1:113, :], in_=pr[0:112, 1536:IC]), r=["PRb"], w=["XM"], cost=5.0)
            P.dma("sp", lambda e: e.dma_start(out=PV[113:128, :], in_=pr[112:127, 1536:IC]), r=["PRb"], w=["XM"])
            if ty == "S":
                P.dma("sp", lambda e: e.dma_start(out=PV[0:128:8, :], in_=sts), w=["XM"])
            elif first:
                P.pool(lambda e: e.memset(PV[0:1, :], 0.0), w=["XM"])
            else:
                P.dma("sp", lambda e: e.dma_start(out=PV[0:1, :], in_=bnd_p[ti - 1:ti, :]), r=["bnd"], w=["XM"])
            P.dve(lambda e: e.tensor_tensor(out=PV[:], in0=PV[:], in1=prw, op=ALU.subtract), r=["XM", "PRb"], w=["XM"], cost=2.0)
            P.dve(lambda e: e.tensor_tensor(out=PV[:], in0=PV[:], in1=MU[:], op=ALU.mult), r=["XM", "MU"], w=["XM"], cost=2.0)
            P.dve(lambda e: e.tensor_tensor(out=XM[:], in0=PV[:], in1=prw, op=ALU.add), r=["XM", "PRb"], w=["XM"], cost=2.0)
            r_ = XM[:, 0:512]
            k_ = XM[:, 512:1024]
            v_ = XM[:, 1024:1536]
            if STOP == 'B':
                return
            if ty == "S":
                P.dma("sp", lambda e: e.dma_start(out=shift_s, in_=pr[7:128:8, 1536:IC]), r=["PRb"])
            else:
                P.dma("sp", lambda e: e.dma_start(out=bnd_p[ti:ti + 1, :], in_=pr[127:128, 1536:IC]), r=["PRb"], w=["bnd"])
            if lastp:
                P.dma("sp", lambda e: e.dma_start(out=shift_p, in_=pr[127:128, 1536:IC]), r=["PRb"])
            yield from proj_chunks((0, 1, 2))
            yield
            cx = CX[par]
            cxk = "CX"
            P.pool(lambda e: e.tensor_tensor(out=cx[:], in0=pr[:, 512:1024], in1=pr[:, 1024:1536], op=ALU.mult), r=["PRa"], w=[cxk])
            P.dma("sp", lambda e: e.dma_start(out=SH1[1:113, :], in_=cx[0:112, :]), r=[cxk], w=["SH1"])
            P.dma("sp", lambda e: e.dma_start(out=SH1[113:128, :], in_=cx[112:127, :]), r=[cxk], w=["SH1"])
            P.dma("sp", lambda e: e.dma_start(out=SH2[2:114, :], in_=cx[0:112, :]), r=[cxk], w=["SH2"])
            P.dma("sp", lambda e: e.dma_start(out=SH2[114:128, :], in_=cx[112:126, :]), r=[cxk], w=["SH2"])
            if ty == "S":
                P.dma("sp", lambda e: e.dma_start(out=SH1[0:128:8, :], in_=stc[:, 1, :]), w=["SH1"])
                P.dma("sp", lambda e: e.dma_start(out=SH2[0:128:8, :], in_=stc[:, 0, :]), w=["SH2"])
                P.dma("sp", lambda e: e.dma_start(out=SH2[1:128:8, :], in_=stc[:, 1, :]), w=["SH2"])
            elif first:
                P.pool(lambda e: e.memset(SH1[0:1, :], 0.0), w=["SH1"])
                P.pool(lambda e: e.memset(SH2[0:2, :], 0.0), w=["SH2"])
            else:
                P.dma("sp", lambda e: e.dma_start(out=SH1[0:1, :], in_=bnd_c[ti - 1, 1:2, :]), r=["bnd"], w=["SH1"])
                P.dma("sp", lambda e: e.dma_start(out=SH2[0:2, :], in_=bnd_c[ti - 1, :, :]), r=["bnd"], w=["SH2"])
            P.pool(lambda e: e.tensor_tensor(out=CT[:], in0=SH2[:], in1=CWt[:, 0:512], op=ALU.mult), r=["SH2", "CWt"], w=["CT"])
            P.pool(lambda e: e.tensor_tensor(out=CT2[:], in0=SH1[:], in1=CWt[:, 512:1024], op=ALU.mult), r=["SH1", "CWt"], w=["CT2"])
            P.pool(lambda e: e.tensor_tensor(out=CT[:], in0=CT[:], in1=CT2[:], op=ALU.add), r=["CT", "CT2"], w=["CT"])
            P.pool(lambda e: e.tensor_tensor(out=CT2[:], in0=cx[:], in1=CWt[:, 1024:1536], op=ALU.mult), r=[cxk, "CWt"], w=["CT2"])
            P.pool(lambda e: e.tensor_tensor(out=CT[:], in0=CT[:], in1=CT2[:], op=ALU.add), r=["CT", "CT2"], w=["CT"])
            P.pool(lambda e: e.tensor_tensor(out=YC[:, 0:512], in0=CT[:], in1=pr[:, 0:512], op=ALU.mult), r=["CT", "PRa"], w=[K("YCa")])
            if ty == "S":
                P.dma("sp", lambda e: e.dma_start(out=conv_s[:, 0, :], in_=cx[6:128:8, :]), r=[cxk])
                P.dma("sp", lambda e: e.dma_start(out=conv_s[:, 1, :], in_=cx[7:128:8, :]), r=[cxk])
            else:
                P.dma("sp", lambda e: e.dma_start(out=bnd_c[ti, :, :], in_=cx[126:128, :]), r=[cxk], w=["bnd"])
            if lastp:
                P.dma("sp", lambda e: e.dma_start(out=conv_p, in_=cx[126:128, :]), r=[cxk])
            yield
            P.act(lambda e: e.activation(out=LOR[:, 0:64], in_=XM[:, 1536:1600], func=AF.Tanh), r=["XM"], w=["LOR"])
            P.act(lambda e: e.copy(out=LOR[:, 64:128], in_=XM[:, 1600:1664]), r=["XM"], w=["LOR"])
            P.act(lambda e: e.activation(out=LOR[:, 128:256], in_=XM[:, 1664:1792], func=AF.Sigmoid), r=["XM"], w=["LOR"])
            bk, bkk = nb()
            for i in range(2):
                P.pe(lambda e, i=i, bk=bk: e.transpose(bk[:, i * 128:(i + 1) * 128], LOR[:, i * 128:(i + 1) * 128], IDN[:]), r=["LOR", "IDN"], w=[bkk])
            P.act(lambda e, bk=bk: e.copy(out=LT[:].rearrange("p a t -> p (a t)"), in_=bk[:, 0:256]), r=[bkk], w=["LT"])
            bk, bkk = nb()
            P.pe(lambda e, bk=bk: e.matmul(bk[:], lhsT=LT[0:64, 0, :], rhs=W2A[0:64, :], start=True, stop=False), r=["LT", "W2A"], w=[bkk], cost=0.85)
            P.pe(lambda e, bk=bk: e.matmul(bk[:], lhsT=ONE1[:], rhs=W0r[:], start=False, stop=True), r=["ONE1", "W0r"], w=[bkk])
            P.act(lambda e, bk=bk: e.activation(out=SG[:], in_=bk[:], func=AF.Sigmoid), r=[bkk], w=["SG"])
            bk, bkk = nb()
            P.pe(lambda e, bk=bk: e.matmul(bk[:], lhsT=LT[64:128, 0, :], rhs=W2A[64:128, :], start=True, stop=False), r=["LT", "W2A"], w=[bkk], cost=0.85)
            P.pe(lambda e, bk=bk: e.matmul(bk[:], lhsT=ONE1[:], rhs=A0r[:], start=False, stop=True), r=["ONE1", "A0r"], w=[bkk])
            P.act(lambda e, bk=bk: e.activation(out=AA[:], in_=bk[:], func=AF.Sigmoid), r=[bkk], w=["AA"])
            bk, bkk = nb()
            P.pe(lambda e, bk=bk: e.matmul(bk[:], lhsT=LT[:, 1, :], rhs=G2[:], start=True, stop=True), r=["LT", "G2"], w=[bkk], cost=0.85)
            P.act(lambda e, bk=bk: e.copy(out=GG[:], in_=bk[:]), r=[bkk], w=[K("GG")])
            yield
            bk, bkk = nb()
            P.pe(lambda e, bk=bk: e.matmul(bk[:], lhsT=MK["tri" + ty][:], rhs=SG[:], start=True, stop=True), r=["SG", "M_tri"], w=[bkk], cost=0.85)
            P.act(lambda e, bk=bk: e.activation(out=EIN[:], in_=bk[:], func=AF.Exp), r=[bkk], w=["EIN"])
            P.act(lambda e, bk=bk: e.activation(out=EINV[:], in_=bk[:], func=AF.Exp, scale=-1.0), r=[bkk], w=["EINV"])
            P.dve(lambda e, bk=bk: e.scalar_tensor_tensor(out=EEX[:], in0=SG[:], scalar=CDEC, in1=bk[:], op0=ALU.mult, op1=ALU.add), r=[bkk, "SG"], w=["EEX"])
            P.act(lambda e: e.activation(out=EEX[:], in_=EEX[:], func=AF.Exp), r=["EEX"], w=["EEX"])
            bk, bkk = nb()
            P.pe(lambda e, bk=bk: e.matmul(bk[:], lhsT=MK["tgt" + ty][:], rhs=SG[:], start=True, stop=True), r=["SG", "M_tgt"], w=[bkk], cost=0.85)
            P.act(lambda e, bk=bk: e.activation(out=EEND[:], in_=bk[:], func=AF.Exp), r=[bkk], w=["EEND"])
            bk, bkk = nb()
            if ty == "P":
                for p in range(4):
                    P.pe(lambda e, p=p, bk=bk: e.matmul(bk[:, p:p + 1], lhsT=SG[:, p * 128:(p + 1) * 128], rhs=SEGP[:], start=True, stop=True), r=["SG", "SEGP"], w=[bkk])
                P.act(lambda e, bk=bk: e.activation(out=PC[:], in_=bk[:, 0:4], func=AF.Exp), r=[bkk], w=[K("PC")])
            else:
                for h in range(NH):
                    P.pe(lambda e, h=h, bk=bk: e.matmul(bk[0:64, h * NSEQ:(h + 1) * NSEQ], lhsT=SG[:, h * 64:(h + 1) * 64], rhs=SEGS[:], start=True, stop=True), r=["SG", "SEGS"], w=[bkk])
                P.act(lambda e, bk=bk: e.activation(out=PCS[:].rearrange("p h b -> p (h b)"), in_=bk[0:64, 0:NH * NSEQ], func=AF.Exp), r=[bkk], w=["PCS"])
            yield
            P.dve(lambda e: e.tensor_tensor(out=KK0[:], in0=k_, in1=KKb[:], op=ALU.mult), r=["XM", "KKb"], w=["KK0"])
            P.dve(lambda e: e.tensor_tensor(out=SQ[:], in0=KK0[:], in1=KK0[:], op=ALU.mult), r=["KK0"], w=["SQ"])
            P.dve(lambda e: e.tensor_reduce(out=S8[:], in_=v3(SQ[:]), axis=AX.X, op=ALU.add), r=["SQ"], w=["S8"])
            P.dve(lambda e: e.tensor_scalar(out=S8[:], in0=S8[:], scalar1=1e-24, scalar2=None, op0=ALU.max), r=["S8"], w=["S8"])
            P.act(lambda e: e.activation(out=S8[:], in_=S8[:], func=AF.Sqrt), r=["S8"], w=["S8"])
            P.dve(lambda e: e.reciprocal(out=S8[:], in_=S8[:]), r=["S8"], w=["S8"])
            P.dve(lambda e: e.tensor_tensor(out=v3(KK0[:]), in0=v3(KK0[:]), in1=S8[:].unsqueeze(2).broadcast_to([128, 8, 64]), op=ALU.mult), r=["KK0", "S8"], w=["KK0"])
            P.dve(lambda e: e.tensor_tensor(out=BB[:], in0=KK0[:], in1=AA[:], op=ALU.mult), r=["KK0", "AA"], w=["BB"])
            P.dve(lambda e: e.scalar_tensor_tensor(out=KP[:], in0=AA[:], scalar=-1.0, in1=KAb[:], op0=ALU.add, op1=ALU.mult), r=["AA", "KAb"], w=["KP"])
            P.dve(lambda e: e.scalar_tensor_tensor(out=KP[:], in0=KP[:], scalar=1.0, in1=k_, op0=ALU.add, op1=ALU.mult), r=["KP", "XM"], w=["KP"])
            P.pool(lambda e: e.tensor_tensor(out=SQ[:], in0=r_, in1=KP[:], op=ALU.mult), r=["XM", "KP", "S8"], w=["SQ"])
            P.pool(lambda e: e.tensor_tensor(out=SQ[:], in0=SQ[:], in1=RKb[:], op=ALU.mult), r=["SQ", "RKb"], w=["SQ"])
            P.dve(lambda e: e.tensor_reduce(out=S8b[:], in_=v3(SQ[:]), axis=AX.X, op=ALU.add), r=["SQ"], w=["S8b"])
            P.dve(lambda e: e.tensor_tensor(out=v3(BON[:]), in0=v3(v_), in1=S8b[:].unsqueeze(2).broadcast_to([128, 8, 64]), op=ALU.mult), r=["XM", "S8b"], w=[K("BON")])
            P.pool(lambda e: e.tensor_tensor(out=BON[:], in0=BON[:], in1=LBb[:], op=ALU.add), r=[K("BON"), "LBb"], w=[K("BON")])
            P.pool(lambda e: e.tensor_tensor(out=BON[:], in0=BON[:], in1=GG[:], op=ALU.mult), r=[K("BON"), K("GG")], w=[K("BON")])
            P.pool(lambda e: e.tensor_tensor(out=GG[:], in0=GG[:], in1=LGb[:], op=ALU.mult), r=[K("GG"), "LGb"], w=[K("GG")])
            P.pool(lambda e: e.tensor_tensor(out=RTm[:], in0=r_, in1=EIN[:], op=ALU.mult), r=["XM", "EIN"], w=[K("RTm")])
            P.dve(lambda e: e.tensor_tensor(out=KTM[:], in0=KK0[:], in1=EEX[:], op=ALU.mult), r=["KK0", "EEX"], w=[K("KTM")])
            P.dve(lambda e: e.tensor_tensor(out=BTm[:], in0=BB[:], in1=EINV[:], op=ALU.mult), r=["BB", "EINV"], w=["BTm"])
            P.pool(lambda e: e.tensor_tensor(out=KTl[:], in0=KP[:], in1=EINV[:], op=ALU.mult), r=["KP", "EINV"], w=["KTl"])
            P.dve(lambda e: e.tensor_tensor(out=BE[:], in0=BB[:], in1=EEND[:], op=ALU.mult), r=["BB", "EEND"], w=[K("BEb")])
            P.pool(lambda e: e.tensor_tensor(out=KE[:], in0=KP[:], in1=EEND[:], op=ALU.mult), r=["KP", "EEND"], w=[K("KEb")])
            P.act(lambda e: e.copy(out=VB[:], in_=v_), r=["XM"], w=[K("VB")])
            if STOP == 'C':
                return
            yield
            for src_t, srck, dst, dstk in ((KTM, K("KTM"), KR[:, :, 0, :], K("KR")), (RTm, K("RTm"), KR[:, :, 1, :], K("KR")),
                                           (BTm, "BTm", BF[:], K("BF")), (KTl, "KTl", KF[:], K("KF"))):
                bk, bkk = nb()
                bkb = bk[:].bitcast(BF16)
                for p in range(4):
                    P.pe(lambda e, p=p, bkb=bkb, src_t=src_t: e.transpose(bkb[:, p * 128:(p + 1) * 128], src_t[:, p * 128:(p + 1) * 128], IDB[:]), r=[srck, "IDB"], w=[bkk])
                P.act(lambda e, bkb=bkb, dst=dst: e.copy(out=dst, in_=bkb[:, 0:512].rearrange("p (a t) -> p a t", t=128)), r=[bkk], w=[dstk])

            if STOP == 'D':
                return
            yield

        def back(ti):
            ty = "S" if ti == NPT else "P"
            first = (ti == 0)
            lastp = (ti == NPT - 1)
            par = ti % 2
            K = lambda n: f"{n}#{par}"
            x = X[par]
            xk = f"X{par}"
            YC, GG, BON, KR, BF, KF, VB = YCs[par], GGs[par], BONs[par], KRs[par], BFs[par], KFs[par], VBs[par]
            KTM, RTm, BE, KE, PC = KTMs[par], RTms[par], BEs[par], KEs[par], PCs[par]
            pr, cx, cxk = PR[par], CX[par], "CX"
            YY, SQ, S8, S8b, YCT = YYb, GNS, S8c, S8d, YCTb
            stprev = STt[1 - par]
            stpk = f"ST{1 - par}"
            def mach(g):
                ga, gb, gt = GA[g], GB[g], GT[g]
                for (lf, lfk, dst, dstk, mk) in ((BF, K("BF"), ga, f"GA{g}", "ma"), (KF, K("KF"), gb, f"GB{g}", "mb")):
                    bks = [nb(), nb()]
                    for i in range(4):
                        h = 4 * g + i
                        p, b0 = h // 2, 64 * (h % 2)
                        bk, bkk = bks[i % 2]
                        c0 = (i // 2) * 256
                        P.pe(lambda e, c0=c0, p=p, b0=b0, bk=bk, lf=lf: e.matmul(bk[:, c0:c0 + 256], lhsT=lf[b0:b0 + 64, p, :],
                                                                                rhs=KR[b0:b0 + 64, p, :, :].rearrange("k a t -> k (a t)"), start=True, stop=True),
                             r=[lfk, K("KR")], w=[bkk])
                    for par2 in range(2):
                        bk, bkk = bks[par2]
                        P.dve(lambda e, bk=bk, dst=dst, par2=par2, mk=mk: e.tensor_tensor(
                            out=dst[:, par2:4:2, :, :].rearrange("p h a t -> p h (a t)"), in0=bk[:].rearrange("p (h x) -> p h x", x=256),
                            in1=MK[mk + ty][:].rearrange("p (h x) -> p h x", x=256), op=ALU.mult),
                            r=[bkk, "M_" + mk], w=[dstk])
                bks = [nb(), nb()]
                for i in range(4):
                    h = 4 * g + i
                    p, b0 = h // 2, 64 * (h % 2)
                    bk, bkk = bks[i % 2]
                    c0 = (i // 2) * 128
                    P.pe(lambda e, c0=c0, p=p, b0=b0, bk=bk: e.matmul(bk[:, c0:c0 + 128], lhsT=KR[b0:b0 + 64, p, 0, :], rhs=BF[b0:b0 + 64, p, :], start=True, stop=True),
                         r=[K("KR"), K("BF")], w=[bkk])
                for par2 in range(2):
                    bk, bkk = bks[par2]
                    P.dve(lambda e, bk=bk, gt=gt, par2=par2: e.tensor_tensor(out=gt[:, par2:4:2, :], in0=bk[:, 0:256].rearrange("p (h t) -> p h t", t=128),
                                                                            in1=MK["mt" + ty][:, 0:256].rearrange("p (h t) -> p h t", t=128), op=ALU.mult),
                          r=[bkk, "M_mt"], w=[f"GT{g}"])
                if STOP == 'E0':
                    return
                yield
                P.pool(lambda e, ga=ga, g=g: e.tensor_tensor(out=TT[g][0][:], in0=ga[:, :, 0, :], in1=IDB[:].unsqueeze(1).broadcast_to([128, 4, 128]), op=ALU.add),
                       r=[f"GA{g}", "IDB"], w=[f"TT{g}0"])
                if STOP == 'E1':
                    return
                Rc, Rck = ga[:, :, 0, :], f"GA{g}"
                RTc, RTck = gt[:], f"GT{g}"
                Tc, Tck = TT[g][0], f"TT{g}0"
                NLV = 6 if ty == "P" else 2
                for lvl in range(1, NLV + 1):
                    sl = lvl % 2
                    if lvl < NLV:
                        bk, bkk = nb()
                        for i in range(4):
                            P.pe(lambda e, i=i, bk=bk, Rc=Rc, RTc=RTc: e.matmul(bk[:, i * 128:(i + 1) * 128], lhsT=RTc[:, i, :], rhs=Rc[:, i, :], start=True, stop=True),
                                 r=[Rck, RTck], w=[bkk])
                        rn, rnk = RN[g][sl], f"RN{g}{sl}"
                        P.act(lambda e, bk=bk, rn=rn: e.copy(out=rn[:].rearrange("p h t -> p (h t)"), in_=bk[:]), r=[bkk], w=[rnk])
                    bk, bkk = nb()
                    for i in range(4):
                        P.pe(lambda e, i=i, bk=bk, Rc=Rc, RTc=RTc: e.matmul(bk[:, i * 128:(i + 1) * 128], lhsT=Rc[:, i, :], rhs=RTc[:, i, :], start=True, stop=True),
                             r=[Rck, RTck], w=[bkk])
                    rtn, rtnk = RTN[g][sl], f"RTN{g}{sl}"
                    P.act(lambda e, bk=bk, rtn=rtn: e.copy(out=rtn[:].rearrange("p h t -> p (h t)"), in_=bk[:]), r=[bkk], w=[rtnk])
                    yield
                    bk, bkk = nb()
                    for i in range(4):
                        P.pe(lambda e, i=i, bk=bk, rtn=rtn, Tc=Tc: e.matmul(bk[:, i * 128:(i + 1) * 128], lhsT=rtn[:, i, :], rhs=Tc[:, i, :], start=True, stop=True),
                             r=[rtnk, Tck], w=[bkk])
                    tn, tnk = TT[g][sl], f"TT{g}{sl}"
                    P.dve(lambda e, bk=bk, tn=tn, Tc=Tc: e.tensor_tensor(out=tn[:].rearrange("p h t -> p (h t)"), in0=bk[:], in1=Tc[:].rearrange("p h t -> p (h t)"), op=ALU.add),
                          r=[bkk, Tck], w=[tnk])
                    yield
                    if lvl < NLV:
                        Rc, Rck = rn[:], rnk
                    RTc, RTck = rtn[:], rtnk
                    Tc, Tck = tn, tnk
                if STOP == 'E':
                    return
                yield
                yield "TD_DONE"
                bk, bkk = nb()
                for i in range(4):
                    h = 4 * g + i
                    P.pe(lambda e, i=i, h=h, bk=bk, gb=gb: e.matmul(bk[:, i * 64:(i + 1) * 64], lhsT=gb[:, i, 0, :], rhs=VB[:, 64 * h:64 * (h + 1)], start=True, stop=True),
                         r=[f"GB{g}", K("VB")], w=[bkk])
                P.act(lambda e, bk=bk, g=g: e.activation(out=X1N[g][:].rearrange("p h v -> p (h v)"), in_=bk[:, 0:256], func=AF.Copy, scale=-1.0), r=[bkk], w=[f"X1N{g}"])
                yield
                bk, bkk = nb()
                for i in range(4):
                    h = 4 * g + i
                    P.pe(lambda e, i=i, bk=bk, Tc=Tc, g=g: e.matmul(bk[:, i * 128:i * 128 + 64], lhsT=Tc[:, i, :], rhs=X1N[g][:, i, :], start=True, stop=True), r=[Tck, f"X1N{g}"], w=[bkk])
                    P.pe(lambda e, i=i, h=h, bk=bk, Tc=Tc: e.matmul(bk[:, i * 128 + 64:(i + 1) * 128], lhsT=Tc[:, i, :], rhs=KTM[:, 64 * h:64 * (h + 1)], start=True, stop=True), r=[Tck, K("KTM")], w=[bkk])
                uk, ukk = UK[g], f"UK{g}"
                bk4 = bk[:].rearrange("p (h a v) -> p h a v", a=2, v=64)
                P.act(lambda e, bk4=bk4, uk=uk: e.copy(out=uk[:, :, 0, :], in_=bk4[:, :, 0, :]), r=[bkk], w=[ukk])
                P.act(lambda e, bk4=bk4, uk=uk: e.activation(out=uk[:, :, 1, :], in_=bk4[:, :, 1, :], func=AF.Copy, scale=-1.0), r=[bkk], w=[ukk])
                if STOP == 'F':
                    return
                yield
                bk, bkk = nb()
                for i in range(4):
                    h = 4 * g + i
                    if ty == "P":
                        ob, col = 64 * (h % 2), (i // 2) * 128
                    else:
                        ob, col = 0, i * 128
                    P.pe(lambda e, h=h, ob=ob, col=col, bk=bk: e.matmul(bk[ob:ob + 64, col:col + 128], lhsT=RTm[:, 64 * h:64 * (h + 1)], rhs=IDB[:], start=True, stop=False), r=[K("RTm"), "IDB"], w=[bkk])
                    P.pe(lambda e, i=i, ob=ob, col=col, bk=bk, uk=uk, ga=ga: e.matmul(bk[ob:ob + 64, col:col + 128], lhsT=uk[:, i, 1, :], rhs=ga[:, i, 1, :], start=False, stop=True), r=[ukk, f"GA{g}"], w=[bkk])
                if ty == "P":
                    rh, rhk = RH[g], [f"RH{g}"]
                else:
                    rh, rhk = RHS, ["RH0", "RH1"]
                if ty == "P":
                    P.act(lambda e, bk=bk, rh=rh: e.copy(out=rh, in_=bk[:, 0:256].rearrange("p (a t) -> p a t", t=128)), r=[bkk], w=rhk)
                else:
                    P.act(lambda e, bk=bk, rh=rh: e.copy(out=rh, in_=bk[0:64, :].rearrange("p (a t) -> p a t", t=128)), r=[bkk], w=rhk)
                if ty == "P":
                    bk, bkk = nb()
                    for i in range(4):
                        h = 4 * g + i
                        ob, col = 64 * (h % 2), (i // 2) * 64
                        P.pe(lambda e, i=i, h=h, ob=ob, col=col, bk=bk, uk=uk: e.matmul(bk[ob:ob + 64, col:col + 64], lhsT=uk[:, i, 1, :], rhs=BE[:, 64 * h:64 * (h + 1)], start=True, stop=True), r=[ukk, K("BEb")], w=[bkk])
                    for j in range(2):
                        P.dve(lambda e, j=j, bk=bk, g=g: e.scalar_tensor_tensor(out=MM[g][:, j, :], in0=ID2[:], scalar=PC[:, 2 * g + j:2 * g + j + 1], in1=bk[:, j * 64:(j + 1) * 64], op0=ALU.mult, op1=ALU.add),
                              r=[bkk, "ID2", K("PC")], w=[f"MM{g}"])
                    for i in range(4):
                        h = 4 * g + i
                        p, b0, j = h // 2, 64 * (h % 2), i // 2
                        yield
                        vh = VB[:, 64 * h:64 * (h + 1)]
                        P.pe(lambda e, i=i, h=h, ga=ga, uk=uk: e.matmul(PSY[:, 64 * h:64 * (h + 1)], lhsT=ga[:, i, 1, :], rhs=uk[:, i, 0, :], start=True, stop=False), r=[f"GA{g}", ukk], w=["psy"])
                        P.pe(lambda e, i=i, h=h, gb=gb, vh=vh: e.matmul(PSY[:, 64 * h:64 * (h + 1)], lhsT=gb[:, i, 1, :], rhs=vh, start=False, stop=first), r=[f"GB{g}", K("VB")], w=["psy"])
                        if not first:
                            P.pe(lambda e, h=h, b0=b0, j=j, p=p, rh=rh: e.matmul(PSY[:, 64 * h:64 * (h + 1)], lhsT=rh[b0:b0 + 64, j, :], rhs=stprev[b0:b0 + 64, p, :], start=False, stop=True), r=[*rhk, stpk], w=["psy"])
                        P.pe(lambda e, i=i, h=h, b0=b0, p=p, uk=uk: e.matmul(PSS[b0:b0 + 64, p * 64:(p + 1) * 64], lhsT=BE[:, 64 * h:64 * (h + 1)], rhs=uk[:, i, 0, :], start=True, stop=False), r=[K("BEb"), ukk], w=["pss"])
                        P.pe(lambda e, h=h, b0=b0, p=p, vh=vh: e.matmul(PSS[b0:b0 + 64, p * 64:(p + 1) * 64], lhsT=KE[:, 64 * h:64 * (h + 1)], rhs=vh, start=False, stop=first), r=[K("KEb"), K("VB")], w=["pss"])
                        if not first:
                            P.pe(lambda e, b0=b0, j=j, p=p, g=g: e.matmul(PSS[b0:b0 + 64, p * 64:(p + 1) * 64], lhsT=MM[g][b0:b0 + 64, j, :], rhs=stprev[b0:b0 + 64, p, :], start=False, stop=True), r=[f"MM{g}", stpk], w=["pss"])
                else:
                    for i in range(4):
                        h = 4 * g + i
                        p, h2 = h // 2, h % 2
                        yield
                        vh = VB[:, 64 * h:64 * (h + 1)]
                        if h2 == 0:
                            P.dma("sp", lambda e, p=p: e.dma_start(out=S0[:], in_=stw[:, 2 * p:2 * p + 2, :, :].rearrange("b h v k -> (h v) b k")), w=["PRa", "PRb"])
                            for q in range(4):
                                bk, bkk = nb()
                                for bb in range(4):
                                    b = q * 4 + bb
                                    P.pe(lambda e, b=b, bb=bb, bk=bk: e.transpose(bk[0:64, bb * 128:(bb + 1) * 128], S0[:, b, :], IDN[:]), r=["PRa", "PRb", "IDN"], w=[bkk])
                                P.act(lambda e, q=q, bk=bk: e.copy(out=S0T[:, 4 * q:4 * q + 4, :].rearrange("p b t -> p (b t)"), in_=bk[0:64, :]), r=[bkk], w=["EIN", "EINV", "EEX", "EEND"])
                        if h == 0:
                            P.pool(lambda e: e.memset(RHm[:], 0.0), w=["SH1", "SH2", "CT", "CT2"])
                        rflat = CONV4[0:64, :]
                        P.pool(lambda e, i=i, rh=rh: e.tensor_copy(out=rflat[:, 0:2040].rearrange("p (b x) -> p b x", x=136)[:, :, 0:8],
                                                                    in_=rh[:, i, 0:120].rearrange("p (b t) -> p b t", t=8)), r=rhk, w=["SH1", "SH2", "CT", "CT2"])
                        P.pool(lambda e, i=i, rh=rh: e.tensor_copy(out=rflat[:, 2040:2048], in_=rh[:, i, 120:128]), r=rhk, w=["SH1", "SH2", "CT", "CT2"])
                        bmb = BM[:].unsqueeze(2).broadcast_to([128, NSEQ, 64])
                        P.pool(lambda e, i=i, uk=uk: e.tensor_tensor(out=KHm[:], in0=uk[:, i, 1, :].unsqueeze(1).broadcast_to([128, NSEQ, 64]), in1=bmb, op=ALU.mult), r=[ukk, "BM"], w=["PRa", "PRb"])
                        P.dve(lambda e, i=i, uk=uk: e.tensor_tensor(out=UTm[:], in0=uk[:, i, 0, :].unsqueeze(1).broadcast_to([128, NSEQ, 64]), in1=bmb, op=ALU.mult), r=[ukk, "BM"], w=["PRa", "PRb"])
                        P.dve(lambda e, vh=vh: e.tensor_tensor(out=Vm[:], in0=vh.unsqueeze(1).broadcast_to([128, NSEQ, 64]), in1=bmb, op=ALU.mult), r=[K("VB"), "BM"], w=["PRa", "PRb"])
                        P.pool(lambda e, h=h: e.tensor_tensor(out=DPC[:], in0=IDN[0:64, 0:64].unsqueeze(1).broadcast_to([64, NSEQ, 64]),
                                                              in1=PCS[:, h, :].unsqueeze(2).broadcast_to([64, NSEQ, 64]), op=ALU.mult), r=["IDN", "PCS"], w=["XM"])
                        P.pe(lambda e, i=i, h=h, ga=ga, uk=uk: e.matmul(PSY[:, 64 * h:64 * (h + 1)], lhsT=ga[:, i, 1, :], rhs=uk[:, i, 0, :], start=True, stop=False), r=[f"GA{g}", ukk], w=["psy"])
                        P.pe(lambda e, i=i, h=h, gb=gb, vh=vh: e.matmul(PSY[:, 64 * h:64 * (h + 1)], lhsT=gb[:, i, 1, :], rhs=vh, start=False, stop=False), r=[f"GB{g}", K("VB")], w=["psy"])
                        for b in range(NSEQ):
                            P.pe(lambda e, b=b, h=h, h2=h2: e.matmul(PSY[:, 64 * h:64 * (h + 1)], lhsT=RHm[:, b, :], rhs=S0T[:, b, 64 * h2:64 * (h2 + 1)], start=False, stop=(b == NSEQ - 1)), r=["SH1", "SH2", "CT", "CT2", "EIN", "EINV", "EEX", "EEND"], w=["psy"])
                        for q in range(2):
                            bk, bkk = nb()
                            for bb in range(8):
                                b = q * 8 + bb
                                P.pe(lambda e, b=b, bb=bb, h=h, bk=bk: e.matmul(bk[0:64, bb * 64:(bb + 1) * 64], lhsT=KHm[:, b, :], rhs=BE[:, 64 * h:64 * (h + 1)], start=True, stop=True), r=["PRa", "PRb", K("BEb")], w=[bkk])
                            P.dve(lambda e, q=q, bk=bk: e.tensor_tensor(out=MS[:, 8 * q:8 * q + 8, :].rearrange("p b k -> p (b k)"), in0=bk[0:64, :],
                                                                        in1=DPC[:, 8 * q:8 * q + 8, :].rearrange("p b k -> p (b k)"), op=ALU.add), r=[bkk, "XM"], w=["KK0", "SQ"])
                        so, sok = DPC, "XM"
                        for q in range(2):
                            bk, bkk = nb()
                            for bb in range(8):
                                b = q * 8 + bb
                                o_ = bk[0:64, bb * 64:(bb + 1) * 64]
                                P.pe(lambda e, b=b, o_=o_, h2=h2: e.matmul(o_, lhsT=S0T[:, b, 64 * h2:64 * (h2 + 1)], rhs=MS[:, b, :], start=True, stop=False), r=["EIN", "EINV", "EEX", "EEND", "KK0", "SQ"], w=[bkk])
                                P.pe(lambda e, b=b, o_=o_, h=h: e.matmul(o_, lhsT=UTm[:, b, :], rhs=BE[:, 64 * h:64 * (h + 1)], start=False, stop=False), r=["PRa", "PRb", K("BEb")], w=[bkk])
                                P.pe(lambda e, b=b, o_=o_, h=h: e.matmul(o_, lhsT=Vm[:, b, :], rhs=KE[:, 64 * h:64 * (h + 1)], start=False, stop=True), r=["PRa", "PRb", K("KEb")], w=[bkk])
                            P.act(lambda e, q=q, bk=bk, so=so: e.copy(out=so[:, 8 * q:8 * q + 8, :].rearrange("p b k -> p (b k)"), in_=bk[0:64, :]), r=[bkk], w=[sok])
                        P.dma("sp", lambda e, h=h, so=so: e.dma_start(out=wkv_s[:, h, :, :].rearrange("b v k -> v b k"), in_=so[:]), r=[sok])
            if ty == "P" and GI_MODE != "none":
                gens = [mach(0), mach(1)]
                alive = [True, True]
                passed = [False, False]
                while any(alive) and not (GI_MODE == "td" and all(passed)):
                    for gi in range(2):
                        if alive[gi] and not (GI_MODE == "td" and passed[gi]):
                            try:
                                if next(gens[gi]) == "TD_DONE":
                                    passed[gi] = True
                            except StopIteration:
                                alive[gi] = False
                    yield
                for gi in range(2):
                    if alive[gi]:
                        for _ in gens[gi]:
                            yield
            else:
                for g in range(2):
                    for _ in mach(g):
                        yield
            yield
            P.act(lambda e: e.copy(out=YY[:], in_=PSY[:]), r=["psy"], w=["YY"])
            if ty == "P":
                stn, stnk = STt[par], f"ST{par}"
                P.act(lambda e, stn=stn: e.copy(out=stn[:].rearrange("p a v -> p (a v)"), in_=PSS[:, 0:256]), r=["pss"], w=[stnk])
                if lastp:
                    bk, bkk = nb()
                    for p in range(4):
                        P.pe(lambda e, p=p, bk=bk, stn=stn: e.transpose(bk[0:64, p * 128:(p + 1) * 128], stn[:, p, :], IDN[:]), r=[stnk, "IDN"], w=[bkk])
                    P.act(lambda e, bk=bk: e.copy(out=WPO[:].rearrange("p a t -> p (a t)"), in_=bk[0:64, :]), r=[bkk], w=["RH0", "RH1"])
                    P.dma("sp", lambda e: e.dma_start(out=wkv_p.rearrange("(p h2) v k -> v p h2 k", h2=2), in_=WPO[:].rearrange("v p (h2 k) -> v p h2 k", h2=2)), r=["RH0", "RH1"])
            yield
            b8 = lambda t: t[:].unsqueeze(2).broadcast_to([128, 8, 64])
            P.dve(lambda e: e.tensor_reduce(out=S8[:], in_=v3(YY[:]), axis=AX.X, op=ALU.add), r=["YY"], w=["S8c"])
            P.dve(lambda e: e.tensor_tensor(out=SQ[:], in0=YY[:], in1=YY[:], op=ALU.mult), r=["YY"], w=["GNS"])
            P.dve(lambda e: e.tensor_reduce(out=S8b[:], in_=v3(SQ[:]), axis=AX.X, op=ALU.add), r=["GNS"], w=["S8d"])
            P.dve(lambda e: e.tensor_scalar(out=S8[:], in0=S8[:], scalar1=1.0 / HS, scalar2=None, op0=ALU.mult), r=["S8c"], w=["S8c"])
            P.dve(lambda e: e.tensor_tensor(out=S8e[:], in0=S8[:], in1=S8[:], op=ALU.mult), r=["S8c"], w=["S8e"])
            P.dve(lambda e: e.scalar_tensor_tensor(out=S8b[:], in0=S8b[:], scalar=1.0 / HS, in1=S8e[:], op0=ALU.mult, op1=ALU.subtract), r=["S8d", "S8e"], w=["S8d"])
            P.dve(lambda e: e.tensor_scalar(out=S8b[:], in0=S8b[:], scalar1=64e-5, scalar2=None, op0=ALU.add), r=["S8d"], w=["S8d"])
            P.act(lambda e: e.activation(out=S8b[:], in_=S8b[:], func=AF.Sqrt), r=["S8d"], w=["S8d"])
            P.dve(lambda e: e.reciprocal(out=S8b[:], in_=S8b[:]), r=["S8d"], w=["S8d"])
            P.dve(lambda e: e.tensor_tensor(out=v3(YY[:]), in0=v3(YY[:]), in1=b8(S8), op=ALU.subtract), r=["YY", "S8c"], w=["YY"])
            P.dve(lambda e: e.tensor_tensor(out=v3(YY[:]), in0=v3(YY[:]), in1=b8(S8b), op=ALU.mult), r=["YY", "S8d"], w=["YY"])
            P.dve(lambda e: e.tensor_tensor(out=YY[:], in0=YY[:], in1=GG[:], op=ALU.mult), r=["YY", K("GG")], w=["YY"])
            P.dve(lambda e: e.tensor_tensor(out=YC[:, 512:1024], in0=YY[:], in1=BON[:], op=ALU.add), r=["YY", K("BON")], w=[K("YCb")])
            if "ycat" in dbg_out:
                dbgdump("ycat", YC[:], K("YCa")) if ti == dbg_ti[0] else None
            if STOP == 'H':
                return
            yield
            for k in range(8):
                P.pe(lambda e, k=k: e.transpose(PSB[:, k * 128:(k + 1) * 128], YC[:, k * 128:(k + 1) * 128], IDB[:]), r=[K("YCa"), K("YCb"), "IDB"], w=["psb"])
            P.act(lambda e: e.copy(out=YCT[:].rearrange("p k t -> p (k t)"), in_=PSB[:]), r=["psb"], w=["YCT"])
            for j in range(2):
                bk, bkk = nb()
                wr, wrk = wchunk(wbo, j * 512, 512)
                for k in range(8):
                    P.pe(lambda e, k=k, bk=bk, wr=wr: e.matmul(bk[:], lhsT=YCT[:, k, :], rhs=wr[:, k, :], start=(k == 0), stop=(k == 7)), r=["YCT", wrk], w=[bkk])
                P.dve(lambda e, j=j, bk=bk: e.tensor_tensor(out=X1T[:, j * 512:(j + 1) * 512], in0=bk[:], in1=x[:, j * 512:(j + 1) * 512], op=ALU.add), r=[bkk, xk], w=["X1T"])
            P.dma("sp", lambda e: e.dma_start(out=x1s[ti * 128:(ti + 1) * 128, :], in_=X1T[:]), r=["X1T"], w=["x1s"])
            yield

        def drain(gen):
            for _ in gen:
                pass

        def load_masks(names):
            for nm in names:
                P.dma("sp", lambda e, nm=nm: e.dma_start(out=MKT[nm][:], in_=cd[nm + "S"]), w=["M_" + nm])

        dbg_ti = [dbg.get("_ti", 0) if dbg else 0]
        PIPE = os.environ.get("KNOPIPE", "") != "1"
        if not PIPE:
            for ti in range(NT):
                if ti == NPT:
                    load_masks(("tri", "tgt", "ma", "mb", "mt"))
                drain(front(ti))
                drain(back(ti))
        else:
            drain(front(0))
            for ti in range(NT):
                if ti == NPT:
                    load_masks(("ma", "mb", "mt"))
                bgen = back(ti)
                fgen = None
                if ti + 1 < NT:
                    if ti + 1 == NPT:
                        load_masks(("tri", "tgt"))
                    fgen = front(ti + 1)
                bdone = fdone = False
                while not (bdone and (fgen is None or fdone)):
                    for _ in range(3):
                        if not bdone:
                            try:
                                next(bgen)
                            except StopIteration:
                                bdone = True
                    if fgen is not None and not fdone:
                        try:
                            next(fgen)
                        except StopIteration:
                            fdone = True

    if _dry:
        return wq
    P.barrier()

    if SKIP2:
        return nc, P.finalize()
    with ExitStack() as st2:
        def SB2(name, shape, dt=F32):
            return st2.enter_context(nc.sbuf_tensor(name, list(shape), dt))

        PS2 = [st2.enter_context(nc.psum_tensor(f"q{i}", [128, 512], F32)) for i in range(6)]
        PSB2 = st2.enter_context(nc.psum_tensor("qb", [128, 1024], BF16))
        bank2 = [0]

        def nb2():
            i = bank2[0] % 6
            bank2[0] += 1
            return PS2[i], f"q{i}"

        NSTG = 8
        RNG = 3
        W1R = [SB2(f"W1R{i}", [128, 8, 512], BF16) for i in range(RNG)]
        W2R = [SB2(f"W2R{i}", [128, 4, D], BF16) for i in range(RNG)]
        X1 = SB2("X1A", [128, NT, D])
        H2T = SB2("H2T", [128, 8, NT * 128], BF16)
        G2T = SB2("G2T", [128, 8])
        NFb = SB2("NFb", [128, D])
        IDB2 = SB2("IDB2", [128, 128], BF16)
        JK2 = SB2("JK2", [128, D])
        HB2 = SB2("HB2", [128, D], BF16)
        SS2 = SB2("SS2", [128, 1])
        RS2 = SB2("RS2", [128, 1])
        HR = [SB2(f"HR{i}", [128, 512]) for i in range(2)]
        HID = [SB2(f"HID{i}", [128, 4, 512], BF16) for i in range(2)]
        YO = [SB2(f"YO{i}", [128, D]) for i in range(2)]
        P.dma("sp", lambda e: e.dma_start(out=G2T[:], in_=g2T_d), w=["G2T"])
        P.dma("sp", lambda e: e.dma_start(out=NFb[:], in_=nf_d.partition_broadcast(128)), w=["NFb"])
        P.dma("pool", lambda e: e.dma_start(out=IDB2[:], in_=cd["ident"]), w=["IDB2"])

        def load_stage(s):
            rb = s % RNG
            P.dma("pool", lambda e: e.dma_start(out=W1R[rb][:], in_=w_ff1[:, s * 512:(s + 1) * 512].rearrange("(k p) f -> p k f", p=128)), w=[f"W1R{rb}"])
            P.dma("pool", lambda e: e.dma_start(out=W2R[rb][:], in_=w_ff2[s * 512:(s + 1) * 512, :].rearrange("(c p) d -> p c d", p=128)), w=[f"W2R{rb}"])

        for s in range(min(RNG, NSTG)):
            load_stage(s)
        for ti in range(NT):
            P.dma("sp", lambda e, ti=ti: e.dma_start(out=X1[:, ti, :], in_=x1s[ti * 128:(ti + 1) * 128, :]), w=[f"X1_{ti}"])
        def preamble(ti):
            xk = f"X1_{ti}"
            P.act(lambda e, ti=ti: e.activation(out=JK2[:], in_=X1[:, ti, :], func=AF.Square, accum_out=SS2[:]), r=[xk], w=["JK2", "SS2"])
            P.dve(lambda e: e.tensor_scalar(out=RS2[:], in0=SS2[:], scalar1=1.0 / D, scalar2=1e-6, op0=ALU.mult, op1=ALU.add), r=["SS2"], w=["RS2"])
            P.act(lambda e: e.activation(out=RS2[:], in_=RS2[:], func=AF.Sqrt), r=["RS2"], w=["RS2"])
            P.dve(lambda e: e.reciprocal(out=RS2[:], in_=RS2[:]), r=["RS2"], w=["RS2"])
            P.act(lambda e, ti=ti: e.activation(out=HB2[:], in_=X1[:, ti, :], func=AF.Copy, scale=RS2[:, 0:1]), r=[xk, "RS2"], w=["HB2"])
            for k in range(8):
                P.pe(lambda e, k=k: e.transpose(PSB2[:, k * 128:(k + 1) * 128], HB2[:, k * 128:(k + 1) * 128], IDB2[:]), r=["HB2", "IDB2"], w=["qb"])
            P.dve(lambda e, ti=ti: e.tensor_tensor(out=H2T[:, :, ti * 128:(ti + 1) * 128], in0=PSB2[:].rearrange("p (k t) -> p k t", t=128),
                                                  in1=G2T[:].unsqueeze(2).broadcast_to([128, 8, 128]), op=ALU.mult), r=["qb", "G2T"], w=[f"H2T_{ti}"])
        groups = []
        t0 = 0
        while t0 < NT:
            n = min(4, NT - t0)
            groups.append((t0, n))
            t0 += n
        def final_tile(ti):
            xk = f"X1_{ti}"
            yo, yok = YO[ti % 2], f"YO{ti % 2}"
            P.act(lambda e, ti=ti: e.activation(out=JK2[:], in_=X1[:, ti, :], func=AF.Square, accum_out=SS2[:]), r=[xk], w=["JK2", "SS2"])
            P.dve(lambda e: e.tensor_scalar(out=RS2[:], in0=SS2[:], scalar1=1.0 / D, scalar2=1e-6, op0=ALU.mult, op1=ALU.add), r=["SS2"], w=["RS2"])
            P.act(lambda e: e.activation(out=RS2[:], in_=RS2[:], func=AF.Sqrt), r=["RS2"], w=["RS2"])
            P.dve(lambda e: e.reciprocal(out=RS2[:], in_=RS2[:]), r=["RS2"], w=["RS2"])
            P.dve(lambda e, ti=ti, yo=yo: e.scalar_tensor_tensor(out=yo[:], in0=X1[:, ti, :], scalar=RS2[:, 0:1], in1=NFb[:], op0=ALU.mult, op1=ALU.mult), r=[xk, "RS2", "NFb"], w=[yok])
            dst = y_s if ti == NPT else y_p[ti * 128:(ti + 1) * 128, :]
            P.dma("sp", lambda e, yo=yo, dst=dst: e.dma_start(out=dst, in_=yo[:]), r=[yok])


        items = [(s_, t0, n) for s_ in range(NSTG) for (t0, n) in groups]

        def ffn1(k):
            s_, t0, n = items[k]
            rb = s_ % RNG
            ntok = n * 128
            hid, hidk = HID[k % 2], f"HID{k % 2}"
            for fc in range(4):
                bk, bkk = nb2()
                for kk in range(8):
                    P.pe(lambda e, kk=kk, fc=fc, bk=bk, t0=t0, ntok=ntok, rb=rb: e.matmul(bk[:, 0:ntok], lhsT=W1R[rb][:, kk, fc * 128:(fc + 1) * 128], rhs=H2T[:, kk, t0 * 128:t0 * 128 + ntok], start=(kk == 0), stop=(kk == 7)),
                         r=[f"W1R{rb}"] + [f"H2T_{t}" for t in range(t0, t0 + n)], w=[bkk], cost=0.22)
                hr, hrk = HR[fc % 2], f"HR{fc % 2}"
                P.act(lambda e, bk=bk, hr=hr, ntok=ntok: e.activation(out=hr[:, 0:ntok], in_=bk[:, 0:ntok], func=AF.Relu), r=[bkk], w=[hrk])
                P.pool(lambda e, hr=hr, hid=hid, fc=fc, ntok=ntok: e.tensor_tensor(out=hid[:, fc, 0:ntok], in0=hr[:, 0:ntok], in1=hr[:, 0:ntok], op=ALU.mult), r=[hrk], w=[hidk])

        def ffn2(k):
            s_, t0, n = items[k]
            rb = s_ % RNG
            hid, hidk = HID[k % 2], f"HID{k % 2}"
            for tl in range(n):
                ti = t0 + tl
                for half in range(2):
                    bk, bkk = nb2()
                    for fc in range(4):
                        P.pe(lambda e, fc=fc, bk=bk, tl=tl, half=half, hid=hid, rb=rb: e.matmul(bk[:], lhsT=hid[:, fc, tl * 128:(tl + 1) * 128], rhs=W2R[rb][:, fc, half * 512:(half + 1) * 512], start=(fc == 0), stop=(fc == 3)),
                             r=[hidk, f"W2R{rb}"], w=[bkk], cost=0.22)
                    P.dve(lambda e, bk=bk, ti=ti, half=half: e.tensor_tensor(out=X1[:, ti, half * 512:(half + 1) * 512], in0=bk[:], in1=X1[:, ti, half * 512:(half + 1) * 512], op=ALU.add),
                          r=[bkk, f"X1_{ti}"], w=[f"X1_{ti}"])
                if s_ == NSTG - 1:
                    final_tile(ti)

        FFNP = os.environ.get("KNOFFNP", "") != "1"

        def pre_group(k):
            s_, t0, n = items[k]
            if s_ == 0:
                for t in range(t0, t0 + n):
                    preamble(t)

        if FFNP:
            pre_group(0)
            ffn1(0)
        for k in range(len(items)):
            if FFNP:
                if k + 1 < len(items):
                    pre_group(k + 1)
                    ffn1(k + 1)
            else:
                pre_group(k)
                ffn1(k)
            ffn2(k)
            s_, t0, n = items[k]
            if (t0, n) == groups[-1] and s_ + RNG < NSTG:
                load_stage(s_ + RNG)
        if os.environ.get('KSCHED', '1') == '1':
            P.reschedule()
        stats = P.finalize()
    return nc, stats


_CACHE = {}


def make_in_maps(inputs, NPT=16, ncores=8):
    f = lambda a: np.ascontiguousarray(np.asarray(a, dtype=np.float32))
    c = _consts()
    shared = {
        "w_in": f(inputs["w_in"][0]), "w_out": f(inputs["w_out"][0]),
        "w_ff1": f(inputs["w_ff1"][0]), "w_ff2": f(inputs["w_ff2"][0]),
        "g1T": f(np.asarray(inputs["norm1_g"][0]).reshape(8, 128).T),
        "g2T": f(np.asarray(inputs["norm2_g"][0]).reshape(8, 128).T),
        "mu": f(inputs["mu"][0]).reshape(1, RC),
        "convw": f(inputs["conv_w"][0]).reshape(1, 3 * CW_),
        "k_k": f(inputs["k_k"][0]).reshape(1, RW), "k_a": f(inputs["k_a"][0]).reshape(1, RW),
        "r_k": f(inputs["r_k"][0]).reshape(1, RW),
        "lnx_g": f(inputs["lnx_g"][0]).reshape(1, RW), "lnx_b": f(inputs["lnx_b"][0]).reshape(1, RW),
        "normf": f(inputs["normf_g"]).reshape(1, D),
        "w0": f(inputs["w0"][0]).reshape(1, RW), "a0": f(inputs["a0"][0]).reshape(1, RW),
        "w2a": f(np.concatenate([np.asarray(inputs["w2"][0]), np.asarray(inputs["a2"][0])], 0)),
        "g2": f(inputs["g2"][0]),
    }
    for k, v in c.items():
        shared["c_" + k] = f(v)
    xpr = np.asarray(inputs["x_prompt"], dtype=np.float32)
    xsm = np.asarray(inputs["x_sample"], dtype=np.float32)
    maps = []
    for ci in range(ncores):
        m = dict(shared)
        m["xp"] = f(xpr[ci, :NPT * 128])
        sl = slice(ci * NSEQ, (ci + 1) * NSEQ)
        m["xs"] = f(xsm[sl].reshape(NSEQ * DT_, D))
        m["stc"] = f(inputs["state_conv"][0][sl])
        m["sts"] = f(inputs["state_shift"][0][sl])
        m["stw"] = f(inputs["state_wkv"][0][sl])
        maps.append(m)
    return maps


def gather(results, NPT=16, ncores=8):
    g = lambda name: [np.asarray(r[name], dtype=np.float32) for r in results]
    y_p = np.stack(g("y_p"), 0)
    y_s = np.stack(g("y_s"), 0).reshape(ncores * NSEQ, DT_, D)
    conv_p = np.stack(g("conv_p"), 0)[None]
    shift_p = np.concatenate(g("shift_p"), 0)[None]
    wkv_p = np.stack(g("wkv_p"), 0)[None]
    conv_s = np.concatenate(g("conv_s"), 0)[None]
    shift_s = np.concatenate(g("shift_s"), 0)[None]
    wkv_s = np.concatenate(g("wkv_s"), 0)[None]
    return (y_p, y_s, conv_p, shift_p, wkv_p, conv_s, shift_s, wkv_s)


def kernel(**inputs):
    NPT = 16
    if "nc" not in _CACHE:
        _CACHE["nc"] = build(NPT)[0]
    nc = _CACHE["nc"]
    maps = make_in_maps(inputs, NPT, 8)
    res = run_bass_kernel_spmd(nc, maps, core_ids=list(range(8)))
    return gather(res.results, NPT, 8)
```

```python
import os
import numpy as np
import concourse.bass as bass
import concourse.mybir as mybir
from concourse.bass_utils import run_bass_kernel_spmd

F32 = mybir.dt.float32
BF16 = mybir.dt.bfloat16
ALU = mybir.AluOpType
AF = mybir.ActivationFunctionType
AX = mybir.AxisListType

D = 1024
CW_ = 512
RW = 512
NH = 8
HS = 64
RC = 1792
IC = 3328
DFF = 4096
NSEQ = 16
DT_ = 8
CDEC = float(np.exp(-0.5))


class Op:
    __slots__ = ("eng", "emit", "deps", "dma", "inc", "done", "dom", "alldeps", "cost", "idx", "fin", "succ", "npend", "prio")

    def __init__(self, eng, emit, dma):
        self.eng = eng
        self.emit = emit
        self.dma = dma
        self.deps = []
        self.alldeps = []
        self.cost = 0.0
        self.inc = False
        self.done = None
        self.dom = (eng + "_dma") if dma else eng


class Prog:
    NSLOT = {"sp": 24, "pool": 6, "act": 4}

    def __init__(self, nc, self_sync=False):
        self.nc = nc
        self.ops = []
        self.lastw = {}
        self.readers = {}
        self.self_sync = self_sync
        self.ndma = {}
        self.lastslot = {}

    COST = {"pe": 0.12, "act": 0.6, "dve": 0.7, "pool": 1.3}

    def op(self, eng, emit, reads=(), writes=(), dma=False, cost=None):
        o = Op(eng, emit, dma)
        o.cost = cost if cost is not None else (3.0 if dma else self.COST[eng])
        deps = {}
        if dma:
            n = self.ndma.get(eng, 0)
            self.ndma[eng] = n + 1
            o.dom = f"{eng}_dma{n % self.NSLOT[eng]}"
            prev = self.lastslot.get(o.dom)
            if prev is not None:
                deps[id(prev)] = (prev, True)
            self.lastslot[o.dom] = o
        for k in reads:
            w = self.lastw.get(k)
            if w is not None:
                deps[id(w)] = (w, True)
        for k in writes:
            w = self.lastw.get(k)
            if w is not None:
                deps[id(w)] = (w, True)
            for r in self.readers.get(k, ()):
                if id(r) not in deps:
                    deps[id(r)] = (r, False)
        for k in reads:
            self.readers.setdefault(k, []).append(o)
        for k in writes:
            self.lastw[k] = o
            self.readers[k] = []
        for d, hard in deps.values():
            if d is o:
                continue
            o.alldeps.append(d)
            if d.dom == o.dom and not o.dma:
                if o.eng != "pe":
                    o.deps.append(d)
                continue
            o.deps.append(d)
        self.ops.append(o)
        return o

    def pe(self, emit, r=(), w=(), cost=None):
        return self.op("pe", emit, r, w, cost=cost)

    def act(self, emit, r=(), w=(), cost=None):
        return self.op("act", emit, r, w, cost=cost)

    def dve(self, emit, r=(), w=(), cost=None):
        return self.op("dve", emit, r, w, cost=cost)

    def pool(self, emit, r=(), w=(), cost=None):
        return self.op("pool", emit, r, w, cost=cost)

    def dma(self, eng, emit, r=(), w=(), cost=None):
        return self.op(eng, emit, r, w, dma=True, cost=cost)

    def barrier(self):
        lastdom = {}
        for o in self.ops:
            if o.emit is not None:
                lastdom[o.dom] = o
        for eng in ("pe", "act", "dve", "pool", "sp"):
            b = Op(eng, None, False)
            b.deps = [o for dom, o in lastdom.items() if dom != eng]
            self.ops.append(b)
        self.lastw = {}
        self.readers = {}

    def reschedule(self, hop=float(os.environ.get("KHOP", "0.15"))):
        segs, cur = [], []
        for o in self.ops:
            if o.emit is None:
                segs.append(cur)
                segs.append([o])
                cur = []
            else:
                cur.append(o)
        segs.append(cur)
        new_ops = []
        for seg in segs:
            if len(seg) <= 1 or seg[0].emit is None:
                new_ops.extend(seg)
                continue
            inseg = {id(o) for o in seg}
            for o in seg:
                o.succ = []
                o.fin = 0.0
            fixed = set(os.environ.get("KFIXED", "pe").split(","))
            lastfixed = {}
            for o in seg:
                o.npend = 0
                for d in o.alldeps:
                    if id(d) in inseg:
                        d.succ.append(o)
                        o.npend += 1
                if o.eng in fixed and not o.dma:
                    p_ = lastfixed.get(o.eng)
                    if p_ is not None and all(p_ is not d for d in o.alldeps):
                        p_.succ.append(o)
                        o.npend += 1
                    lastfixed[o.eng] = o
            for o in reversed(seg):
                o.prio = o.cost + max([x.prio + (hop if x.eng != o.eng else 0.0) for x in o.succ], default=0.0)
            ready = {}
            for o in seg:
                if o.npend == 0:
                    ready.setdefault(o.eng, []).append(o)
            cursor = {}
            order = []
            nleft = len(seg)
            while nleft:
                best = None
                for eng, lst in ready.items():
                    if not lst:
                        continue
                    cur_t = cursor.get(eng, 0.0)
                    for o in lst:
                        est = cur_t
                        for d in o.alldeps:
                            if id(d) in inseg:
                                t = d.fin + (hop if d.eng != o.eng or d.dma != o.dma else 0.0)
                                if t > est:
                                    est = t
                        key = (est, -o.prio)
                        if best is None or key < best[0]:
                            best = (key, o, est)
                _, o, est = best
                ready[o.eng].remove(o)
                o.fin = est + o.cost
                cursor[o.eng] = (est + 0.1) if o.dma else o.fin
                order.append(o)
                nleft -= 1
                for x in o.succ:
                    x.npend -= 1
                    if x.npend == 0:
                        ready.setdefault(x.eng, []).append(x)
            new_ops.extend(order)
        self.ops = new_ops

    def finalize(self, final_eng="sp"):
        nc = self.nc
        ops = self.ops
        fin = Op(final_eng, None, False)
        lastdom = {}
        for o in ops:
            if o.emit is not None:
                lastdom[o.dom] = o
        for dom, o in lastdom.items():
            if "_dma" in dom:
                fin.deps.append(o)
        ops = ops + [fin]
        for o in ops:
            if o.dma:
                o.inc = True
            for d in o.deps:
                d.inc = True
        cnt = {}
        for o in ops:
            if o.inc:
                cnt[o.dom] = cnt.get(o.dom, 0) + (16 if o.dma else 1)
                o.done = cnt[o.dom]
        doms = sorted({o.dom for o in ops if o.inc})
        sems = {d: nc.alloc_semaphore(name="s_" + d) for d in doms}
        per_eng = {}
        for o in ops:
            per_eng.setdefault(o.eng, []).append(o)
        engs = {"pe": "tensor", "act": "scalar", "dve": "vector", "pool": "gpsimd", "sp": "sync"}

        def run(engname, e):
            waited = {}
            for o in per_eng.get(engname, []):
                need = {}
                for d in o.deps:
                    if d.done > need.get(d.dom, 0):
                        need[d.dom] = d.done
                for dom, v in need.items():
                    if v > waited.get(dom, 0):
                        e.wait_ge(sems[dom], v)
                        waited[dom] = v
                if o.emit is None:
                    continue
                ins = o.emit(e)
                if o.inc:
                    ins.then_inc(sems[o.dom], 16 if o.dma else 1)

        with nc.Block() as block:
            for engname, attr in engs.items():
                if engname not in per_eng:
                    continue

                def mk(engname=engname):
                    def f(e):
                        run(engname, e)
                    return f
                getattr(block, attr)(mk())
        return dict(nops=len(ops), cnt=cnt)


def _consts():
    c = {}
    idx = np.arange(128)
    c["ident"] = np.eye(128, dtype=np.float32)
    c["id2"] = (idx[:, None] % 64 == np.arange(64)[None, :]).astype(np.float32)
    for ty in ("P", "S"):
        if ty == "P":
            blk = np.zeros(128, np.int64)
        else:
            blk = idx // DT_
        same = blk[:, None] == blk[None, :]
        s = idx[:, None]
        t = idx[None, :]
        lt = (same & (s < t)).astype(np.float32)
        le = (same & (s <= t)).astype(np.float32)
        gt = (same & (s > t)).astype(np.float32)
        c["tri" + ty] = (-CDEC) * le
        c["tgt" + ty] = (-CDEC) * gt
        c["ma" + ty] = np.concatenate([-lt, le, -lt, le], 1)
        c["mb" + ty] = np.concatenate([lt, le, lt, le], 1)
        c["mt" + ty] = np.concatenate([-gt] * 4, 1)
    c["segP"] = np.full((128, 1), -CDEC, np.float32)
    bm = (idx[:, None] // DT_ == np.arange(NSEQ)[None, :]).astype(np.float32)
    c["segS"] = (-CDEC) * bm
    c["bm"] = bm
    return c


CONST_SHAPES = {k: v.shape for k, v in _consts().items()}


def build(NPT=16, dbg=None, _dry=False, _order=None):
    if not _dry and _order is None:
        _order = build(NPT, None, _dry=True)
    STOP = os.environ.get('KSTOP', '')
    GI_MODE = os.environ.get('KGI', 'td')
    SKIP2 = os.environ.get('KSKIP2', '') == '1'
    NT = NPT + 1
    nc = bass.Bass("TRN2", target_bir_lowering=False)
    P = Prog(nc)
    din = {}

    def DI(name, shape):
        din[name] = nc.dram_tensor(name, list(shape), F32, kind="ExternalInput").ap()
        return din[name]

    def DO(name, shape):
        return nc.dram_tensor(name, list(shape), F32, kind="ExternalOutput").ap()

    xp = DI("xp", [NPT * 128, D])
    xs = DI("xs", [128, D])
    stc = DI("stc", [NSEQ, 2, CW_])
    sts = DI("sts", [NSEQ, RC])
    stw = DI("stw", [NSEQ, NH, HS, HS])
    w_in = DI("w_in", [D, IC])
    w_out = DI("w_out", [D, D])
    w_ff1 = DI("w_ff1", [D, DFF])
    w_ff2 = DI("w_ff2", [DFF, D])
    g1T_d = DI("g1T", [128, 8])
    g2T_d = DI("g2T", [128, 8])
    mu_d = DI("mu", [1, RC])
    cw_d = DI("convw", [1, 3 * CW_])
    kk_d = DI("k_k", [1, RW])
    ka_d = DI("k_a", [1, RW])
    rk_d = DI("r_k", [1, RW])
    lg_d = DI("lnx_g", [1, RW])
    lb_d = DI("lnx_b", [1, RW])
    nf_d = DI("normf", [1, D])
    w0_d = DI("w0", [1, RW])
    a0_d = DI("a0", [1, RW])
    w2a_d = DI("w2a", [128, RW])
    g2_d = DI("g2", [128, RW])
    cd = {k: DI("c_" + k, shp) for k, shp in CONST_SHAPES.items()}

    y_p = DO("y_p", [NPT * 128, D])
    y_s = DO("y_s", [128, D])
    conv_p = DO("conv_p", [2, CW_])
    shift_p = DO("shift_p", [1, RC])
    wkv_p = DO("wkv_p", [NH, HS, HS])
    conv_s = DO("conv_s", [NSEQ, 2, CW_])
    shift_s = DO("shift_s", [NSEQ, RC])
    wkv_s = DO("wkv_s", [NSEQ, NH, HS, HS])
    x1s = nc.dram_tensor("x1s", [NT * 128, D], F32).ap()
    wbi = nc.dram_tensor("wbi", [7, 128, 8, 512], BF16).ap()
    wbo = nc.dram_tensor("wbo", [2, 128, 8, 512], BF16).ap()
    bnd_p = nc.dram_tensor("bnd_p", [NT, RC], F32).ap()
    bnd_c = nc.dram_tensor("bnd_c", [NT, 2, CW_], F32).ap()
    dbg_out = {}
    if dbg:
        for name, shape in dbg.items():
            if not name.startswith("_"):
                dbg_out[name] = DO("dbg_" + name, shape)

    from contextlib import ExitStack

    with ExitStack() as st0:
        TW = [st0.enter_context(nc.sbuf_tensor(f"TW{i}", [128, IC], BF16)) for i in range(2)]
        n0 = 0
        for (src_w, dst_w, wd) in ((w_in, wbi, IC), (w_out, wbo, D)):
            for k in range(8):
                tw, twk = TW[n0 % 2], f"TW{n0 % 2}"
                n0 += 1
                P.dma("pool", lambda e, tw=tw, src_w=src_w, k=k, wd=wd: e.dma_start(out=tw[:, 0:wd], in_=src_w[k * 128:(k + 1) * 128, :]), w=[twk])
                nfull = wd // 512
                P.dma("sp", lambda e, tw=tw, dst_w=dst_w, k=k, nfull=nfull: e.dma_start(out=dst_w[0:nfull, :, k, :].rearrange("j p n -> p j n"),
                                                                                  in_=tw[:, 0:nfull * 512].rearrange("p (j n) -> p j n", n=512)), r=[twk], w=["wb"])
                if wd % 512:
                    P.dma("sp", lambda e, tw=tw, dst_w=dst_w, k=k, nfull=nfull, wd=wd: e.dma_start(out=dst_w[nfull, :, k, 0:wd - nfull * 512], in_=tw[:, nfull * 512:wd]), r=[twk], w=["wb"])
    P.barrier()

    with ExitStack() as st1:
        def SB(name, shape, dt=F32):
            return st1.enter_context(nc.sbuf_tensor(name, list(shape), dt))

        def PSF(name, shape, dt=F32):
            return st1.enter_context(nc.psum_tensor(name, list(shape), dt))

        NB = 5
        PS = [PSF(f"ps{i}", [128, 512]) for i in range(NB)]
        PSY = PSF("psy", [128, 512])
        PSS = PSF("pss", [128, 512])
        PSB = PSF("psb", [128, 1024], BF16)
        bank_i = [0]

        def nb():
            i = bank_i[0] % NB
            bank_i[0] += 1
            return PS[i], f"ps{i}"

        NRING = 3
        WR = [SB(f"WR{i}", [128, 8, 512], BF16) for i in range(NRING)]
        wq = list(_order) if _order is not None else []
        wsrc = {"wbi": wbi, "wbo": wbo}
        wstate = {"issued": 0, "used": 0}

        def wchunk(srcw, c0, wd):
            name = "wbi" if srcw is wbi else "wbo"
            n = wstate["used"]
            wstate["used"] += 1
            if _dry:
                wq.append((name, c0, wd))
                return WR[n % NRING], f"WR{n % NRING}"
            assert wq[n] == (name, c0, wd), (n, wq[n], name, c0, wd)
            while wstate["issued"] < min(len(wq), n + NRING):
                m = wstate["issued"]
                nm2, c2, wd2 = wq[m]
                P.dma("sp", lambda e, m=m, nm2=nm2, c2=c2, wd2=wd2: e.dma_start(out=WR[m % NRING][:, :, 0:wd2], in_=wsrc[nm2][c2 // 512, :, :, 0:wd2]),
                      w=[f"WR{m % NRING}"], cost=6.0)
                wstate["issued"] += 1
            return WR[n % NRING], f"WR{n % NRING}"

        def bc_tile(name, dram, n):
            t = SB(name, [128, n])
            P.dma("sp", lambda e: e.dma_start(out=t[:], in_=dram.partition_broadcast(128)), w=[name])
            return t

        def ld_tile(name, dram, shape, dt=F32, eng="sp"):
            t = SB(name, shape, dt)
            P.dma(eng, lambda e: e.dma_start(out=t[:], in_=dram), w=[name])
            return t

        IDN = ld_tile("IDN", cd["ident"], [128, 128])
        IDB = ld_tile("IDB", cd["ident"], [128, 128], BF16, eng="pool")
        ID2 = ld_tile("ID2", cd["id2"], [128, 64])
        G1T = ld_tile("G1T", g1T_d, [128, 8])
        MU = bc_tile("MU", mu_d, RC)
        CWt = bc_tile("CWt", cw_d, 3 * CW_)
        KKb = bc_tile("KKb", kk_d, RW)
        KAb = bc_tile("KAb", ka_d, RW)
        RKb = bc_tile("RKb", rk_d, RW)
        LGb = bc_tile("LGb", lg_d, RW)
        LBb = bc_tile("LBb", lb_d, RW)
        W0r = ld_tile("W0r", w0_d, [1, RW])
        A0r = ld_tile("A0r", a0_d, [1, RW])
        W2A = ld_tile("W2A", w2a_d, [128, RW])
        G2 = ld_tile("G2", g2_d, [128, RW])
        ONE1 = SB("ONE1", [1, 128])
        P.pool(lambda e: e.memset(ONE1[:], 1.0), w=["ONE1"])
        MKT = {}
        for nm in ("tri", "tgt"):
            MKT[nm] = ld_tile("M_" + nm, cd[nm + "P"], [128, 128])
        for nm in ("ma", "mb", "mt"):
            MKT[nm] = ld_tile("M_" + nm, cd[nm + "P"], [128, 512])
        MK = {}
        for ty in ("P", "S"):
            for nm in ("tri", "tgt", "ma", "mb", "mt"):
                MK[nm + ty] = MKT[nm]
        SEGP = ld_tile("SEGP", cd["segP"], [128, 1])
        SEGS = ld_tile("SEGS", cd["segS"], [128, NSEQ])
        BM = ld_tile("BM", cd["bm"], [128, NSEQ])

        X = [SB(f"X{i}", [128, D]) for i in range(2)]
        SS = SB("SS", [128, 1])
        RS = SB("RS", [128, 1])
        HB = SB("HB", [128, D], BF16)
        HT = SB("HT", [128, 8, 128], BF16)
        PR0 = SB("PR0", [128, IC])
        PR = [PR0, PR0]
        CX0 = SB("CX0", [128, CW_])
        CX = [CX0, CX0]
        CONV4 = SB("CONV4", [128, 4 * CW_])
        SH1, SH2, CT, CT2 = (CONV4[:, i * 512:(i + 1) * 512] for i in range(4))
        YCs = [SB(f"YC{i}", [128, D], BF16) for i in range(2)]
        X1T = SB("X1T", [128, D])
        JUNK = CONV4[:, 0:1024]
        YCTb = SB("YCTb", [128, 8, 128], BF16)
        PV = SB("PV", [128, RC])
        XM = PV
        LOR = SB("LOR", [128, 256])
        LT = SB("LT", [128, 2, 128])
        SG = SB("SG", [128, RW])
        YYb = SB("YYb", [128, RW])
        GNS = SB("GNS", [128, RW])
        S8c = SB("S8c", [128, 8])
        S8d = SB("S8d", [128, 8])
        S8e = SB("S8e", [128, 8])
        AA = SB("AA", [128, RW])
        GGs = [SB(f"GG{i}", [128, RW]) for i in range(2)]
        EXP4 = SB("EXP4", [128, 4 * RW])
        EIN, EINV, EEX, EEND = (EXP4[:, i * 512:(i + 1) * 512] for i in range(4))
        BTm = SB("BTm", [128, RW], BF16)
        KTl = SB("KTl", [128, RW], BF16)
        KS2 = SB("KS2", [128, 2 * RW])
        KK0, SQ = KS2[:, 0:512], KS2[:, 512:1024]
        S8 = SB("S8", [128, 8])
        S8b = SB("S8b", [128, 8])
        BB = SB("BB", [128, RW])
        KP = SB("KP", [128, RW])
        BEs = [SB(f"BEb{i}", [128, RW], BF16) for i in range(2)]
        KEs = [SB(f"KEb{i}", [128, RW], BF16) for i in range(2)]
        VBs = [SB(f"VB{i}", [128, RW], BF16) for i in range(2)]
        BONs = [SB(f"BON{i}", [128, RW]) for i in range(2)]
        RTms = [SB(f"RTm{i}", [128, RW], BF16) for i in range(2)]
        KTMs = [SB(f"KTM{i}", [128, RW], BF16) for i in range(2)]
        KRs = [SB(f"KR{i}", [128, 4, 2, 128], BF16) for i in range(2)]
        BFs = [SB(f"BF{i}", [128, 4, 128], BF16) for i in range(2)]
        KFs = [SB(f"KF{i}", [128, 4, 128], BF16) for i in range(2)]
        PCs = [SB(f"PC{i}", [128, 4]) for i in range(2)]
        PCS = SB("PCS", [64, NH, NSEQ])
        GA = [SB(f"GA{g}", [128, 4, 2, 128], BF16) for g in range(2)]
        GB = [SB(f"GB{g}", [128, 4, 2, 128], BF16) for g in range(2)]
        GT = [SB(f"GT{g}", [128, 4, 128], BF16) for g in range(2)]
        RN = [[SB(f"RN{g}{i}", [128, 4, 128], BF16) for i in range(2)] for g in range(2)]
        RTN = [[SB(f"RTN{g}{i}", [128, 4, 128], BF16) for i in range(2)] for g in range(2)]
        TT = [[SB(f"TT{g}{i}", [128, 4, 128], BF16) for i in range(2)] for g in range(2)]
        X1N = [SB(f"X1N{g}", [128, 4, 64], BF16) for g in range(2)]
        UK = [SB(f"UK{g}", [128, 4, 2, 64], BF16) for g in range(2)]
        RHb = SB("RHb", [128, 2, 2, 128])
        RH = [RHb[:, 0, :, :], RHb[:, 1, :, :]]
        RHS = RHb[0:64, :, :, :].rearrange("p a b t -> p (a b) t")
        MM = [SB(f"MM{g}", [128, 2, 64]) for g in range(2)]
        STt = [SB(f"ST{i}", [128, 4, 64]) for i in range(2)]
        S0 = PR0[:, 0:1024].rearrange("p (b k) -> p b k", k=64)
        S0T = EXP4[0:64, :].rearrange("p (b t) -> p b t", t=128)
        RHm = CONV4[0:64, :].rearrange("p (b t) -> p b t", t=128)
        KHm = PR0[:, 1024:1536].bitcast(BF16).rearrange("p (b k) -> p b k", k=64)
        UTm = PR0[:, 1536:2048].bitcast(BF16).rearrange("p (b k) -> p b k", k=64)
        Vm = PR0[:, 2048:2560].bitcast(BF16).rearrange("p (b k) -> p b k", k=64)
        DPC = PV[0:64, 0:1024].rearrange("p (b k) -> p b k", k=64)
        MS = KS2[0:64, :].rearrange("p (b k) -> p b k", k=64)
        WPO = RHb[0:64, :, :, :].rearrange("p a b t -> p (a b) t")
        SO = [DPC, DPC]

        def v3(ap, k=64):
            return ap.rearrange("p (h k) -> p h k", k=k)

        def dbgdump(name, ap, key):
            if name in dbg_out:
                P.dma("sp", lambda e: e.dma_start(out=dbg_out[name], in_=ap), r=[key])

        def front(ti):
            ty = "S" if ti == NPT else "P"
            first = (ti == 0)
            lastp = (ti == NPT - 1)
            par = ti % 2
            K = lambda n: f"{n}#{par}"
            x = X[par]
            xk = f"X{par}"
            YC, GG, BON, KR, BF, KF, VB = YCs[par], GGs[par], BONs[par], KRs[par], BFs[par], KFs[par], VBs[par]
            KTM, RTm, BE, KE, PC = KTMs[par], RTms[par], BEs[par], KEs[par], PCs[par]
            src = xs if ty == "S" else xp[ti * 128:(ti + 1) * 128, :]
            P.dma("sp", lambda e: e.dma_start(out=x[:], in_=src), w=[xk])
            P.act(lambda e: e.activation(out=JUNK[:], in_=x[:], func=AF.Square, accum_out=SS[:]), r=[xk], w=["SH1", "SH2", "SS"])
            P.dve(lambda e: e.tensor_scalar(out=RS[:], in0=SS[:], scalar1=1.0 / D, scalar2=1e-6, op0=ALU.mult, op1=ALU.add), r=["SS"], w=["RS"])
            P.act(lambda e: e.activation(out=RS[:], in_=RS[:], func=AF.Sqrt), r=["RS"], w=["RS"])
            P.dve(lambda e: e.reciprocal(out=RS[:], in_=RS[:]), r=["RS"], w=["RS"])
            P.act(lambda e: e.activation(out=HB[:], in_=x[:], func=AF.Copy, scale=RS[:, 0:1]), r=[xk, "RS"], w=["HB"], cost=1.2)
            for k in range(8):
                P.pe(lambda e, k=k: e.transpose(PSB[:, k * 128:(k + 1) * 128], HB[:, k * 128:(k + 1) * 128], IDB[:]), r=["HB", "IDB"], w=["psb"])
            P.dve(lambda e: e.tensor_tensor(out=HT[:], in0=PSB[:].rearrange("p (k t) -> p k t", t=128),
                                            in1=G1T[:].unsqueeze(2).broadcast_to([128, 8, 128]), op=ALU.mult), r=["psb", "G1T"], w=["HT"])
            pr = PR[par]

            def proj_chunks(js):
                for j in js:
                    wd = 512 if j < 6 else 256
                    bk, bkk = nb()
                    wr, wrk = wchunk(wbi, j * 512, wd)
                    for k in range(8):
                        P.pe(lambda e, k=k, wd=wd, bk=bk, wr=wr: e.matmul(bk[:, 0:wd], lhsT=HT[:, k, :], rhs=wr[:, k, 0:wd], start=(k == 0), stop=(k == 7)),
                             r=["HT", wrk], w=[bkk], cost=0.22)
                    P.act(lambda e, j=j, wd=wd, bk=bk: e.copy(out=pr[:, j * 512:j * 512 + wd], in_=bk[:, 0:wd]), r=[bkk], w=["PRb" if j >= 3 else "PRa"])
                    yield

            yield from proj_chunks((3, 4, 5, 6))
            yield
            prw = pr[:, 1536:IC]
            P.dma("sp", lambda e: e.dma_start(out=PV[1:113, :], in_=pr[0:112, 1536:IC]), r=["PRb"], w=["XM"], cost=5.0)
            P.dma("sp", lambda e: e.dma_start(out=PV[113:128, :], in_=pr[112:127, 1536:IC]), r=["PRb"], w=["XM"])
            if ty == "S":
                P.dma("sp", lambda e: e.dma_start(out=PV[0:128:8, :], in_=sts), w=["XM"])
            elif first:
                P.pool(lambda e: e.memset(PV[0:1, :], 0.0), w=["XM"])
            else:
                P.dma("sp", lambda e: e.dma_start(out=PV[0:1, :], in_=bnd_p[ti - 1:ti, :]), r=["bnd"], w=["XM"])
            P.dve(lambda e: e.tensor_tensor(out=PV[:], in0=PV[:], in1=prw, op=ALU.subtract), r=["XM", "PRb"], w=["XM"], cost=2.0)
            P.dve(lambda e: e.tensor_tensor(out=PV[:], in0=PV[:], in1=MU[:], op=ALU.mult), r=["XM", "MU"], w=["XM"], cost=2.0)
            P.dve(lambda e: e.tensor_tensor(out=XM[:], in0=PV[:], in1=prw, op=ALU.add), r=["XM", "PRb"], w=["XM"], cost=2.0)
            r_ = XM[:, 0:512]
            k_ = XM[:, 512:1024]
            v_ = XM[:, 1024:1536]
            if STOP == 'B':
                return
            if ty == "S":
                P.dma("sp", lambda e: e.dma_start(out=shift_s, in_=pr[7:128:8, 1536:IC]), r=["PRb"])
            else:
                P.dma("sp", lambda e: e.dma_start(out=bnd_p[ti:ti + 1, :], in_=pr[127:128, 1536:IC]), r=["PRb"], w=["bnd"])
            if lastp:
                P.dma("sp", lambda e: e.dma_start(out=shift_p, in_=pr[127:128, 1536:IC]), r=["PRb"])
            yield from proj_chunks((0, 1, 2))
            yield
            cx = CX[par]
            cxk = "CX"
            P.pool(lambda e: e.tensor_tensor(out=cx[:], in0=pr[:, 512:1024], in1=pr[:, 1024:1536], op=ALU.mult), r=["PRa"], w=[cxk])
            P.dma("sp", lambda e: e.dma_start(out=SH1[1:113, :], in_=cx[0:112, :]), r=[cxk], w=["SH1"])
            P.dma("sp", lambda e: e.dma_start(out=SH1[113:128, :], in_=cx[112:127, :]), r=[cxk], w=["SH1"])
            P.dma("sp", lambda e: e.dma_start(out=SH2[2:114, :], in_=cx[0:112, :]), r=[cxk], w=["SH2"])
            P.dma("sp", lambda e: e.dma_start(out=SH2[114:128, :], in_=cx[112:126, :]), r=[cxk], w=["SH2"])
            if ty == "S":
                P.dma("sp", lambda e: e.dma_start(out=SH1[0:128:8, :], in_=stc[:, 1, :]), w=["SH1"])
                P.dma("sp", lambda e: e.dma_start(out=SH2[0:128:8, :], in_=stc[:, 0, :]), w=["SH2"])
                P.dma("sp", lambda e: e.dma_start(out=SH2[1:128:8, :], in_=stc[:, 1, :]), w=["SH2"])
            elif first:
                P.pool(lambda e: e.memset(SH1[0:1, :], 0.0), w=["SH1"])
                P.pool(lambda e: e.memset(SH2[0:2, :], 0.0), w=["SH2"])
            else:
                P.dma("sp", lambda e: e.dma_start(out=SH1[0:1, :], in_=bnd_c[ti - 1, 1:2, :]), r=["bnd"], w=["SH1"])
                P.dma("sp", lambda e: e.dma_start(out=SH2[0:2, :], in_=bnd_c[ti - 1, :, :]), r=["bnd"], w=["SH2"])
            P.pool(lambda e: e.tensor_tensor(out=CT[:], in0=SH2[:], in1=CWt[:, 0:512], op=ALU.mult), r=["SH2", "CWt"], w=["CT"])
            P.pool(lambda e: e.tensor_tensor(out=CT2[:], in0=SH1[:], in1=CWt[:, 512:1024], op=ALU.mult), r=["SH1", "CWt"], w=["CT2"])
            P.pool(lambda e: e.tensor_tensor(out=CT[:], in0=CT[:], in1=CT2[:], op=ALU.add), r=["CT", "CT2"], w=["CT"])
            P.pool(lambda e: e.tensor_tensor(out=CT2[:], in0=cx[:], in1=CWt[:, 1024:1536], op=ALU.mult), r=[cxk, "CWt"], w=["CT2"])
            P.pool(lambda e: e.tensor_tensor(out=CT[:], in0=CT[:], in1=CT2[:], op=ALU.add), r=["CT", "CT2"], w=["CT"])
            P.pool(lambda e: e.tensor_tensor(out=YC[:, 0:512], in0=CT[:], in1=pr[:, 0:512], op=ALU.mult), r=["CT", "PRa"], w=[K("YCa")])
            if ty == "S":
                P.dma("sp", lambda e: e.dma_start(out=conv_s[:, 0, :], in_=cx[6:128:8, :]), r=[cxk])
                P.dma("sp", lambda e: e.dma_start(out=conv_s[:, 1, :], in_=cx[7:128:8, :]), r=[cxk])
            else:
                P.dma("sp", lambda e: e.dma_start(out=bnd_c[ti, :, :], in_=cx[126:128, :]), r=[cxk], w=["bnd"])
            if lastp:
                P.dma("sp", lambda e: e.dma_start(out=conv_p, in_=cx[126:128, :]), r=[cxk])
            yield
            P.act(lambda e: e.activation(out=LOR[:, 0:64], in_=XM[:, 1536:1600], func=AF.Tanh), r=["XM"], w=["LOR"])
            P.act(lambda e: e.copy(out=LOR[:, 64:128], in_=XM[:, 1600:1664]), r=["XM"], w=["LOR"])
            P.act(lambda e: e.activation(out=LOR[:, 128:256], in_=XM[:, 1664:1792], func=AF.Sigmoid), r=["XM"], w=["LOR"])
            bk, bkk = nb()
            for i in range(2):
                P.pe(lambda e, i=i, bk=bk: e.transpose(bk[:, i * 128:(i + 1) * 128], LOR[:, i * 128:(i + 1) * 128], IDN[:]), r=["LOR", "IDN"], w=[bkk])
            P.act(lambda e, bk=bk: e.copy(out=LT[:].rearrange("p a t -> p (a t)"), in_=bk[:, 0:256]), r=[bkk], w=["LT"])
            bk, bkk = nb()
            P.pe(lambda e, bk=bk: e.matmul(bk[:], lhsT=LT[0:64, 0, :], rhs=W2A[0:64, :], start=True, stop=False), r=["LT", "W2A"], w=[bkk], cost=0.85)
            P.pe(lambda e, bk=bk: e.matmul(bk[:], lhsT=ONE1[:], rhs=W0r[:], start=False, stop=True), r=["ONE1", "W0r"], w=[bkk])
            P.act(lambda e, bk=bk: e.activation(out=SG[:], in_=bk[:], func=AF.Sigmoid), r=[bkk], w=["SG"])
            bk, bkk = nb()
            P.pe(lambda e, bk=bk: e.matmul(bk[:], lhsT=LT[64:128, 0, :], rhs=W2A[64:128, :], start=True, stop=False), r=["LT", "W2A"], w=[bkk], cost=0.85)
            P.pe(lambda e, bk=bk: e.matmul(bk[:], lhsT=ONE1[:], rhs=A0r[:], start=False, stop=True), r=["ONE1", "A0r"], w=[bkk])
            P.act(lambda e, bk=bk: e.activation(out=AA[:], in_=bk[:], func=AF.Sigmoid), r=[bkk], w=["AA"])
            bk, bkk = nb()
            P.pe(lambda e, bk=bk: e.matmul(bk[:], lhsT=LT[:, 1, :], rhs=G2[:], start=True, stop=True), r=["LT", "G2"], w=[bkk], cost=0.85)
            P.act(lambda e, bk=bk: e.copy(out=GG[:], in_=bk[:]), r=[bkk], w=[K("GG")])
            yield
            bk, bkk = nb()
            P.pe(lambda e, bk=bk: e.matmul(bk[:], lhsT=MK["tri" + ty][:], rhs=SG[:], start=True, stop=True), r=["SG", "M_tri"], w=[bkk], cost=0.85)
            P.act(lambda e, bk=bk: e.activation(out=EIN[:], in_=bk[:], func=AF.Exp), r=[bkk], w=["EIN"])
            P.act(lambda e, bk=bk: e.activation(out=EINV[:], in_=bk[:], func=AF.Exp, scale=-1.0), r=[bkk], w=["EINV"])
            P.dve(lambda e, bk=bk: e.scalar_tensor_tensor(out=EEX[:], in0=SG[:], scalar=CDEC, in1=bk[:], op0=ALU.mult, op1=ALU.add), r=[bkk, "SG"], w=["EEX"])
            P.act(lambda e: e.activation(out=EEX[:], in_=EEX[:], func=AF.Exp), r=["EEX"], w=["EEX"])
            bk, bkk = nb()
            P.pe(lambda e, bk=bk: e.matmul(bk[:], lhsT=MK["tgt" + ty][:], rhs=SG[:], start=True, stop=True), r=["SG", "M_tgt"], w=[bkk], cost=0.85)
            P.act(lambda e, bk=bk: e.activation(out=EEND[:], in_=bk[:], func=AF.Exp), r=[bkk], w=["EEND"])
            bk, bkk = nb()
            if ty == "P":
                for p in range(4):
                    P.pe(lambda e, p=p, bk=bk: e.matmul(bk[:, p:p + 1], lhsT=SG[:, p * 128:(p + 1) * 128], rhs=SEGP[:], start=True, stop=True), r=["SG", "SEGP"], w=[bkk])
                P.act(lambda e, bk=bk: e.activation(out=PC[:], in_=bk[:, 0:4], func=AF.Exp), r=[bkk], w=[K("PC")])
            else:
                for h in range(NH):
                    P.pe(lambda e, h=h, bk=bk: e.matmul(bk[0:64, h * NSEQ:(h + 1) * NSEQ], lhsT=SG[:, h * 64:(h + 1) * 64], rhs=SEGS[:], start=True, stop=True), r=["SG", "SEGS"], w=[bkk])
                P.act(lambda e, bk=bk: e.activation(out=PCS[:].rearrange("p h b -> p (h b)"), in_=bk[0:64, 0:NH * NSEQ], func=AF.Exp), r=[bkk], w=["PCS"])
            yield
            P.dve(lambda e: e.tensor_tensor(out=KK0[:], in0=k_, in1=KKb[:], op=ALU.mult), r=["XM", "KKb"], w=["KK0"])
            P.dve(lambda e: e.tensor_tensor(out=SQ[:], in0=KK0[:], in1=KK0[:], op=ALU.mult), r=["KK0"], w=["SQ"])
            P.dve(lambda e: e.tensor_reduce(out=S8[:], in_=v3(SQ[:]), axis=AX.X, op=ALU.add), r=["SQ"], w=["S8"])
            P.dve(lambda e: e.tensor_scalar(out=S8[:], in0=S8[:], scalar1=1e-24, scalar2=None, op0=ALU.max), r=["S8"], w=["S8"])
            P.act(lambda e: e.activation(out=S8[:], in_=S8[:], func=AF.Sqrt), r=["S8"], w=["S8"])
            P.dve(lambda e: e.reciprocal(out=S8[:], in_=S8[:]), r=["S8"], w=["S8"])
            P.dve(lambda e: e.tensor_tensor(out=v3(KK0[:]), in0=v3(KK0[:]), in1=S8[:].unsqueeze(2).broadcast_to([128, 8, 64]), op=ALU.mult), r=["KK0", "S8"], w=["KK0"])
            P.dve(lambda e: e.tensor_tensor(out=BB[:], in0=KK0[:], in1=AA[:], op=ALU.mult), r=["KK0", "AA"], w=["BB"])
            P.dve(lambda e: e.scalar_tensor_tensor(out=KP[:], in0=AA[:], scalar=-1.0, in1=KAb[:], op0=ALU.add, op1=ALU.mult), r=["AA", "KAb"], w=["KP"])
            P.dve(lambda e: e.scalar_tensor_tensor(out=KP[:], in0=KP[:], scalar=1.0, in1=k_, op0=ALU.add, op1=ALU.mult), r=["KP", "XM"], w=["KP"])
            P.pool(lambda e: e.tensor_tensor(out=SQ[:], in0=r_, in1=KP[:], op=ALU.mult), r=["XM", "KP", "S8"], w=["SQ"])
            P.pool(lambda e: e.tensor_tensor(out=SQ[:], in0=SQ[:], in1=RKb[:], op=ALU.mult), r=["SQ", "RKb"], w=["SQ"])
            P.dve(lambda e: e.tensor_reduce(out=S8b[:], in_=v3(SQ[:]), axis=AX.X, op=ALU.add), r=["SQ"], w=["S8b"])
            P.dve(lambda e: e.tensor_tensor(out=v3(BON[:]), in0=v3(v_), in1=S8b[:].unsqueeze(2).broadcast_to([128, 8, 64]), op=ALU.mult), r=["XM", "S8b"], w=[K("BON")])
            P.pool(lambda e: e.tensor_tensor(out=BON[:], in0=BON[:], in1=LBb[:], op=ALU.add), r=[K("BON"), "LBb"], w=[K("BON")])
            P.pool(lambda e: e.tensor_tensor(out=BON[:], in0=BON[:], in1=GG[:], op=ALU.mult), r=[K("BON"), K("GG")], w=[K("BON")])
            P.pool(lambda e: e.tensor_tensor(out=GG[:], in0=GG[:], in1=LGb[:], op=ALU.mult), r=[K("GG"), "LGb"], w=[K("GG")])
            P.pool(lambda e: e.tensor_tensor(out=RTm[:], in0=r_, in1=EIN[:], op=ALU.mult), r=["XM", "EIN"], w=[K("RTm")])
            P.dve(lambda e: e.tensor_tensor(out=KTM[:], in0=KK0[:], in1=EEX[:], op=ALU.mult), r=["KK0", "EEX"], w=[K("KTM")])
            P.dve(lambda e: e.tensor_tensor(out=BTm[:], in0=BB[:], in1=EINV[:], op=ALU.mult), r=["BB", "EINV"], w=["BTm"])
            P.pool(lambda e: e.tensor_tensor(out=KTl[:], in0=KP[:], in1=EINV[:], op=ALU.mult), r=["KP", "EINV"], w=["KTl"])
            P.dve(lambda e: e.tensor_tensor(out=BE[:], in0=BB[:], in1=EEND[:], op=ALU.mult), r=["BB", "EEND"], w=[K("BEb")])
            P.pool(lambda e: e.tensor_tensor(out=KE[:], in0=KP[:], in1=EEND[:], op=ALU.mult), r=["KP", "EEND"], w=[K("KEb")])
            P.act(lambda e: e.copy(out=VB[:], in_=v_), r=["XM"], w=[K("VB")])
            if STOP == 'C':
                return
            yield
            for src_t, srck, dst, dstk in ((KTM, K("KTM"), KR[:, :, 0, :], K("KR")), (RTm, K("RTm"), KR[:, :, 1, :], K("KR")),
                                           (BTm, "BTm", BF[:], K("BF")), (KTl, "KTl", KF[:], K("KF"))):
                bk, bkk = nb()
                bkb = bk[:].bitcast(BF16)
                for p in range(4):
                    P.pe(lambda e, p=p, bkb=bkb, src_t=src_t: e.transpose(bkb[:, p * 128:(p + 1) * 128], src_t[:, p * 128:(p + 1) * 128], IDB[:]), r=[srck, "IDB"], w=[bkk])
                P.act(lambda e, bkb=bkb, dst=dst: e.copy(out=dst, in_=bkb[:, 0:512].rearrange("p (a t) -> p a t", t=128)), r=[bkk], w=[dstk])

            if STOP == 'D':
                return
            yield

        def back(ti):
            ty = "S" if ti == NPT else "P"
            first = (ti == 0)
            lastp = (ti == NPT - 1)
            par = ti % 2
            K = lambda n: f"{n}#{par}"
            x = X[par]
            xk = f"X{par}"
            YC, GG, BON, KR, BF, KF, VB = YCs[par], GGs[par], BONs[par], KRs[par], BFs[par], KFs[par], VBs[par]
            KTM, RTm, BE, KE, PC = KTMs[par], RTms[par], BEs[par], KEs[par], PCs[par]
            pr, cx, cxk = PR[par], CX[par], "CX"
            YY, SQ, S8, S8b, YCT = YYb, GNS, S8c, S8d, YCTb
            stprev = STt[1 - par]
            stpk = f"ST{1 - par}"
            def mach(g):
                ga, gb, gt = GA[g], GB[g], GT[g]
                for (lf, lfk, dst, dstk, mk) in ((BF, K("BF"), ga, f"GA{g}", "ma"), (KF, K("KF"), gb, f"GB{g}", "mb")):
                    bks = [nb(), nb()]
                    for i in range(4):
                        h = 4 * g + i
                        p, b0 = h // 2, 64 * (h % 2)
                        bk, bkk = bks[i % 2]
                        c0 = (i // 2) * 256
                        P.pe(lambda e, c0=c0, p=p, b0=b0, bk=bk, lf=lf: e.matmul(bk[:, c0:c0 + 256], lhsT=lf[b0:b0 + 64, p, :],
                                                                                rhs=KR[b0:b0 + 64, p, :, :].rearrange("k a t -> k (a t)"), start=True, stop=True),
                             r=[lfk, K("KR")], w=[bkk])
                    for par2 in range(2):
                        bk, bkk = bks[par2]
                        P.dve(lambda e, bk=bk, dst=dst, par2=par2, mk=mk: e.tensor_tensor(
                            out=dst[:, par2:4:2, :, :].rearrange("p h a t -> p h (a t)"), in0=bk[:].rearrange("p (h x) -> p h x", x=256),
                            in1=MK[mk + ty][:].rearrange("p (h x) -> p h x", x=256), op=ALU.mult),
                            r=[bkk, "M_" + mk], w=[dstk])
                bks = [nb(), nb()]
                for i in range(4):
                    h = 4 * g + i
                    p, b0 = h // 2, 64 * (h % 2)
                    bk, bkk = bks[i % 2]
                    c0 = (i // 2) * 128
                    P.pe(lambda e, c0=c0, p=p, b0=b0, bk=bk: e.matmul(bk[:, c0:c0 + 128], lhsT=KR[b0:b0 + 64, p, 0, :], rhs=BF[b0:b0 + 64, p, :], start=True, stop=True),
                         r=[K("KR"), K("BF")], w=[bkk])
                for par2 in range(2):
                    bk, bkk = bks[par2]
                    P.dve(lambda e, bk=bk, gt=gt, par2=par2: e.tensor_tensor(out=gt[:, par2:4:2, :], in0=bk[:, 0:256].rearrange("p (h t) -> p h t", t=128),
                                                                            in1=MK["mt" + ty][:, 0:256].rearrange("p (h t) -> p h t", t=128), op=ALU.mult),
                          r=[bkk, "M_mt"], w=[f"GT{g}"])
                if STOP == 'E0':
                    return
                yield
                P.pool(lambda e, ga=ga, g=g: e.tensor_tensor(out=TT[g][0][:], in0=ga[:, :, 0, :], in1=IDB[:].unsqueeze(1).broadcast_to([128, 4, 128]), op=ALU.add),
                       r=[f"GA{g}", "IDB"], w=[f"TT{g}0"])
                if STOP == 'E1':
                    return
                Rc, Rck = ga[:, :, 0, :], f"GA{g}"
                RTc, RTck = gt[:], f"GT{g}"
                Tc, Tck = TT[g][0], f"TT{g}0"
                NLV = 6 if ty == "P" else 2
                for lvl in range(1, NLV + 1):
                    sl = lvl % 2
                    if lvl < NLV:
                        bk, bkk = nb()
                        for i in range(4):
                            P.pe(lambda e, i=i, bk=bk, Rc=Rc, RTc=RTc: e.matmul(bk[:, i * 128:(i + 1) * 128], lhsT=RTc[:, i, :], rhs=Rc[:, i, :], start=True, stop=True),
                                 r=[Rck, RTck], w=[bkk])
                        rn, rnk = RN[g][sl], f"RN{g}{sl}"
                        P.act(lambda e, bk=bk, rn=rn: e.copy(out=rn[:].rearrange("p h t -> p (h t)"), in_=bk[:]), r=[bkk], w=[rnk])
                    bk, bkk = nb()
                    for i in range(4):
                        P.pe(lambda e, i=i, bk=bk, Rc=Rc, RTc=RTc: e.matmul(bk[:, i * 128:(i + 1) * 128], lhsT=Rc[:, i, :], rhs=RTc[:, i, :], start=True, stop=True),
                             r=[Rck, RTck], w=[bkk])
                    rtn, rtnk = RTN[g][sl], f"RTN{g}{sl}"
                    P.act(lambda e, bk=bk, rtn=rtn: e.copy(out=rtn[:].rearrange("p h t -> p (h t)"), in_=bk[:]), r=[bkk], w=[rtnk])
                    yield
                    bk, bkk = nb()
                    for i in range(4):
                        P.pe(lambda e, i=i, bk=bk, rtn=rtn, Tc=Tc: e.matmul(bk[:, i * 128:(i + 1) * 128], lhsT=rtn[:, i, :], rhs=Tc[:, i, :], start=True, stop=True),
                             r=[rtnk, Tck], w=[bkk])
                    tn, tnk = TT[g][sl], f"TT{g}{sl}"
                    P.dve(lambda e, bk=bk, tn=tn, Tc=Tc: e.tensor_tensor(out=tn[:].rearrange("p h t -> p (h t)"), in0=bk[:], in1=Tc[:].rearrange("p h t -> p (h t)"), op=ALU.add),
                          r=[bkk, Tck], w=[tnk])
                    yield
                    if lvl < NLV:
                        Rc, Rck = rn[:], rnk
                    RTc, RTck = rtn[:], rtnk
                    Tc, Tck = tn, tnk
                if STOP == 'E':
                    return
                yield
                yield "TD_DONE"
                bk, bkk = nb()
                for i in range(4):
                    h = 4 * g + i
                    P.pe(lambda e, i=i, h=h, bk=bk, gb=gb: e.matmul(bk[:, i * 64:(i + 1) * 64], lhsT=gb[:, i, 0, :], rhs=VB[:, 64 * h:64 * (h + 1)], start=True, stop=True),
                         r=[f"GB{g}", K("VB")], w=[bkk])
                P.act(lambda e, bk=bk, g=g: e.activation(out=X1N[g][:].rearrange("p h v -> p (h v)"), in_=bk[:, 0:256], func=AF.Copy, scale=-1.0), r=[bkk], w=[f"X1N{g}"])
                yield
                bk, bkk = nb()
                for i in range(4):
                    h = 4 * g + i
                    P.pe(lambda e, i=i, bk=bk, Tc=Tc, g=g: e.matmul(bk[:, i * 128:i * 128 + 64], lhsT=Tc[:, i, :], rhs=X1N[g][:, i, :], start=True, stop=True), r=[Tck, f"X1N{g}"], w=[bkk])
                    P.pe(lambda e, i=i, h=h, bk=bk, Tc=Tc: e.matmul(bk[:, i * 128 + 64:(i + 1) * 128], lhsT=Tc[:, i, :], rhs=KTM[:, 64 * h:64 * (h + 1)], start=True, stop=True), r=[Tck, K("KTM")], w=[bkk])
                uk, ukk = UK[g], f"UK{g}"
                bk4 = bk[:].rearrange("p (h a v) -> p h a v", a=2, v=64)
                P.act(lambda e, bk4=bk4, uk=uk: e.copy(out=uk[:, :, 0, :], in_=bk4[:, :, 0, :]), r=[bkk], w=[ukk])
                P.act(lambda e, bk4=bk4, uk=uk: e.activation(out=uk[:, :, 1, :], in_=bk4[:, :, 1, :], func=AF.Copy, scale=-1.0), r=[bkk], w=[ukk])
                if STOP == 'F':
                    return
                yield
                bk, bkk = nb()
                for i in range(4):
                    h = 4 * g + i
                    if ty == "P":
                        ob, col = 64 * (h % 2), (i // 2) * 128
                    else:
                        ob, col = 0, i * 128
                    P.pe(lambda e, h=h, ob=ob, col=col, bk=bk: e.matmul(bk[ob:ob + 64, col:col + 128], lhsT=RTm[:, 64 * h:64 * (h + 1)], rhs=IDB[:], start=True, stop=False), r=[K("RTm"), "IDB"], w=[bkk])
                    P.pe(lambda e, i=i, ob=ob, col=col, bk=bk, uk=uk, ga=ga: e.matmul(bk[ob:ob + 64, col:col + 128], lhsT=uk[:, i, 1, :], rhs=ga[:, i, 1, :], start=False, stop=True), r=[ukk, f"GA{g}"], w=[bkk])
                if ty == "P":
                    rh, rhk = RH[g], [f"RH{g}"]
                else:
                    rh, rhk = RHS, ["RH0", "RH1"]
                if ty == "P":
                    P.act(lambda e, bk=bk, rh=rh: e.copy(out=rh, in_=bk[:, 0:256].rearrange("p (a t) -> p a t", t=128)), r=[bkk], w=rhk)
                else:
                    P.act(lambda e, bk=bk, rh=rh: e.copy(out=rh, in_=bk[0:64, :].rearrange("p (a t) -> p a t", t=128)), r=[bkk], w=rhk)
                if ty == "P":
                    bk, bkk = nb()
                    for i in range(4):
                        h = 4 * g + i
                        ob, col = 64 * (h % 2), (i // 2) * 64
                        P.pe(lambda e, i=i, h=h, ob=ob, col=col, bk=bk, uk=uk: e.matmul(bk[ob:ob + 64, col:col + 64], lhsT=uk[:, i, 1, :], rhs=BE[:, 64 * h:64 * (h + 1)], start=True, stop=True), r=[ukk, K("BEb")], w=[bkk])
                    for j in range(2):
                        P.dve(lambda e, j=j, bk=bk, g=g: e.scalar_tensor_tensor(out=MM[g][:, j, :], in0=ID2[:], scalar=PC[:, 2 * g + j:2 * g + j + 1], in1=bk[:, j * 64:(j + 1) * 64], op0=ALU.mult, op1=ALU.add),
                              r=[bkk, "ID2", K("PC")], w=[f"MM{g}"])
                    for i in range(4):
                        h = 4 * g + i
                        p, b0, j = h // 2, 64 * (h % 2), i // 2
                        yield
                        vh = VB[:, 64 * h:64 * (h + 1)]
                        P.pe(lambda e, i=i, h=h, ga=ga, uk=uk: e.matmul(PSY[:, 64 * h:64 * (h + 1)], lhsT=ga[:, i, 1, :], rhs=uk[:, i, 0, :], start=True, stop=False), r=[f"GA{g}", ukk], w=["psy"])
                        P.pe(lambda e, i=i, h=h, gb=gb, vh=vh: e.matmul(PSY[:, 64 * h:64 * (h + 1)], lhsT=gb[:, i, 1, :], rhs=vh, start=False, stop=first), r=[f"GB{g}", K("VB")], w=["psy"])
                        if not first:
                            P.pe(lambda e, h=h, b0=b0, j=j, p=p, rh=rh: e.matmul(PSY[:, 64 * h:64 * (h + 1)], lhsT=rh[b0:b0 + 64, j, :], rhs=stprev[b0:b0 + 64, p, :], start=False, stop=True), r=[*rhk, stpk], w=["psy"])
                        P.pe(lambda e, i=i, h=h, b0=b0, p=p, uk=uk: e.matmul(PSS[b0:b0 + 64, p * 64:(p + 1) * 64], lhsT=BE[:, 64 * h:64 * (h + 1)], rhs=uk[:, i, 0, :], start=True, stop=False), r=[K("BEb"), ukk], w=["pss"])
                        P.pe(lambda e, h=h, b0=b0, p=p, vh=vh: e.matmul(PSS[b0:b0 + 64, p * 64:(p + 1) * 64], lhsT=KE[:, 64 * h:64 * (h + 1)], rhs=vh, start=False, stop=first), r=[K("KEb"), K("VB")], w=["pss"])
                        if not first:
                            P.pe(lambda e, b0=b0, j=j, p=p, g=g: e.matmul(PSS[b0:b0 + 64, p * 64:(p + 1) * 64], lhsT=MM[g][b0:b0 + 64, j, :], rhs=stprev[b0:b0 + 64, p, :], start=False, stop=True), r=[f"MM{g}", stpk], w=["pss"])
                else:
                    for i in range(4):
                        h = 4 * g + i
                        p, h2 = h // 2, h % 2
                        yield
                        vh = VB[:, 64 * h:64 * (h + 1)]
                        if h2 == 0:
                            P.dma("sp", lambda e, p=p: e.dma_start(out=S0[:], in_=stw[:, 2 * p:2 * p + 2, :, :].rearrange("b h v k -> (h v) b k")), w=["PRa", "PRb"])
                            for q in range(4):
                                bk, bkk = nb()
                                for bb in range(4):
                                    b = q * 4 + bb
                                    P.pe(lambda e, b=b, bb=bb, bk=bk: e.transpose(bk[0:64, bb * 128:(bb + 1) * 128], S0[:, b, :], IDN[:]), r=["PRa", "PRb", "IDN"], w=[bkk])
                                P.act(lambda e, q=q, bk=bk: e.copy(out=S0T[:, 4 * q:4 * q + 4, :].rearrange("p b t -> p (b t)"), in_=bk[0:64, :]), r=[bkk], w=["EIN", "EINV", "EEX", "EEND"])
                        if h == 0:
                            P.pool(lambda e: e.memset(RHm[:], 0.0), w=["SH1", "SH2", "CT", "CT2"])
                        rflat = CONV4[0:64, :]
                        P.pool(lambda e, i=i, rh=rh: e.tensor_copy(out=rflat[:, 0:2040].rearrange("p (b x) -> p b x", x=136)[:, :, 0:8],
                                                                    in_=rh[:, i, 0:120].rearrange("p (b t) -> p b t", t=8)), r=rhk, w=["SH1", "SH2", "CT", "CT2"])
                        P.pool(lambda e, i=i, rh=rh: e.tensor_copy(out=rflat[:, 2040:2048], in_=rh[:, i, 120:128]), r=rhk, w=["SH1", "SH2", "CT", "CT2"])
                        bmb = BM[:].unsqueeze(2).broadcast_to([128, NSEQ, 64])
                        P.pool(lambda e, i=i, uk=uk: e.tensor_tensor(out=KHm[:], in0=uk[:, i, 1, :].unsqueeze(1).broadcast_to([128, NSEQ, 64]), in1=bmb, op=ALU.mult), r=[ukk, "BM"], w=["PRa", "PRb"])
                        P.dve(lambda e, i=i, uk=uk: e.tensor_tensor(out=UTm[:], in0=uk[:, i, 0, :].unsqueeze(1).broadcast_to([128, NSEQ, 64]), in1=bmb, op=ALU.mult), r=[ukk, "BM"], w=["PRa", "PRb"])
                        P.dve(lambda e, vh=vh: e.tensor_tensor(out=Vm[:], in0=vh.unsqueeze(1).broadcast_to([128, NSEQ, 64]), in1=bmb, op=ALU.mult), r=[K("VB"), "BM"], w=["PRa", "PRb"])
                        P.pool(lambda e, h=h: e.tensor_tensor(out=DPC[:], in0=IDN[0:64, 0:64].unsqueeze(1).broadcast_to([64, NSEQ, 64]),
                                                              in1=PCS[:, h, :].unsqueeze(2).broadcast_to([64, NSEQ, 64]), op=ALU.mult), r=["IDN", "PCS"], w=["XM"])
                        P.pe(lambda e, i=i, h=h, ga=ga, uk=uk: e.matmul(PSY[:, 64 * h:64 * (h + 1)], lhsT=ga[:, i, 1, :], rhs=uk[:, i, 0, :], start=True, stop=False), r=[f"GA{g}", ukk], w=["psy"])
                        P.pe(lambda e, i=i, h=h, gb=gb, vh=vh: e.matmul(PSY[:, 64 * h:64 * (h + 1)], lhsT=gb[:, i, 1, :], rhs=vh, start=False, stop=False), r=[f"GB{g}", K("VB")], w=["psy"])
                        for b in range(NSEQ):
                            P.pe(lambda e, b=b, h=h, h2=h2: e.matmul(PSY[:, 64 * h:64 * (h + 1)], lhsT=RHm[:, b, :], rhs=S0T[:, b, 64 * h2:64 * (h2 + 1)], start=False, stop=(b == NSEQ - 1)), r=["SH1", "SH2", "CT", "CT2", "EIN", "EINV", "EEX", "EEND"], w=["psy"])
                        for q in range(2):
                            bk, bkk = nb()
                            for bb in range(8):
                                b = q * 8 + bb
                                P.pe(lambda e, b=b, bb=bb, h=h, bk=bk: e.matmul(bk[0:64, bb * 64:(bb + 1) * 64], lhsT=KHm[:, b, :], rhs=BE[:, 64 * h:64 * (h + 1)], start=True, stop=True), r=["PRa", "PRb", K("BEb")], w=[bkk])
                            P.dve(lambda e, q=q, bk=bk: e.tensor_tensor(out=MS[:, 8 * q:8 * q + 8, :].rearrange("p b k -> p (b k)"), in0=bk[0:64, :],
                                                                        in1=DPC[:, 8 * q:8 * q + 8, :].rearrange("p b k -> p (b k)"), op=ALU.add), r=[bkk, "XM"], w=["KK0", "SQ"])
                        so, sok = DPC, "XM"
                        for q in range(2):
                            bk, bkk = nb()
                            for bb in range(8):
                                b = q * 8 + bb
                                o_ = bk[0:64, bb * 64:(bb + 1) * 64]
                                P.pe(lambda e, b=b, o_=o_, h2=h2: e.matmul(o_, lhsT=S0T[:, b, 64 * h2:64 * (h2 + 1)], rhs=MS[:, b, :], start=True, stop=False), r=["EIN", "EINV", "EEX", "EEND", "KK0", "SQ"], w=[bkk])
                                P.pe(lambda e, b=b, o_=o_, h=h: e.matmul(o_, lhsT=UTm[:, b, :], rhs=BE[:, 64 * h:64 * (h + 1)], start=False, stop=False), r=["PRa", "PRb", K("BEb")], w=[bkk])
                                P.pe(lambda e, b=b, o_=o_, h=h: e.matmul(o_, lhsT=Vm[:, b, :], rhs=KE[:, 64 * h:64 * (h + 1)], start=False, stop=True), r=["PRa", "PRb", K("KEb")], w=[bkk])
                            P.act(lambda e, q=q, bk=bk, so=so: e.copy(out=so[:, 8 * q:8 * q + 8, :].rearrange("p b k -> p (b k)"), in_=bk[0:64, :]), r=[bkk], w=[sok])
                        P.dma("sp", lambda e, h=h, so=so: e.dma_start(out=wkv_s[:, h, :, :].rearrange("b v k -> v b k"), in_=so[:]), r=[sok])
            if ty == "P" and GI_MODE != "none":
                gens = [mach(0), mach(1)]
                alive = [True, True]
                passed = [False, False]
                while any(alive) and not (GI_MODE == "td" and all(passed)):
                    for gi in range(2):
                        if alive[gi] and not (GI_MODE == "td" and passed[gi]):
                            try:
                                if next(gens[gi]) == "TD_DONE":
                                    passed[gi] = True
                            except StopIteration:
                                alive[gi] = False
                    yield
                for gi in range(2):
                    if alive[gi]:
                        for _ in gens[gi]:
                            yield
            else:
                for g in range(2):
                    for _ in mach(g):
                        yield
            yield
            P.act(lambda e: e.copy(out=YY[:], in_=PSY[:]), r=["psy"], w=["YY"])
            if ty == "P":
                stn, stnk = STt[par], f"ST{par}"
                P.act(lambda e, stn=stn: e.copy(out=stn[:].rearrange("p a v -> p (a v)"), in_=PSS[:, 0:256]), r=["pss"], w=[stnk])
                if lastp:
                    bk, bkk = nb()
                    for p in range(4):
                        P.pe(lambda e, p=p, bk=bk, stn=stn: e.transpose(bk[0:64, p * 128:(p + 1) * 128], stn[:, p, :], IDN[:]), r=[stnk, "IDN"], w=[bkk])
                    P.act(lambda e, bk=bk: e.copy(out=WPO[:].rearrange("p a t -> p (a t)"), in_=bk[0:64, :]), r=[bkk], w=["RH0", "RH1"])
                    P.dma("sp", lambda e: e.dma_start(out=wkv_p.rearrange("(p h2) v k -> v p h2 k", h2=2), in_=WPO[:].rearrange("v p (h2 k) -> v p h2 k", h2=2)), r=["RH0", "RH1"])
            yield
            b8 = lambda t: t[:].unsqueeze(2).broadcast_to([128, 8, 64])
            P.dve(lambda e: e.tensor_reduce(out=S8[:], in_=v3(YY[:]), axis=AX.X, op=ALU.add), r=["YY"], w=["S8c"])
            P.dve(lambda e: e.tensor_tensor(out=SQ[:], in0=YY[:], in1=YY[:], op=ALU.mult), r=["YY"], w=["GNS"])
            P.dve(lambda e: e.tensor_reduce(out=S8b[:], in_=v3(SQ[:]), axis=AX.X, op=ALU.add), r=["GNS"], w=["S8d"])
            P.dve(lambda e: e.tensor_scalar(out=S8[:], in0=S8[:], scalar1=1.0 / HS, scalar2=None, op0=ALU.mult), r=["S8c"], w=["S8c"])
            P.dve(lambda e: e.tensor_tensor(out=S8e[:], in0=S8[:], in1=S8[:], op=ALU.mult), r=["S8c"], w=["S8e"])
            P.dve(lambda e: e.scalar_tensor_tensor(out=S8b[:], in0=S8b[:], scalar=1.0 / HS, in1=S8e[:], op0=ALU.mult, op1=ALU.subtract), r=["S8d", "S8e"], w=["S8d"])
            P.dve(lambda e: e.tensor_scalar(out=S8b[:], in0=S8b[:], scalar1=64e-5, scalar2=None, op0=ALU.add), r=["S8d"], w=["S8d"])
            P.act(lambda e: e.activation(out=S8b[:], in_=S8b[:], func=AF.Sqrt), r=["S8d"], w=["S8d"])
            P.dve(lambda e: e.reciprocal(out=S8b[:], in_=S8b[:]), r=["S8d"], w=["S8d"])
            P.dve(lambda e: e.tensor_tensor(out=v3(YY[:]), in0=v3(YY[:]), in1=b8(S8), op=ALU.subtract), r=["YY", "S8c"], w=["YY"])
            P.dve(lambda e: e.tensor_tensor(out=v3(YY[:]), in0=v3(YY[:]), in1=b8(S8b), op=ALU.mult), r=["YY", "S8d"], w=["YY"])
            P.dve(lambda e: e.tensor_tensor(out=YY[:], in0=YY[:], in1=GG[:], op=ALU.mult), r=["YY", K("GG")], w=["YY"])
            P.dve(lambda e: e.tensor_tensor(out=YC[:, 512:1024], in0=YY[:], in1=BON[:], op=ALU.add), r=["YY", K("BON")], w=[K("YCb")])
            if "ycat" in dbg_out:
                dbgdump("ycat", YC[:], K("YCa")) if ti == dbg_ti[0] else None
            if STOP == 'H':
                return
            yield
            for k in range(8):
                P.pe(lambda e, k=k: e.transpose(PSB[:, k * 128:(k + 1) * 128], YC[:, k * 128:(k + 1) * 128], IDB[:]), r=[K("YCa"), K("YCb"), "IDB"], w=["psb"])
            P.act(lambda e: e.copy(out=YCT[:].rearrange("p k t -> p (k t)"), in_=PSB[:]), r=["psb"], w=["YCT"])
            for j in range(2):
                bk, bkk = nb()
                wr, wrk = wchunk(wbo, j * 512, 512)
                for k in range(8):
                    P.pe(lambda e, k=k, bk=bk, wr=wr: e.matmul(bk[:], lhsT=YCT[:, k, :], rhs=wr[:, k, :], start=(k == 0), stop=(k == 7)), r=["YCT", wrk], w=[bkk])
                P.dve(lambda e, j=j, bk=bk: e.tensor_tensor(out=X1T[:, j * 512:(j + 1) * 512], in0=bk[:], in1=x[:, j * 512:(j + 1) * 512], op=ALU.add), r=[bkk, xk], w=["X1T"])
            P.dma("sp", lambda e: e.dma_start(out=x1s[ti * 128:(ti + 1) * 128, :], in_=X1T[:]), r=["X1T"], w=["x1s"])
            yield

        def drain(gen):
            for _ in gen:
                pass

        def load_masks(names):
            for nm in names:
                P.dma("sp", lambda e, nm=nm: e.dma_start(out=MKT[nm][:], in_=cd[nm + "S"]), w=["M_" + nm])

        dbg_ti = [dbg.get("_ti", 0) if dbg else 0]
        PIPE = os.environ.get("KNOPIPE", "") != "1"
        if not PIPE:
            for ti in range(NT):
                if ti == NPT:
                    load_masks(("tri", "tgt", "ma", "mb", "mt"))
                drain(front(ti))
                drain(back(ti))
        else:
            drain(front(0))
            for ti in range(NT):
                if ti == NPT:
                    load_masks(("ma", "mb", "mt"))
                bgen = back(ti)
                fgen = None
                if ti + 1 < NT:
                    if ti + 1 == NPT:
                        load_masks(("tri", "tgt"))
                    fgen = front(ti + 1)
                bdone = fdone = False
                while not (bdone and (fgen is None or fdone)):
                    for _ in range(int(os.environ.get('KRR', '4'))):
                        if not bdone:
                            try:
                                next(bgen)
                            except StopIteration:
                                bdone = True
                    if fgen is not None and not fdone:
                        try:
                            next(fgen)
                        except StopIteration:
                            fdone = True

    if _dry:
        return wq
    P.barrier()

    if SKIP2:
        return nc, P.finalize()
    with ExitStack() as st2:
        def SB2(name, shape, dt=F32):
            return st2.enter_context(nc.sbuf_tensor(name, list(shape), dt))

        PS2 = [st2.enter_context(nc.psum_tensor(f"q{i}", [128, 512], F32)) for i in range(6)]
        PSB2 = st2.enter_context(nc.psum_tensor("qb", [128, 1024], BF16))
        bank2 = [0]

        def nb2():
            i = bank2[0] % 6
            bank2[0] += 1
            return PS2[i], f"q{i}"

        NSTG = 8
        RNG = 3
        W1R = [SB2(f"W1R{i}", [128, 8, 512], BF16) for i in range(RNG)]
        W2R = [SB2(f"W2R{i}", [128, 4, D], BF16) for i in range(RNG)]
        X1 = SB2("X1A", [128, NT, D])
        H2T = SB2("H2T", [128, 8, NT * 128], BF16)
        G2T = SB2("G2T", [128, 8])
        NFb = SB2("NFb", [128, D])
        IDB2 = SB2("IDB2", [128, 128], BF16)
        JK2 = SB2("JK2", [128, D])
        HB2 = SB2("HB2", [128, D], BF16)
        SS2 = SB2("SS2", [128, 1])
        RS2 = SB2("RS2", [128, 1])
        HR = [SB2(f"HR{i}", [128, 512]) for i in range(2)]
        HID = [SB2(f"HID{i}", [128, 4, 512], BF16) for i in range(2)]
        YO = [SB2(f"YO{i}", [128, D]) for i in range(2)]
        P.dma("sp", lambda e: e.dma_start(out=G2T[:], in_=g2T_d), w=["G2T"])
        P.dma("sp", lambda e: e.dma_start(out=NFb[:], in_=nf_d.partition_broadcast(128)), w=["NFb"])
        P.dma("pool", lambda e: e.dma_start(out=IDB2[:], in_=cd["ident"]), w=["IDB2"])

        def load_stage(s):
            rb = s % RNG
            P.dma("pool", lambda e: e.dma_start(out=W1R[rb][:], in_=w_ff1[:, s * 512:(s + 1) * 512].rearrange("(k p) f -> p k f", p=128)), w=[f"W1R{rb}"])
            P.dma("pool", lambda e: e.dma_start(out=W2R[rb][:], in_=w_ff2[s * 512:(s + 1) * 512, :].rearrange("(c p) d -> p c d", p=128)), w=[f"W2R{rb}"])

        for s in range(min(RNG, NSTG)):
            load_stage(s)
        for ti in range(NT):
            P.dma("sp", lambda e, ti=ti: e.dma_start(out=X1[:, ti, :], in_=x1s[ti * 128:(ti + 1) * 128, :]), w=[f"X1_{ti}"])
        def preamble(ti):
            xk = f"X1_{ti}"
            P.act(lambda e, ti=ti: e.activation(out=JK2[:], in_=X1[:, ti, :], func=AF.Square, accum_out=SS2[:]), r=[xk], w=["JK2", "SS2"])
            P.dve(lambda e: e.tensor_scalar(out=RS2[:], in0=SS2[:], scalar1=1.0 / D, scalar2=1e-6, op0=ALU.mult, op1=ALU.add), r=["SS2"], w=["RS2"])
            P.act(lambda e: e.activation(out=RS2[:], in_=RS2[:], func=AF.Sqrt), r=["RS2"], w=["RS2"])
            P.dve(lambda e: e.reciprocal(out=RS2[:], in_=RS2[:]), r=["RS2"], w=["RS2"])
            P.act(lambda e, ti=ti: e.activation(out=HB2[:], in_=X1[:, ti, :], func=AF.Copy, scale=RS2[:, 0:1]), r=[xk, "RS2"], w=["HB2"])
            for k in range(8):
                P.pe(lambda e, k=k: e.transpose(PSB2[:, k * 128:(k + 1) * 128], HB2[:, k * 128:(k + 1) * 128], IDB2[:]), r=["HB2", "IDB2"], w=["qb"])
            P.dve(lambda e, ti=ti: e.tensor_tensor(out=H2T[:, :, ti * 128:(ti + 1) * 128], in0=PSB2[:].rearrange("p (k t) -> p k t", t=128),
                                                  in1=G2T[:].unsqueeze(2).broadcast_to([128, 8, 128]), op=ALU.mult), r=["qb", "G2T"], w=[f"H2T_{ti}"])
        groups = []
        t0 = 0
        while t0 < NT:
            n = min(4, NT - t0)
            groups.append((t0, n))
            t0 += n
        def final_tile(ti):
            xk = f"X1_{ti}"
            yo, yok = YO[ti % 2], f"YO{ti % 2}"
            P.act(lambda e, ti=ti: e.activation(out=JK2[:], in_=X1[:, ti, :], func=AF.Square, accum_out=SS2[:]), r=[xk], w=["JK2", "SS2"])
            P.dve(lambda e: e.tensor_scalar(out=RS2[:], in0=SS2[:], scalar1=1.0 / D, scalar2=1e-6, op0=ALU.mult, op1=ALU.add), r=["SS2"], w=["RS2"])
            P.act(lambda e: e.activation(out=RS2[:], in_=RS2[:], func=AF.Sqrt), r=["RS2"], w=["RS2"])
            P.dve(lambda e: e.reciprocal(out=RS2[:], in_=RS2[:]), r=["RS2"], w=["RS2"])
            P.dve(lambda e, ti=ti, yo=yo: e.scalar_tensor_tensor(out=yo[:], in0=X1[:, ti, :], scalar=RS2[:, 0:1], in1=NFb[:], op0=ALU.mult, op1=ALU.mult), r=[xk, "RS2", "NFb"], w=[yok])
            dst = y_s if ti == NPT else y_p[ti * 128:(ti + 1) * 128, :]
            P.dma("sp", lambda e, yo=yo, dst=dst: e.dma_start(out=dst, in_=yo[:]), r=[yok])


        items = [(s_, t0, n) for s_ in range(NSTG) for (t0, n) in groups]

        def ffn1(k):
            s_, t0, n = items[k]
            rb = s_ % RNG
            ntok = n * 128
            hid, hidk = HID[k % 2], f"HID{k % 2}"
            for fc in range(4):
                bk, bkk = nb2()
                for kk in range(8):
                    P.pe(lambda e, kk=kk, fc=fc, bk=bk, t0=t0, ntok=ntok, rb=rb: e.matmul(bk[:, 0:ntok], lhsT=W1R[rb][:, kk, fc * 128:(fc + 1) * 128], rhs=H2T[:, kk, t0 * 128:t0 * 128 + ntok], start=(kk == 0), stop=(kk == 7)),
                         r=[f"W1R{rb}"] + [f"H2T_{t}" for t in range(t0, t0 + n)], w=[bkk], cost=0.22)
                hr, hrk = HR[fc % 2], f"HR{fc % 2}"
                P.act(lambda e, bk=bk, hr=hr, ntok=ntok: e.activation(out=hr[:, 0:ntok], in_=bk[:, 0:ntok], func=AF.Relu), r=[bkk], w=[hrk])
                P.pool(lambda e, hr=hr, hid=hid, fc=fc, ntok=ntok: e.tensor_tensor(out=hid[:, fc, 0:ntok], in0=hr[:, 0:ntok], in1=hr[:, 0:ntok], op=ALU.mult), r=[hrk], w=[hidk])

        def ffn2(k):
            s_, t0, n = items[k]
            rb = s_ % RNG
            hid, hidk = HID[k % 2], f"HID{k % 2}"
            for tl in range(n):
                ti = t0 + tl
                for half in range(2):
                    bk, bkk = nb2()
                    for fc in range(4):
                        P.pe(lambda e, fc=fc, bk=bk, tl=tl, half=half, hid=hid, rb=rb: e.matmul(bk[:], lhsT=hid[:, fc, tl * 128:(tl + 1) * 128], rhs=W2R[rb][:, fc, half * 512:(half + 1) * 512], start=(fc == 0), stop=(fc == 3)),
                             r=[hidk, f"W2R{rb}"], w=[bkk], cost=0.22)
                    P.dve(lambda e, bk=bk, ti=ti, half=half: e.tensor_tensor(out=X1[:, ti, half * 512:(half + 1) * 512], in0=bk[:], in1=X1[:, ti, half * 512:(half + 1) * 512], op=ALU.add),
                          r=[bkk, f"X1_{ti}"], w=[f"X1_{ti}"])
                if s_ == NSTG - 1:
                    final_tile(ti)

        FFNP = os.environ.get("KNOFFNP", "") != "1"

        def pre_group(k):
            s_, t0, n = items[k]
            if s_ == 0:
                for t in range(t0, t0 + n):
                    preamble(t)

        if FFNP:
            pre_group(0)
            ffn1(0)
        for k in range(len(items)):
            if FFNP:
                if k + 1 < len(items):
                    pre_group(k + 1)
                    ffn1(k + 1)
            else:
                pre_group(k)
                ffn1(k)
            ffn2(k)
            s_, t0, n = items[k]
            if (t0, n) == groups[-1] and s_ + RNG < NSTG:
                load_stage(s_ + RNG)
        if os.environ.get('KSCHED', '1') == '1':
            P.reschedule()
        stats = P.finalize()
    return nc, stats


_CACHE = {}


def make_in_maps(inputs, NPT=16, ncores=8):
    f = lambda a: np.ascontiguousarray(np.asarray(a, dtype=np.float32))
    c = _consts()
    shared = {
        "w_in": f(inputs["w_in"][0]), "w_out": f(inputs["w_out"][0]),
        "w_ff1": f(inputs["w_ff1"][0]), "w_ff2": f(inputs["w_ff2"][0]),
        "g1T": f(np.asarray(inputs["norm1_g"][0]).reshape(8, 128).T),
        "g2T": f(np.asarray(inputs["norm2_g"][0]).reshape(8, 128).T),
        "mu": f(inputs["mu"][0]).reshape(1, RC),
        "convw": f(inputs["conv_w"][0]).reshape(1, 3 * CW_),
        "k_k": f(inputs["k_k"][0]).reshape(1, RW), "k_a": f(inputs["k_a"][0]).reshape(1, RW),
        "r_k": f(inputs["r_k"][0]).reshape(1, RW),
        "lnx_g": f(inputs["lnx_g"][0]).reshape(1, RW), "lnx_b": f(inputs["lnx_b"][0]).reshape(1, RW),
        "normf": f(inputs["normf_g"]).reshape(1, D),
        "w0": f(inputs["w0"][0]).reshape(1, RW), "a0": f(inputs["a0"][0]).reshape(1, RW),
        "w2a": f(np.concatenate([np.asarray(inputs["w2"][0]), np.asarray(inputs["a2"][0])], 0)),
        "g2": f(inputs["g2"][0]),
    }
    for k, v in c.items():
        shared["c_" + k] = f(v)
    xpr = np.asarray(inputs["x_prompt"], dtype=np.float32)
    xsm = np.asarray(inputs["x_sample"], dtype=np.float32)
    maps = []
    for ci in range(ncores):
        m = dict(shared)
        m["xp"] = f(xpr[ci, :NPT * 128])
        sl = slice(ci * NSEQ, (ci + 1) * NSEQ)
        m["xs"] = f(xsm[sl].reshape(NSEQ * DT_, D))
        m["stc"] = f(inputs["state_conv"][0][sl])
        m["sts"] = f(inputs["state_shift"][0][sl])
        m["stw"] = f(inputs["state_wkv"][0][sl])
        maps.append(m)
    return maps


def gather(results, NPT=16, ncores=8):
    g = lambda name: [np.asarray(r[name], dtype=np.float32) for r in results]
    y_p = np.stack(g("y_p"), 0)
    y_s = np.stack(g("y_s"), 0).reshape(ncores * NSEQ, DT_, D)
    conv_p = np.stack(g("conv_p"), 0)[None]
    shift_p = np.concatenate(g("shift_p"), 0)[None]
    wkv_p = np.stack(g("wkv_p"), 0)[None]
    conv_s = np.concatenate(g("conv_s"), 0)[None]
    shift_s = np.concatenate(g("shift_s"), 0)[None]
    wkv_s = np.concatenate(g("wkv_s"), 0)[None]
    return (y_p, y_s, conv_p, shift_p, wkv_p, conv_s, shift_s, wkv_s)


def kernel(**inputs):
    NPT = 16
    if "nc" not in _CACHE:
        _CACHE["nc"] = build(NPT)[0]
    nc = _CACHE["nc"]
    maps = make_in_maps(inputs, NPT, 8)
    res = run_bass_kernel_spmd(nc, maps, core_ids=list(range(8)))
    return gather(res.results, NPT, 8)
```

```python
import os
import numpy as np
import concourse.bass as bass
import concourse.mybir as mybir
from concourse.bass_utils import run_bass_kernel_spmd

F32 = mybir.dt.float32
BF16 = mybir.dt.bfloat16
ALU = mybir.AluOpType
AF = mybir.ActivationFunctionType
AX = mybir.AxisListType

D = 1024
CW_ = 512
RW = 512
NH = 8
HS = 64
RC = 1792
IC = 3328
DFF = 4096
NSEQ = 16
DT_ = 8
CDEC = float(np.exp(-0.5))


class Op:
    __slots__ = ("eng", "emit", "deps", "dma", "inc", "done", "dom", "alldeps", "cost", "idx", "fin", "succ", "npend", "prio")

    def __init__(self, eng, emit, dma):
        self.eng = eng
        self.emit = emit
        self.dma = dma
        self.deps = []
        self.alldeps = []
        self.cost = 0.0
        self.inc = False
        self.done = None
        self.dom = (eng + "_dma") if dma else eng


class Prog:
    NSLOT = {"sp": 24, "pool": 6, "act": 4}

    def __init__(self, nc, self_sync=False):
        self.nc = nc
        self.ops = []
        self.lastw = {}
        self.readers = {}
        self.self_sync = self_sync
        self.ndma = {}
        self.lastslot = {}

    COST = {"pe": 0.12, "act": 0.6, "dve": 0.7, "pool": 1.3}

    def op(self, eng, emit, reads=(), writes=(), dma=False, cost=None):
        o = Op(eng, emit, dma)
        o.cost = cost if cost is not None else (3.0 if dma else self.COST[eng])
        deps = {}
        if dma:
            n = self.ndma.get(eng, 0)
            self.ndma[eng] = n + 1
            o.dom = f"{eng}_dma{n % self.NSLOT[eng]}"
            prev = self.lastslot.get(o.dom)
            if prev is not None:
                deps[id(prev)] = (prev, True)
            self.lastslot[o.dom] = o
        for k in reads:
            w = self.lastw.get(k)
            if w is not None:
                deps[id(w)] = (w, True)
        for k in writes:
            w = self.lastw.get(k)
            if w is not None:
                deps[id(w)] = (w, True)
            for r in self.readers.get(k, ()):
                if id(r) not in deps:
                    deps[id(r)] = (r, False)
        for k in reads:
            self.readers.setdefault(k, []).append(o)
        for k in writes:
            self.lastw[k] = o
            self.readers[k] = []
        for d, hard in deps.values():
            if d is o:
                continue
            o.alldeps.append(d)
            if d.dom == o.dom and not o.dma:
                if o.eng != "pe":
                    o.deps.append(d)
                continue
            o.deps.append(d)
        self.ops.append(o)
        return o

    def pe(self, emit, r=(), w=(), cost=None):
        return self.op("pe", emit, r, w, cost=cost)

    def act(self, emit, r=(), w=(), cost=None):
        return self.op("act", emit, r, w, cost=cost)

    def dve(self, emit, r=(), w=(), cost=None):
        return self.op("dve", emit, r, w, cost=cost)

    def pool(self, emit, r=(), w=(), cost=None):
        return self.op("pool", emit, r, w, cost=cost)

    def dma(self, eng, emit, r=(), w=(), cost=None):
        return self.op(eng, emit, r, w, dma=True, cost=cost)

    def barrier(self):
        lastdom = {}
        for o in self.ops:
            if o.emit is not None:
                lastdom[o.dom] = o
        for eng in ("pe", "act", "dve", "pool", "sp"):
            b = Op(eng, None, False)
            b.deps = [o for dom, o in lastdom.items() if dom != eng]
            self.ops.append(b)
        self.lastw = {}
        self.readers = {}

    def reschedule(self, hop=float(os.environ.get("KHOP", "0.15"))):
        segs, cur = [], []
        for o in self.ops:
            if o.emit is None:
                segs.append(cur)
                segs.append([o])
                cur = []
            else:
                cur.append(o)
        segs.append(cur)
        new_ops = []
        for seg in segs:
            if len(seg) <= 1 or seg[0].emit is None:
                new_ops.extend(seg)
                continue
            inseg = {id(o) for o in seg}
            for o in seg:
                o.succ = []
                o.fin = 0.0
            fixed = set(os.environ.get("KFIXED", "pe").split(","))
            lastfixed = {}
            for o in seg:
                o.npend = 0
                for d in o.alldeps:
                    if id(d) in inseg:
                        d.succ.append(o)
                        o.npend += 1
                if o.eng in fixed and not o.dma:
                    p_ = lastfixed.get(o.eng)
                    if p_ is not None and all(p_ is not d for d in o.alldeps):
                        p_.succ.append(o)
                        o.npend += 1
                    lastfixed[o.eng] = o
            for o in reversed(seg):
                o.prio = o.cost + max([x.prio + (hop if x.eng != o.eng else 0.0) for x in o.succ], default=0.0)
            ready = {}
            for o in seg:
                if o.npend == 0:
                    ready.setdefault(o.eng, []).append(o)
            cursor = {}
            order = []
            nleft = len(seg)
            while nleft:
                best = None
                for eng, lst in ready.items():
                    if not lst:
                        continue
                    cur_t = cursor.get(eng, 0.0)
                    for o in lst:
                        est = cur_t
                        for d in o.alldeps:
                            if id(d) in inseg:
                                t = d.fin + (hop if d.eng != o.eng or d.dma != o.dma else 0.0)
                                if t > est:
                                    est = t
                        key = (est, -o.prio)
                        if best is None or key < best[0]:
                            best = (key, o, est)
                _, o, est = best
                ready[o.eng].remove(o)
                o.fin = est + o.cost
                cursor[o.eng] = (est + 0.1) if o.dma else o.fin
                order.append(o)
                nleft -= 1
                for x in o.succ:
                    x.npend -= 1
                    if x.npend == 0:
                        ready.setdefault(x.eng, []).append(x)
            new_ops.extend(order)
        self.ops = new_ops

    def finalize(self, final_eng="sp"):
        nc = self.nc
        ops = self.ops
        fin = Op(final_eng, None, False)
        lastdom = {}
        for o in ops:
            if o.emit is not None:
                lastdom[o.dom] = o
        for dom, o in lastdom.items():
            if "_dma" in dom:
                fin.deps.append(o)
        ops = ops + [fin]
        for o in ops:
            if o.dma:
                o.inc = True
            for d in o.deps:
                d.inc = True
        cnt = {}
        for o in ops:
            if o.inc:
                cnt[o.dom] = cnt.get(o.dom, 0) + (16 if o.dma else 1)
                o.done = cnt[o.dom]
        doms = sorted({o.dom for o in ops if o.inc})
        sems = {d: nc.alloc_semaphore(name="s_" + d) for d in doms}
        per_eng = {}
        for o in ops:
            per_eng.setdefault(o.eng, []).append(o)
        engs = {"pe": "tensor", "act": "scalar", "dve": "vector", "pool": "gpsimd", "sp": "sync"}

        def run(engname, e):
            waited = {}
            for o in per_eng.get(engname, []):
                need = {}
                for d in o.deps:
                    if d.done > need.get(d.dom, 0):
                        need[d.dom] = d.done
                for dom, v in need.items():
                    if v > waited.get(dom, 0):
                        e.wait_ge(sems[dom], v)
                        waited[dom] = v
                if o.emit is None:
                    continue
                ins = o.emit(e)
                if o.inc:
                    ins.then_inc(sems[o.dom], 16 if o.dma else 1)

        with nc.Block() as block:
            for engname, attr in engs.items():
                if engname not in per_eng:
                    continue

                def mk(engname=engname):
                    def f(e):
                        run(engname, e)
                    return f
                getattr(block, attr)(mk())
        return dict(nops=len(ops), cnt=cnt)


def _consts():
    c = {}
    idx = np.arange(128)
    c["ident"] = np.eye(128, dtype=np.float32)
    c["id2"] = (idx[:, None] % 64 == np.arange(64)[None, :]).astype(np.float32)
    for ty in ("P", "S"):
        if ty == "P":
            blk = np.zeros(128, np.int64)
        else:
            blk = idx // DT_
        same = blk[:, None] == blk[None, :]
        s = idx[:, None]
        t = idx[None, :]
        lt = (same & (s < t)).astype(np.float32)
        le = (same & (s <= t)).astype(np.float32)
        gt = (same & (s > t)).astype(np.float32)
        c["tri" + ty] = (-CDEC) * le
        c["tgt" + ty] = (-CDEC) * gt
        c["ma" + ty] = np.concatenate([-lt, le, -lt, le], 1)
        c["mb" + ty] = np.concatenate([lt, le, lt, le], 1)
        c["mt" + ty] = np.concatenate([-gt] * 4, 1)
    c["segP"] = np.full((128, 1), -CDEC, np.float32)
    bm = (idx[:, None] // DT_ == np.arange(NSEQ)[None, :]).astype(np.float32)
    c["segS"] = (-CDEC) * bm
    c["bm"] = bm
    return c


CONST_SHAPES = {k: v.shape for k, v in _consts().items()}


def build(NPT=16, dbg=None, _dry=False, _order=None):
    if not _dry and _order is None:
        _order = build(NPT, None, _dry=True)
    STOP = os.environ.get('KSTOP', '')
    GI_MODE = os.environ.get('KGI', 'td')
    SKIP2 = os.environ.get('KSKIP2', '') == '1'
    NT = NPT + 1
    nc = bass.Bass("TRN2", target_bir_lowering=False)
    P = Prog(nc)
    din = {}

    def DI(name, shape):
        din[name] = nc.dram_tensor(name, list(shape), F32, kind="ExternalInput").ap()
        return din[name]

    def DO(name, shape):
        return nc.dram_tensor(name, list(shape), F32, kind="ExternalOutput").ap()

    xp = DI("xp", [NPT * 128, D])
    xs = DI("xs", [128, D])
    stc = DI("stc", [NSEQ, 2, CW_])
    sts = DI("sts", [NSEQ, RC])
    stw = DI("stw", [NSEQ, NH, HS, HS])
    w_in = DI("w_in", [D, IC])
    w_out = DI("w_out", [D, D])
    w_ff1 = DI("w_ff1", [D, DFF])
    w_ff2 = DI("w_ff2", [DFF, D])
    g1T_d = DI("g1T", [128, 8])
    g2T_d = DI("g2T", [128, 8])
    mu_d = DI("mu", [1, RC])
    cw_d = DI("convw", [1, 3 * CW_])
    kk_d = DI("k_k", [1, RW])
    ka_d = DI("k_a", [1, RW])
    rk_d = DI("r_k", [1, RW])
    lg_d = DI("lnx_g", [1, RW])
    lb_d = DI("lnx_b", [1, RW])
    nf_d = DI("normf", [1, D])
    w0_d = DI("w0", [1, RW])
    a0_d = DI("a0", [1, RW])
    w2a_d = DI("w2a", [128, RW])
    g2_d = DI("g2", [128, RW])
    cd = {k: DI("c_" + k, shp) for k, shp in CONST_SHAPES.items()}

    y_p = DO("y_p", [NPT * 128, D])
    y_s = DO("y_s", [128, D])
    conv_p = DO("conv_p", [2, CW_])
    shift_p = DO("shift_p", [1, RC])
    wkv_p = DO("wkv_p", [NH, HS, HS])
    conv_s = DO("conv_s", [NSEQ, 2, CW_])
    shift_s = DO("shift_s", [NSEQ, RC])
    wkv_s = DO("wkv_s", [NSEQ, NH, HS, HS])
    x1s = nc.dram_tensor("x1s", [NT * 128, D], F32).ap()
    wbi = nc.dram_tensor("wbi", [7, 128, 8, 512], BF16).ap()
    wbo = nc.dram_tensor("wbo", [2, 128, 8, 512], BF16).ap()
    bnd_p = nc.dram_tensor("bnd_p", [NT, RC], F32).ap()
    bnd_c = nc.dram_tensor("bnd_c", [NT, 2, CW_], F32).ap()
    dbg_out = {}
    if dbg:
        for name, shape in dbg.items():
            if not name.startswith("_"):
                dbg_out[name] = DO("dbg_" + name, shape)

    from contextlib import ExitStack

    with ExitStack() as st0:
        TW = [st0.enter_context(nc.sbuf_tensor(f"TW{i}", [128, IC], BF16)) for i in range(2)]
        n0 = 0
        for (src_w, dst_w, wd) in ((w_in, wbi, IC), (w_out, wbo, D)):
            for k in range(8):
                tw, twk = TW[n0 % 2], f"TW{n0 % 2}"
                n0 += 1
                P.dma("pool", lambda e, tw=tw, src_w=src_w, k=k, wd=wd: e.dma_start(out=tw[:, 0:wd], in_=src_w[k * 128:(k + 1) * 128, :]), w=[twk])
                nfull = wd // 512
                P.dma("sp", lambda e, tw=tw, dst_w=dst_w, k=k, nfull=nfull: e.dma_start(out=dst_w[0:nfull, :, k, :].rearrange("j p n -> p j n"),
                                                                                  in_=tw[:, 0:nfull * 512].rearrange("p (j n) -> p j n", n=512)), r=[twk], w=["wb"])
                if wd % 512:
                    P.dma("sp", lambda e, tw=tw, dst_w=dst_w, k=k, nfull=nfull, wd=wd: e.dma_start(out=dst_w[nfull, :, k, 0:wd - nfull * 512], in_=tw[:, nfull * 512:wd]), r=[twk], w=["wb"])
    P.barrier()

    with ExitStack() as st1:
        def SB(name, shape, dt=F32):
            return st1.enter_context(nc.sbuf_tensor(name, list(shape), dt))

        def PSF(name, shape, dt=F32):
            return st1.enter_context(nc.psum_tensor(name, list(shape), dt))

        NB = 5
        PS = [PSF(f"ps{i}", [128, 512]) for i in range(NB)]
        PSY = PSF("psy", [128, 512])
        PSS = PSF("pss", [128, 512])
        PSB = PSF("psb", [128, 1024], BF16)
        bank_i = [0]

        def nb():
            i = bank_i[0] % NB
            bank_i[0] += 1
            return PS[i], f"ps{i}"

        NRING = 3
        WR = [SB(f"WR{i}", [128, 8, 512], BF16) for i in range(NRING)]
        wq = list(_order) if _order is not None else []
        wsrc = {"wbi": wbi, "wbo": wbo}
        wstate = {"issued": 0, "used": 0}

        def wchunk(srcw, c0, wd):
            name = "wbi" if srcw is wbi else "wbo"
            n = wstate["used"]
            wstate["used"] += 1
            if _dry:
                wq.append((name, c0, wd))
                return WR[n % NRING], f"WR{n % NRING}"
            assert wq[n] == (name, c0, wd), (n, wq[n], name, c0, wd)
            while wstate["issued"] < min(len(wq), n + NRING):
                m = wstate["issued"]
                nm2, c2, wd2 = wq[m]
                P.dma("sp", lambda e, m=m, nm2=nm2, c2=c2, wd2=wd2: e.dma_start(out=WR[m % NRING][:, :, 0:wd2], in_=wsrc[nm2][c2 // 512, :, :, 0:wd2]),
                      w=[f"WR{m % NRING}"], cost=6.0)
                wstate["issued"] += 1
            return WR[n % NRING], f"WR{n % NRING}"

        def bc_tile(name, dram, n):
            t = SB(name, [128, n])
            P.dma("sp", lambda e: e.dma_start(out=t[:], in_=dram.partition_broadcast(128)), w=[name])
            return t

        def ld_tile(name, dram, shape, dt=F32, eng="sp"):
            t = SB(name, shape, dt)
            P.dma(eng, lambda e: e.dma_start(out=t[:], in_=dram), w=[name])
            return t

        IDN = ld_tile("IDN", cd["ident"], [128, 128])
        IDB = ld_tile("IDB", cd["ident"], [128, 128], BF16, eng="pool")
        ID2 = ld_tile("ID2", cd["id2"], [128, 64])
        G1T = ld_tile("G1T", g1T_d, [128, 8])
        MU = bc_tile("MU", mu_d, RC)
        CWt = bc_tile("CWt", cw_d, 3 * CW_)
        KKb = bc_tile("KKb", kk_d, RW)
        KAb = bc_tile("KAb", ka_d, RW)
        RKb = bc_tile("RKb", rk_d, RW)
        LGb = bc_tile("LGb", lg_d, RW)
        LBb = bc_tile("LBb", lb_d, RW)
        W0r = ld_tile("W0r", w0_d, [1, RW])
        A0r = ld_tile("A0r", a0_d, [1, RW])
        W2A = ld_tile("W2A", w2a_d, [128, RW])
        G2 = ld_tile("G2", g2_d, [128, RW])
        ONE1 = SB("ONE1", [1, 128])
        P.pool(lambda e: e.memset(ONE1[:], 1.0), w=["ONE1"])
        MKT = {}
        for nm in ("tri", "tgt"):
            MKT[nm] = ld_tile("M_" + nm, cd[nm + "P"], [128, 128])
        for nm in ("ma", "mb", "mt"):
            MKT[nm] = ld_tile("M_" + nm, cd[nm + "P"], [128, 512])
        MK = {}
        for ty in ("P", "S"):
            for nm in ("tri", "tgt", "ma", "mb", "mt"):
                MK[nm + ty] = MKT[nm]
        SEGP = ld_tile("SEGP", cd["segP"], [128, 1])
        SEGS = ld_tile("SEGS", cd["segS"], [128, NSEQ])
        BM = ld_tile("BM", cd["bm"], [128, NSEQ])

        X = [SB(f"X{i}", [128, D]) for i in range(2)]
        SS = SB("SS", [128, 1])
        RS = SB("RS", [128, 1])
        HB = SB("HB", [128, D], BF16)
        HT = SB("HT", [128, 8, 128], BF16)
        PR0 = SB("PR0", [128, IC])
        PR = [PR0, PR0]
        CX0 = SB("CX0", [128, CW_])
        CX = [CX0, CX0]
        CONV4 = SB("CONV4", [128, 4 * CW_])
        SH1, SH2, CT, CT2 = (CONV4[:, i * 512:(i + 1) * 512] for i in range(4))
        YCs = [SB(f"YC{i}", [128, D], BF16) for i in range(2)]
        X1T = SB("X1T", [128, D])
        JUNK = CONV4[:, 0:1024]
        YCTb = SB("YCTb", [128, 8, 128], BF16)
        PV = SB("PV", [128, RC])
        XM = PV
        LOR = SB("LOR", [128, 256])
        LT = SB("LT", [128, 2, 128])
        SG = SB("SG", [128, RW])
        YYb = SB("YYb", [128, RW])
        GNS = SB("GNS", [128, RW])
        S8c = SB("S8c", [128, 8])
        S8d = SB("S8d", [128, 8])
        S8e = SB("S8e", [128, 8])
        AA = SB("AA", [128, RW])
        GGs = [SB(f"GG{i}", [128, RW]) for i in range(2)]
        EXP4 = SB("EXP4", [128, 4 * RW])
        EIN, EINV, EEX, EEND = (EXP4[:, i * 512:(i + 1) * 512] for i in range(4))
        BTm = SB("BTm", [128, RW], BF16)
        KTl = SB("KTl", [128, RW], BF16)
        KS2 = SB("KS2", [128, 2 * RW])
        KK0, SQ = KS2[:, 0:512], KS2[:, 512:1024]
        S8 = SB("S8", [128, 8])
        S8b = SB("S8b", [128, 8])
        BB = SB("BB", [128, RW])
        KP = SB("KP", [128, RW])
        BEs = [SB(f"BEb{i}", [128, RW], BF16) for i in range(2)]
        KEs = [SB(f"KEb{i}", [128, RW], BF16) for i in range(2)]
        VBs = [SB(f"VB{i}", [128, RW], BF16) for i in range(2)]
        BONs = [SB(f"BON{i}", [128, RW]) for i in range(2)]
        RTms = [SB(f"RTm{i}", [128, RW], BF16) for i in range(2)]
        KTMs = [SB(f"KTM{i}", [128, RW], BF16) for i in range(2)]
        KRs = [SB(f"KR{i}", [128, 4, 2, 128], BF16) for i in range(2)]
        BFs = [SB(f"BF{i}", [128, 4, 128], BF16) for i in range(2)]
        KFs = [SB(f"KF{i}", [128, 4, 128], BF16) for i in range(2)]
        PCs = [SB(f"PC{i}", [128, 4]) for i in range(2)]
        PCS = SB("PCS", [64, NH, NSEQ])
        GA = [SB(f"GA{g}", [128, 4, 2, 128], BF16) for g in range(2)]
        GB = [SB(f"GB{g}", [128, 4, 2, 128], BF16) for g in range(2)]
        GT = [SB(f"GT{g}", [128, 4, 128], BF16) for g in range(2)]
        RN = [[SB(f"RN{g}{i}", [128, 4, 128], BF16) for i in range(2)] for g in range(2)]
        RTN = [[SB(f"RTN{g}{i}", [128, 4, 128], BF16) for i in range(2)] for g in range(2)]
        TT = [[SB(f"TT{g}{i}", [128, 4, 128], BF16) for i in range(2)] for g in range(2)]
        X1N = [SB(f"X1N{g}", [128, 4, 64], BF16) for g in range(2)]
        UK = [SB(f"UK{g}", [128, 4, 2, 64], BF16) for g in range(2)]
        RHb = SB("RHb", [128, 2, 2, 128])
        RH = [RHb[:, 0, :, :], RHb[:, 1, :, :]]
        RHS = RHb[0:64, :, :, :].rearrange("p a b t -> p (a b) t")
        MM = [SB(f"MM{g}", [128, 2, 64]) for g in range(2)]
        STt = [SB(f"ST{i}", [128, 4, 64]) for i in range(2)]
        S0 = PR0[:, 0:1024].rearrange("p (b k) -> p b k", k=64)
        S0T = EXP4[0:64, :].rearrange("p (b t) -> p b t", t=128)
        RHm = CONV4[0:64, :].rearrange("p (b t) -> p b t", t=128)
        KHm = PR0[:, 1024:1536].bitcast(BF16).rearrange("p (b k) -> p b k", k=64)
        UTm = PR0[:, 1536:2048].bitcast(BF16).rearrange("p (b k) -> p b k", k=64)
        Vm = PR0[:, 2048:2560].bitcast(BF16).rearrange("p (b k) -> p b k", k=64)
        DPC = PV[0:64, 0:1024].rearrange("p (b k) -> p b k", k=64)
        MS = KS2[0:64, :].rearrange("p (b k) -> p b k", k=64)
        WPO = RHb[0:64, :, :, :].rearrange("p a b t -> p (a b) t")
        SO = [DPC, DPC]

        def v3(ap, k=64):
            return ap.rearrange("p (h k) -> p h k", k=k)

        def dbgdump(name, ap, key):
            if name in dbg_out:
                P.dma("sp", lambda e: e.dma_start(out=dbg_out[name], in_=ap), r=[key])

        def front(ti):
            ty = "S" if ti == NPT else "P"
            first = (ti == 0)
            lastp = (ti == NPT - 1)
            par = ti % 2
            K = lambda n: f"{n}#{par}"
            x = X[par]
            xk = f"X{par}"
            YC, GG, BON, KR, BF, KF, VB = YCs[par], GGs[par], BONs[par], KRs[par], BFs[par], KFs[par], VBs[par]
            KTM, RTm, BE, KE, PC = KTMs[par], RTms[par], BEs[par], KEs[par], PCs[par]
            src = xs if ty == "S" else xp[ti * 128:(ti + 1) * 128, :]
            P.dma("sp", lambda e: e.dma_start(out=x[:], in_=src), w=[xk])
            P.act(lambda e: e.activation(out=JUNK[:], in_=x[:], func=AF.Square, accum_out=SS[:]), r=[xk], w=["SH1", "SH2", "SS"])
            P.dve(lambda e: e.tensor_scalar(out=RS[:], in0=SS[:], scalar1=1.0 / D, scalar2=1e-6, op0=ALU.mult, op1=ALU.add), r=["SS"], w=["RS"])
            P.act(lambda e: e.activation(out=RS[:], in_=RS[:], func=AF.Sqrt), r=["RS"], w=["RS"])
            P.dve(lambda e: e.reciprocal(out=RS[:], in_=RS[:]), r=["RS"], w=["RS"])
            P.act(lambda e: e.activation(out=HB[:], in_=x[:], func=AF.Copy, scale=RS[:, 0:1]), r=[xk, "RS"], w=["HB"], cost=1.2)
            for k in range(8):
                P.pe(lambda e, k=k: e.transpose(PSB[:, k * 128:(k + 1) * 128], HB[:, k * 128:(k + 1) * 128], IDB[:]), r=["HB", "IDB"], w=["psb"])
            P.dve(lambda e: e.tensor_tensor(out=HT[:], in0=PSB[:].rearrange("p (k t) -> p k t", t=128),
                                            in1=G1T[:].unsqueeze(2).broadcast_to([128, 8, 128]), op=ALU.mult), r=["psb", "G1T"], w=["HT"])
            pr = PR[par]

            def proj_chunks(js):
                for j in js:
                    wd = 512 if j < 6 else 256
                    bk, bkk = nb()
                    wr, wrk = wchunk(wbi, j * 512, wd)
                    for k in range(8):
                        P.pe(lambda e, k=k, wd=wd, bk=bk, wr=wr: e.matmul(bk[:, 0:wd], lhsT=HT[:, k, :], rhs=wr[:, k, 0:wd], start=(k == 0), stop=(k == 7)),
                             r=["HT", wrk], w=[bkk], cost=0.22)
                    P.act(lambda e, j=j, wd=wd, bk=bk: e.copy(out=pr[:, j * 512:j * 512 + wd], in_=bk[:, 0:wd]), r=[bkk], w=["PRb" if j >= 3 else "PRa"])
                    yield

            yield from proj_chunks((3, 4, 5, 6))
            yield
            prw = pr[:, 1536:IC]
            P.dma("sp", lambda e: e.dma_start(out=PV[1:113, :], in_=pr[0:112, 1536:IC]), r=["PRb"], w=["XM"], cost=5.0)
            P.dma("sp", lambda e: e.dma_start(out=PV[113:128, :], in_=pr[112:127, 1536:IC]), r=["PRb"], w=["XM"])
            if ty == "S":
                P.dma("sp", lambda e: e.dma_start(out=PV[0:128:8, :], in_=sts), w=["XM"])
            elif first:
                P.pool(lambda e: e.memset(PV[0:1, :], 0.0), w=["XM"])
            else:
                P.dma("sp", lambda e: e.dma_start(out=PV[0:1, :], in_=bnd_p[ti - 1:ti, :]), r=["bnd"], w=["XM"])
            P.dve(lambda e: e.tensor_tensor(out=PV[:], in0=PV[:], in1=prw, op=ALU.subtract), r=["XM", "PRb"], w=["XM"], cost=2.0)
            P.dve(lambda e: e.tensor_tensor(out=PV[:], in0=PV[:], in1=MU[:], op=ALU.mult), r=["XM", "MU"], w=["XM"], cost=2.0)
            P.dve(lambda e: e.tensor_tensor(out=XM[:], in0=PV[:], in1=prw, op=ALU.add), r=["XM", "PRb"], w=["XM"], cost=2.0)
            r_ = XM[:, 0:512]
            k_ = XM[:, 512:1024]
            v_ = XM[:, 1024:1536]
            if STOP == 'B':
                return
            if ty == "S":
                P.dma("sp", lambda e: e.dma_start(out=shift_s, in_=pr[7:128:8, 1536:IC]), r=["PRb"])
            else:
                P.dma("sp", lambda e: e.dma_start(out=bnd_p[ti:ti + 1, :], in_=pr[127:128, 1536:IC]), r=["PRb"], w=["bnd"])
            if lastp:
                P.dma("sp", lambda e: e.dma_start(out=shift_p, in_=pr[127:128, 1536:IC]), r=["PRb"])
            yield from proj_chunks((0, 1, 2))
            yield
            cx = CX[par]
            cxk = "CX"
            P.pool(lambda e: e.tensor_tensor(out=cx[:], in0=pr[:, 512:1024], in1=pr[:, 1024:1536], op=ALU.mult), r=["PRa"], w=[cxk])
            P.dma("sp", lambda e: e.dma_start(out=SH1[1:113, :], in_=cx[0:112, :]), r=[cxk], w=["SH1"])
            P.dma("sp", lambda e: e.dma_start(out=SH1[113:128, :], in_=cx[112:127, :]), r=[cxk], w=["SH1"])
            P.dma("sp", lambda e: e.dma_start(out=SH2[2:114, :], in_=cx[0:112, :]), r=[cxk], w=["SH2"])
            P.dma("sp", lambda e: e.dma_start(out=SH2[114:128, :], in_=cx[112:126, :]), r=[cxk], w=["SH2"])
            if ty == "S":
                P.dma("sp", lambda e: e.dma_start(out=SH1[0:128:8, :], in_=stc[:, 1, :]), w=["SH1"])
                P.dma("sp", lambda e: e.dma_start(out=SH2[0:128:8, :], in_=stc[:, 0, :]), w=["SH2"])
                P.dma("sp", lambda e: e.dma_start(out=SH2[1:128:8, :], in_=stc[:, 1, :]), w=["SH2"])
            elif first:
                P.pool(lambda e: e.memset(SH1[0:1, :], 0.0), w=["SH1"])
                P.pool(lambda e: e.memset(SH2[0:2, :], 0.0), w=["SH2"])
            else:
                P.dma("sp", lambda e: e.dma_start(out=SH1[0:1, :], in_=bnd_c[ti - 1, 1:2, :]), r=["bnd"], w=["SH1"])
                P.dma("sp", lambda e: e.dma_start(out=SH2[0:2, :], in_=bnd_c[ti - 1, :, :]), r=["bnd"], w=["SH2"])
            P.pool(lambda e: e.tensor_tensor(out=CT[:], in0=SH2[:], in1=CWt[:, 0:512], op=ALU.mult), r=["SH2", "CWt"], w=["CT"])
            P.pool(lambda e: e.tensor_tensor(out=CT2[:], in0=SH1[:], in1=CWt[:, 512:1024], op=ALU.mult), r=["SH1", "CWt"], w=["CT2"])
            P.pool(lambda e: e.tensor_tensor(out=CT[:], in0=CT[:], in1=CT2[:], op=ALU.add), r=["CT", "CT2"], w=["CT"])
            P.pool(lambda e: e.tensor_tensor(out=CT2[:], in0=cx[:], in1=CWt[:, 1024:1536], op=ALU.mult), r=[cxk, "CWt"], w=["CT2"])
            P.pool(lambda e: e.tensor_tensor(out=CT[:], in0=CT[:], in1=CT2[:], op=ALU.add), r=["CT", "CT2"], w=["CT"])
            P.pool(lambda e: e.tensor_tensor(out=YC[:, 0:512], in0=CT[:], in1=pr[:, 0:512], op=ALU.mult), r=["CT", "PRa"], w=[K("YCa")])
            if ty == "S":
                P.dma("sp", lambda e: e.dma_start(out=conv_s[:, 0, :], in_=cx[6:128:8, :]), r=[cxk])
                P.dma("sp", lambda e: e.dma_start(out=conv_s[:, 1, :], in_=cx[7:128:8, :]), r=[cxk])
            else:
                P.dma("sp", lambda e: e.dma_start(out=bnd_c[ti, :, :], in_=cx[126:128, :]), r=[cxk], w=["bnd"])
            if lastp:
                P.dma("sp", lambda e: e.dma_start(out=conv_p, in_=cx[126:128, :]), r=[cxk])
            yield
            P.act(lambda e: e.activation(out=LOR[:, 0:64], in_=XM[:, 1536:1600], func=AF.Tanh), r=["XM"], w=["LOR"])
            P.act(lambda e: e.copy(out=LOR[:, 64:128], in_=XM[:, 1600:1664]), r=["XM"], w=["LOR"])
            P.act(lambda e: e.activation(out=LOR[:, 128:256], in_=XM[:, 1664:1792], func=AF.Sigmoid), r=["XM"], w=["LOR"])
            bk, bkk = nb()
            for i in range(2):
                P.pe(lambda e, i=i, bk=bk: e.transpose(bk[:, i * 128:(i + 1) * 128], LOR[:, i * 128:(i + 1) * 128], IDN[:]), r=["LOR", "IDN"], w=[bkk])
            P.act(lambda e, bk=bk: e.copy(out=LT[:].rearrange("p a t -> p (a t)"), in_=bk[:, 0:256]), r=[bkk], w=["LT"])
            bk, bkk = nb()
            P.pe(lambda e, bk=bk: e.matmul(bk[:], lhsT=LT[0:64, 0, :], rhs=W2A[0:64, :], start=True, stop=False), r=["LT", "W2A"], w=[bkk], cost=0.85)
            P.pe(lambda e, bk=bk: e.matmul(bk[:], lhsT=ONE1[:], rhs=W0r[:], start=False, stop=True), r=["ONE1", "W0r"], w=[bkk])
            P.act(lambda e, bk=bk: e.activation(out=SG[:], in_=bk[:], func=AF.Sigmoid), r=[bkk], w=["SG"])
            bk, bkk = nb()
            P.pe(lambda e, bk=bk: e.matmul(bk[:], lhsT=LT[64:128, 0, :], rhs=W2A[64:128, :], start=True, stop=False), r=["LT", "W2A"], w=[bkk], cost=0.85)
            P.pe(lambda e, bk=bk: e.matmul(bk[:], lhsT=ONE1[:], rhs=A0r[:], start=False, stop=True), r=["ONE1", "A0r"], w=[bkk])
            P.act(lambda e, bk=bk: e.activation(out=AA[:], in_=bk[:], func=AF.Sigmoid), r=[bkk], w=["AA"])
            bk, bkk = nb()
            P.pe(lambda e, bk=bk: e.matmul(bk[:], lhsT=LT[:, 1, :], rhs=G2[:], start=True, stop=True), r=["LT", "G2"], w=[bkk], cost=0.85)
            P.act(lambda e, bk=bk: e.copy(out=GG[:], in_=bk[:]), r=[bkk], w=[K("GG")])
            yield
            bk, bkk = nb()
            P.pe(lambda e, bk=bk: e.matmul(bk[:], lhsT=MK["tri" + ty][:], rhs=SG[:], start=True, stop=True), r=["SG", "M_tri"], w=[bkk], cost=0.85)
            P.act(lambda e, bk=bk: e.activation(out=EIN[:], in_=bk[:], func=AF.Exp), r=[bkk], w=["EIN"])
            P.act(lambda e, bk=bk: e.activation(out=EINV[:], in_=bk[:], func=AF.Exp, scale=-1.0), r=[bkk], w=["EINV"])
            P.dve(lambda e, bk=bk: e.scalar_tensor_tensor(out=EEX[:], in0=SG[:], scalar=CDEC, in1=bk[:], op0=ALU.mult, op1=ALU.add), r=[bkk, "SG"], w=["EEX"])
            P.act(lambda e: e.activation(out=EEX[:], in_=EEX[:], func=AF.Exp), r=["EEX"], w=["EEX"])
            bk, bkk = nb()
            P.pe(lambda e, bk=bk: e.matmul(bk[:], lhsT=MK["tgt" + ty][:], rhs=SG[:], start=True, stop=True), r=["SG", "M_tgt"], w=[bkk], cost=0.85)
            P.act(lambda e, bk=bk: e.activation(out=EEND[:], in_=bk[:], func=AF.Exp), r=[bkk], w=["EEND"])
            bk, bkk = nb()
            if ty == "P":
                for p in range(4):
                    P.pe(lambda e, p=p, bk=bk: e.matmul(bk[:, p:p + 1], lhsT=SG[:, p * 128:(p + 1) * 128], rhs=SEGP[:], start=True, stop=True), r=["SG", "SEGP"], w=[bkk])
                P.act(lambda e, bk=bk: e.activation(out=PC[:], in_=bk[:, 0:4], func=AF.Exp), r=[bkk], w=[K("PC")])
            else:
                for h in range(NH):
                    P.pe(lambda e, h=h, bk=bk: e.matmul(bk[0:64, h * NSEQ:(h + 1) * NSEQ], lhsT=SG[:, h * 64:(h + 1) * 64], rhs=SEGS[:], start=True, stop=True), r=["SG", "SEGS"], w=[bkk])
                P.act(lambda e, bk=bk: e.activation(out=PCS[:].rearrange("p h b -> p (h b)"), in_=bk[0:64, 0:NH * NSEQ], func=AF.Exp), r=[bkk], w=["PCS"])
            yield
            P.dve(lambda e: e.tensor_tensor(out=KK0[:], in0=k_, in1=KKb[:], op=ALU.mult), r=["XM", "KKb"], w=["KK0"])
            P.dve(lambda e: e.tensor_tensor(out=SQ[:], in0=KK0[:], in1=KK0[:], op=ALU.mult), r=["KK0"], w=["SQ"])
            P.dve(lambda e: e.tensor_reduce(out=S8[:], in_=v3(SQ[:]), axis=AX.X, op=ALU.add), r=["SQ"], w=["S8"])
            P.dve(lambda e: e.tensor_scalar(out=S8[:], in0=S8[:], scalar1=1e-24, scalar2=None, op0=ALU.max), r=["S8"], w=["S8"])
            P.act(lambda e: e.activation(out=S8[:], in_=S8[:], func=AF.Sqrt), r=["S8"], w=["S8"])
            P.dve(lambda e: e.reciprocal(out=S8[:], in_=S8[:]), r=["S8"], w=["S8"])
            P.dve(lambda e: e.tensor_tensor(out=v3(KK0[:]), in0=v3(KK0[:]), in1=S8[:].unsqueeze(2).broadcast_to([128, 8, 64]), op=ALU.mult), r=["KK0", "S8"], w=["KK0"])
            P.dve(lambda e: e.tensor_tensor(out=BB[:], in0=KK0[:], in1=AA[:], op=ALU.mult), r=["KK0", "AA"], w=["BB"])
            P.dve(lambda e: e.scalar_tensor_tensor(out=KP[:], in0=AA[:], scalar=-1.0, in1=KAb[:], op0=ALU.add, op1=ALU.mult), r=["AA", "KAb"], w=["KP"])
            P.dve(lambda e: e.scalar_tensor_tensor(out=KP[:], in0=KP[:], scalar=1.0, in1=k_, op0=ALU.add, op1=ALU.mult), r=["KP", "XM"], w=["KP"])
            P.pool(lambda e: e.tensor_tensor(out=SQ[:], in0=r_, in1=KP[:], op=ALU.mult), r=["XM", "KP", "S8"], w=["SQ"])
            P.pool(lambda e: e.tensor_tensor(out=SQ[:], in0=SQ[:], in1=RKb[:], op=ALU.mult), r=["SQ", "RKb"], w=["SQ"])
            P.dve(lambda e: e.tensor_reduce(out=S8b[:], in_=v3(SQ[:]), axis=AX.X, op=ALU.add), r=["SQ"], w=["S8b"])
            P.dve(lambda e: e.tensor_tensor(out=v3(BON[:]), in0=v3(v_), in1=S8b[:].unsqueeze(2).broadcast_to([128, 8, 64]), op=ALU.mult), r=["XM", "S8b"], w=[K("BON")])
            P.pool(lambda e: e.tensor_tensor(out=BON[:], in0=BON[:], in1=LBb[:], op=ALU.add), r=[K("BON"), "LBb"], w=[K("BON")])
            P.pool(lambda e: e.tensor_tensor(out=BON[:], in0=BON[:], in1=GG[:], op=ALU.mult), r=[K("BON"), K("GG")], w=[K("BON")])
            P.pool(lambda e: e.tensor_tensor(out=GG[:], in0=GG[:], in1=LGb[:], op=ALU.mult), r=[K("GG"), "LGb"], w=[K("GG")])
            P.pool(lambda e: e.tensor_tensor(out=RTm[:], in0=r_, in1=EIN[:], op=ALU.mult), r=["XM", "EIN"], w=[K("RTm")])
            P.dve(lambda e: e.tensor_tensor(out=KTM[:], in0=KK0[:], in1=EEX[:], op=ALU.mult), r=["KK0", "EEX"], w=[K("KTM")])
            P.dve(lambda e: e.tensor_tensor(out=BTm[:], in0=BB[:], in1=EINV[:], op=ALU.mult), r=["BB", "EINV"], w=["BTm"])
            P.pool(lambda e: e.tensor_tensor(out=KTl[:], in0=KP[:], in1=EINV[:], op=ALU.mult), r=["KP", "EINV"], w=["KTl"])
            P.dve(lambda e: e.tensor_tensor(out=BE[:], in0=BB[:], in1=EEND[:], op=ALU.mult), r=["BB", "EEND"], w=[K("BEb")])
            P.pool(lambda e: e.tensor_tensor(out=KE[:], in0=KP[:], in1=EEND[:], op=ALU.mult), r=["KP", "EEND"], w=[K("KEb")])
            P.act(lambda e: e.copy(out=VB[:], in_=v_), r=["XM"], w=[K("VB")])
            if STOP == 'C':
                return
            yield
            for src_t, srck, dst, dstk in ((KTM, K("KTM"), KR[:, :, 0, :], K("KR")), (RTm, K("RTm"), KR[:, :, 1, :], K("KR")),
                                           (BTm, "BTm", BF[:], K("BF")), (KTl, "KTl", KF[:], K("KF"))):
                bk, bkk = nb()
                bkb = bk[:].bitcast(BF16)
                for p in range(4):
                    P.pe(lambda e, p=p, bkb=bkb, src_t=src_t: e.transpose(bkb[:, p * 128:(p + 1) * 128], src_t[:, p * 128:(p + 1) * 128], IDB[:]), r=[srck, "IDB"], w=[bkk])
                P.act(lambda e, bkb=bkb, dst=dst: e.copy(out=dst, in_=bkb[:, 0:512].rearrange("p (a t) -> p a t", t=128)), r=[bkk], w=[dstk])

            if STOP == 'D':
                return
            yield

        def back(ti):
            ty = "S" if ti == NPT else "P"
            first = (ti == 0)
            lastp = (ti == NPT - 1)
            par = ti % 2
            K = lambda n: f"{n}#{par}"
            x = X[par]
            xk = f"X{par}"
            YC, GG, BON, KR, BF, KF, VB = YCs[par], GGs[par], BONs[par], KRs[par], BFs[par], KFs[par], VBs[par]
            KTM, RTm, BE, KE, PC = KTMs[par], RTms[par], BEs[par], KEs[par], PCs[par]
            pr, cx, cxk = PR[par], CX[par], "CX"
            YY, SQ, S8, S8b, YCT = YYb, GNS, S8c, S8d, YCTb
            stprev = STt[1 - par]
            stpk = f"ST{1 - par}"
            def mach(g):
                ga, gb, gt = GA[g], GB[g], GT[g]
                for (lf, lfk, dst, dstk, mk) in ((BF, K("BF"), ga, f"GA{g}", "ma"), (KF, K("KF"), gb, f"GB{g}", "mb")):
                    bks = [nb(), nb()]
                    for i in range(4):
                        h = 4 * g + i
                        p, b0 = h // 2, 64 * (h % 2)
                        bk, bkk = bks[i % 2]
                        c0 = (i // 2) * 256
                        P.pe(lambda e, c0=c0, p=p, b0=b0, bk=bk, lf=lf: e.matmul(bk[:, c0:c0 + 256], lhsT=lf[b0:b0 + 64, p, :],
                                                                                rhs=KR[b0:b0 + 64, p, :, :].rearrange("k a t -> k (a t)"), start=True, stop=True),
                             r=[lfk, K("KR")], w=[bkk])
                    for par2 in range(2):
                        bk, bkk = bks[par2]
                        P.dve(lambda e, bk=bk, dst=dst, par2=par2, mk=mk: e.tensor_tensor(
                            out=dst[:, par2:4:2, :, :].rearrange("p h a t -> p h (a t)"), in0=bk[:].rearrange("p (h x) -> p h x", x=256),
                            in1=MK[mk + ty][:].rearrange("p (h x) -> p h x", x=256), op=ALU.mult),
                            r=[bkk, "M_" + mk], w=[dstk])
                bks = [nb(), nb()]
                for i in range(4):
                    h = 4 * g + i
                    p, b0 = h // 2, 64 * (h % 2)
                    bk, bkk = bks[i % 2]
                    c0 = (i // 2) * 128
                    P.pe(lambda e, c0=c0, p=p, b0=b0, bk=bk: e.matmul(bk[:, c0:c0 + 128], lhsT=KR[b0:b0 + 64, p, 0, :], rhs=BF[b0:b0 + 64, p, :], start=True, stop=True),
                         r=[K("KR"), K("BF")], w=[bkk])
                for par2 in range(2):
                    bk, bkk = bks[par2]
                    P.dve(lambda e, bk=bk, gt=gt, par2=par2: e.tensor_tensor(out=gt[:, par2:4:2, :], in0=bk[:, 0:256].rearrange("p (h t) -> p h t", t=128),
                                                                            in1=MK["mt" + ty][:, 0:256].rearrange("p (h t) -> p h t", t=128), op=ALU.mult),
                          r=[bkk, "M_mt"], w=[f"GT{g}"])
                if STOP == 'E0':
                    return
                yield
                P.pool(lambda e, ga=ga, g=g: e.tensor_tensor(out=TT[g][0][:], in0=ga[:, :, 0, :], in1=IDB[:].unsqueeze(1).broadcast_to([128, 4, 128]), op=ALU.add),
                       r=[f"GA{g}", "IDB"], w=[f"TT{g}0"])
                if STOP == 'E1':
                    return
                Rc, Rck = ga[:, :, 0, :], f"GA{g}"
                RTc, RTck = gt[:], f"GT{g}"
                Tc, Tck = TT[g][0], f"TT{g}0"
                NLV = 6 if ty == "P" else 2
                for lvl in range(1, NLV + 1):
                    sl = lvl % 2
                    if lvl < NLV:
                        bk, bkk = nb()
                        for i in range(4):
                            P.pe(lambda e, i=i, bk=bk, Rc=Rc, RTc=RTc: e.matmul(bk[:, i * 128:(i + 1) * 128], lhsT=RTc[:, i, :], rhs=Rc[:, i, :], start=True, stop=True),
                                 r=[Rck, RTck], w=[bkk])
                        rn, rnk = RN[g][sl], f"RN{g}{sl}"
                        P.act(lambda e, bk=bk, rn=rn: e.copy(out=rn[:].rearrange("p h t -> p (h t)"), in_=bk[:]), r=[bkk], w=[rnk])
                    bk, bkk = nb()
                    for i in range(4):
                        P.pe(lambda e, i=i, bk=bk, Rc=Rc, RTc=RTc: e.matmul(bk[:, i * 128:(i + 1) * 128], lhsT=Rc[:, i, :], rhs=RTc[:, i, :], start=True, stop=True),
                             r=[Rck, RTck], w=[bkk])
                    rtn, rtnk = RTN[g][sl], f"RTN{g}{sl}"
                    P.act(lambda e, bk=bk, rtn=rtn: e.copy(out=rtn[:].rearrange("p h t -> p (h t)"), in_=bk[:]), r=[bkk], w=[rtnk])
                    yield
                    bk, bkk = nb()
                    for i in range(4):
                        P.pe(lambda e, i=i, bk=bk, rtn=rtn, Tc=Tc: e.matmul(bk[:, i * 128:(i + 1) * 128], lhsT=rtn[:, i, :], rhs=Tc[:, i, :], start=True, stop=True),
                             r=[rtnk, Tck], w=[bkk])
                    tn, tnk = TT[g][sl], f"TT{g}{sl}"
                    P.dve(lambda e, bk=bk, tn=tn, Tc=Tc: e.tensor_tensor(out=tn[:].rearrange("p h t -> p (h t)"), in0=bk[:], in1=Tc[:].rearrange("p h t -> p (h t)"), op=ALU.add),
                          r=[bkk, Tck], w=[tnk])
                    yield
                    if lvl < NLV:
                        Rc, Rck = rn[:], rnk
                    RTc, RTck = rtn[:], rtnk
                    Tc, Tck = tn, tnk
                if STOP == 'E':
                    return
                yield
                yield "TD_DONE"
                bk, bkk = nb()
                for i in range(4):
                    h = 4 * g + i
                    P.pe(lambda e, i=i, h=h, bk=bk, gb=gb: e.matmul(bk[:, i * 64:(i + 1) * 64], lhsT=gb[:, i, 0, :], rhs=VB[:, 64 * h:64 * (h + 1)], start=True, stop=True),
                         r=[f"GB{g}", K("VB")], w=[bkk])
                P.act(lambda e, bk=bk, g=g: e.activation(out=X1N[g][:].rearrange("p h v -> p (h v)"), in_=bk[:, 0:256], func=AF.Copy, scale=-1.0), r=[bkk], w=[f"X1N{g}"])
                yield
                bk, bkk = nb()
                for i in range(4):
                    h = 4 * g + i
                    P.pe(lambda e, i=i, bk=bk, Tc=Tc, g=g: e.matmul(bk[:, i * 128:i * 128 + 64], lhsT=Tc[:, i, :], rhs=X1N[g][:, i, :], start=True, stop=True), r=[Tck, f"X1N{g}"], w=[bkk])
                    P.pe(lambda e, i=i, h=h, bk=bk, Tc=Tc: e.matmul(bk[:, i * 128 + 64:(i + 1) * 128], lhsT=Tc[:, i, :], rhs=KTM[:, 64 * h:64 * (h + 1)], start=True, stop=True), r=[Tck, K("KTM")], w=[bkk])
                uk, ukk = UK[g], f"UK{g}"
                bk4 = bk[:].rearrange("p (h a v) -> p h a v", a=2, v=64)
                P.act(lambda e, bk4=bk4, uk=uk: e.copy(out=uk[:, :, 0, :], in_=bk4[:, :, 0, :]), r=[bkk], w=[ukk])
                P.act(lambda e, bk4=bk4, uk=uk: e.activation(out=uk[:, :, 1, :], in_=bk4[:, :, 1, :], func=AF.Copy, scale=-1.0), r=[bkk], w=[ukk])
                if STOP == 'F':
                    return
                yield
                bk, bkk = nb()
                for i in range(4):
                    h = 4 * g + i
                    if ty == "P":
                        ob, col = 64 * (h % 2), (i // 2) * 128
                    else:
                        ob, col = 0, i * 128
                    P.pe(lambda e, h=h, ob=ob, col=col, bk=bk: e.matmul(bk[ob:ob + 64, col:col + 128], lhsT=RTm[:, 64 * h:64 * (h + 1)], rhs=IDB[:], start=True, stop=False), r=[K("RTm"), "IDB"], w=[bkk])
                    P.pe(lambda e, i=i, ob=ob, col=col, bk=bk, uk=uk, ga=ga: e.matmul(bk[ob:ob + 64, col:col + 128], lhsT=uk[:, i, 1, :], rhs=ga[:, i, 1, :], start=False, stop=True), r=[ukk, f"GA{g}"], w=[bkk])
                if ty == "P":
                    rh, rhk = RH[g], [f"RH{g}"]
                else:
                    rh, rhk = RHS, ["RH0", "RH1"]
                if ty == "P":
                    P.act(lambda e, bk=bk, rh=rh: e.copy(out=rh, in_=bk[:, 0:256].rearrange("p (a t) -> p a t", t=128)), r=[bkk], w=rhk)
                else:
                    P.act(lambda e, bk=bk, rh=rh: e.copy(out=rh, in_=bk[0:64, :].rearrange("p (a t) -> p a t", t=128)), r=[bkk], w=rhk)
                if ty == "P":
                    bk, bkk = nb()
                    for i in range(4):
                        h = 4 * g + i
                        ob, col = 64 * (h % 2), (i // 2) * 64
                        P.pe(lambda e, i=i, h=h, ob=ob, col=col, bk=bk, uk=uk: e.matmul(bk[ob:ob + 64, col:col + 64], lhsT=uk[:, i, 1, :], rhs=BE[:, 64 * h:64 * (h + 1)], start=True, stop=True), r=[ukk, K("BEb")], w=[bkk])
                    for j in range(2):
                        P.dve(lambda e, j=j, bk=bk, g=g: e.scalar_tensor_tensor(out=MM[g][:, j, :], in0=ID2[:], scalar=PC[:, 2 * g + j:2 * g + j + 1], in1=bk[:, j * 64:(j + 1) * 64], op0=ALU.mult, op1=ALU.add),
                              r=[bkk, "ID2", K("PC")], w=[f"MM{g}"])
                    for i in range(4):
                        h = 4 * g + i
                        p, b0, j = h // 2, 64 * (h % 2), i // 2
                        yield
                        vh = VB[:, 64 * h:64 * (h + 1)]
                        P.pe(lambda e, i=i, h=h, ga=ga, uk=uk: e.matmul(PSY[:, 64 * h:64 * (h + 1)], lhsT=ga[:, i, 1, :], rhs=uk[:, i, 0, :], start=True, stop=False), r=[f"GA{g}", ukk], w=["psy"])
                        P.pe(lambda e, i=i, h=h, gb=gb, vh=vh: e.matmul(PSY[:, 64 * h:64 * (h + 1)], lhsT=gb[:, i, 1, :], rhs=vh, start=False, stop=first), r=[f"GB{g}", K("VB")], w=["psy"])
                        if not first:
                            P.pe(lambda e, h=h, b0=b0, j=j, p=p, rh=rh: e.matmul(PSY[:, 64 * h:64 * (h + 1)], lhsT=rh[b0:b0 + 64, j, :], rhs=stprev[b0:b0 + 64, p, :], start=False, stop=True), r=[*rhk, stpk], w=["psy"])
                        P.pe(lambda e, i=i, h=h, b0=b0, p=p, uk=uk: e.matmul(PSS[b0:b0 + 64, p * 64:(p + 1) * 64], lhsT=BE[:, 64 * h:64 * (h + 1)], rhs=uk[:, i, 0, :], start=True, stop=False), r=[K("BEb"), ukk], w=["pss"])
                        P.pe(lambda e, h=h, b0=b0, p=p, vh=vh: e.matmul(PSS[b0:b0 + 64, p * 64:(p + 1) * 64], lhsT=KE[:, 64 * h:64 * (h + 1)], rhs=vh, start=False, stop=first), r=[K("KEb"), K("VB")], w=["pss"])
                        if not first:
                            P.pe(lambda e, b0=b0, j=j, p=p, g=g: e.matmul(PSS[b0:b0 + 64, p * 64:(p + 1) * 64], lhsT=MM[g][b0:b0 + 64, j, :], rhs=stprev[b0:b0 + 64, p, :], start=False, stop=True), r=[f"MM{g}", stpk], w=["pss"])
                else:
                    for i in range(4):
                        h = 4 * g + i
                        p, h2 = h // 2, h % 2
                        yield
                        vh = VB[:, 64 * h:64 * (h + 1)]
                        if h2 == 0:
                            P.dma("sp", lambda e, p=p: e.dma_start(out=S0[:], in_=stw[:, 2 * p:2 * p + 2, :, :].rearrange("b h v k -> (h v) b k")), w=["PRa", "PRb"])
                            for q in range(4):
                                bk, bkk = nb()
                                for bb in range(4):
                                    b = q * 4 + bb
                                    P.pe(lambda e, b=b, bb=bb, bk=bk: e.transpose(bk[0:64, bb * 128:(bb + 1) * 128], S0[:, b, :], IDN[:]), r=["PRa", "PRb", "IDN"], w=[bkk])
                                P.act(lambda e, q=q, bk=bk: e.copy(out=S0T[:, 4 * q:4 * q + 4, :].rearrange("p b t -> p (b t)"), in_=bk[0:64, :]), r=[bkk], w=["EIN", "EINV", "EEX", "EEND"])
                        if h == 0:
                            P.pool(lambda e: e.memset(RHm[:], 0.0), w=["SH1", "SH2", "CT", "CT2"])
                        rflat = CONV4[0:64, :]
                        P.pool(lambda e, i=i, rh=rh: e.tensor_copy(out=rflat[:, 0:2040].rearrange("p (b x) -> p b x", x=136)[:, :, 0:8],
                                                                    in_=rh[:, i, 0:120].rearrange("p (b t) -> p b t", t=8)), r=rhk, w=["SH1", "SH2", "CT", "CT2"])
                        P.pool(lambda e, i=i, rh=rh: e.tensor_copy(out=rflat[:, 2040:2048], in_=rh[:, i, 120:128]), r=rhk, w=["SH1", "SH2", "CT", "CT2"])
                        bmb = BM[:].unsqueeze(2).broadcast_to([128, NSEQ, 64])
                        P.pool(lambda e, i=i, uk=uk: e.tensor_tensor(out=KHm[:], in0=uk[:, i, 1, :].unsqueeze(1).broadcast_to([128, NSEQ, 64]), in1=bmb, op=ALU.mult), r=[ukk, "BM"], w=["PRa", "PRb"])
                        P.dve(lambda e, i=i, uk=uk: e.tensor_tensor(out=UTm[:], in0=uk[:, i, 0, :].unsqueeze(1).broadcast_to([128, NSEQ, 64]), in1=bmb, op=ALU.mult), r=[ukk, "BM"], w=["PRa", "PRb"])
                        P.dve(lambda e, vh=vh: e.tensor_tensor(out=Vm[:], in0=vh.unsqueeze(1).broadcast_to([128, NSEQ, 64]), in1=bmb, op=ALU.mult), r=[K("VB"), "BM"], w=["PRa", "PRb"])
                        P.pool(lambda e, h=h: e.tensor_tensor(out=DPC[:], in0=IDN[0:64, 0:64].unsqueeze(1).broadcast_to([64, NSEQ, 64]),
                                                              in1=PCS[:, h, :].unsqueeze(2).broadcast_to([64, NSEQ, 64]), op=ALU.mult), r=["IDN", "PCS"], w=["XM"])
                        P.pe(lambda e, i=i, h=h, ga=ga, uk=uk: e.matmul(PSY[:, 64 * h:64 * (h + 1)], lhsT=ga[:, i, 1, :], rhs=uk[:, i, 0, :], start=True, stop=False), r=[f"GA{g}", ukk], w=["psy"])
                        P.pe(lambda e, i=i, h=h, gb=gb, vh=vh: e.matmul(PSY[:, 64 * h:64 * (h + 1)], lhsT=gb[:, i, 1, :], rhs=vh, start=False, stop=False), r=[f"GB{g}", K("VB")], w=["psy"])
                        for b in range(NSEQ):
                            P.pe(lambda e, b=b, h=h, h2=h2: e.matmul(PSY[:, 64 * h:64 * (h + 1)], lhsT=RHm[:, b, :], rhs=S0T[:, b, 64 * h2:64 * (h2 + 1)], start=False, stop=(b == NSEQ - 1)), r=["SH1", "SH2", "CT", "CT2", "EIN", "EINV", "EEX", "EEND"], w=["psy"])
                        for q in range(2):
                            bk, bkk = nb()
                            for bb in range(8):
                                b = q * 8 + bb
                                P.pe(lambda e, b=b, bb=bb, h=h, bk=bk: e.matmul(bk[0:64, bb * 64:(bb + 1) * 64], lhsT=KHm[:, b, :], rhs=BE[:, 64 * h:64 * (h + 1)], start=True, stop=True), r=["PRa", "PRb", K("BEb")], w=[bkk])
                            P.dve(lambda e, q=q, bk=bk: e.tensor_tensor(out=MS[:, 8 * q:8 * q + 8, :].rearrange("p b k -> p (b k)"), in0=bk[0:64, :],
                                                                        in1=DPC[:, 8 * q:8 * q + 8, :].rearrange("p b k -> p (b k)"), op=ALU.add), r=[bkk, "XM"], w=["KK0", "SQ"])
                        so, sok = DPC, "XM"
                        for q in range(2):
                            bk, bkk = nb()
                            for bb in range(8):
                                b = q * 8 + bb
                                o_ = bk[0:64, bb * 64:(bb + 1) * 64]
                                P.pe(lambda e, b=b, o_=o_, h2=h2: e.matmul(o_, lhsT=S0T[:, b, 64 * h2:64 * (h2 + 1)], rhs=MS[:, b, :], start=True, stop=False), r=["EIN", "EINV", "EEX", "EEND", "KK0", "SQ"], w=[bkk])
                                P.pe(lambda e, b=b, o_=o_, h=h: e.matmul(o_, lhsT=UTm[:, b, :], rhs=BE[:, 64 * h:64 * (h + 1)], start=False, stop=False), r=["PRa", "PRb", K("BEb")], w=[bkk])
                                P.pe(lambda e, b=b, o_=o_, h=h: e.matmul(o_, lhsT=Vm[:, b, :], rhs=KE[:, 64 * h:64 * (h + 1)], start=False, stop=True), r=["PRa", "PRb", K("KEb")], w=[bkk])
                            P.act(lambda e, q=q, bk=bk, so=so: e.copy(out=so[:, 8 * q:8 * q + 8, :].rearrange("p b k -> p (b k)"), in_=bk[0:64, :]), r=[bkk], w=[sok])
                        P.dma("sp", lambda e, h=h, so=so: e.dma_start(out=wkv_s[:, h, :, :].rearrange("b v k -> v b k"), in_=so[:]), r=[sok])
            if ty == "P" and GI_MODE != "none":
                gens = [mach(0), mach(1)]
                alive = [True, True]
                passed = [False, False]
                while any(alive) and not (GI_MODE == "td" and all(passed)):
                    for gi in range(2):
                        if alive[gi] and not (GI_MODE == "td" and passed[gi]):
                            try:
                                if next(gens[gi]) == "TD_DONE":
                                    passed[gi] = True
                            except StopIteration:
                                alive[gi] = False
                    yield
                for gi in range(2):
                    if alive[gi]:
                        for _ in gens[gi]:
                            yield
            else:
                for g in range(2):
                    for _ in mach(g):
                        yield
            yield
            P.act(lambda e: e.copy(out=YY[:], in_=PSY[:]), r=["psy"], w=["YY"])
            if ty == "P":
                stn, stnk = STt[par], f"ST{par}"
                P.act(lambda e, stn=stn: e.copy(out=stn[:].rearrange("p a v -> p (a v)"), in_=PSS[:, 0:256]), r=["pss"], w=[stnk])
                if lastp:
                    bk, bkk = nb()
                    for p in range(4):
                        P.pe(lambda e, p=p, bk=bk, stn=stn: e.transpose(bk[0:64, p * 128:(p + 1) * 128], stn[:, p, :], IDN[:]), r=[stnk, "IDN"], w=[bkk])
                    P.act(lambda e, bk=bk: e.copy(out=WPO[:].rearrange("p a t -> p (a t)"), in_=bk[0:64, :]), r=[bkk], w=["RH0", "RH1"])
                    P.dma("sp", lambda e: e.dma_start(out=wkv_p.rearrange("(p h2) v k -> v p h2 k", h2=2), in_=WPO[:].rearrange("v p (h2 k) -> v p h2 k", h2=2)), r=["RH0", "RH1"])
            yield
            b8 = lambda t: t[:].unsqueeze(2).broadcast_to([128, 8, 64])
            P.dve(lambda e: e.tensor_reduce(out=S8[:], in_=v3(YY[:]), axis=AX.X, op=ALU.add), r=["YY"], w=["S8c"])
            P.dve(lambda e: e.tensor_tensor(out=SQ[:], in0=YY[:], in1=YY[:], op=ALU.mult), r=["YY"], w=["GNS"])
            P.dve(lambda e: e.tensor_reduce(out=S8b[:], in_=v3(SQ[:]), axis=AX.X, op=ALU.add), r=["GNS"], w=["S8d"])
            P.dve(lambda e: e.tensor_scalar(out=S8[:], in0=S8[:], scalar1=1.0 / HS, scalar2=None, op0=ALU.mult), r=["S8c"], w=["S8c"])
            P.dve(lambda e: e.tensor_tensor(out=S8e[:], in0=S8[:], in1=S8[:], op=ALU.mult), r=["S8c"], w=["S8e"])
            P.dve(lambda e: e.scalar_tensor_tensor(out=S8b[:], in0=S8b[:], scalar=1.0 / HS, in1=S8e[:], op0=ALU.mult, op1=ALU.subtract), r=["S8d", "S8e"], w=["S8d"])
            P.dve(lambda e: e.tensor_scalar(out=S8b[:], in0=S8b[:], scalar1=64e-5, scalar2=None, op0=ALU.add), r=["S8d"], w=["S8d"])
            P.act(lambda e: e.activation(out=S8b[:], in_=S8b[:], func=AF.Sqrt), r=["S8d"], w=["S8d"])
            P.dve(lambda e: e.reciprocal(out=S8b[:], in_=S8b[:]), r=["S8d"], w=["S8d"])
            P.dve(lambda e: e.tensor_tensor(out=v3(YY[:]), in0=v3(YY[:]), in1=b8(S8), op=ALU.subtract), r=["YY", "S8c"], w=["YY"])
            P.dve(lambda e: e.tensor_tensor(out=v3(YY[:]), in0=v3(YY[:]), in1=b8(S8b), op=ALU.mult), r=["YY", "S8d"], w=["YY"])
            P.dve(lambda e: e.tensor_tensor(out=YY[:], in0=YY[:], in1=GG[:], op=ALU.mult), r=["YY", K("GG")], w=["YY"])
            P.dve(lambda e: e.tensor_tensor(out=YC[:, 512:1024], in0=YY[:], in1=BON[:], op=ALU.add), r=["YY", K("BON")], w=[K("YCb")])
            if "ycat" in dbg_out:
                dbgdump("ycat", YC[:], K("YCa")) if ti == dbg_ti[0] else None
            if STOP == 'H':
                return
            yield
            for k in range(8):
                P.pe(lambda e, k=k: e.transpose(PSB[:, k * 128:(k + 1) * 128], YC[:, k * 128:(k + 1) * 128], IDB[:]), r=[K("YCa"), K("YCb"), "IDB"], w=["psb"])
            P.act(lambda e: e.copy(out=YCT[:].rearrange("p k t -> p (k t)"), in_=PSB[:]), r=["psb"], w=["YCT"])
            for j in range(2):
                bk, bkk = nb()
                wr, wrk = wchunk(wbo, j * 512, 512)
                for k in range(8):
                    P.pe(lambda e, k=k, bk=bk, wr=wr: e.matmul(bk[:], lhsT=YCT[:, k, :], rhs=wr[:, k, :], start=(k == 0), stop=(k == 7)), r=["YCT", wrk], w=[bkk])
                P.dve(lambda e, j=j, bk=bk: e.tensor_tensor(out=X1T[:, j * 512:(j + 1) * 512], in0=bk[:], in1=x[:, j * 512:(j + 1) * 512], op=ALU.add), r=[bkk, xk], w=["X1T"])
            P.dma("sp", lambda e: e.dma_start(out=x1s[ti * 128:(ti + 1) * 128, :], in_=X1T[:]), r=["X1T"], w=["x1s"])
            yield

        def drain(gen):
            for _ in gen:
                pass

        def load_masks(names):
            for nm in names:
                P.dma("sp", lambda e, nm=nm: e.dma_start(out=MKT[nm][:], in_=cd[nm + "S"]), w=["M_" + nm])

        dbg_ti = [dbg.get("_ti", 0) if dbg else 0]
        PIPE = os.environ.get("KNOPIPE", "") != "1"
        if not PIPE:
            for ti in range(NT):
                if ti == NPT:
                    load_masks(("tri", "tgt", "ma", "mb", "mt"))
                drain(front(ti))
                drain(back(ti))
        else:
            drain(front(0))
            for ti in range(NT):
                if ti == NPT:
                    load_masks(("ma", "mb", "mt"))
                bgen = back(ti)
                fgen = None
                if ti + 1 < NT:
                    if ti + 1 == NPT:
                        load_masks(("tri", "tgt"))
                    fgen = front(ti + 1)
                bdone = fdone = False
                while not (bdone and (fgen is None or fdone)):
                    for _ in range(int(os.environ.get('KRR', '6'))):
                        if not bdone:
                            try:
                                next(bgen)
                            except StopIteration:
                                bdone = True
                    if fgen is not None and not fdone:
                        try:
                            next(fgen)
                        except StopIteration:
                            fdone = True

    if _dry:
        return wq
    P.barrier()

    if SKIP2:
        return nc, P.finalize()
    with ExitStack() as st2:
        def SB2(name, shape, dt=F32):
            return st2.enter_context(nc.sbuf_tensor(name, list(shape), dt))

        PS2 = [st2.enter_context(nc.psum_tensor(f"q{i}", [128, 512], F32)) for i in range(6)]
        PSB2 = st2.enter_context(nc.psum_tensor("qb", [128, 1024], BF16))
        bank2 = [0]

        def nb2():
            i = bank2[0] % 6
            bank2[0] += 1
            return PS2[i], f"q{i}"

        NSTG = 8
        RNG = 3
        W1R = [SB2(f"W1R{i}", [128, 8, 512], BF16) for i in range(RNG)]
        W2R = [SB2(f"W2R{i}", [128, 4, D], BF16) for i in range(RNG)]
        X1 = SB2("X1A", [128, NT, D])
        H2T = SB2("H2T", [128, 8, NT * 128], BF16)
        G2T = SB2("G2T", [128, 8])
        NFb = SB2("NFb", [128, D])
        IDB2 = SB2("IDB2", [128, 128], BF16)
        JK2 = SB2("JK2", [128, D])
        HB2 = SB2("HB2", [128, D], BF16)
        SS2 = SB2("SS2", [128, 1])
        RS2 = SB2("RS2", [128, 1])
        HR = [SB2(f"HR{i}", [128, 512]) for i in range(2)]
        HID = [SB2(f"HID{i}", [128, 4, 512], BF16) for i in range(2)]
        YO = [SB2(f"YO{i}", [128, D]) for i in range(2)]
        P.dma("sp", lambda e: e.dma_start(out=G2T[:], in_=g2T_d), w=["G2T"])
        P.dma("sp", lambda e: e.dma_start(out=NFb[:], in_=nf_d.partition_broadcast(128)), w=["NFb"])
        P.dma("pool", lambda e: e.dma_start(out=IDB2[:], in_=cd["ident"]), w=["IDB2"])

        def load_stage(s):
            rb = s % RNG
            P.dma("pool", lambda e: e.dma_start(out=W1R[rb][:], in_=w_ff1[:, s * 512:(s + 1) * 512].rearrange("(k p) f -> p k f", p=128)), w=[f"W1R{rb}"])
            P.dma("pool", lambda e: e.dma_start(out=W2R[rb][:], in_=w_ff2[s * 512:(s + 1) * 512, :].rearrange("(c p) d -> p c d", p=128)), w=[f"W2R{rb}"])

        for s in range(min(RNG, NSTG)):
            load_stage(s)
        for ti in range(NT):
            P.dma("sp", lambda e, ti=ti: e.dma_start(out=X1[:, ti, :], in_=x1s[ti * 128:(ti + 1) * 128, :]), w=[f"X1_{ti}"])
        def preamble(ti):
            xk = f"X1_{ti}"
            P.act(lambda e, ti=ti: e.activation(out=JK2[:], in_=X1[:, ti, :], func=AF.Square, accum_out=SS2[:]), r=[xk], w=["JK2", "SS2"])
            P.dve(lambda e: e.tensor_scalar(out=RS2[:], in0=SS2[:], scalar1=1.0 / D, scalar2=1e-6, op0=ALU.mult, op1=ALU.add), r=["SS2"], w=["RS2"])
            P.act(lambda e: e.activation(out=RS2[:], in_=RS2[:], func=AF.Sqrt), r=["RS2"], w=["RS2"])
            P.dve(lambda e: e.reciprocal(out=RS2[:], in_=RS2[:]), r=["RS2"], w=["RS2"])
            P.act(lambda e, ti=ti: e.activation(out=HB2[:], in_=X1[:, ti, :], func=AF.Copy, scale=RS2[:, 0:1]), r=[xk, "RS2"], w=["HB2"])
            for k in range(8):
                P.pe(lambda e, k=k: e.transpose(PSB2[:, k * 128:(k + 1) * 128], HB2[:, k * 128:(k + 1) * 128], IDB2[:]), r=["HB2", "IDB2"], w=["qb"])
            P.dve(lambda e, ti=ti: e.tensor_tensor(out=H2T[:, :, ti * 128:(ti + 1) * 128], in0=PSB2[:].rearrange("p (k t) -> p k t", t=128),
                                                  in1=G2T[:].unsqueeze(2).broadcast_to([128, 8, 128]), op=ALU.mult), r=["qb", "G2T"], w=[f"H2T_{ti}"])
        groups = []
        t0 = 0
        while t0 < NT:
            n = min(4, NT - t0)
            groups.append((t0, n))
            t0 += n
        def final_tile(ti):
            xk = f"X1_{ti}"
            yo, yok = YO[ti % 2], f"YO{ti % 2}"
            P.act(lambda e, ti=ti: e.activation(out=JK2[:], in_=X1[:, ti, :], func=AF.Square, accum_out=SS2[:]), r=[xk], w=["JK2", "SS2"])
            P.dve(lambda e: e.tensor_scalar(out=RS2[:], in0=SS2[:], scalar1=1.0 / D, scalar2=1e-6, op0=ALU.mult, op1=ALU.add), r=["SS2"], w=["RS2"])
            P.act(lambda e: e.activation(out=RS2[:], in_=RS2[:], func=AF.Sqrt), r=["RS2"], w=["RS2"])
            P.dve(lambda e: e.reciprocal(out=RS2[:], in_=RS2[:]), r=["RS2"], w=["RS2"])
            P.dve(lambda e, ti=ti, yo=yo: e.scalar_tensor_tensor(out=yo[:], in0=X1[:, ti, :], scalar=RS2[:, 0:1], in1=NFb[:], op0=ALU.mult, op1=ALU.mult), r=[xk, "RS2", "NFb"], w=[yok])
            dst = y_s if ti == NPT else y_p[ti * 128:(ti + 1) * 128, :]
            P.dma("sp", lambda e, yo=yo, dst=dst: e.dma_start(out=dst, in_=yo[:]), r=[yok])


        items = [(s_, t0, n) for s_ in range(NSTG) for (t0, n) in groups]

        def ffn1(k):
            s_, t0, n = items[k]
            rb = s_ % RNG
            ntok = n * 128
            hid, hidk = HID[k % 2], f"HID{k % 2}"
            for fc in range(4):
                bk, bkk = nb2()
                for kk in range(8):
                    P.pe(lambda e, kk=kk, fc=fc, bk=bk, t0=t0, ntok=ntok, rb=rb: e.matmul(bk[:, 0:ntok], lhsT=W1R[rb][:, kk, fc * 128:(fc + 1) * 128], rhs=H2T[:, kk, t0 * 128:t0 * 128 + ntok], start=(kk == 0), stop=(kk == 7)),
                         r=[f"W1R{rb}"] + [f"H2T_{t}" for t in range(t0, t0 + n)], w=[bkk], cost=0.22)
                hr, hrk = HR[fc % 2], f"HR{fc % 2}"
                P.act(lambda e, bk=bk, hr=hr, ntok=ntok: e.activation(out=hr[:, 0:ntok], in_=bk[:, 0:ntok], func=AF.Relu), r=[bkk], w=[hrk])
                P.pool(lambda e, hr=hr, hid=hid, fc=fc, ntok=ntok: e.tensor_tensor(out=hid[:, fc, 0:ntok], in0=hr[:, 0:ntok], in1=hr[:, 0:ntok], op=ALU.mult), r=[hrk], w=[hidk])

        def ffn2(k):
            s_, t0, n = items[k]
            rb = s_ % RNG
            hid, hidk = HID[k % 2], f"HID{k % 2}"
            for tl in range(n):
                ti = t0 + tl
                for half in range(2):
                    bk, bkk = nb2()
                    for fc in range(4):
                        P.pe(lambda e, fc=fc, bk=bk, tl=tl, half=half, hid=hid, rb=rb: e.matmul(bk[:], lhsT=hid[:, fc, tl * 128:(tl + 1) * 128], rhs=W2R[rb][:, fc, half * 512:(half + 1) * 512], start=(fc == 0), stop=(fc == 3)),
                             r=[hidk, f"W2R{rb}"], w=[bkk], cost=0.22)
                    P.dve(lambda e, bk=bk, ti=ti, half=half: e.tensor_tensor(out=X1[:, ti, half * 512:(half + 1) * 512], in0=bk[:], in1=X1[:, ti, half * 512:(half + 1) * 512], op=ALU.add),
                          r=[bkk, f"X1_{ti}"], w=[f"X1_{ti}"])
                if s_ == NSTG - 1:
                    final_tile(ti)

        FFNP = os.environ.get("KNOFFNP", "") != "1"

        def pre_group(k):
            s_, t0, n = items[k]
            if s_ == 0:
                for t in range(t0, t0 + n):
                    preamble(t)

        if FFNP:
            pre_group(0)
            ffn1(0)
        for k in range(len(items)):
            if FFNP:
                if k + 1 < len(items):
                    pre_group(k + 1)
                    ffn1(k + 1)
            else:
                pre_group(k)
                ffn1(k)
            ffn2(k)
            s_, t0, n = items[k]
            if (t0, n) == groups[-1] and s_ + RNG < NSTG:
                load_stage(s_ + RNG)
        if os.environ.get('KSCHED', '1') == '1':
            P.reschedule()
        stats = P.finalize()
    return nc, stats


_CACHE = {}


def make_in_maps(inputs, NPT=16, ncores=8):
    f = lambda a: np.ascontiguousarray(np.asarray(a, dtype=np.float32))
    c = _consts()
    shared = {
        "w_in": f(inputs["w_in"][0]), "w_out": f(inputs["w_out"][0]),
        "w_ff1": f(inputs["w_ff1"][0]), "w_ff2": f(inputs["w_ff2"][0]),
        "g1T": f(np.asarray(inputs["norm1_g"][0]).reshape(8, 128).T),
        "g2T": f(np.asarray(inputs["norm2_g"][0]).reshape(8, 128).T),
        "mu": f(inputs["mu"][0]).reshape(1, RC),
        "convw": f(inputs["conv_w"][0]).reshape(1, 3 * CW_),
        "k_k": f(inputs["k_k"][0]).reshape(1, RW), "k_a": f(inputs["k_a"][0]).reshape(1, RW),
        "r_k": f(inputs["r_k"][0]).reshape(1, RW),
        "lnx_g": f(inputs["lnx_g"][0]).reshape(1, RW), "lnx_b": f(inputs["lnx_b"][0]).reshape(1, RW),
        "normf": f(inputs["normf_g"]).reshape(1, D),
        "w0": f(inputs["w0"][0]).reshape(1, RW), "a0": f(inputs["a0"][0]).reshape(1, RW),
        "w2a": f(np.concatenate([np.asarray(inputs["w2"][0]), np.asarray(inputs["a2"][0])], 0)),
        "g2": f(inputs["g2"][0]),
    }
    for k, v in c.items():
        shared["c_" + k] = f(v)
    xpr = np.asarray(inputs["x_prompt"], dtype=np.float32)
    xsm = np.asarray(inputs["x_sample"], dtype=np.float32)
    maps = []
    for ci in range(ncores):
        m = dict(shared)
        m["xp"] = f(xpr[ci, :NPT * 128])
        sl = slice(ci * NSEQ, (ci + 1) * NSEQ)
        m["xs"] = f(xsm[sl].reshape(NSEQ * DT_, D))
        m["stc"] = f(inputs["state_conv"][0][sl])
        m["sts"] = f(inputs["state_shift"][0][sl])
        m["stw"] = f(inputs["state_wkv"][0][sl])
        maps.append(m)
    return maps


def gather(results, NPT=16, ncores=8):
    g = lambda name: [np.asarray(r[name], dtype=np.float32) for r in results]
    y_p = np.stack(g("y_p"), 0)
    y_s = np.stack(g("y_s"), 0).reshape(ncores * NSEQ, DT_, D)
    conv_p = np.stack(g("conv_p"), 0)[None]
    shift_p = np.concatenate(g("shift_p"), 0)[None]
    wkv_p = np.stack(g("wkv_p"), 0)[None]
    conv_s = np.concatenate(g("conv_s"), 0)[None]
    shift_s = np.concatenate(g("shift_s"), 0)[None]
    wkv_s = np.concatenate(g("wkv_s"), 0)[None]
    return (y_p, y_s, conv_p, shift_p, wkv_p, conv_s, shift_s, wkv_s)


def kernel(**inputs):
    NPT = 16
    if "nc" not in _CACHE:
        _CACHE["nc"] = build(NPT)[0]
    nc = _CACHE["nc"]
    maps = make_in_maps(inputs, NPT, 8)
    res = run_bass_kernel_spmd(nc, maps, core_ids=list(range(8)))
    return gather(res.results, NPT, 8)
```
